# Optimizing a Trainium2 kernel written in Bass

```python
import jax, jax.numpy as jnp
from jax import lax
import numpy as np

D_MODEL = 1024
BATCH = 4
SEQ = 8192
DEPTH = 1

ATT_GROUPS = ((128, 1), (512, 4), (2048, 16))
HEADS_PER_GROUP = 4
N_ATT_HEADS = HEADS_PER_GROUP * len(ATT_GROUPS)
ATT_HEAD_DIM = 64
ATT_WIDTH = N_ATT_HEADS * ATT_HEAD_DIM
ATT_BLOCK = 64
NEG_INF = -1e30
GLA_HEADS = 4
GLA_DK = 64
GLA_DV = 128
GLA_KEY_WIDTH = GLA_HEADS * GLA_DK
GLA_VAL_WIDTH = GLA_HEADS * GLA_DV
GLA_GATE_RANK = 16
GLA_TAU = 16.0
GLA_CHUNK = 64
N_DIRECTIONS = 2
REL_BUCKETS = 32
REL_MAX_DISTANCE = 1024
DEEPNORM_ALPHA = (2.0 * DEPTH) ** 0.25
DEEPNORM_BETA = (8.0 * DEPTH) ** -0.25
LN_EPS = 1e-5
RMS_EPS = 1e-6

_IN_SPLITS = (ATT_WIDTH, ATT_WIDTH, ATT_WIDTH, ATT_WIDTH,
              GLA_KEY_WIDTH, GLA_KEY_WIDTH, GLA_VAL_WIDTH, GLA_VAL_WIDTH,
              GLA_GATE_RANK, GLA_GATE_RANK,
              D_MODEL, D_MODEL)
_IN_WIDTH = sum(_IN_SPLITS)
_SPLIT_POINTS = tuple(int(v) for v in np.cumsum(_IN_SPLITS)[:-1])
_VALUE_SEGMENTS = (2, 6)

kernel_name = "hybrid_dilated_gla_encoder_layer"


def _t5_bucket(rel):
    half = REL_BUCKETS // 2
    max_exact = half // 2
    ret = (rel > 0).astype(np.int32) * half
    n = np.abs(rel)
    large = max_exact + (np.log(np.maximum(n, 1) / max_exact)
                         / np.log(REL_MAX_DISTANCE / max_exact) * (half - max_exact)).astype(np.int32)
    large = np.minimum(large, half - 1)
    return ret + np.where(n < max_exact, n, large)


def _dilated_window_attention(q, k, v, bias_table, window, dilation):
    bsz, seq, nh, hd = q.shape
    half = window // (2 * dilation)
    blk = ATT_BLOCK
    L = seq // dilation
    nb = -(-L // blk)
    Lp = nb * blk

    def strided(t):
        t = t.reshape(bsz, L, dilation, nh, hd).transpose(0, 2, 3, 1, 4)
        return jnp.pad(t, ((0, 0), (0, 0), (0, 0), (0, Lp - L), (0, 0)))

    def band(t):
        t = jnp.pad(strided(t), ((0, 0), (0, 0), (0, 0), (blk, blk), (0, 0)))
        t = t.reshape(bsz, dilation, nh, nb + 2, blk, hd)
        return jnp.concatenate([t[:, :, :, :-2], t[:, :, :, 1:-1], t[:, :, :, 2:]], axis=-2)

    qs = strided(q).reshape(bsz, dilation, nh, nb, blk, hd)
    kb, vb = band(k), band(v)

    a = np.arange(blk)[:, None]
    c = np.arange(3 * blk)[None, :]
    dl = c - blk - a
    bias = jnp.transpose(bias_table[_t5_bucket(dl * dilation)], (2, 0, 1)).astype(jnp.float32)
    kpos = np.arange(nb)[:, None, None] * blk + (c - blk)[None]
    valid = (np.abs(dl)[None] <= half) & (kpos >= 0) & (kpos < L)

    logits = jnp.einsum('brhnqd,brhnkd->brhnqk', qs, kb).astype(jnp.float32) * (hd ** -0.5)
    logits = jnp.where(valid, logits + bias[:, None], NEG_INF)
    m = jnp.max(logits, axis=-1, keepdims=True)
    p = jnp.exp(logits - m)
    den = jnp.sum(p, axis=-1, keepdims=True)
    out = jnp.einsum('brhnqk,brhnkd->brhnqd', p, vb.astype(jnp.float32)) / den
    lse = (m + jnp.log(den))[..., 0]

    out = out.reshape(bsz, dilation, nh, Lp, hd)[:, :, :, :L]
    out = out.transpose(0, 3, 1, 2, 4).reshape(bsz, seq, nh, hd)
    lse = lse.reshape(bsz, dilation, nh, Lp)[:, :, :, :L]
    lse = lse.transpose(0, 3, 1, 2).reshape(bsz, seq, nh)
    return out, lse


def _gla_direction(q, k, v, log_a):
    bsz, seq, nh, dk = q.shape
    dv = v.shape[-1]
    C = GLA_CHUNK
    n = seq // C

    def chunk(t):
        return t.reshape(bsz, n, C, nh, t.shape[-1]).transpose(0, 3, 1, 2, 4)

    q, k, v, log_a = chunk(q), chunk(k), chunk(v), chunk(log_a)
    b = jnp.cumsum(log_a, axis=3)
    b_last = b[:, :, :, -1:]
    q_dec = q * jnp.exp(b)
    k_dec = k * jnp.exp(-b)
    lower = np.tril(np.ones((C, C), dtype=bool))
    scores = jnp.where(lower, jnp.einsum('bhnik,bhnjk->bhnij', q_dec, k_dec), 0.0)
    o_intra = jnp.einsum('bhnij,bhnjv->bhniv', scores, v)

    kv = jnp.einsum('bhnjk,bhnjv->bhnkv', k * jnp.exp(b_last - b), v)
    decay = jnp.exp(b_last[:, :, :, 0])

    def step(state, inp):
        dec, kv_c = inp
        return dec[..., None] * state + kv_c, state

    _, states = lax.scan(step, jnp.zeros((bsz, nh, dk, dv), q.dtype),
                         (jnp.moveaxis(decay, 2, 0), jnp.moveaxis(kv, 2, 0)))
    states = jnp.moveaxis(states, 0, 2)
    o = o_intra + jnp.einsum('bhnik,bhnkv->bhniv', q_dec, states)
    return o.transpose(0, 2, 3, 1, 4).reshape(bsz, seq, nh, dv)


def _hybrid_layer(x, w_in, gla_gate_w2, gla_gate_b, gla_norm_g, rel_bias,
                  w_att_out, w_gla_out, w_out, ln_g, ln_b):
    bsz, seq, _ = x.shape
    f32 = lambda t: t.astype(jnp.float32)
    proj = x @ w_in
    (qa, ka, va, ga, qb, kb, vb, gb, lr_f, lr_b, gate_att, gate_gla) = jnp.split(
        proj, _SPLIT_POINTS, axis=-1)

    qa = qa.reshape(bsz, seq, N_ATT_HEADS, ATT_HEAD_DIM)
    ka = ka.reshape(bsz, seq, N_ATT_HEADS, ATT_HEAD_DIM)
    va = va.reshape(bsz, seq, N_ATT_HEADS, ATT_HEAD_DIM)
    outs, lses = [], []
    for g, (window, dilation) in enumerate(ATT_GROUPS):
        sl = slice(g * HEADS_PER_GROUP, (g + 1) * HEADS_PER_GROUP)
        o, l = _dilated_window_attention(qa[:, :, sl], ka[:, :, sl], va[:, :, sl],
                                         rel_bias[:, sl], window, dilation)
        outs.append(o)
        lses.append(l)
    o_all = jnp.stack(outs, axis=2)
    w_grp = jax.nn.softmax(jnp.stack(lses, axis=2), axis=2)
    y_att = (o_all * w_grp[..., None]).reshape(bsz, seq, ATT_WIDTH).astype(x.dtype)
    y_att = (y_att * jax.nn.silu(ga)) @ w_att_out

    qg = f32(qb).reshape(bsz, seq, GLA_HEADS, GLA_DK) * (GLA_DK ** -0.5)
    kg = f32(kb).reshape(bsz, seq, GLA_HEADS, GLA_DK)
    vg = f32(vb).reshape(bsz, seq, GLA_HEADS, GLA_DV)

    def log_decay(lr, w2, bias):
        z = f32(lr) @ f32(w2) + f32(bias)
        return (jax.nn.log_sigmoid(z) / GLA_TAU).reshape(bsz, seq, GLA_HEADS, GLA_DK)

    flip = lambda t: t[:, ::-1]
    o_fwd = _gla_direction(qg, kg, vg, log_decay(lr_f, gla_gate_w2[0], gla_gate_b[0]))
    o_bwd = flip(_gla_direction(flip(qg), flip(kg), flip(vg),
                                flip(log_decay(lr_b, gla_gate_w2[1], gla_gate_b[1]))))
    o = o_fwd + o_bwd
    o = o * lax.rsqrt(jnp.mean(o * o, axis=-1, keepdims=True) + RMS_EPS)
    o = o.reshape(bsz, seq, GLA_VAL_WIDTH) * f32(gla_norm_g)
    y_gla = (o.astype(x.dtype) * jax.nn.silu(gb)) @ w_gla_out

    merged = jax.nn.sigmoid(gate_att) * y_att + jax.nn.sigmoid(gate_gla) * y_gla
    h = f32(DEEPNORM_ALPHA * x + merged @ w_out)
    mu = jnp.mean(h, axis=-1, keepdims=True)
    var = jnp.mean(jnp.square(h - mu), axis=-1, keepdims=True)
    y = (h - mu) * lax.rsqrt(var + LN_EPS) * f32(ln_g) + f32(ln_b)
    return y.astype(x.dtype)


def setup_inputs(seed: int = 0) -> dict:
    key = jax.random.key(seed)
    ks = jax.random.split(key, 12)
    nrm = lambda k, shape, scale: jax.random.normal(k, shape, jnp.float32) * scale
    col_scale = jnp.concatenate([
        jnp.full((n,), DEEPNORM_BETA if i in _VALUE_SEGMENTS else 1.0, jnp.float32)
        for i, n in enumerate(_IN_SPLITS)])
    return {
        "x": nrm(ks[0], (BATCH, SEQ, D_MODEL), 1.0),
        "w_in": nrm(ks[1], (DEPTH, D_MODEL, _IN_WIDTH), D_MODEL ** -0.5) * col_scale,
        "gla_gate_w2": nrm(ks[2], (DEPTH, N_DIRECTIONS, GLA_GATE_RANK, GLA_KEY_WIDTH), GLA_GATE_RANK ** -0.5),
        "gla_gate_b": nrm(ks[3], (DEPTH, N_DIRECTIONS, GLA_KEY_WIDTH), 0.1),
        "gla_norm_g": 1.0 + nrm(ks[4], (DEPTH, GLA_VAL_WIDTH), 0.01),
        "rel_bias": nrm(ks[5], (REL_BUCKETS, N_ATT_HEADS), 0.5),
        "w_att_out": nrm(ks[6], (DEPTH, ATT_WIDTH, D_MODEL), ATT_WIDTH ** -0.5 * DEEPNORM_BETA),
        "w_gla_out": nrm(ks[7], (DEPTH, GLA_VAL_WIDTH, D_MODEL), GLA_VAL_WIDTH ** -0.5 * DEEPNORM_BETA),
        "w_out": nrm(ks[8], (DEPTH, D_MODEL, D_MODEL), D_MODEL ** -0.5 * DEEPNORM_BETA),
        "ln_g": 1.0 + nrm(ks[9], (DEPTH, D_MODEL), 0.01),
        "ln_b": nrm(ks[10], (DEPTH, D_MODEL), 0.01),
    }


def reference(x, w_in, gla_gate_w2, gla_gate_b, gla_norm_g, rel_bias,
              w_att_out, w_gla_out, w_out, ln_g, ln_b):
    for layer in range(DEPTH):
        x = _hybrid_layer(x, w_in[layer], gla_gate_w2[layer], gla_gate_b[layer],
                          gla_norm_g[layer], rel_bias, w_att_out[layer], w_gla_out[layer],
                          w_out[layer], ln_g[layer], ln_b[layer])
    return x
```

```python
import os
import numpy as np
import ml_dtypes
from contextlib import ExitStack
import concourse.bass as bass
import concourse.mybir as mybir
from concourse.bass_utils import run_bass_kernel_spmd

F32 = mybir.dt.float32
BF16 = mybir.dt.bfloat16
AF = mybir.ActivationFunctionType
ALU = mybir.AluOpType

D_MODEL = 1024
SEQ = 8192
OWN = 4096
NCORES = 8
IN_W = 6688
C_QA, C_KA, C_VA, C_GA, C_QB, C_KB, C_VB, C_GB, C_LRF, C_LRB, C_GATT, C_GGLA = (
    0, 768, 1536, 2304, 3072, 3328, 3584, 4096, 4608, 4624, 4640, 5664)
DILS = (1, 4, 16)
ALPHA = 2.0 ** 0.25
LN_EPS = 1e-5
RMS_EPS = 1e-6

ENGS = ("pe", "act", "dve", "pool", "sp")


class Prog:
    def __init__(self, nc):
        self.nc = nc
        self.sems = {e: nc.alloc_semaphore("s_" + e) for e in ("pe", "act", "dve", "pool")}
        self.cnt = {e: 0 for e in ("pe", "act", "dve", "pool")}
        self.dsem = {}
        self.ops = []
        self.tagmap = {}

    def op(self, eng, fn, reads=(), writes=()):
        self.ops.append(dict(eng=eng, fn=fn, reads=tuple(reads), writes=tuple(writes), dma=None))

    def dma(self, eng, fn, dsem, reads=(), writes=()):
        kind = "s" if eng == "pool" else "h"
        nk = sum(1 for k_ in self.tagmap if k_[1] == kind)
        dsem = self.tagmap.setdefault((dsem, kind), "g%s%d" % (kind, nk))
        if dsem not in self.dsem:
            self.dsem[dsem] = [self.nc.alloc_semaphore("d_" + dsem), 0]
        self.ops.append(dict(eng=eng, fn=fn, reads=tuple(reads), writes=tuple(writes), dma=dsem))

    def flush(self):
        ops = self.ops
        self.ops = []
        self.tagmap = {}
        n = len(ops)
        last_w = {}
        readers = {}
        deps = [None] * n
        for i, o in enumerate(ops):
            d = set()
            for k in o["reads"]:
                if k in last_w:
                    d.add(last_w[k])
            for k in o["writes"]:
                if k in last_w:
                    d.add(last_w[k])
                for r in readers.get(k, ()):
                    d.add(r)
            d.discard(i)
            for k in o["writes"]:
                last_w[k] = i
                readers[k] = []
            for k in o["reads"]:
                if k not in o["writes"]:
                    readers.setdefault(k, []).append(i)
            if o["eng"] == "pe" and o["dma"] is None:
                d = {j for j in d if not (ops[j]["eng"] == "pe" and ops[j]["dma"] is None)}
            if o["dma"] is not None:
                sk = {j for j in d if (ops[j]["dma"] is not None and set(ops[j]["writes"]) & set(o["writes"])
                                       and not (set(ops[j]["reads"]) | set(o["reads"])))}
                for j in sk:
                    d = d | deps[j]
                d = d - sk
            deps[i] = d
        flagged = set()
        for d in deps:
            flagged |= d
        val = [None] * n
        cnt = dict(self.cnt)
        dcnt = {k: v[1] for k, v in self.dsem.items()}
        for i, o in enumerate(ops):
            if o["dma"] is not None:
                dcnt[o["dma"]] += 16
                val[i] = (o["dma"], dcnt[o["dma"]])
            elif i in flagged:
                cnt[o["eng"]] += 1
                val[i] = (o["eng"], cnt[o["eng"]])
        streams = {e: [] for e in ENGS}
        known = {e: {} for e in ENGS}
        run_d = {k: v[1] for k, v in self.dsem.items()}
        for i, o in enumerate(ops):
            waits = {}
            for j in deps[i]:
                s, v = val[j]
                if ops[j]["dma"] is not None:
                    v = max(v, run_d[s])
                if waits.get(s, 0) < v:
                    waits[s] = v
            wl = []
            kn = known[o["eng"]]
            for s, v in waits.items():
                if kn.get(s, 0) < v:
                    kn[s] = v
                    wl.append((s, v))
            streams[o["eng"]].append((o, wl, val[i]))
            if o["dma"] is not None:
                run_d[o["dma"]] = val[i][1]
        final_waits = []
        for k, v in dcnt.items():
            if known["sp"].get(k, 0) < v:
                final_waits.append((k, v))
        self.cnt = cnt
        for k in self.dsem:
            self.dsem[k][1] = dcnt[k]
        self._emit(streams, final_waits)

    def _sem(self, s):
        if s in self.sems:
            return self.sems[s]
        return self.dsem[s][0]

    def _emit(self, streams, final_waits):
        nc = self.nc
        me = self

        def run(eng_obj, name):
            for (o, wl, v) in streams[name]:
                for (s, x) in wl:
                    eng_obj.wait_ge(me._sem(s), x)
                ins = o["fn"](eng_obj)
                if v is not None:
                    ins.then_inc(me._sem(v[0]), 16 if o["dma"] is not None else 1)
            if name == "sp":
                for (s, x) in final_waits:
                    eng_obj.wait_ge(me._sem(s), x)

        with nc.Block() as blk:
            @blk.sync
            def _(e):
                run(e, "sp")

            @blk.tensor
            def _(e):
                run(e, "pe")

            @blk.scalar
            def _(e):
                run(e, "act")

            @blk.vector
            def _(e):
                run(e, "dve")

            @blk.gpsimd
            def _(e):
                run(e, "pool")


class Ctx:
    pass


def stage_P(nc, P, T):
    fm = []
    for i in range(6):
        fm.append((C_QA + 128 * i, 128, T.qaT, 128 * i, False, BF16, 8))
    for i in range(6):
        fm.append((C_KA + 128 * i, 128, T.kaT, 128 * i, False, BF16, 10))
    for i in range(6):
        fm.append((C_VA + 128 * i, 128, T.vaT, 128 * i, False, BF16, 10))
    for i in range(6):
        fm.append((C_GA + 128 * i, 128, T.sgaT, 128 * i, True, BF16, 8))
    for i in range(2):
        fm.append((C_QB + 128 * i, 128, T.qgT, 128 * i, False, F32, 8))
    for i in range(2):
        fm.append((C_KB + 128 * i, 128, T.kgT, 128 * i, False, F32, 8))
    for i in range(4):
        fm.append((C_GB + 128 * i, 128, T.sgbT, 128 * i, True, BF16, 8))
    fm.append((C_LRF, 32, T.lrT, 0, False, F32, 16))
    NFM = len(fm)
    offs = []
    o = 0
    for c in fm:
        offs.append(o)
        o += c[1]
    WF = o
    w_v = T.w_in.rearrange("(k p) c -> p k c", p=128)
    x_v = T.xT.rearrange("(k p) t -> p k t", p=128)
    with ExitStack() as es:
        sb = lambda name, shape, dt: es.enter_context(nc.sbuf_tensor("p_" + name, shape, dt))
        ps = lambda name, shape, dt: es.enter_context(nc.psum_tensor("p_" + name, shape, dt))
        wfm = sb("wfm", [128, 8, WF], BF16)
        wtm = sb("wtm", [128, 8, 768], BF16)
        xt = [sb(f"xt{i}", [128, 8, 512], BF16) for i in range(2)]
        ob = [sb(f"ob{i}", [128, 512], BF16) for i in range(4)]
        of = [sb(f"of{i}", [128, 512], F32) for i in range(2)]
        kst = [sb(f"kst{i}", [128, 256], F32) for i in range(2)]
        vst = [sb(f"vst{i}", [128, 512], BF16) for i in range(2)]
        pf = [ps(f"pf{i}", [128, 512], F32) for i in range(4)]
        pt = [ps(f"pt{i}", [128, 512], F32) for i in range(4)]

        def load_x(tt):
            s = tt % 2
            for kh in range(2):
                P.dma("pool", lambda e, s=s, tt=tt, kh=kh: e.dma_start(
                    out=xt[s][:, 4 * kh:4 * kh + 4, :], in_=x_v[:, 4 * kh:4 * kh + 4, 512 * tt:512 * tt + 512]),
                    f"xt{s}", writes=[("xt", s)])

        def load_w(ci):
            c0, ncol = fm[ci][0], fm[ci][1]
            for kh in range(2):
                P.dma("pool", lambda e, c0=c0, ncol=ncol, kh=kh, o=offs[ci]: e.dma_start(
                    out=wfm[:, 4 * kh:4 * kh + 4, o:o + ncol], in_=w_v[:, 4 * kh:4 * kh + 4, c0:c0 + ncol]),
                    f"wfm{ci}", writes=[("wfm", ci)])

        load_x(0)
        for ci in range(NFM):
            load_w(ci)
        for kh in range(2):
            for half in range(2):
                P.dma("pool", lambda e, kh=kh, half=half: e.dma_start(
                    out=wtm[:, 4 * kh:4 * kh + 4, 384 * half:384 * half + 384],
                    in_=w_v[:, 4 * kh:4 * kh + 4, C_KB + 384 * half:C_KB + 384 * half + 384]),
                    "wtm", writes=["wtm"])
        ev = 0
        nb = 0
        nf = 0
        npf = 0
        NTT = int(os.environ.get('P_NTT', '16'))
        for tt in range(NTT):
            s = tt % 2
            if tt + 1 < NTT:
                load_x(tt + 1)
            tok = slice(512 * tt, 512 * tt + 512)
            for ci in range(NFM):
                c0, ncol, dst, r0, silu, odt, ntiles = fm[ci]
                if tt >= ntiles or ci >= int(os.environ.get('P_NCH', '99')):
                    continue
                pp = npf % 4
                npf += 1
                for k in range(8):
                    P.op("pe", lambda e, pp=pp, k=k, s=s, o=offs[ci], ncol=ncol: e.matmul(
                        pf[pp][0:ncol, :], lhsT=wfm[:, k, o:o + ncol], rhs=xt[s][:, k, :],
                        start=(k == 0), stop=(k == 7)),
                        reads=[("wfm", ci), ("xt", s)], writes=[("pf", pp)])
                if odt == BF16:
                    bi = nb % 4
                    nb += 1
                    buf, bkey, dkey = ob[bi], ("ob", bi), f"ob{bi}"
                else:
                    bi = nf % 2
                    nf += 1
                    buf, bkey, dkey = of[bi], ("of", bi), f"of{bi}"
                if silu:
                    P.op("act", lambda e, buf=buf, pp=pp, ncol=ncol: e.activation(
                        out=buf[0:ncol, :], in_=pf[pp][0:ncol, :], func=AF.Silu),
                        writes=[("pf", pp), bkey])
                elif ev % 2 == 0:
                    P.op("dve", lambda e, buf=buf, pp=pp, ncol=ncol: e.tensor_copy(
                        out=buf[0:ncol, :], in_=pf[pp][0:ncol, :]),
                        writes=[("pf", pp), bkey])
                    ev += 1
                else:
                    P.op("act", lambda e, buf=buf, pp=pp, ncol=ncol: e.activation(
                        out=buf[0:ncol, :], in_=pf[pp][0:ncol, :], func=AF.Copy),
                        writes=[("pf", pp), bkey])
                    ev += 1
                P.dma("sp", lambda e, buf=buf, dst=dst, r0=r0, ncol=ncol, tok=tok: e.dma_start(
                    out=dst[r0:r0 + ncol, tok], in_=buf[0:ncol, :]), dkey, reads=[bkey], writes=[])
            for sub in range(0 if os.environ.get('P_NOTM') else 4):
                st = 4 * tt + sub
                pa, pb = pt[(2 * st) % 4], pt[(2 * st + 1) % 4]
                ka, kb = ("pt", (2 * st) % 4), ("pt", (2 * st + 1) % 4)
                for half, (pp, pk) in enumerate(((pa, ka), (pb, kb))):
                    for k in range(8):
                        P.op("pe", lambda e, pp=pp, k=k, s=s, sub=sub, half=half: e.matmul(
                            pp[:, 0:384], lhsT=xt[s][:, k, 128 * sub:128 * sub + 128],
                            rhs=wtm[:, k, 384 * half:384 * half + 384], start=(k == 0), stop=(k == 7)),
                            reads=["wtm", ("xt", s)], writes=[pk])
                b2 = st % 2
                TMM = int(os.environ.get('P_TMMODE', '3'))
                if TMM < 2:
                    continue
                P.op("act", lambda e, b2=b2, pa=pa: e.activation(out=kst[b2][:], in_=pa[:, 0:256], func=AF.Copy),
                     writes=[ka, ("kst", b2)])
                P.op("act", lambda e, b2=b2, pa=pa: e.activation(out=vst[b2][:, 0:128], in_=pa[:, 256:384], func=AF.Copy),
                     writes=[ka, ("vstA", b2)])
                P.op("dve", lambda e, b2=b2, pb=pb: e.tensor_copy(out=vst[b2][:, 128:512], in_=pb[:, 0:384]),
                     writes=[kb, ("vstB", b2)])
                if TMM < 3:
                    continue
                P.dma("sp", lambda e, b2=b2, st=st: e.dma_start(out=T.kTM[128 * st:128 * st + 128, :], in_=kst[b2][:]),
                      f"kst{b2}", reads=[("kst", b2)])
                P.dma("sp", lambda e, b2=b2, st=st: e.dma_start(out=T.vTM[128 * st:128 * st + 128, :], in_=vst[b2][:]),
                      f"vst{b2}", reads=[("vstA", b2), ("vstB", b2)])
        P.flush()


def run_pipeline(items, nph, lag):
    n = len(items)
    for step in range(n + (nph - 1) * lag):
        for ph in range(nph - 1, -1, -1):
            i = step - ph * lag
            if 0 <= i < n and items[i][ph] is not None:
                items[i][ph]()


def stage_S1(nc, P, T):
    LAG = 1
    with ExitStack() as es:
        sb = lambda name, shape, dt: es.enter_context(nc.sbuf_tensor("a_" + name, shape, dt))
        ps = lambda name, shape, dt: es.enter_context(nc.psum_tensor("a_" + name, shape, dt))
        ebm = sb("ebm", [128, 12, 256], F32)
        ebm0 = sb("ebm0", [64, 12, 128], F32)
        emk = sb("emk", [128, 256], F32)
        emk0 = sb("emk0", [64, 128], F32)
        identb = sb("identb", [128, 128], BF16)
        qs = [sb(f"qs{i}", [128, 4096], BF16) for i in range(2)]
        ks = [sb(f"ks{i}", [128, 5120], BF16) for i in range(2)]
        vs = [sb(f"vs{i}", [128, 5120], BF16) for i in range(2)]
        va = [sb(f"va{i}", [128, 48, 2, 65], BF16) for i in range(2)]
        NE = 4
        esb = [sb(f"esb{i}", [128, 2, 256], F32) for i in range(NE)]
        ptb = [sb(f"ptb{i}", [128, 2, 256], BF16) for i in range(NE)]
        nst = [sb(f"nst{i}", [64, 2, 2048], F32) for i in range(2)]
        dst_ = [sb(f"dst{i}", [128, 2, 2048], F32) for i in range(2)]
        sT = [ps(f"sT{i}", [128, 1024], F32) for i in range(2)]
        oT = [ps(f"oT{i}", [128, 512], F32) for i in range(2)]
        tp = [ps(f"tp{i}", [128, 1024], BF16) for i in range(2)]

        P.dma("sp", lambda e: e.dma_start(out=ebm[:], in_=T.ebias.rearrange("p (h c) -> p h c", h=12)), "c1", writes=["ebm"])
        P.dma("sp", lambda e: e.dma_start(out=ebm0[:], in_=T.ebias0.rearrange("p (h c) -> p h c", h=12)), "c1", writes=["ebm0"])
        P.dma("sp", lambda e: e.dma_start(out=emk[:], in_=T.emask), "c1", writes=["emk"])
        P.dma("sp", lambda e: e.dma_start(out=emk0[:], in_=T.emask0), "c1", writes=["emk0"])
        P.dma("pool", lambda e: e.dma_start(out=identb[:], in_=T.ident), "c2", writes=["identb"])

        def load_pair(p):
            s = p % 2
            P.dma("sp", lambda e, s=s, p=p: e.dma_start(out=qs[s][:], in_=T.qaT[128 * p:128 * p + 128, :]),
                  f"qs{s}", writes=[("qs", s)])
            P.dma("sp", lambda e, s=s, p=p: e.dma_start(out=ks[s][:], in_=T.kaT[128 * p:128 * p + 128, :]),
                  f"ks{s}", writes=[("ks", s)])
            P.dma("sp", lambda e, s=s, p=p: e.dma_start(out=vs[s][:], in_=T.vaT[128 * p:128 * p + 128, :]),
                  f"vs{s}", writes=[("vs", s)])

        load_pair(0)
        load_pair(1)
        P.op("act", lambda e: e.activation(out=ebm[:], in_=ebm[:], func=AF.Exp), reads=["ebm"], writes=["ebm"])
        P.op("act", lambda e: e.activation(out=ebm0[:], in_=ebm0[:], func=AF.Exp), reads=["ebm0"], writes=["ebm0"])
        for h in range(12):
            P.op("dve", lambda e, h=h: e.tensor_tensor(out=ebm[:, h, :], in0=ebm[:, h, :], in1=emk[:], op=ALU.mult),
                 reads=["ebm", "emk"], writes=["ebm"])
            P.op("dve", lambda e, h=h: e.tensor_tensor(out=ebm0[:, h, :], in0=ebm0[:, h, :], in1=emk0[:], op=ALU.mult),
                 reads=["ebm0", "emk0"], writes=["ebm0"])
        for i_ in range(2):
            P.op("pool", lambda e, i_=i_: e.memset(va[i_][:], 1.0), writes=[("va", i_, c_) for c_ in range(48)])

        items = []
        cnt = dict(t=0, tp=0, sup=0)

        def add_vbuild(p):
            s = p % 2
            d = DILS[p // 2]
            Lr = OWN // d
            NC = Lr // 128 + 1
            for r in range(d):
                for c in range(NC):
                    if c == 0:
                        sl, n = slice(r, r + 63 * d + 1, d), 64
                    else:
                        st = d * (128 * c - 64) + r
                        sl, n = slice(st, st + 127 * d + 1, d), 128
                    idx = r * NC + c
                    tq = cnt["tp"] % 2
                    cnt["tp"] += 1

                    def fA(tq=tq, s=s, sl=sl, n=n):
                        P.op("pe", lambda e: e.transpose(out=tp[tq][0:n, 0:128], in_=vs[s][:, sl], identity=identb[:]),
                             reads=[("vs", s), "identb"], writes=[("tp", tq)])

                    def fB(tq=tq, s=s, idx=idx, n=n):
                        if idx % 2 == 0:
                            P.op("act", lambda e: e.activation(
                                out=va[s][0:n, idx, :, 0:64], in_=tp[tq][0:n, 0:128].rearrange("p (h e) -> p h e", h=2),
                                func=AF.Copy), writes=[("tp", tq), ("va", s, idx)])
                        else:
                            P.op("dve", lambda e: e.tensor_copy(
                                out=va[s][0:n, idx, :, 0:64], in_=tp[tq][0:n, 0:128].rearrange("p (h e) -> p h e", h=2)),
                                writes=[("tp", tq), ("va", s, idx)])
                    items.append([fA, fB, None, None, None, None, None])

        def add_attn(p):
            s = p % 2
            g = p // 2
            d = DILS[g]
            Lr = OWN // d
            NC = Lr // 128 + 1

            def ksl(r, c):
                if c == 0:
                    return slice(r, r + 63 * d + 1, d), 64
                st = d * (128 * c - 64) + r
                return slice(st, st + 127 * d + 1, d), 128

            for u in range(2):
                if d == 1:
                    tiles = [(0, m) for m in range(16 * u, 16 * u + 16)]
                elif d == 4:
                    tiles = [(r, m) for m in range(4 * u, 4 * u + 4) for r in range(4)]
                else:
                    tiles = [(r, u) for r in range(16)]
                sbuf_i = cnt["sup"] % 2
                cnt["sup"] += 1
                for ti, (r, m) in enumerate(tiles):
                    qst = d * 128 * m + r
                    qsl = slice(qst, qst + 127 * d + 1, d)
                    osl = slice(qst - 2048 * u, qst - 2048 * u + 127 * d + 1, d)
                    slA, nA = ksl(r, m)
                    slB, nB = ksl(r, m + 1)
                    iA, iB = r * NC + m, r * NC + m + 1
                    t2 = cnt["t"] % 2
                    t4 = cnt["t"] % NE
                    cnt["t"] += 1
                    last = (ti == len(tiles) - 1)

                    def fA(t2=t2, s=s, slA=slA, nA=nA, slB=slB, qsl=qsl):
                        for hh in range(2):
                            pb = 64 * hh
                            c0 = 512 * hh
                            P.op("pe", lambda e, pb=pb, c0=c0: e.matmul(
                                sT[t2][0:nA, c0:c0 + 128], lhsT=ks[s][pb:pb + 64, slA], rhs=qs[s][pb:pb + 64, qsl],
                                start=True, stop=True), reads=[("ks", s), ("qs", s)], writes=[("sT", t2, hh)])
                            P.op("pe", lambda e, pb=pb, c0=c0: e.matmul(
                                sT[t2][:, c0 + 128:c0 + 256], lhsT=ks[s][pb:pb + 64, slB], rhs=qs[s][pb:pb + 64, qsl],
                                start=True, stop=True), reads=[("ks", s), ("qs", s)], writes=[("sT", t2, hh)])

                    def fB(t2=t2, t4=t4, nA=nA):
                        sv = sT[t2][:].rearrange("p (h c) -> p h c", h=2)
                        wk = [("sT", t2, 0), ("sT", t2, 1), ("esb", t4)]
                        if nA == 128:
                            P.op("act", lambda e: e.activation(out=esb[t4][:], in_=sv[:, :, 0:256], func=AF.Exp, scale=0.125), writes=wk)
                        else:
                            P.op("act", lambda e: e.activation(out=esb[t4][0:64, :, 0:128], in_=sv[0:64, :, 0:128], func=AF.Exp, scale=0.125), writes=wk)
                            P.op("act", lambda e: e.activation(out=esb[t4][:, :, 128:256], in_=sv[:, :, 128:256], func=AF.Exp, scale=0.125), writes=wk)

                    def fC(t4=t4, nA=nA, p=p):
                        if nA == 128:
                            P.op("dve", lambda e: e.tensor_tensor(out=ptb[t4][:], in0=esb[t4][:], in1=ebm[:, 2 * p:2 * p + 2, :], op=ALU.mult),
                                 reads=[("esb", t4), "ebm"], writes=[("ptb", t4)])
                        else:
                            P.op("dve", lambda e: e.tensor_tensor(out=ptb[t4][0:64, :, 0:128], in0=esb[t4][0:64, :, 0:128],
                                                                  in1=ebm0[:, 2 * p:2 * p + 2, :], op=ALU.mult),
                                 reads=[("esb", t4), "ebm0"], writes=[("ptb", t4)])
                            P.op("dve", lambda e: e.tensor_tensor(out=ptb[t4][:, :, 128:256], in0=esb[t4][:, :, 128:256],
                                                                  in1=ebm[:, 2 * p:2 * p + 2, 128:256], op=ALU.mult),
                                 reads=[("esb", t4), "ebm"], writes=[("ptb", t4)])

                    def fD(t2=t2, t4=t4, s=s, iA=iA, iB=iB, nA=nA):
                        for hh in range(2):
                            P.op("pe", lambda e, hh=hh: e.matmul(
                                oT[t2][0:65, 128 * hh:128 * hh + 128], lhsT=va[s][0:nA, iA, hh, :], rhs=ptb[t4][0:nA, hh, 0:128],
                                start=True, stop=False), reads=[("va", s, iA), ("ptb", t4)], writes=[("oT", t2)])
                            P.op("pe", lambda e, hh=hh: e.matmul(
                                oT[t2][0:65, 128 * hh:128 * hh + 128], lhsT=va[s][:, iB, hh, :], rhs=ptb[t4][:, hh, 128:256],
                                start=False, stop=True), reads=[("va", s, iB), ("ptb", t4)], writes=[("oT", t2)])

                    def fE(t2=t2, sbuf_i=sbuf_i, osl=osl, last=last, p=p, u=u):
                        P.op("act", lambda e: e.activation(
                            out=nst[sbuf_i][0:64, :, osl], in_=oT[t2][0:64, 0:256].rearrange("p (h c) -> p h c", h=2), func=AF.Copy),
                            writes=[("oT", t2), ("nst", sbuf_i)])
                        P.op("dve", lambda e: e.tensor_copy(
                            out=dst_[sbuf_i][64:65, :, osl], in_=oT[t2][64:65, 0:256].rearrange("p (h c) -> p h c", h=2)),
                            writes=[("oT", t2), ("dst", sbuf_i)])
                        if last:
                            tk = slice(2048 * u, 2048 * u + 2048)
                            P.dma("pool", lambda e: e.dma_start(
                                out=T.numT[128 * p:128 * p + 128, tk].rearrange("(h e) t -> e h t", h=2), in_=nst[sbuf_i][:]),
                                f"nst{sbuf_i}", reads=[("nst", sbuf_i)])
                            P.dma("pool", lambda e: e.dma_start(
                                out=T.den[2 * p:2 * p + 2, tk].rearrange("(o h) t -> o h t", o=1), in_=dst_[sbuf_i][64:65, :, :]),
                                f"dst{sbuf_i}", reads=[("dst", sbuf_i)])
                    items.append([fA, fB, None, fC, None, fD, fE])

        def add_load(p):
            items.append([lambda p=p: load_pair(p), None, None, None, None, None, None])

        add_vbuild(0)
        for p in range(6):
            n0 = len(items)
            add_attn(p)
            at = items[n0:]
            del items[n0:]
            vb = []
            if p + 1 < 6:
                add_vbuild(p + 1)
                vb = items[n0:]
                del items[n0:]
            merged = at[:12]
            rest = at[12:]
            for j in range(max(len(rest), len(vb))):
                if j < len(rest):
                    merged.append(rest[j])
                if j < len(vb):
                    merged.append(vb[j])
            items.extend(merged)
            if p + 2 < 6:
                add_load(p + 2)
        run_pipeline(items, 7, LAG)
        P.flush()


def stage_S2(nc, P, T, prefetch=None):
    NR = 8
    with ExitStack() as es:
        sb = lambda name, shape, dt: es.enter_context(nc.sbuf_tensor("g_" + name, shape, dt))
        ps = lambda name, shape, dt: es.enter_context(nc.psum_tensor("g_" + name, shape, dt))
        tri = sb("tri", [128, 512], F32)
        ind = sb("ind", [128, 2], F32)
        smask = sb("smask", [128, 256], F32)
        smask4 = sb("smask4", [128, 512], F32)
        onesm = sb("onesm", [128, 128], BF16)
        ones32 = sb("ones32", [128, 128], F32)
        w2d = [sb(f"w2d{i}", [17, 256], F32) for i in range(2)]
        gng = sb("gng", [128, 4], F32)
        gexp = sb("gexp", [128, 2, 2, 128], F32)
        rm = [sb(f"rm{i}", [128, 1], F32) for i in range(2)]
        eb_all = sb("eb_all", [128, 2, 64, 128], BF16)
        efb = sb("efb", [128, 2, NR, 128], BF16)
        RF = sb("RF", [128, 2, NR, 128], F32)
        RB = sb("RB", [128, 2, NR, 128], F32)
        lrd = [[sb(f"lr{d}{i}", [17, 512], F32) for i in range(2)] for d in range(2)]
        lrh = [[sb(f"lrh{d}{i}", [17, 512], BF16) for i in range(2)] for d in range(2)]
        lrl = [[sb(f"lrl{d}{i}", [17, 512], BF16) for i in range(2)] for d in range(2)]
        w2h = [sb(f"w2h{i}", [17, 256], BF16) for i in range(2)]
        w2l = [sb(f"w2l{i}", [17, 256], BF16) for i in range(2)]
        tri_bf = sb("tri_bf", [128, 512], BF16)
        ind_bf = sb("ind_bf", [128, 2], BF16)
        sph = [sb(f"sph{i}", [128, 512], BF16) for i in range(2)]
        spl = [sb(f"spl{i}", [128, 512], BF16) for i in range(2)]
        DK, DV, DQ, DS = 8, 8, 6, 8
        ktm = [sb(f"ktm{i}", [128, 256], F32) for i in range(DK)]
        vtm = [sb(f"vtm{i}", [128, 512], BF16) for i in range(DV)]
        qg = [sb(f"qg{i}", [128, 2, 128], F32) for i in range(DQ)]
        kg = [sb(f"kg{i}", [128, 2, 128], F32) for i in range(DQ)]
        sgb = [sb(f"sgb{i}", [128, 4, 128], BF16) for i in range(DS)]
        yst = [sb(f"yst{i}", [128, 4, 512], BF16) for i in range(2)]
        sp_sb = [sb(f"sp_sb{i}", [128, 512], F32) for i in range(2)]
        e_sb = sp_sb
        ewd = [sb(f"ewd{i}", [128, 260], F32) for i in range(4)]
        ebp = [sb(f"ebp{i}", [128, 512], F32) for i in range(2)]
        ebn = [sb(f"ebn{i}", [128, 512], F32) for i in range(2)]
        kpc = [[sb(f"kp{c}{i}", [128, 256], BF16) for i in range(2)] for c in range(2)]
        qd = [sb(f"qd{i}", [128, 4, 128], BF16) for i in range(4)]
        kd = [sb(f"kd{i}", [128, 4, 128], BF16) for i in range(4)]
        scm = [sb(f"scm{i}", [128, 2, 512], BF16) for i in range(2)]
        osb = [sb(f"osb{i}", [128, 2, 2, 128], F32) for i in range(4)]
        osq = [sb(f"osq{i}", [128, 2, 2, 128], BF16) for i in range(2)]
        rstd = [sb(f"rstd{i}", [128, 512], F32) for i in range(2)]
        rs1 = rstd
        Zp = ps("Z", [128, 512], F32)
        Wp = ps("W", [128, 512], F32)
        Bp = ps("B", [128, 512], F32)
        SC = [ps(f"SC{i}", [128, 512], F32) for i in range(2)]
        KV = ps("KV", [128, 512], F32)
        Op = [ps(f"O{i}", [128, 512], F32) for i in range(2)]

        P.dma("sp", lambda e: e.dma_start(out=tri[:], in_=T.tri), "c1", writes=["tri"])
        P.dma("sp", lambda e: e.dma_start(out=ind[:], in_=T.ind), "c1", writes=["ind"])
        P.dma("sp", lambda e: e.dma_start(out=smask[:], in_=T.smask), "c1", writes=["smask"])
        P.dma("sp", lambda e: e.dma_start(out=gng[:], in_=T.gng), "c1", writes=["gng"])
        for d in range(2):
            P.dma("sp", lambda e, d=d: e.dma_start(out=w2d[d][0:16, :], in_=T.w2[16 * d:16 * d + 16, :]), "c1", writes=[("w2d", d)])
            P.dma("sp", lambda e, d=d: e.dma_start(out=w2d[d][16:17, :], in_=T.gbias[0:1, 256 * d:256 * d + 256]), "c1", writes=[("w2d", d)])
            for i in range(2):
                P.op("dve", lambda e, d=d, i=i: e.memset(lrd[d][i][:], 1.0), writes=[("lr", d, i)])
            P.op("act", lambda e, d=d: e.activation(out=w2h[d][:], in_=w2d[d][:], func=AF.Copy), reads=[("w2d", d)], writes=[("w2h", d)])
            P.op("dve", lambda e, d=d: e.tensor_tensor(out=w2l[d][:], in0=w2d[d][:], in1=w2h[d][:], op=ALU.subtract),
                 reads=[("w2d", d), ("w2h", d)], writes=[("w2l", d)])
        P.op("dve", lambda e: e.memset(onesm[:], 1.0 / 128.0), writes=["onesm"])
        P.op("act", lambda e: e.activation(out=tri_bf[:], in_=tri[:], func=AF.Copy), reads=["tri"], writes=["tri_bf"])
        P.op("act", lambda e: e.activation(out=ind_bf[:], in_=ind[:], func=AF.Copy), reads=["ind"], writes=["ind_bf"])
        P.op("dve", lambda e: e.memset(ones32[:], 1.0), writes=["ones32"])
        P.op("dve", lambda e: e.memset(rm[0][:], 0.0), writes=["rm"])
        P.op("dve", lambda e: e.memset(rm[1][:], 0.0), writes=["rm"])
        P.op("dve", lambda e: e.memset(rm[0][0:64, :], 1.0), writes=["rm"])
        P.op("dve", lambda e: e.memset(rm[1][64:128, :], 1.0), writes=["rm"])
        for cc_ in range(2):
            for i_ in range(2):
                P.op("pool", lambda e, cc_=cc_, i_=i_: e.memset(kpc[cc_][i_][:], 0.0), writes=[("kp", cc_, i_)])
        P.op("pool", lambda e: e.memset(RF[:], 0.0), writes=["RFall"])
        P.op("pool", lambda e: e.memset(RB[:], 0.0), writes=["RBall"])
        for d in range(2):
            for hf in range(2):
                P.op("dve", lambda e, d=d, hf=hf: e.tensor_copy(out=smask4[:, (2 * d + hf) * 128:(2 * d + hf) * 128 + 128],
                                                                 in_=smask[:, 128 * d:128 * d + 128]),
                     reads=["smask"], writes=["smask4"])
        for h in range(4):
            P.op("dve", lambda e, h=h: e.tensor_scalar(out=gexp[:, h % 2, h // 2, :], in0=ones32[:], scalar1=gng[:, h:h + 1], scalar2=None,
                                                       op0=ALU.mult), reads=["ones32", "gng"], writes=["gexp"])
        if prefetch is not None:
            prefetch()

        cn = dict(k=0, v=0, q=0, s=0, it=0)
        lr_slot = {}

        def kv_mm(kslot, vslot, order, bank=None, bkey="KV"):
            KVb = KV if bank is None else bank
            for ci, cc in enumerate(order):
                for hf in range(2):
                    for q_ in range(2):
                        col = (2 * ci + hf) * 128
                        P.op("pe", lambda e, cc=cc, hf=hf, q_=q_, col=col: e.matmul(
                            KVb[64 * q_:64 * q_ + 64, col:col + 128], lhsT=kpc[cc][kslot][:, 128 * hf + 64 * q_:128 * hf + 64 * q_ + 64],
                            rhs=vtm[vslot][:, (2 * hf + q_) * 128:(2 * hf + q_) * 128 + 128], start=True, stop=True),
                            reads=[("kp", cc, kslot), ("vtm", vslot)], writes=[bkey])

        def kp_ops(it2, kslot, wslot):
            for cc in range(2):
                rows = slice(64 * cc, 64 * cc + 64)
                P.op("pool", lambda e, cc=cc, rows=rows: e.tensor_tensor(
                    out=kpc[cc][it2][rows, :], in0=ktm[kslot][rows, :], in1=ewd[wslot][rows, 0:256], op=ALU.mult),
                    reads=[("ktm", kslot), ("ewd", wslot)], writes=[("kp", cc, it2)])

        def lr_split(d, ls):
            P.op("act", lambda e: e.activation(out=lrh[d][ls][:], in_=lrd[d][ls][:], func=AF.Copy),
                 reads=[("lr", d, ls)], writes=[("lrh", d, ls)])
            P.op("dve", lambda e: e.tensor_tensor(out=lrl[d][ls][:], in0=lrd[d][ls][:], in1=lrh[d][ls][:], op=ALU.subtract),
                 reads=[("lr", d, ls), ("lrh", d, ls)], writes=[("lrl", d, ls)])

        def z_mm(d, ls, tl, bank=None, bkey="Z"):
            Zb = Zp if bank is None else bank
            ops = ((lrh, w2h), (lrh, w2l), (lrl, w2h))
            for i_, (a_, b_) in enumerate(ops):
                P.op("pe", lambda e, a_=a_, b_=b_, i_=i_: e.matmul(
                    Zb[:, 256 * d:256 * d + 256], lhsT=a_[d][ls][0:17, tl], rhs=b_[d][0:17, :], start=(i_ == 0), stop=(i_ == 2)),
                    reads=[("lrh", d, ls), ("lrl", d, ls), ("w2h", d), ("w2l", d)], writes=[bkey])

        def softplus(it2, cs, bank=None, bkey="Z"):
            Zb = Zp if bank is None else bank
            P.op("act", lambda e: e.activation(out=sp_sb[it2][:, cs], in_=Zb[:, cs], func=AF.Exp, scale=-1.0), writes=[bkey, ("sp", it2)])
            P.op("act", lambda e: e.activation(out=sph[it2][:, cs], in_=sp_sb[it2][:, cs], func=AF.Ln, bias=1.0),
                 reads=[("sp", it2)], writes=[("sph", it2)])
            P.op("act", lambda e: e.activation(out=sp_sb[it2][:, cs], in_=sp_sb[it2][:, cs], func=AF.Ln, bias=1.0), writes=[("sp", it2)])

        def sp_lo(it2, cs):
            P.op("dve", lambda e: e.tensor_tensor(out=spl[it2][:, cs], in0=sp_sb[it2][:, cs], in1=sph[it2][:, cs], op=ALU.subtract),
                 reads=[("sp", it2), ("sph", it2)], writes=[("spl", it2)])

        def w_tot_mm(it2, d, bank=None, bkey="W"):
            Wb = Wp if bank is None else bank
            c0 = 256 * d
            wsl = slice(256 + 128 * d, 256 + 128 * d + 128)
            for i_, src in enumerate((sph, spl)):
                P.op("pe", lambda e, src=src, i_=i_: e.matmul(Wb[:, 0:256], lhsT=tri_bf[:, wsl], rhs=src[it2][:, c0:c0 + 256],
                                                              start=(i_ == 0), stop=(i_ == 1)),
                     reads=["tri_bf", ("sph", it2), ("spl", it2)], writes=[bkey])
            for hf in range(2):
                for i_, src in enumerate((sph, spl)):
                    P.op("pe", lambda e, src=src, i_=i_, hf=hf: e.matmul(
                        Wb[:, 256 + 2 * hf:258 + 2 * hf], lhsT=src[it2][:, c0 + 128 * hf:c0 + 128 * hf + 128], rhs=ind_bf[:],
                        start=(i_ == 0), stop=(i_ == 1)), reads=["ind_bf", ("sph", it2), ("spl", it2)], writes=[bkey])

        items = []
        for st in range(63, -1, -1):
            it = cn["it"]
            cn["it"] += 1
            it2, it4 = it % 2, it % 4
            blk = st // 4
            kslot = cn["k"] % DK
            cn["k"] += 1
            vslot = cn["v"] % DV
            cn["v"] += 1
            if st % 4 == 3:
                lr_slot[blk] = (15 - blk) % 2
            ls = lr_slot[blk]
            tl = slice(128 * (st % 4), 128 * (st % 4) + 128)

            def f0(st=st, blk=blk, ls=ls):
                if st % 4 == 3:
                    P.dma("sp", lambda e: e.dma_start(out=lrd[1][ls][0:16, :], in_=T.lrT[16:32, 512 * blk:512 * blk + 512]),
                          f"lr1{ls}", writes=[("lr", 1, ls)])

            def f0b(st=st, ls=ls):
                if st % 4 == 3:
                    lr_split(1, ls)

            zb, zk = (Zp, "Z") if it2 == 0 else (SC[0], ("SC", 0))
            wb_, wk = (Wp, "W") if it2 == 0 else (SC[1], ("SC", 1))
            kb_, kk = (KV, "KV") if it2 == 0 else (Bp, "B")

            def f1(st=st, ls=ls, tl=tl, kslot=kslot, vslot=vslot, zb=zb, zk=zk):
                z_mm(1, ls, tl, zb, zk)
                P.dma("sp", lambda e: e.dma_start(out=ktm[kslot][:], in_=T.kTM[128 * st:128 * st + 128, :]),
                      f"ktm{kslot}", writes=[("ktm", kslot)])
                P.dma("sp", lambda e: e.dma_start(out=vtm[vslot][:], in_=T.vTM[128 * st:128 * st + 128, :]),
                      f"vtm{vslot}", writes=[("vtm", vslot)])

            def f2(it2=it2, zb=zb, zk=zk):
                softplus(it2, slice(256, 512), zb, zk)

            def f2b(it2=it2):
                sp_lo(it2, slice(256, 512))

            def f3(it2=it2, wb_=wb_, wk=wk):
                w_tot_mm(it2, 1, wb_, wk)

            def f4(it4=it4, wb_=wb_, wk=wk):
                P.op("act", lambda e: e.activation(out=ewd[it4][:], in_=wb_[:, 0:260], func=AF.Exp), writes=[wk, ("ewd", it4)])

            def f5(it2=it2, it4=it4, kslot=kslot):
                kp_ops(it2, kslot, it4)

            def f6(it2=it2, vslot=vslot, kb_=kb_, kk=kk):
                kv_mm(it2, vslot, (1, 0), kb_, kk)

            def f7(st=st, it4=it4, kb_=kb_, kk=kk):
                for ci, cc in enumerate((1, 0)):
                    c = 2 * st + cc
                    s0, s1 = c % NR, (c - 1) % NR
                    if c < 64:
                        P.op("act", lambda e, c=c, s0=s0: e.activation(out=eb_all[:, :, c, :], in_=RB[:, :, s0, :], func=AF.Copy),
                             reads=[("RB", 0, s0), ("RB", 1, s0), "RBall"], writes=[("eb_all", c)])
                    for hf in range(2):
                        col = (2 * ci + hf) * 128
                        P.op("dve", lambda e, hf=hf, s0=s0, s1=s1, col=col, cc=cc: e.scalar_tensor_tensor(
                            out=RB[:, hf, s1, :], in0=RB[:, hf, s0, :], scalar=ewd[it4][:, 256 + 2 * hf + cc:257 + 2 * hf + cc],
                            in1=kb_[:, col:col + 128], op0=ALU.mult, op1=ALU.add),
                            reads=[("RB", hf, s0), ("ewd", it4), "RBall"], writes=[("RB", hf, s1), kk])
            items.append([f0, f0b, f1, f2, f2b, f3, f4, f5, f6, f7])
        run_pipeline(items, 10, 1)

        items = []
        NPH = 15
        for st in range(32):
            it = cn["it"]
            cn["it"] += 1
            it2, it4 = it % 2, it % 4
            blk = st // 4
            bs3 = blk % 2
            kslot = cn["k"] % DK
            cn["k"] += 1
            vslot = cn["v"] % DV
            cn["v"] += 1
            qslot = cn["q"] % DQ
            cn["q"] += 1
            sslot = cn["s"] % DS
            cn["s"] += 1
            ls = blk % 2
            tl = slice(128 * (st % 4), 128 * (st % 4) + 128)
            tg = slice(128 * st, 128 * st + 128)

            def f0(st=st, blk=blk, ls=ls):
                if st % 4 == 0:
                    for d in range(2):
                        P.dma("sp", lambda e, d=d: e.dma_start(out=lrd[d][ls][0:16, :], in_=T.lrT[16 * d:16 * d + 16, 512 * blk:512 * blk + 512]),
                              f"lr{d}{ls}", writes=[("lr", d, ls)])

            def f0b(st=st, ls=ls):
                if st % 4 == 0:
                    for d in range(2):
                        lr_split(d, ls)

            def f1(ls=ls, tl=tl):
                for d in range(2):
                    z_mm(d, ls, tl)

            def f2(it2=it2, st=st, kslot=kslot, qslot=qslot, tg=tg):
                softplus(it2, slice(0, 512))
                P.dma("sp", lambda e: e.dma_start(out=ktm[kslot][:], in_=T.kTM[128 * st:128 * st + 128, :]),
                      f"ktm{kslot}", writes=[("ktm", kslot)])
                P.dma("sp", lambda e: e.dma_start(out=qg[qslot][:], in_=T.qgT.rearrange("(a p) t -> p a t", p=128)[:, :, tg]),
                      f"qg{qslot}", writes=[("qg", qslot)])
                P.dma("sp", lambda e: e.dma_start(out=kg[qslot][:], in_=T.kgT.rearrange("(a p) t -> p a t", p=128)[:, :, tg]),
                      f"kg{qslot}", writes=[("kg", qslot)])

            def f2b(it2=it2):
                sp_lo(it2, slice(0, 512))

            def f3(it2=it2, st=st, vslot=vslot):
                w_tot_mm(it2, 0)
                for d in range(2):
                    for hf in range(2):
                        cb = (2 * d + hf) * 128
                        for i_, src in enumerate((sph, spl)):
                            P.op("pe", lambda e, d=d, hf=hf, cb=cb, src=src, i_=i_: e.matmul(
                                Bp[:, cb:cb + 128], lhsT=src[it2][:, 256 * d + 128 * hf:256 * d + 128 * hf + 128], rhs=tri_bf[:, 128 * d:128 * d + 128],
                                start=(i_ == 0), stop=(i_ == 1)), reads=["tri_bf", ("sph", it2), ("spl", it2)], writes=["B"])
                P.dma("sp", lambda e: e.dma_start(out=vtm[vslot][:], in_=T.vTM[128 * st:128 * st + 128, :]),
                      f"vtm{vslot}", writes=[("vtm", vslot)])

            def f4(it2=it2, it4=it4):
                P.op("act", lambda e: e.activation(out=ewd[it4][:], in_=Wp[:, 0:260], func=AF.Exp), writes=["W", ("ewd", it4)])
                P.op("act", lambda e: e.activation(out=ebp[it2][:], in_=Bp[:], func=AF.Exp), writes=["B", ("ebp", it2)])
                P.op("act", lambda e: e.activation(out=ebn[it2][:], in_=Bp[:], func=AF.Exp, scale=-1.0), writes=["B", ("ebn", it2)])

            def f5(it2=it2, it4=it4, kslot=kslot, qslot=qslot):
                kp_ops(it2, kslot, it4)
                for d in range(2):
                    P.op("dve", lambda e, d=d: e.scalar_tensor_tensor(
                        out=qd[it4][:, 2 * d:2 * d + 2, :], in0=qg[qslot][:], scalar=0.125,
                        in1=ebp[it2][:, 256 * d:256 * d + 256].rearrange("p (a t) -> p a t", a=2), op0=ALU.mult, op1=ALU.mult),
                        reads=[("qg", qslot), ("ebp", it2)], writes=[("qd", it4)])
                    P.op("dve", lambda e, d=d: e.tensor_tensor(
                        out=kd[it4][:, 2 * d:2 * d + 2, :], in0=kg[qslot][:],
                        in1=ebn[it2][:, 256 * d:256 * d + 256].rearrange("p (a t) -> p a t", a=2), op=ALU.mult),
                        reads=[("kg", qslot), ("ebn", it2)], writes=[("kd", it4)])

            def f6(it2=it2, it4=it4, vslot=vslot):
                kv_mm(it2, vslot, (0, 1))
                for d in range(2):
                    for h in range(4):
                        hf, hb = h // 2, 64 * (h % 2)
                        cb = (2 * d + hf) * 128
                        P.op("pe", lambda e, d=d, h=h, hf=hf, hb=hb, cb=cb: e.matmul(
                            SC[h % 2][:, cb:cb + 128], lhsT=kd[it4][hb:hb + 64, 2 * d + hf, :], rhs=qd[it4][hb:hb + 64, 2 * d + hf, :],
                            start=True, stop=True), reads=[("kd", it4), ("qd", it4)], writes=[("SC", h % 2)])

            def f7(st=st, it2=it2, it4=it4, sslot=sslot, tg=tg):
                for ci, cc in enumerate((0, 1)):
                    c = 2 * st + cc
                    s0, s1 = c % NR, (c + 1) % NR
                    P.op("pool", lambda e, s0=s0: e.tensor_copy(out=efb[:, :, s0, :], in_=RF[:, :, s0, :]),
                         reads=[("RF", 0, s0), ("RF", 1, s0), "RFall"], writes=[("efb", s0)])
                    for hf in range(2):
                        col = (2 * ci + hf) * 128
                        P.op("dve", lambda e, hf=hf, s0=s0, s1=s1, col=col, cc=cc: e.scalar_tensor_tensor(
                            out=RF[:, hf, s1, :], in0=RF[:, hf, s0, :], scalar=ewd[it4][:, 256 + 2 * hf + cc:257 + 2 * hf + cc],
                            in1=KV[:, col:col + 128], op0=ALU.mult, op1=ALU.add),
                            reads=[("RF", hf, s0), ("ewd", it4), "RFall"], writes=[("RF", hf, s1), "KV"])
                for par in range(2):
                    P.op("dve", lambda e, par=par: e.tensor_tensor(out=scm[it2][:, par, :], in0=SC[par][:], in1=smask4[:], op=ALU.mult),
                         reads=["smask4"], writes=[("SC", par), ("scm", it2, par)])
                P.dma("sp", lambda e: e.dma_start(out=sgb[sslot][:], in_=T.sgbT.rearrange("(a p) t -> p a t", p=128)[:, :, tg]),
                      f"sgb{sslot}", writes=[("sgb", sslot)])

            def f8(st=st, it2=it2, it4=it4, vslot=vslot):
                for h in range(4):
                    hf, par = h // 2, h % 2
                    hb = 64 * par
                    po = Op[par]
                    ob = 128 * hf
                    P.op("pe", lambda e, po=po, h=h, hf=hf, par=par, ob=ob: e.matmul(
                        po[:, ob:ob + 128], lhsT=vtm[vslot][:, 128 * h:128 * h + 128], rhs=scm[it2][:, par, 128 * hf:128 * hf + 128],
                        start=True, stop=False), reads=[("vtm", vslot), ("scm", it2, par)], writes=[("O", par)])
                    P.op("pe", lambda e, po=po, h=h, hf=hf, par=par, ob=ob: e.matmul(
                        po[:, ob:ob + 128], lhsT=vtm[vslot][:, 128 * h:128 * h + 128], rhs=scm[it2][:, par, 256 + 128 * hf:256 + 128 * hf + 128],
                        start=False, stop=False), reads=[("vtm", vslot), ("scm", it2, par)], writes=[("O", par)])
                    for cc in range(2):
                        c = 2 * st + cc
                        P.op("pe", lambda e, po=po, hf=hf, hb=hb, c=c, cc=cc, ob=ob: e.matmul(
                            po[:, ob + 64 * cc:ob + 64 * cc + 64], lhsT=efb[hb:hb + 64, hf, c % NR, :], rhs=qd[it4][hb:hb + 64, hf, 64 * cc:64 * cc + 64],
                            start=False, stop=False), reads=[("efb", c % NR), ("qd", it4)], writes=[("O", par)])
                        P.op("pe", lambda e, po=po, hf=hf, hb=hb, c=c, cc=cc, ob=ob: e.matmul(
                            po[:, ob + 64 * cc:ob + 64 * cc + 64], lhsT=eb_all[hb:hb + 64, hf, c, :], rhs=qd[it4][hb:hb + 64, 2 + hf, 64 * cc:64 * cc + 64],
                            start=False, stop=(cc == 1)), reads=[("eb_all", c), ("qd", it4)], writes=[("O", par)])

            def f9(it2=it2, it4=it4):
                for par in range(2):
                    P.op("act", lambda e, par=par: e.activation(out=osb[it4][:, par, :, :], in_=Op[par][:, 0:256].rearrange("p (a t) -> p a t", a=2),
                                                                func=AF.Copy), writes=[("O", par), ("osb", it4, par)])
                    P.op("act", lambda e, par=par: e.activation(out=osq[it2][:, par, :, :], in_=Op[par][:, 0:256].rearrange("p (a t) -> p a t", a=2),
                                                                func=AF.Square), writes=[("O", par), ("osq", it2, par)])

            def f10(it2=it2):
                for par in range(2):
                    P.op("pe", lambda e, par=par: e.matmul(Op[par][:, 256:512], lhsT=onesm[:], rhs=osq[it2][:, par, :, :],
                                                           start=True, stop=True), reads=["onesm", ("osq", it2, par)], writes=[("O", par)])

            def f11(it2=it2):
                for par in range(2):
                    P.op("act", lambda e, par=par: e.activation(out=rs1[it2][:, 256 * par:256 * par + 256], in_=Op[par][:, 256:512], func=AF.Ln, bias=RMS_EPS),
                         writes=[("O", par), ("rstd", it2)])
                P.op("act", lambda e: e.activation(out=rstd[it2][:], in_=rs1[it2][:], func=AF.Exp, scale=-0.5),
                     writes=[("rstd", it2)])

            def f12(st=st, it2=it2, it4=it4, sslot=sslot, bs3=bs3, tl=tl, blk=blk):
                P.op("dve", lambda e: e.tensor_tensor(out=osb[it4][:], in0=osb[it4][:], in1=rstd[it2][:].rearrange("p (a b t) -> p a b t", a=2, b=2), op=ALU.mult),
                     reads=[("rstd", it2)], writes=[("osb", it4, 0), ("osb", it4, 1)])
                P.op("dve", lambda e: e.tensor_tensor(out=osb[it4][:], in0=osb[it4][:], in1=gexp[:], op=ALU.mult),
                     reads=["gexp"], writes=[("osb", it4, 0), ("osb", it4, 1)])
                for par in range(2):
                    P.op("dve", lambda e, par=par: e.tensor_tensor(
                        out=yst[bs3][:, :, tl].rearrange("p (j q) t -> p q j t", q=2)[:, par, :, :], in0=osb[it4][:, par, :, :],
                        in1=sgb[sslot][:].rearrange("p (j q) t -> p q j t", q=2)[:, par, :, :], op=ALU.mult),
                        reads=[("osb", it4, par), ("sgb", sslot)], writes=[("yst", bs3)])
                if st % 4 == 3:
                    tk = slice(512 * blk, 512 * blk + 512)
                    P.dma("pool", lambda e: e.dma_start(out=T.ygT.rearrange("(a p) t -> p a t", p=128)[:, :, tk], in_=yst[bs3][:]),
                          f"yst{bs3}", reads=[("yst", bs3)])
            items.append([f0, f0b, f1, f2, f2b, f3, f4, f5, f6, f7, f8, f9, f10, f11, f12])
        run_pipeline(items, NPH, 1)
        P.flush()


def alloc_S3_weights(nc, es, T):
    W = Ctx()
    sb = lambda name, shape, dt: es.enter_context(nc.sbuf_tensor("f_" + name, shape, dt))
    W.wgt = sb("wgt", [128, 8, 2048], BF16)
    T.s3w = W


def load_S3_weights(P, T, which):
    W = T.s3w
    w_v = T.w_in.rearrange("(k p) c -> p k c", p=128)
    wa_v = T.w_att_out.rearrange("(k p) c -> p k c", p=128)
    wg_v = T.w_gla_out.rearrange("(k p) c -> p k c", p=128)
    wo_v = T.w_out.rearrange("(k p) c -> p k c", p=128)
    if "wgt" in which:
        for k in range(8):
            for half in range(2):
                P.dma("pool", lambda e, k=k, half=half: e.dma_start(
                    out=W.wgt[:, k, 1024 * half:1024 * half + 1024], in_=w_v[:, k, C_GATT + 1024 * half:C_GATT + 1024 * half + 1024]),
                    "wgt", writes=["wgt"])
    if "cast" in which:
        for src, dst, n in ((T.w_att_out, T.wao_b, 6), (T.w_gla_out, T.wgo_b, 4), (T.w_out, T.wo_b, 8)):
            for k in range(n):
                P.dma("pool", lambda e, src=src, dst=dst, k=k: e.dma_start(out=dst[128 * k:128 * k + 128, :], in_=src[128 * k:128 * k + 128, :]),
                      "wcast", writes=["wcast"])
    if "rest_b" in which:
        wa_b = T.wao_b.rearrange("(k p) c -> p k c", p=128)
        wg_b = T.wgo_b.rearrange("(k p) c -> p k c", p=128)
        wo_b = T.wo_b.rearrange("(k p) c -> p k c", p=128)
        for k in range(6):
            P.dma("sp" if k % 2 == 0 else "act", lambda e, k=k: e.dma_start(out=W.wao[:, k, :], in_=wa_b[:, k, :]), "wao", writes=["wao"])
        for k in range(4):
            P.dma("sp" if k % 2 == 0 else "act", lambda e, k=k: e.dma_start(out=W.wgo[:, k, :], in_=wg_b[:, k, :]), "wgo", writes=["wgo"])
        for k in range(8):
            P.dma("sp" if k % 2 == 0 else "act", lambda e, k=k: e.dma_start(out=W.wo[:, k, :], in_=wo_b[:, k, :]), "wo", writes=["wo"])
    if "rest" not in which:
        return
    for k in range(6):
        P.dma("pool", lambda e, k=k: e.dma_start(out=W.wao[:, k, :], in_=wa_v[:, k, :]), "wao", writes=["wao"])
    for k in range(4):
        P.dma("pool", lambda e, k=k: e.dma_start(out=W.wgo[:, k, :], in_=wg_v[:, k, :]), "wgo", writes=["wgo"])
    for k in range(8):
        P.dma("pool", lambda e, k=k: e.dma_start(out=W.wo[:, k, :], in_=wo_v[:, k, :]), "wo", writes=["wo"])


def stage_S3(nc, P, T, weights_loaded):
    x_v = T.xT.rearrange("(k p) t -> p k t", p=128)
    with ExitStack() as es:
        sb = lambda name, shape, dt: es.enter_context(nc.sbuf_tensor("f_" + name, shape, dt))
        ps = lambda name, shape, dt: es.enter_context(nc.psum_tensor("f_" + name, shape, dt))
        wgt = T.s3w.wgt
        T.s3w.wao = wao = sb("wao", [128, 6, 1024], BF16)
        T.s3w.wgo = wgo = sb("wgo", [128, 4, 1024], BF16)
        T.s3w.wo = wo = sb("wo", [128, 8, 1024], BF16)
        lng = sb("lng", [128, 1024], F32)
        lnb = sb("lnb", [128, 1024], F32)
        xt = [sb(f"xt{i}", [128, 8, 512], BF16) for i in range(2)]
        ygs = [sb(f"ygs{i}", [128, 4, 512], BF16) for i in range(2)]
        num = sb("num", [128, 6, 512], F32)
        sga = sb("sga", [128, 6, 512], BF16)
        dn = sb("dn", [128, 2, 3, 512], F32)
        xtm = [sb(f"xtm{i}", [128, 1024], F32) for i in range(4)]
        ya = [sb(f"ya{i}", [128, 6, 512], BF16) for i in range(2)]
        NS = 2
        sA = [sb(f"sA{i}", [128, 512], F32) for i in range(NS)]
        sG = [sb(f"sG{i}", [128, 512], F32) for i in range(NS)]
        m1 = [sb(f"m1{i}", [128, 512], F32) for i in range(2)]
        m2 = [sb(f"m2{i}", [128, 512], F32) for i in range(2)]
        mg = [sb(f"mg{i}", [128, 8, 512], BF16) for i in range(2)]
        NH = 4
        hs = [sb(f"hs{i}", [128, 1024], F32) for i in range(NH)]
        stt = [sb(f"stt{i}", [128, 12], F32) for i in range(NH)]
        mv = [sb(f"mv{i}", [128, 2], F32) for i in range(NH)]
        sm = [sb(f"sm{i}", [128, 4], F32) for i in range(NH)]
        cneg = sb("cneg", [128, 1], F32)
        pGA = ps("pGA", [128, 512], F32)
        pGG = ps("pGG", [128, 512], F32)
        pYA = ps("pYA", [128, 512], F32)
        pYG = ps("pYG", [128, 512], F32)
        pH = [ps(f"pH{i}", [128, 512], F32) for i in range(4)]

        def load_a(tt):
            s = tt % 2
            tk = slice(512 * tt, 512 * tt + 512)
            for kh in range(2):
                P.dma("pool", lambda e, s=s, kh=kh, tk=tk: e.dma_start(
                    out=xt[s][:, 4 * kh:4 * kh + 4, :], in_=x_v[:, 4 * kh:4 * kh + 4, tk]), f"xt{s}", writes=[("xt", s)])
            P.dma("sp", lambda e, s=s, tk=tk: e.dma_start(
                out=ygs[s][:], in_=T.ygT.rearrange("(a p) t -> p a t", p=128)[:, :, tk]), f"ygs{s}", writes=[("ygs", s)])

        def load_b(tt):
            tk = slice(512 * tt, 512 * tt + 512)
            P.dma("sp", lambda e, tk=tk: e.dma_start(
                out=num[:], in_=T.numT.rearrange("(a p) t -> p a t", p=128)[:, :, tk]), "num", writes=["num"])
            P.dma("sp", lambda e, tk=tk: e.dma_start(
                out=sga[:], in_=T.sgaT.rearrange("(a p) t -> p a t", p=128)[:, :, tk]), "sga", writes=["sga"])
            for jp in range(2):
                for q_ in range(2):
                    src = bass.AP(tensor=T.den.tensor, offset=(2 * jp + q_) * OWN + 512 * tt,
                                  ap=[[0, 64], [4 * OWN, 3], [1, 512]])
                    P.dma("sp", lambda e, jp=jp, q_=q_, src=src: e.dma_start(
                        out=dn[64 * q_:64 * q_ + 64, jp, :, :], in_=src), "dn", writes=["dn"])

        def ya_parts(tt):
            yb = tt % 2
            parts = []
            for jp in range(2):
                def g(jp=jp):
                    P.op("dve", lambda e: e.tensor_tensor(out=dn[:, jp, 0, :], in0=dn[:, jp, 0, :], in1=dn[:, jp, 1, :], op=ALU.add),
                         writes=["dn"])
                    P.op("dve", lambda e: e.tensor_tensor(out=dn[:, jp, 0, :], in0=dn[:, jp, 0, :], in1=dn[:, jp, 2, :], op=ALU.add),
                         writes=["dn"])
                    P.op("dve", lambda e: e.reciprocal(out=dn[:, jp, 0, :], in_=dn[:, jp, 0, :]), writes=["dn"])
                parts.append(g)
            for ft in range(6):
                def g(ft=ft, jp=ft % 2):
                    P.op("dve", lambda e: e.tensor_tensor(out=num[:, ft, :], in0=num[:, ft, :], in1=dn[:, jp, 0, :], op=ALU.mult),
                         reads=["dn"], writes=["num"])
                    P.op("dve", lambda e: e.tensor_tensor(out=ya[yb][:, ft, :], in0=num[:, ft, :], in1=sga[:, ft, :], op=ALU.mult),
                         reads=["num", "sga"], writes=[("ya", yb, ft)])
                parts.append(g)
            return parts

        def compute_ya(tt):
            for g in ya_parts(tt):
                g()

        load_a(0)
        load_b(0)
        P.op("pool", lambda e: e.memset(cneg[:], -0.5), writes=["cneg"])
        load_S3_weights(P, T, ("rest_b",) if weights_loaded else ("wgt", "rest"))
        P.dma("sp", lambda e: e.dma_start(out=lng[:], in_=T.ln_g.partition_broadcast(128)), "c1", writes=["lng"])
        P.dma("sp", lambda e: e.dma_start(out=lnb[:], in_=T.ln_b.partition_broadcast(128)), "c1", writes=["lnb"])
        wkeys = [] if weights_loaded else None

        NPH = 10
        items = []
        cn = dict(d=0, h=0)

        def rd(*keys):
            return [k for k in keys if not (weights_loaded and k == "wgt")]

        def add_dt(tt, dt_, extra=None):
            s = tt % 2
            yb = tt % 2
            mb = tt % 2
            di = cn["d"]
            cn["d"] += 1
            b3 = di % NS
            b2 = di % 2
            dsl = slice(128 * dt_, 128 * dt_ + 128)

            def fA():
                for k in range(8):
                    P.op("pe", lambda e, k=k: e.matmul(pGA[:], lhsT=wgt[:, k, dsl], rhs=xt[s][:, k, :], start=(k == 0), stop=(k == 7)),
                         reads=rd("wgt", ("xt", s)), writes=["pGA"])
                for k in range(8):
                    P.op("pe", lambda e, k=k: e.matmul(pGG[:], lhsT=wgt[:, k, 1024 + 128 * dt_:1024 + 128 * dt_ + 128], rhs=xt[s][:, k, :], start=(k == 0), stop=(k == 7)),
                         reads=rd("wgt", ("xt", s)), writes=["pGG"])
                for ft in range(6):
                    P.op("pe", lambda e, ft=ft: e.matmul(pYA[:], lhsT=wao[:, ft, dsl], rhs=ya[yb][:, ft, :], start=(ft == 0), stop=(ft == 5)),
                         reads=rd("wao", ("ya", yb, ft)), writes=["pYA"])
                for gt in range(4):
                    P.op("pe", lambda e, gt=gt: e.matmul(pYG[:], lhsT=wgo[:, gt, dsl], rhs=ygs[s][:, gt, :], start=(gt == 0), stop=(gt == 3)),
                         reads=rd("wgo", ("ygs", s)), writes=["pYG"])

            def fB():
                P.op("act", lambda e: e.activation(out=sA[b3][:], in_=pGA[:], func=AF.Sigmoid), writes=["pGA", ("sA", b3)])
                P.op("act", lambda e: e.activation(out=sG[b3][:], in_=pGG[:], func=AF.Sigmoid), writes=["pGG", ("sG", b3)])

            def fC():
                P.op("dve", lambda e: e.tensor_tensor(out=m1[b2][:], in0=sA[b3][:], in1=pYA[:], op=ALU.mult),
                     reads=[("sA", b3)], writes=["pYA", ("m1", b2)])
                P.op("dve", lambda e: e.tensor_tensor(out=m2[b2][:], in0=sG[b3][:], in1=pYG[:], op=ALU.mult),
                     reads=[("sG", b3)], writes=["pYG", ("m2", b2)])

            def fD():
                P.op("pool", lambda e: e.tensor_tensor(out=mg[mb][:, dt_, :], in0=m1[b2][:], in1=m2[b2][:], op=ALU.add),
                     reads=[("m1", b2), ("m2", b2)], writes=[("mg", mb, dt_)])
            items.append([fA, lambda: (fB(), fC(), extra() if extra is not None else None), fD] + [None] * (NPH - 3))

        def add_sub(tt, sub, early=None):
            mb = tt % 2
            hi = cn["h"]
            cn["h"] += 1
            x2 = hi % 4
            h4 = hi % NH
            row0 = 512 * tt + 128 * sub
            pis = [(2 * hi) % 4, (2 * hi + 1) % 4]

            def f0():
                P.dma("sp", lambda e: e.dma_start(out=xtm[x2][:], in_=T.xtm[row0:row0 + 128, :]),
                      f"xtm{x2}", writes=[("xtm", x2)])

            def fA():
                for half in range(2):
                    ph = pH[pis[half]]
                    for dt_ in range(8):
                        P.op("pe", lambda e, ph=ph, dt_=dt_, half=half: e.matmul(
                            ph[:], lhsT=mg[mb][:, dt_, 128 * sub:128 * sub + 128], rhs=wo[:, dt_, 512 * half:512 * half + 512],
                            start=(dt_ == 0), stop=(dt_ == 7)), reads=rd(("mg", mb, dt_), "wo"), writes=[("pH", pis[half])])

            def fB():
                for half in range(2):
                    ph = pH[pis[half]]
                    P.op("dve", lambda e, ph=ph, half=half: e.scalar_tensor_tensor(
                        out=hs[h4][:, 512 * half:512 * half + 512], in0=xtm[x2][:, 512 * half:512 * half + 512], scalar=ALPHA,
                        in1=ph[:], op0=ALU.mult, op1=ALU.add), reads=[("xtm", x2)], writes=[("pH", pis[half]), ("hs", h4)])
                    P.op("dve", lambda e, half=half: e.bn_stats(out=stt[h4][:, 6 * half:6 * half + 6], in_=hs[h4][:, 512 * half:512 * half + 512]),
                         reads=[("hs", h4)], writes=[("stt", h4)])
                P.op("dve", lambda e: e.bn_aggr(out=mv[h4][:], in_=stt[h4][:]), reads=[("stt", h4)], writes=[("mv", h4)])

            def fC():
                P.op("pool", lambda e: e.tensor_scalar(out=sm[h4][:, 0:1], in0=mv[h4][:, 1:2], scalar1=LN_EPS, scalar2=None, op0=ALU.add),
                     reads=[("mv", h4)], writes=[("sm0", h4)])
                P.op("pool", lambda e: e.tensor_tensor(out=sm[h4][:, 1:2], in0=sm[h4][:, 0:1], in1=cneg[:, 0:1], op=ALU.pow),
                     reads=[("sm0", h4), "cneg"], writes=[("sm1", h4)])

            def fD():
                P.op("dve", lambda e: e.tensor_scalar(out=hs[h4][:], in0=hs[h4][:], scalar1=mv[h4][:, 0:1], scalar2=sm[h4][:, 1:2],
                                                      op0=ALU.subtract, op1=ALU.mult),
                     reads=[("mv", h4), ("sm1", h4)], writes=[("hs", h4)])
                P.op("dve", lambda e: e.tensor_tensor(out=hs[h4][:], in0=hs[h4][:], in1=lng[:], op=ALU.mult),
                     reads=["lng"], writes=[("hs", h4)])

            def fE():
                P.op("pool", lambda e: e.tensor_tensor(out=hs[h4][:], in0=hs[h4][:], in1=lnb[:], op=ALU.add),
                     reads=["lnb"], writes=[("hs", h4)])
                P.dma("pool", lambda e: e.dma_start(out=T.y[row0:row0 + 128, :], in_=hs[h4][:]),
                      f"yo{h4}", reads=[("hs", h4)])
            items.append([None, None, lambda: (f0(), early() if early is not None else None), None, fA, fB, fC, fD, fE, None])

        compute_ya(0)
        load_b(1)
        for tt in range(8):
            if tt + 1 < 8:
                items.append([lambda tt=tt: load_a(tt + 1)] + [None] * (NPH - 1))
            parts = ya_parts(tt + 1) if tt + 1 < 8 else [None] * 8
            for dt_ in range(8):
                add_dt(tt, dt_, extra=parts[dt_])
            for sub in range(4):
                add_sub(tt, sub, early=(lambda tt=tt: load_b(tt + 2)) if (sub == 0 and tt + 2 < 8) else None)
        run_pipeline(items, NPH, 1)
        P.flush()


def build_program(debug=False, stages=(0, 1, 2, 3)):
    nc = bass.Bass("TRN2", target_bir_lowering=False)
    T = Ctx()

    def din(name, shape, dt=F32):
        return nc.dram_tensor(name, list(shape), dt, kind="ExternalInput").ap()

    def scr(name, shape, dt):
        return nc.dram_tensor(name, list(shape), dt, kind=("ExternalOutput" if debug else "Internal")).ap()

    T.xT = din("xT", [D_MODEL, SEQ])
    T.xtm = din("xtm", [OWN, D_MODEL])
    T.w_in = din("w_in", [D_MODEL, IN_W])
    T.w2 = din("w2", [32, 256])
    T.gbias = din("gbias", [1, 512])
    T.gng = din("gng", [128, 4])
    T.ebias = din("ebias", [128, 12 * 256])
    T.ebias0 = din("ebias0", [64, 12 * 128])
    T.emask = din("emask", [128, 256])
    T.emask0 = din("emask0", [64, 128])
    T.ident = din("ident", [128, 128])
    T.tri = din("tri", [128, 512])
    T.ind = din("ind", [128, 2])
    T.smask = din("smask", [128, 256])
    T.w_att_out = din("w_att_out", [768, D_MODEL])
    T.w_gla_out = din("w_gla_out", [512, D_MODEL])
    T.w_out = din("w_out", [D_MODEL, D_MODEL])
    T.ln_g = din("ln_g", [1, D_MODEL])
    T.ln_b = din("ln_b", [1, D_MODEL])
    T.y = nc.dram_tensor("y", [OWN, D_MODEL], F32, kind="ExternalOutput").ap()
    T.qaT = scr("qaT", [768, OWN], BF16)
    T.kaT = scr("kaT", [768, 5120], BF16)
    T.vaT = scr("vaT", [768, 5120], BF16)
    T.sgaT = scr("sgaT", [768, OWN], BF16)
    T.qgT = scr("qgT", [256, OWN], F32)
    T.kgT = scr("kgT", [256, OWN], F32)
    T.sgbT = scr("sgbT", [512, OWN], BF16)
    T.lrT = scr("lrT", [32, SEQ], F32)
    T.kTM = scr("kTM", [SEQ, 256], F32)
    T.vTM = scr("vTM", [SEQ, 512], BF16)
    T.numT = scr("numT", [768, OWN], F32)
    T.den = scr("den", [12, OWN], F32)
    T.ygT = scr("ygT", [512, OWN], BF16)
    T.wao_b = scr("wao_b", [768, D_MODEL], BF16)
    T.wgo_b = scr("wgo_b", [512, D_MODEL], BF16)
    T.wo_b = scr("wo_b", [D_MODEL, D_MODEL], BF16)
    P = Prog(nc)
    if 0 in stages:
        stage_P(nc, P, T)
    if 1 in stages:
        stage_S1(nc, P, T)
    with ExitStack() as es3:
        alloc_S3_weights(nc, es3, T)
        if 2 in stages:
            stage_S2(nc, P, T, prefetch=lambda: load_S3_weights(P, T, ("wgt", "cast")))
        if 3 in stages:
            stage_S3(nc, P, T, weights_loaded=(2 in stages))
    return nc


def _t5_bucket(rel):
    half = 16
    max_exact = 8
    ret = (rel > 0).astype(np.int32) * half
    n = np.abs(rel)
    large = max_exact + (np.log(np.maximum(n, 1) / max_exact) / np.log(1024 / max_exact) * (half - max_exact)).astype(np.int32)
    large = np.minimum(large, half - 1)
    return ret + np.where(n < max_exact, n, large)


def _static_tables():
    a = np.arange(128)[:, None]
    i = np.arange(128)[None, :]
    dlA = a - 64 - i
    dlB = a + 64 - i
    dl = np.concatenate([dlA, dlB], axis=1)
    valid = (np.abs(dl) <= 64)
    s = np.arange(128)[:, None]
    t = np.arange(128)[None, :]
    same = (s // 64) == (t // 64)
    c = -1.0 / 16.0
    bF = np.where(same & (s <= t), c, 0.0)
    bB = np.where(same & (s >= t), c, 0.0)
    wF = np.where(same & (s > t), c, 0.0)
    wB = np.where(same & (s < t), c, 0.0)
    tri = np.concatenate([bF, bB, wF, wB], axis=1).astype(np.float32)
    ind = np.zeros((128, 2), np.float32)
    ind[:64, 0] = c
    ind[64:, 1] = c
    mF = np.where(same & (s <= t), 1.0, 0.0)
    mB = np.where(same & (s >= t), 1.0, 0.0)
    smask = np.concatenate([mF, mB], axis=1).astype(np.float32)
    return dl, valid, tri, ind, smask


def make_in_maps(x, w_in, gla_gate_w2, gla_gate_b, gla_norm_g, rel_bias, w_att_out, w_gla_out, w_out, ln_g, ln_b):
    x = np.asarray(x, np.float32)
    w_in0 = np.asarray(w_in, np.float32)[0]
    w2 = np.asarray(gla_gate_w2, np.float32)[0]
    gb = np.asarray(gla_gate_b, np.float32)[0]
    gng = np.ascontiguousarray(np.asarray(gla_norm_g, np.float32)[0].reshape(4, 128).T)
    rb = np.asarray(rel_bias, np.float32)
    dl, valid, tri, ind, smask = _static_tables()
    emask = valid.astype(np.float32)
    common = dict(
        gng=gng, emask=emask, emask0=np.ascontiguousarray(emask[64:128, 0:128]),
        ident=np.eye(128, dtype=np.float32), tri=tri, ind=ind, smask=smask,
        w_att_out=np.ascontiguousarray(np.asarray(w_att_out, np.float32)[0]),
        w_gla_out=np.ascontiguousarray(np.asarray(w_gla_out, np.float32)[0]),
        w_out=np.ascontiguousarray(np.asarray(w_out, np.float32)[0]),
        ln_g=np.ascontiguousarray(np.asarray(ln_g, np.float32)[0].reshape(1, D_MODEL)),
        ln_b=np.ascontiguousarray(np.asarray(ln_b, np.float32)[0].reshape(1, D_MODEL)),
    )
    per_parity = []
    for par in range(2):
        sign = 1 if par == 0 else -1
        eb = np.zeros((128, 12, 256), np.float32)
        for h in range(12):
            d = DILS[h // 4]
            bk = _t5_bucket(sign * dl * d)
            eb[:, h, :] = np.where(valid, rb[bk, h], 0.0)
        w_c = w_in0
        w2_c = w2
        gb_c = gb
        if par == 1:
            w_c = w_in0.copy()
            w_c[:, C_LRF:C_LRF + 16] = w_in0[:, C_LRB:C_LRB + 16]
            w_c[:, C_LRB:C_LRB + 16] = w_in0[:, C_LRF:C_LRF + 16]
            w2_c = w2[::-1]
            gb_c = gb[::-1]
        per_parity.append(dict(
            ebias=np.ascontiguousarray(eb.reshape(128, 12 * 256)),
            ebias0=np.ascontiguousarray(eb[64:128, :, 0:128].reshape(64, 12 * 128)),
            w_in=np.ascontiguousarray(w_c),
            w2=np.ascontiguousarray(w2_c.reshape(32, 256)),
            gbias=np.ascontiguousarray(gb_c.reshape(1, 512)),
        ))
    in_maps = []
    for c in range(NCORES):
        b, par = c // 2, c % 2
        xs = x[b] if par == 0 else x[b][::-1]
        m = dict(common)
        m.update(per_parity[par])
        m["xT"] = np.ascontiguousarray(xs.T)
        m["xtm"] = np.ascontiguousarray(xs[:OWN])
        in_maps.append(m)
    return in_maps


_NC_CACHE = {}


def kernel(x, w_in, gla_gate_w2, gla_gate_b, gla_norm_g, rel_bias, w_att_out, w_gla_out, w_out, ln_g, ln_b):
    in_maps = make_in_maps(x, w_in, gla_gate_w2, gla_gate_b, gla_norm_g, rel_bias,
                           w_att_out, w_gla_out, w_out, ln_g, ln_b)
    if "nc" not in _NC_CACHE:
        _NC_CACHE["nc"] = build_program()
    nc = _NC_CACHE["nc"]
    res = run_bass_kernel_spmd(nc, in_maps, core_ids=list(range(NCORES)))
    out = np.empty((4, SEQ, D_MODEL), np.float32)
    for c in range(NCORES):
        y = np.asarray(res.results[c]["y"], np.float32)
        b, par = c // 2, c % 2
        if par == 0:
            out[b, :OWN] = y
        else:
            out[b, OWN:] = y[::-1]
    return out
```

```python
import os
import numpy as np
import ml_dtypes
from contextlib import ExitStack
import concourse.bass as bass
import concourse.mybir as mybir
from concourse.bass_utils import run_bass_kernel_spmd

F32 = mybir.dt.float32
BF16 = mybir.dt.bfloat16
AF = mybir.ActivationFunctionType
ALU = mybir.AluOpType

D_MODEL = 1024
SEQ = 8192
OWN = 4096
NCORES = 8
IN_W = 6688
C_QA, C_KA, C_VA, C_GA, C_QB, C_KB, C_VB, C_GB, C_LRF, C_LRB, C_GATT, C_GGLA = (
    0, 768, 1536, 2304, 3072, 3328, 3584, 4096, 4608, 4624, 4640, 5664)
DILS = (1, 4, 16)
ALPHA = 2.0 ** 0.25
LN_EPS = 1e-5
RMS_EPS = 1e-6

ENGS = ("pe", "act", "dve", "pool", "sp")


class Prog:
    def __init__(self, nc):
        self.nc = nc
        self.sems = {e: nc.alloc_semaphore("s_" + e) for e in ("pe", "act", "dve", "pool")}
        self.cnt = {e: 0 for e in ("pe", "act", "dve", "pool")}
        self.dsem = {}
        self.ops = []
        self.tagmap = {}

    def op(self, eng, fn, reads=(), writes=()):
        self.ops.append(dict(eng=eng, fn=fn, reads=tuple(reads), writes=tuple(writes), dma=None))

    def dma(self, eng, fn, dsem, reads=(), writes=()):
        kind = "s" if eng == "pool" else "h"
        nk = sum(1 for k_ in self.tagmap if k_[1] == kind)
        dsem = self.tagmap.setdefault((dsem, kind), "g%s%d" % (kind, nk))
        if dsem not in self.dsem:
            self.dsem[dsem] = [self.nc.alloc_semaphore("d_" + dsem), 0]
        self.ops.append(dict(eng=eng, fn=fn, reads=tuple(reads), writes=tuple(writes), dma=dsem))

    def flush(self):
        ops = self.ops
        self.ops = []
        self.tagmap = {}
        n = len(ops)
        last_w = {}
        readers = {}
        deps = [None] * n
        for i, o in enumerate(ops):
            d = set()
            for k in o["reads"]:
                if k in last_w:
                    d.add(last_w[k])
            for k in o["writes"]:
                if k in last_w:
                    d.add(last_w[k])
                for r in readers.get(k, ()):
                    d.add(r)
            d.discard(i)
            for k in o["writes"]:
                last_w[k] = i
                readers[k] = []
            for k in o["reads"]:
                if k not in o["writes"]:
                    readers.setdefault(k, []).append(i)
            if o["eng"] == "pe" and o["dma"] is None:
                d = {j for j in d if not (ops[j]["eng"] == "pe" and ops[j]["dma"] is None)}
            if o["dma"] is not None:
                sk = {j for j in d if (ops[j]["dma"] is not None and set(ops[j]["writes"]) & set(o["writes"])
                                       and not (set(ops[j]["reads"]) | set(o["reads"])))}
                for j in sk:
                    d = d | deps[j]
                d = d - sk
            deps[i] = d
        flagged = set()
        for d in deps:
            flagged |= d
        val = [None] * n
        cnt = dict(self.cnt)
        dcnt = {k: v[1] for k, v in self.dsem.items()}
        for i, o in enumerate(ops):
            if o["dma"] is not None:
                dcnt[o["dma"]] += 16
                val[i] = (o["dma"], dcnt[o["dma"]])
            elif i in flagged:
                cnt[o["eng"]] += 1
                val[i] = (o["eng"], cnt[o["eng"]])
        streams = {e: [] for e in ENGS}
        known = {e: {} for e in ENGS}
        run_d = {k: v[1] for k, v in self.dsem.items()}
        for i, o in enumerate(ops):
            waits = {}
            for j in deps[i]:
                s, v = val[j]
                if ops[j]["dma"] is not None:
                    v = max(v, run_d[s])
                if waits.get(s, 0) < v:
                    waits[s] = v
            wl = []
            kn = known[o["eng"]]
            for s, v in waits.items():
                if kn.get(s, 0) < v:
                    kn[s] = v
                    wl.append((s, v))
            streams[o["eng"]].append((o, wl, val[i]))
            if o["dma"] is not None:
                run_d[o["dma"]] = val[i][1]
        final_waits = []
        for k, v in dcnt.items():
            if known["sp"].get(k, 0) < v:
                final_waits.append((k, v))
        self.cnt = cnt
        for k in self.dsem:
            self.dsem[k][1] = dcnt[k]
        self._emit(streams, final_waits)

    def _sem(self, s):
        if s in self.sems:
            return self.sems[s]
        return self.dsem[s][0]

    def _emit(self, streams, final_waits):
        nc = self.nc
        me = self

        def run(eng_obj, name):
            for (o, wl, v) in streams[name]:
                for (s, x) in wl:
                    eng_obj.wait_ge(me._sem(s), x)
                ins = o["fn"](eng_obj)
                if v is not None:
                    ins.then_inc(me._sem(v[0]), 16 if o["dma"] is not None else 1)
            if name == "sp":
                for (s, x) in final_waits:
                    eng_obj.wait_ge(me._sem(s), x)

        with nc.Block() as blk:
            @blk.sync
            def _(e):
                run(e, "sp")

            @blk.tensor
            def _(e):
                run(e, "pe")

            @blk.scalar
            def _(e):
                run(e, "act")

            @blk.vector
            def _(e):
                run(e, "dve")

            @blk.gpsimd
            def _(e):
                run(e, "pool")


class Ctx:
    pass


def stage_P(nc, P, T):
    fm = []
    for i in range(6):
        fm.append((C_QA + 128 * i, 128, T.qaT, 128 * i, False, BF16, 8))
    for i in range(6):
        fm.append((C_KA + 128 * i, 128, T.kaT, 128 * i, False, BF16, 10))
    for i in range(6):
        fm.append((C_VA + 128 * i, 128, T.vaT, 128 * i, False, BF16, 10))
    for i in range(6):
        fm.append((C_GA + 128 * i, 128, T.sgaT, 128 * i, True, BF16, 8))
    for i in range(2):
        fm.append((C_QB + 128 * i, 128, T.qgT, 128 * i, False, F32, 8))
    for i in range(2):
        fm.append((C_KB + 128 * i, 128, T.kgT, 128 * i, False, F32, 8))
    for i in range(4):
        fm.append((C_GB + 128 * i, 128, T.sgbT, 128 * i, True, BF16, 8))
    fm.append((C_LRF, 32, T.lrT, 0, False, F32, 16))
    NFM = len(fm)
    offs = []
    o = 0
    for c in fm:
        offs.append(o)
        o += c[1]
    WF = o
    w_v = T.w_in.rearrange("(k p) c -> p k c", p=128)
    x_v = T.xT.rearrange("(k p) t -> p k t", p=128)
    with ExitStack() as es:
        sb = lambda name, shape, dt: es.enter_context(nc.sbuf_tensor("p_" + name, shape, dt))
        ps = lambda name, shape, dt: es.enter_context(nc.psum_tensor("p_" + name, shape, dt))
        wfm = sb("wfm", [128, 8, WF], BF16)
        wtm = sb("wtm", [128, 8, 768], BF16)
        xt = [sb(f"xt{i}", [128, 8, 512], BF16) for i in range(2)]
        ob = [sb(f"ob{i}", [128, 512], BF16) for i in range(4)]
        of = [sb(f"of{i}", [128, 512], F32) for i in range(2)]
        kst = [sb(f"kst{i}", [128, 256], F32) for i in range(2)]
        vst = [sb(f"vst{i}", [128, 512], BF16) for i in range(2)]
        pf = [ps(f"pf{i}", [128, 512], F32) for i in range(4)]
        pt = [ps(f"pt{i}", [128, 512], F32) for i in range(4)]

        def load_x(tt):
            s = tt % 2
            for kh in range(2):
                P.dma("pool", lambda e, s=s, tt=tt, kh=kh: e.dma_start(
                    out=xt[s][:, 4 * kh:4 * kh + 4, :], in_=x_v[:, 4 * kh:4 * kh + 4, 512 * tt:512 * tt + 512]),
                    f"xt{s}", writes=[("xt", s)])

        def load_w(ci):
            c0, ncol = fm[ci][0], fm[ci][1]
            for kh in range(2):
                P.dma("pool", lambda e, c0=c0, ncol=ncol, kh=kh, o=offs[ci]: e.dma_start(
                    out=wfm[:, 4 * kh:4 * kh + 4, o:o + ncol], in_=w_v[:, 4 * kh:4 * kh + 4, c0:c0 + ncol]),
                    f"wfm{ci}", writes=[("wfm", ci)])

        load_x(0)
        for ci in range(NFM):
            load_w(ci)
        for kh in range(2):
            for half in range(2):
                P.dma("pool", lambda e, kh=kh, half=half: e.dma_start(
                    out=wtm[:, 4 * kh:4 * kh + 4, 384 * half:384 * half + 384],
                    in_=w_v[:, 4 * kh:4 * kh + 4, C_KB + 384 * half:C_KB + 384 * half + 384]),
                    "wtm", writes=["wtm"])
        ev = 0
        nb = 0
        nf = 0
        npf = 0
        NTT = int(os.environ.get('P_NTT', '16'))
        for tt in range(NTT):
            s = tt % 2
            if tt + 1 < NTT:
                load_x(tt + 1)
            tok = slice(512 * tt, 512 * tt + 512)
            for ci in range(NFM):
                c0, ncol, dst, r0, silu, odt, ntiles = fm[ci]
                if tt >= ntiles or ci >= int(os.environ.get('P_NCH', '99')):
                    continue
                pp = npf % 4
                npf += 1
                for k in range(8):
                    P.op("pe", lambda e, pp=pp, k=k, s=s, o=offs[ci], ncol=ncol: e.matmul(
                        pf[pp][0:ncol, :], lhsT=wfm[:, k, o:o + ncol], rhs=xt[s][:, k, :],
                        start=(k == 0), stop=(k == 7)),
                        reads=[("wfm", ci), ("xt", s)], writes=[("pf", pp)])
                if odt == BF16:
                    bi = nb % 4
                    nb += 1
                    buf, bkey, dkey = ob[bi], ("ob", bi), f"ob{bi}"
                else:
                    bi = nf % 2
                    nf += 1
                    buf, bkey, dkey = of[bi], ("of", bi), f"of{bi}"
                if silu:
                    P.op("act", lambda e, buf=buf, pp=pp, ncol=ncol: e.activation(
                        out=buf[0:ncol, :], in_=pf[pp][0:ncol, :], func=AF.Silu),
                        writes=[("pf", pp), bkey])
                elif ev % 2 == 0:
                    P.op("dve", lambda e, buf=buf, pp=pp, ncol=ncol: e.tensor_copy(
                        out=buf[0:ncol, :], in_=pf[pp][0:ncol, :]),
                        writes=[("pf", pp), bkey])
                    ev += 1
                else:
                    P.op("act", lambda e, buf=buf, pp=pp, ncol=ncol: e.activation(
                        out=buf[0:ncol, :], in_=pf[pp][0:ncol, :], func=AF.Copy),
                        writes=[("pf", pp), bkey])
                    ev += 1
                P.dma("sp", lambda e, buf=buf, dst=dst, r0=r0, ncol=ncol, tok=tok: e.dma_start(
                    out=dst[r0:r0 + ncol, tok], in_=buf[0:ncol, :]), dkey, reads=[bkey], writes=[])
            for sub in range(0 if os.environ.get('P_NOTM') else 4):
                st = 4 * tt + sub
                pa, pb = pt[(2 * st) % 4], pt[(2 * st + 1) % 4]
                ka, kb = ("pt", (2 * st) % 4), ("pt", (2 * st + 1) % 4)
                for half, (pp, pk) in enumerate(((pa, ka), (pb, kb))):
                    for k in range(8):
                        P.op("pe", lambda e, pp=pp, k=k, s=s, sub=sub, half=half: e.matmul(
                            pp[:, 0:384], lhsT=xt[s][:, k, 128 * sub:128 * sub + 128],
                            rhs=wtm[:, k, 384 * half:384 * half + 384], start=(k == 0), stop=(k == 7)),
                            reads=["wtm", ("xt", s)], writes=[pk])
                b2 = st % 2
                TMM = int(os.environ.get('P_TMMODE', '3'))
                if TMM < 2:
                    continue
                P.op("act", lambda e, b2=b2, pa=pa: e.activation(out=kst[b2][:], in_=pa[:, 0:256], func=AF.Copy),
                     writes=[ka, ("kst", b2)])
                P.op("act", lambda e, b2=b2, pa=pa: e.activation(out=vst[b2][:, 0:128], in_=pa[:, 256:384], func=AF.Copy),
                     writes=[ka, ("vstA", b2)])
                P.op("dve", lambda e, b2=b2, pb=pb: e.tensor_copy(out=vst[b2][:, 128:512], in_=pb[:, 0:384]),
                     writes=[kb, ("vstB", b2)])
                if TMM < 3:
                    continue
                P.dma("sp", lambda e, b2=b2, st=st: e.dma_start(out=T.kTM[128 * st:128 * st + 128, :], in_=kst[b2][:]),
                      f"kst{b2}", reads=[("kst", b2)])
                P.dma("sp", lambda e, b2=b2, st=st: e.dma_start(out=T.vTM[128 * st:128 * st + 128, :], in_=vst[b2][:]),
                      f"vst{b2}", reads=[("vstA", b2), ("vstB", b2)])
        P.flush()


def run_pipeline(items, nph, lag):
    n = len(items)
    for step in range(n + (nph - 1) * lag):
        for ph in range(nph - 1, -1, -1):
            i = step - ph * lag
            if 0 <= i < n and items[i][ph] is not None:
                items[i][ph]()


def stage_S1(nc, P, T):
    LAG = 1
    with ExitStack() as es:
        sb = lambda name, shape, dt: es.enter_context(nc.sbuf_tensor("a_" + name, shape, dt))
        ps = lambda name, shape, dt: es.enter_context(nc.psum_tensor("a_" + name, shape, dt))
        ebm = sb("ebm", [128, 12, 256], F32)
        ebm0 = sb("ebm0", [64, 12, 128], F32)
        emk = sb("emk", [128, 256], F32)
        emk0 = sb("emk0", [64, 128], F32)
        identb = sb("identb", [128, 128], BF16)
        qs = [sb(f"qs{i}", [128, 4096], BF16) for i in range(2)]
        ks = [sb(f"ks{i}", [128, 5120], BF16) for i in range(2)]
        vs = [sb(f"vs{i}", [128, 5120], BF16) for i in range(2)]
        va = [sb(f"va{i}", [128, 48, 2, 65], BF16) for i in range(2)]
        NE = 4
        esb = [sb(f"esb{i}", [128, 2, 256], F32) for i in range(NE)]
        ptb = [sb(f"ptb{i}", [128, 2, 256], BF16) for i in range(NE)]
        nst = [sb(f"nst{i}", [64, 2, 2048], F32) for i in range(2)]
        dst_ = [sb(f"dst{i}", [128, 2, 2048], F32) for i in range(2)]
        sT = [ps(f"sT{i}", [128, 1024], F32) for i in range(2)]
        oT = [ps(f"oT{i}", [128, 512], F32) for i in range(2)]
        tp = [ps(f"tp{i}", [128, 1024], BF16) for i in range(2)]

        P.dma("sp", lambda e: e.dma_start(out=ebm[:], in_=T.ebias.rearrange("p (h c) -> p h c", h=12)), "c1", writes=["ebm"])
        P.dma("sp", lambda e: e.dma_start(out=ebm0[:], in_=T.ebias0.rearrange("p (h c) -> p h c", h=12)), "c1", writes=["ebm0"])
        P.dma("sp", lambda e: e.dma_start(out=emk[:], in_=T.emask), "c1", writes=["emk"])
        P.dma("sp", lambda e: e.dma_start(out=emk0[:], in_=T.emask0), "c1", writes=["emk0"])
        P.dma("pool", lambda e: e.dma_start(out=identb[:], in_=T.ident), "c2", writes=["identb"])

        def load_pair(p):
            s = p % 2
            P.dma("sp", lambda e, s=s, p=p: e.dma_start(out=qs[s][:], in_=T.qaT[128 * p:128 * p + 128, :]),
                  f"qs{s}", writes=[("qs", s)])
            P.dma("sp", lambda e, s=s, p=p: e.dma_start(out=ks[s][:], in_=T.kaT[128 * p:128 * p + 128, :]),
                  f"ks{s}", writes=[("ks", s)])
            P.dma("sp", lambda e, s=s, p=p: e.dma_start(out=vs[s][:], in_=T.vaT[128 * p:128 * p + 128, :]),
                  f"vs{s}", writes=[("vs", s)])

        load_pair(0)
        load_pair(1)
        P.op("act", lambda e: e.activation(out=ebm[:], in_=ebm[:], func=AF.Exp), reads=["ebm"], writes=["ebm"])
        P.op("act", lambda e: e.activation(out=ebm0[:], in_=ebm0[:], func=AF.Exp), reads=["ebm0"], writes=["ebm0"])
        for h in range(12):
            P.op("dve", lambda e, h=h: e.tensor_tensor(out=ebm[:, h, :], in0=ebm[:, h, :], in1=emk[:], op=ALU.mult),
                 reads=["ebm", "emk"], writes=["ebm"])
            P.op("dve", lambda e, h=h: e.tensor_tensor(out=ebm0[:, h, :], in0=ebm0[:, h, :], in1=emk0[:], op=ALU.mult),
                 reads=["ebm0", "emk0"], writes=["ebm0"])
        for i_ in range(2):
            P.op("pool", lambda e, i_=i_: e.memset(va[i_][:], 1.0), writes=[("va", i_, c_) for c_ in range(48)])

        items = []
        cnt = dict(t=0, tp=0, sup=0)

        def add_vbuild(p):
            s = p % 2
            d = DILS[p // 2]
            Lr = OWN // d
            NC = Lr // 128 + 1
            for r in range(d):
                for c in range(NC):
                    if c == 0:
                        sl, n = slice(r, r + 63 * d + 1, d), 64
                    else:
                        st = d * (128 * c - 64) + r
                        sl, n = slice(st, st + 127 * d + 1, d), 128
                    idx = r * NC + c
                    tq = cnt["tp"] % 2
                    cnt["tp"] += 1

                    def fA(tq=tq, s=s, sl=sl, n=n):
                        P.op("pe", lambda e: e.transpose(out=tp[tq][0:n, 0:128], in_=vs[s][:, sl], identity=identb[:]),
                             reads=[("vs", s), "identb"], writes=[("tp", tq)])

                    def fB(tq=tq, s=s, idx=idx, n=n):
                        if idx % 2 == 0:
                            P.op("act", lambda e: e.activation(
                                out=va[s][0:n, idx, :, 0:64], in_=tp[tq][0:n, 0:128].rearrange("p (h e) -> p h e", h=2),
                                func=AF.Copy), writes=[("tp", tq), ("va", s, idx)])
                        else:
                            P.op("dve", lambda e: e.tensor_copy(
                                out=va[s][0:n, idx, :, 0:64], in_=tp[tq][0:n, 0:128].rearrange("p (h e) -> p h e", h=2)),
                                writes=[("tp", tq), ("va", s, idx)])
                    items.append([fA, fB, None, None, None, None, None])

        def add_attn(p):
            s = p % 2
            g = p // 2
            d = DILS[g]
            Lr = OWN // d
            NC = Lr // 128 + 1

            def ksl(r, c):
                if c == 0:
                    return slice(r, r + 63 * d + 1, d), 64
                st = d * (128 * c - 64) + r
                return slice(st, st + 127 * d + 1, d), 128

            for u in range(2):
                if d == 1:
                    tiles = [(0, m) for m in range(16 * u, 16 * u + 16)]
                elif d == 4:
                    tiles = [(r, m) for m in range(4 * u, 4 * u + 4) for r in range(4)]
                else:
                    tiles = [(r, u) for r in range(16)]
                sbuf_i = cnt["sup"] % 2
                cnt["sup"] += 1
                for ti, (r, m) in enumerate(tiles):
                    qst = d * 128 * m + r
                    qsl = slice(qst, qst + 127 * d + 1, d)
                    osl = slice(qst - 2048 * u, qst - 2048 * u + 127 * d + 1, d)
                    slA, nA = ksl(r, m)
                    slB, nB = ksl(r, m + 1)
                    iA, iB = r * NC + m, r * NC + m + 1
                    t2 = cnt["t"] % 2
                    t4 = cnt["t"] % NE
                    cnt["t"] += 1
                    last = (ti == len(tiles) - 1)

                    def fA(t2=t2, s=s, slA=slA, nA=nA, slB=slB, qsl=qsl):
                        for hh in range(2):
                            pb = 64 * hh
                            c0 = 512 * hh
                            P.op("pe", lambda e, pb=pb, c0=c0: e.matmul(
                                sT[t2][0:nA, c0:c0 + 128], lhsT=ks[s][pb:pb + 64, slA], rhs=qs[s][pb:pb + 64, qsl],
                                start=True, stop=True), reads=[("ks", s), ("qs", s)], writes=[("sT", t2, hh)])
                            P.op("pe", lambda e, pb=pb, c0=c0: e.matmul(
                                sT[t2][:, c0 + 128:c0 + 256], lhsT=ks[s][pb:pb + 64, slB], rhs=qs[s][pb:pb + 64, qsl],
                                start=True, stop=True), reads=[("ks", s), ("qs", s)], writes=[("sT", t2, hh)])

                    def fB(t2=t2, t4=t4, nA=nA):
                        sv = sT[t2][:].rearrange("p (h c) -> p h c", h=2)
                        wk = [("sT", t2, 0), ("sT", t2, 1), ("esb", t4)]
                        if nA == 128:
                            P.op("act", lambda e: e.activation(out=esb[t4][:], in_=sv[:, :, 0:256], func=AF.Exp, scale=0.125), writes=wk)
                        else:
                            P.op("act", lambda e: e.activation(out=esb[t4][0:64, :, 0:128], in_=sv[0:64, :, 0:128], func=AF.Exp, scale=0.125), writes=wk)
                            P.op("act", lambda e: e.activation(out=esb[t4][:, :, 128:256], in_=sv[:, :, 128:256], func=AF.Exp, scale=0.125), writes=wk)

                    def fC(t4=t4, nA=nA, p=p):
                        if nA == 128:
                            P.op("dve", lambda e: e.tensor_tensor(out=ptb[t4][:], in0=esb[t4][:], in1=ebm[:, 2 * p:2 * p + 2, :], op=ALU.mult),
                                 reads=[("esb", t4), "ebm"], writes=[("ptb", t4)])
                        else:
                            P.op("dve", lambda e: e.tensor_tensor(out=ptb[t4][0:64, :, 0:128], in0=esb[t4][0:64, :, 0:128],
                                                                  in1=ebm0[:, 2 * p:2 * p + 2, :], op=ALU.mult),
                                 reads=[("esb", t4), "ebm0"], writes=[("ptb", t4)])
                            P.op("dve", lambda e: e.tensor_tensor(out=ptb[t4][:, :, 128:256], in0=esb[t4][:, :, 128:256],
                                                                  in1=ebm[:, 2 * p:2 * p + 2, 128:256], op=ALU.mult),
                                 reads=[("esb", t4), "ebm"], writes=[("ptb", t4)])

                    def fD(t2=t2, t4=t4, s=s, iA=iA, iB=iB, nA=nA):
                        for hh in range(2):
                            P.op("pe", lambda e, hh=hh: e.matmul(
                                oT[t2][0:65, 128 * hh:128 * hh + 128], lhsT=va[s][0:nA, iA, hh, :], rhs=ptb[t4][0:nA, hh, 0:128],
                                start=True, stop=False), reads=[("va", s, iA), ("ptb", t4)], writes=[("oT", t2)])
                            P.op("pe", lambda e, hh=hh: e.matmul(
                                oT[t2][0:65, 128 * hh:128 * hh + 128], lhsT=va[s][:, iB, hh, :], rhs=ptb[t4][:, hh, 128:256],
                                start=False, stop=True), reads=[("va", s, iB), ("ptb", t4)], writes=[("oT", t2)])

                    def fE(t2=t2, sbuf_i=sbuf_i, osl=osl, last=last, p=p, u=u):
                        P.op("act", lambda e: e.activation(
                            out=nst[sbuf_i][0:64, :, osl], in_=oT[t2][0:64, 0:256].rearrange("p (h c) -> p h c", h=2), func=AF.Copy),
                            writes=[("oT", t2), ("nst", sbuf_i)])
                        P.op("dve", lambda e: e.tensor_copy(
                            out=dst_[sbuf_i][64:65, :, osl], in_=oT[t2][64:65, 0:256].rearrange("p (h c) -> p h c", h=2)),
                            writes=[("oT", t2), ("dst", sbuf_i)])
                        if last:
                            tk = slice(2048 * u, 2048 * u + 2048)
                            P.dma("pool", lambda e: e.dma_start(
                                out=T.numT[128 * p:128 * p + 128, tk].rearrange("(h e) t -> e h t", h=2), in_=nst[sbuf_i][:]),
                                f"nst{sbuf_i}", reads=[("nst", sbuf_i)])
                            P.dma("pool", lambda e: e.dma_start(
                                out=T.den[2 * p:2 * p + 2, tk].rearrange("(o h) t -> o h t", o=1), in_=dst_[sbuf_i][64:65, :, :]),
                                f"dst{sbuf_i}", reads=[("dst", sbuf_i)])
                    items.append([fA, fB, None, fC, None, fD, fE])

        def add_load(p):
            items.append([lambda p=p: load_pair(p), None, None, None, None, None, None])

        add_vbuild(0)
        for p in range(6):
            n0 = len(items)
            add_attn(p)
            at = items[n0:]
            del items[n0:]
            vb = []
            if p + 1 < 6:
                add_vbuild(p + 1)
                vb = items[n0:]
                del items[n0:]
            merged = at[:12]
            rest = at[12:]
            for j in range(max(len(rest), len(vb))):
                if j < len(rest):
                    merged.append(rest[j])
                if j < len(vb):
                    merged.append(vb[j])
            items.extend(merged)
            if p + 2 < 6:
                add_load(p + 2)
        run_pipeline(items, 7, LAG)
        P.flush()


def stage_S2(nc, P, T, prefetch=None):
    NR = 8
    with ExitStack() as es:
        sb = lambda name, shape, dt: es.enter_context(nc.sbuf_tensor("g_" + name, shape, dt))
        ps = lambda name, shape, dt: es.enter_context(nc.psum_tensor("g_" + name, shape, dt))
        tri = sb("tri", [128, 512], F32)
        ind = sb("ind", [128, 2], F32)
        smask = sb("smask", [128, 256], F32)
        smask4 = sb("smask4", [128, 512], F32)
        onesm = sb("onesm", [128, 128], BF16)
        ones32 = sb("ones32", [128, 128], F32)
        w2d = [sb(f"w2d{i}", [17, 256], F32) for i in range(2)]
        gng = sb("gng", [128, 4], F32)
        gexp = sb("gexp", [128, 2, 2, 128], F32)
        rm = [sb(f"rm{i}", [128, 1], F32) for i in range(2)]
        eb_all = sb("eb_all", [128, 2, 64, 128], BF16)
        efb = sb("efb", [128, 2, NR, 128], BF16)
        RF = sb("RF", [128, 2, NR, 128], F32)
        RB = sb("RB", [128, 2, NR, 128], F32)
        lrd = [[sb(f"lr{d}{i}", [17, 512], F32) for i in range(2)] for d in range(2)]
        lrh = [[sb(f"lrh{d}{i}", [17, 512], BF16) for i in range(2)] for d in range(2)]
        lrl = [[sb(f"lrl{d}{i}", [17, 512], BF16) for i in range(2)] for d in range(2)]
        w2h = [sb(f"w2h{i}", [17, 256], BF16) for i in range(2)]
        w2l = [sb(f"w2l{i}", [17, 256], BF16) for i in range(2)]
        tri_bf = sb("tri_bf", [128, 512], BF16)
        ind_bf = sb("ind_bf", [128, 2], BF16)
        sph = [sb(f"sph{i}", [128, 512], BF16) for i in range(2)]
        spl = [sb(f"spl{i}", [128, 512], BF16) for i in range(2)]
        DK, DV, DQ, DS = 8, 8, 6, 8
        ktm = [sb(f"ktm{i}", [128, 256], F32) for i in range(DK)]
        vtm = [sb(f"vtm{i}", [128, 512], BF16) for i in range(DV)]
        qg = [sb(f"qg{i}", [128, 2, 128], F32) for i in range(DQ)]
        kg = [sb(f"kg{i}", [128, 2, 128], F32) for i in range(DQ)]
        sgb = [sb(f"sgb{i}", [128, 4, 128], BF16) for i in range(DS)]
        yst = [sb(f"yst{i}", [128, 4, 512], BF16) for i in range(2)]
        sp_sb = [sb(f"sp_sb{i}", [128, 512], F32) for i in range(2)]
        e_sb = sp_sb
        ewd = [sb(f"ewd{i}", [128, 260], F32) for i in range(4)]
        ebp = [sb(f"ebp{i}", [128, 512], F32) for i in range(2)]
        ebn = [sb(f"ebn{i}", [128, 512], F32) for i in range(2)]
        kpc = [[sb(f"kp{c}{i}", [128, 256], BF16) for i in range(2)] for c in range(2)]
        qd = [sb(f"qd{i}", [128, 4, 128], BF16) for i in range(4)]
        kd = [sb(f"kd{i}", [128, 4, 128], BF16) for i in range(4)]
        scm = [sb(f"scm{i}", [128, 2, 512], BF16) for i in range(2)]
        osb = [sb(f"osb{i}", [128, 2, 2, 128], F32) for i in range(4)]
        osq = [sb(f"osq{i}", [128, 2, 2, 128], BF16) for i in range(2)]
        rstd = [sb(f"rstd{i}", [128, 512], F32) for i in range(2)]
        rs1 = rstd
        Zp = ps("Z", [128, 512], F32)
        Wp = ps("W", [128, 512], F32)
        Bp = ps("B", [128, 512], F32)
        SC = [ps(f"SC{i}", [128, 512], F32) for i in range(2)]
        KV = ps("KV", [128, 512], F32)
        Op = [ps(f"O{i}", [128, 512], F32) for i in range(2)]

        P.dma("sp", lambda e: e.dma_start(out=tri[:], in_=T.tri), "c1", writes=["tri"])
        P.dma("sp", lambda e: e.dma_start(out=ind[:], in_=T.ind), "c1", writes=["ind"])
        P.dma("sp", lambda e: e.dma_start(out=smask[:], in_=T.smask), "c1", writes=["smask"])
        P.dma("sp", lambda e: e.dma_start(out=gng[:], in_=T.gng), "c1", writes=["gng"])
        for d in range(2):
            P.dma("sp", lambda e, d=d: e.dma_start(out=w2d[d][0:16, :], in_=T.w2[16 * d:16 * d + 16, :]), "c1", writes=[("w2d", d)])
            P.dma("sp", lambda e, d=d: e.dma_start(out=w2d[d][16:17, :], in_=T.gbias[0:1, 256 * d:256 * d + 256]), "c1", writes=[("w2d", d)])
            for i in range(2):
                P.op("dve", lambda e, d=d, i=i: e.memset(lrd[d][i][:], 1.0), writes=[("lr", d, i)])
            P.op("act", lambda e, d=d: e.activation(out=w2h[d][:], in_=w2d[d][:], func=AF.Copy), reads=[("w2d", d)], writes=[("w2h", d)])
            P.op("dve", lambda e, d=d: e.tensor_tensor(out=w2l[d][:], in0=w2d[d][:], in1=w2h[d][:], op=ALU.subtract),
                 reads=[("w2d", d), ("w2h", d)], writes=[("w2l", d)])
        P.op("dve", lambda e: e.memset(onesm[:], 1.0 / 128.0), writes=["onesm"])
        P.op("act", lambda e: e.activation(out=tri_bf[:], in_=tri[:], func=AF.Copy), reads=["tri"], writes=["tri_bf"])
        P.op("act", lambda e: e.activation(out=ind_bf[:], in_=ind[:], func=AF.Copy), reads=["ind"], writes=["ind_bf"])
        P.op("dve", lambda e: e.memset(ones32[:], 1.0), writes=["ones32"])
        P.op("dve", lambda e: e.memset(rm[0][:], 0.0), writes=["rm"])
        P.op("dve", lambda e: e.memset(rm[1][:], 0.0), writes=["rm"])
        P.op("dve", lambda e: e.memset(rm[0][0:64, :], 1.0), writes=["rm"])
        P.op("dve", lambda e: e.memset(rm[1][64:128, :], 1.0), writes=["rm"])
        for cc_ in range(2):
            for i_ in range(2):
                P.op("pool", lambda e, cc_=cc_, i_=i_: e.memset(kpc[cc_][i_][:], 0.0), writes=[("kp", cc_, i_)])
        P.op("pool", lambda e: e.memset(RF[:], 0.0), writes=["RFall"])
        P.op("pool", lambda e: e.memset(RB[:], 0.0), writes=["RBall"])
        for d in range(2):
            for hf in range(2):
                P.op("dve", lambda e, d=d, hf=hf: e.tensor_copy(out=smask4[:, (2 * d + hf) * 128:(2 * d + hf) * 128 + 128],
                                                                 in_=smask[:, 128 * d:128 * d + 128]),
                     reads=["smask"], writes=["smask4"])
        for h in range(4):
            P.op("dve", lambda e, h=h: e.tensor_scalar(out=gexp[:, h % 2, h // 2, :], in0=ones32[:], scalar1=gng[:, h:h + 1], scalar2=None,
                                                       op0=ALU.mult), reads=["ones32", "gng"], writes=["gexp"])
        if prefetch is not None:
            prefetch()

        cn = dict(k=0, v=0, q=0, s=0, it=0)
        lr_slot = {}

        def kv_mm(kslot, vslot, order, bank=None, bkey="KV"):
            KVb = KV if bank is None else bank
            for ci, cc in enumerate(order):
                for hf in range(2):
                    for q_ in range(2):
                        col = (2 * ci + hf) * 128
                        P.op("pe", lambda e, cc=cc, hf=hf, q_=q_, col=col: e.matmul(
                            KVb[64 * q_:64 * q_ + 64, col:col + 128], lhsT=kpc[cc][kslot][:, 128 * hf + 64 * q_:128 * hf + 64 * q_ + 64],
                            rhs=vtm[vslot][:, (2 * hf + q_) * 128:(2 * hf + q_) * 128 + 128], start=True, stop=True),
                            reads=[("kp", cc, kslot), ("vtm", vslot)], writes=[bkey])

        def kp_ops(it2, kslot, wslot):
            for cc in range(2):
                rows = slice(64 * cc, 64 * cc + 64)
                P.op("pool", lambda e, cc=cc, rows=rows: e.tensor_tensor(
                    out=kpc[cc][it2][rows, :], in0=ktm[kslot][rows, :], in1=ewd[wslot][rows, 0:256], op=ALU.mult),
                    reads=[("ktm", kslot), ("ewd", wslot)], writes=[("kp", cc, it2)])

        def lr_split(d, ls):
            P.op("act", lambda e: e.activation(out=lrh[d][ls][:], in_=lrd[d][ls][:], func=AF.Copy),
                 reads=[("lr", d, ls)], writes=[("lrh", d, ls)])
            P.op("dve", lambda e: e.tensor_tensor(out=lrl[d][ls][:], in0=lrd[d][ls][:], in1=lrh[d][ls][:], op=ALU.subtract),
                 reads=[("lr", d, ls), ("lrh", d, ls)], writes=[("lrl", d, ls)])

        def z_mm(d, ls, tl, bank=None, bkey="Z"):
            Zb = Zp if bank is None else bank
            ops = ((lrh, w2h), (lrh, w2l), (lrl, w2h))
            for i_, (a_, b_) in enumerate(ops):
                P.op("pe", lambda e, a_=a_, b_=b_, i_=i_: e.matmul(
                    Zb[:, 256 * d:256 * d + 256], lhsT=a_[d][ls][0:17, tl], rhs=b_[d][0:17, :], start=(i_ == 0), stop=(i_ == 2)),
                    reads=[("lrh", d, ls), ("lrl", d, ls), ("w2h", d), ("w2l", d)], writes=[bkey])

        def softplus(it2, cs, bank=None, bkey="Z"):
            Zb = Zp if bank is None else bank
            P.op("act", lambda e: e.activation(out=sp_sb[it2][:, cs], in_=Zb[:, cs], func=AF.Exp, scale=-1.0), writes=[bkey, ("sp", it2)])
            P.op("act", lambda e: e.activation(out=sph[it2][:, cs], in_=sp_sb[it2][:, cs], func=AF.Ln, bias=1.0),
                 reads=[("sp", it2)], writes=[("sph", it2)])
            P.op("act", lambda e: e.activation(out=sp_sb[it2][:, cs], in_=sp_sb[it2][:, cs], func=AF.Ln, bias=1.0), writes=[("sp", it2)])

        def sp_lo(it2, cs):
            P.op("dve", lambda e: e.tensor_tensor(out=spl[it2][:, cs], in0=sp_sb[it2][:, cs], in1=sph[it2][:, cs], op=ALU.subtract),
                 reads=[("sp", it2), ("sph", it2)], writes=[("spl", it2)])

        def w_tot_mm(it2, d, bank=None, bkey="W"):
            Wb = Wp if bank is None else bank
            c0 = 256 * d
            wsl = slice(256 + 128 * d, 256 + 128 * d + 128)
            for i_, src in enumerate((sph, spl)):
                P.op("pe", lambda e, src=src, i_=i_: e.matmul(Wb[:, 0:256], lhsT=tri_bf[:, wsl], rhs=src[it2][:, c0:c0 + 256],
                                                              start=(i_ == 0), stop=(i_ == 1)),
                     reads=["tri_bf", ("sph", it2), ("spl", it2)], writes=[bkey])
            for hf in range(2):
                for i_, src in enumerate((sph, spl)):
                    P.op("pe", lambda e, src=src, i_=i_, hf=hf: e.matmul(
                        Wb[:, 256 + 2 * hf:258 + 2 * hf], lhsT=src[it2][:, c0 + 128 * hf:c0 + 128 * hf + 128], rhs=ind_bf[:],
                        start=(i_ == 0), stop=(i_ == 1)), reads=["ind_bf", ("sph", it2), ("spl", it2)], writes=[bkey])

        items = []
        for st in range(63, -1, -1):
            it = cn["it"]
            cn["it"] += 1
            it2, it4 = it % 2, it % 4
            blk = st // 4
            kslot = cn["k"] % DK
            cn["k"] += 1
            vslot = cn["v"] % DV
            cn["v"] += 1
            if st % 4 == 3:
                lr_slot[blk] = (15 - blk) % 2
            ls = lr_slot[blk]
            tl = slice(128 * (st % 4), 128 * (st % 4) + 128)

            def f0(st=st, blk=blk, ls=ls):
                if st % 4 == 3:
                    P.dma("sp", lambda e: e.dma_start(out=lrd[1][ls][0:16, :], in_=T.lrT[16:32, 512 * blk:512 * blk + 512]),
                          f"lr1{ls}", writes=[("lr", 1, ls)])

            def f0b(st=st, ls=ls):
                if st % 4 == 3:
                    lr_split(1, ls)

            zb, zk = (Zp, "Z") if it2 == 0 else (SC[0], ("SC", 0))
            wb_, wk = (Wp, "W") if it2 == 0 else (SC[1], ("SC", 1))
            kb_, kk = (KV, "KV") if it2 == 0 else (Bp, "B")

            def f1(st=st, ls=ls, tl=tl, kslot=kslot, vslot=vslot, zb=zb, zk=zk):
                z_mm(1, ls, tl, zb, zk)
                P.dma("sp", lambda e: e.dma_start(out=ktm[kslot][:], in_=T.kTM[128 * st:128 * st + 128, :]),
                      f"ktm{kslot}", writes=[("ktm", kslot)])
                P.dma("sp", lambda e: e.dma_start(out=vtm[vslot][:], in_=T.vTM[128 * st:128 * st + 128, :]),
                      f"vtm{vslot}", writes=[("vtm", vslot)])

            def f2(it2=it2, zb=zb, zk=zk):
                softplus(it2, slice(256, 512), zb, zk)

            def f2b(it2=it2):
                sp_lo(it2, slice(256, 512))

            def f3(it2=it2, wb_=wb_, wk=wk):
                w_tot_mm(it2, 1, wb_, wk)

            def f4(it4=it4, wb_=wb_, wk=wk):
                P.op("act", lambda e: e.activation(out=ewd[it4][:], in_=wb_[:, 0:260], func=AF.Exp), writes=[wk, ("ewd", it4)])

            def f5(it2=it2, it4=it4, kslot=kslot):
                kp_ops(it2, kslot, it4)

            def f6(it2=it2, vslot=vslot, kb_=kb_, kk=kk):
                kv_mm(it2, vslot, (1, 0), kb_, kk)

            def f7(st=st, it4=it4, kb_=kb_, kk=kk):
                for ci, cc in enumerate((1, 0)):
                    c = 2 * st + cc
                    s0, s1 = c % NR, (c - 1) % NR
                    if c < 64:
                        P.op("act", lambda e, c=c, s0=s0: e.activation(out=eb_all[:, :, c, :], in_=RB[:, :, s0, :], func=AF.Copy),
                             reads=[("RB", 0, s0), ("RB", 1, s0), "RBall"], writes=[("eb_all", c)])
                    for hf in range(2):
                        col = (2 * ci + hf) * 128
                        P.op("dve", lambda e, hf=hf, s0=s0, s1=s1, col=col, cc=cc: e.scalar_tensor_tensor(
                            out=RB[:, hf, s1, :], in0=RB[:, hf, s0, :], scalar=ewd[it4][:, 256 + 2 * hf + cc:257 + 2 * hf + cc],
                            in1=kb_[:, col:col + 128], op0=ALU.mult, op1=ALU.add),
                            reads=[("RB", hf, s0), ("ewd", it4), "RBall"], writes=[("RB", hf, s1), kk])
            items.append([f0, f0b, f1, f2, f2b, f3, f4, f5, f6, f7])
        run_pipeline(items, 10, 1)

        items = []
        NPH = 15
        for st in range(32):
            it = cn["it"]
            cn["it"] += 1
            it2, it4 = it % 2, it % 4
            blk = st // 4
            bs3 = blk % 2
            kslot = cn["k"] % DK
            cn["k"] += 1
            vslot = cn["v"] % DV
            cn["v"] += 1
            qslot = cn["q"] % DQ
            cn["q"] += 1
            sslot = cn["s"] % DS
            cn["s"] += 1
            ls = blk % 2
            tl = slice(128 * (st % 4), 128 * (st % 4) + 128)
            tg = slice(128 * st, 128 * st + 128)

            def f0(st=st, blk=blk, ls=ls):
                if st % 4 == 0:
                    for d in range(2):
                        P.dma("sp", lambda e, d=d: e.dma_start(out=lrd[d][ls][0:16, :], in_=T.lrT[16 * d:16 * d + 16, 512 * blk:512 * blk + 512]),
                              f"lr{d}{ls}", writes=[("lr", d, ls)])

            def f0b(st=st, ls=ls):
                if st % 4 == 0:
                    for d in range(2):
                        lr_split(d, ls)

            def f1(ls=ls, tl=tl):
                for d in range(2):
                    z_mm(d, ls, tl)

            def f2(it2=it2, st=st, kslot=kslot, qslot=qslot, tg=tg):
                softplus(it2, slice(0, 512))
                P.dma("sp", lambda e: e.dma_start(out=ktm[kslot][:], in_=T.kTM[128 * st:128 * st + 128, :]),
                      f"ktm{kslot}", writes=[("ktm", kslot)])
                P.dma("sp", lambda e: e.dma_start(out=qg[qslot][:], in_=T.qgT.rearrange("(a p) t -> p a t", p=128)[:, :, tg]),
                      f"qg{qslot}", writes=[("qg", qslot)])
                P.dma("sp", lambda e: e.dma_start(out=kg[qslot][:], in_=T.kgT.rearrange("(a p) t -> p a t", p=128)[:, :, tg]),
                      f"kg{qslot}", writes=[("kg", qslot)])

            def f2b(it2=it2):
                sp_lo(it2, slice(0, 512))

            def f3(it2=it2, st=st, vslot=vslot):
                w_tot_mm(it2, 0)
                for d in range(2):
                    for hf in range(2):
                        cb = (2 * d + hf) * 128
                        for i_, src in enumerate((sph, spl)):
                            P.op("pe", lambda e, d=d, hf=hf, cb=cb, src=src, i_=i_: e.matmul(
                                Bp[:, cb:cb + 128], lhsT=src[it2][:, 256 * d + 128 * hf:256 * d + 128 * hf + 128], rhs=tri_bf[:, 128 * d:128 * d + 128],
                                start=(i_ == 0), stop=(i_ == 1)), reads=["tri_bf", ("sph", it2), ("spl", it2)], writes=["B"])
                P.dma("sp", lambda e: e.dma_start(out=vtm[vslot][:], in_=T.vTM[128 * st:128 * st + 128, :]),
                      f"vtm{vslot}", writes=[("vtm", vslot)])

            def f4(it2=it2, it4=it4):
                P.op("act", lambda e: e.activation(out=ewd[it4][:], in_=Wp[:, 0:260], func=AF.Exp), writes=["W", ("ewd", it4)])
                P.op("act", lambda e: e.activation(out=ebp[it2][:], in_=Bp[:], func=AF.Exp), writes=["B", ("ebp", it2)])
                P.op("act", lambda e: e.activation(out=ebn[it2][:], in_=Bp[:], func=AF.Exp, scale=-1.0), writes=["B", ("ebn", it2)])

            def f5(it2=it2, it4=it4, kslot=kslot, qslot=qslot):
                kp_ops(it2, kslot, it4)
                for d in range(2):
                    P.op("dve", lambda e, d=d: e.scalar_tensor_tensor(
                        out=qd[it4][:, 2 * d:2 * d + 2, :], in0=qg[qslot][:], scalar=0.125,
                        in1=ebp[it2][:, 256 * d:256 * d + 256].rearrange("p (a t) -> p a t", a=2), op0=ALU.mult, op1=ALU.mult),
                        reads=[("qg", qslot), ("ebp", it2)], writes=[("qd", it4)])
                    P.op("dve", lambda e, d=d: e.tensor_tensor(
                        out=kd[it4][:, 2 * d:2 * d + 2, :], in0=kg[qslot][:],
                        in1=ebn[it2][:, 256 * d:256 * d + 256].rearrange("p (a t) -> p a t", a=2), op=ALU.mult),
                        reads=[("kg", qslot), ("ebn", it2)], writes=[("kd", it4)])

            def f6(it2=it2, it4=it4, vslot=vslot):
                kv_mm(it2, vslot, (0, 1))
                for d in range(2):
                    for h in range(4):
                        hf, hb = h // 2, 64 * (h % 2)
                        cb = (2 * d + hf) * 128
                        P.op("pe", lambda e, d=d, h=h, hf=hf, hb=hb, cb=cb: e.matmul(
                            SC[h % 2][:, cb:cb + 128], lhsT=kd[it4][hb:hb + 64, 2 * d + hf, :], rhs=qd[it4][hb:hb + 64, 2 * d + hf, :],
                            start=True, stop=True), reads=[("kd", it4), ("qd", it4)], writes=[("SC", h % 2)])

            def f7(st=st, it2=it2, it4=it4, sslot=sslot, tg=tg):
                for ci, cc in enumerate((0, 1)):
                    c = 2 * st + cc
                    s0, s1 = c % NR, (c + 1) % NR
                    P.op("pool", lambda e, s0=s0: e.tensor_copy(out=efb[:, :, s0, :], in_=RF[:, :, s0, :]),
                         reads=[("RF", 0, s0), ("RF", 1, s0), "RFall"], writes=[("efb", s0)])
                    for hf in range(2):
                        col = (2 * ci + hf) * 128
                        P.op("dve", lambda e, hf=hf, s0=s0, s1=s1, col=col, cc=cc: e.scalar_tensor_tensor(
                            out=RF[:, hf, s1, :], in0=RF[:, hf, s0, :], scalar=ewd[it4][:, 256 + 2 * hf + cc:257 + 2 * hf + cc],
                            in1=KV[:, col:col + 128], op0=ALU.mult, op1=ALU.add),
                            reads=[("RF", hf, s0), ("ewd", it4), "RFall"], writes=[("RF", hf, s1), "KV"])
                for par in range(2):
                    P.op("dve", lambda e, par=par: e.tensor_tensor(out=scm[it2][:, par, :], in0=SC[par][:], in1=smask4[:], op=ALU.mult),
                         reads=["smask4"], writes=[("SC", par), ("scm", it2, par)])
                P.dma("sp", lambda e: e.dma_start(out=sgb[sslot][:], in_=T.sgbT.rearrange("(a p) t -> p a t", p=128)[:, :, tg]),
                      f"sgb{sslot}", writes=[("sgb", sslot)])

            def f8(st=st, it2=it2, it4=it4, vslot=vslot):
                for h in range(4):
                    hf, par = h // 2, h % 2
                    hb = 64 * par
                    po = Op[par]
                    ob = 128 * hf
                    P.op("pe", lambda e, po=po, h=h, hf=hf, par=par, ob=ob: e.matmul(
                        po[:, ob:ob + 128], lhsT=vtm[vslot][:, 128 * h:128 * h + 128], rhs=scm[it2][:, par, 128 * hf:128 * hf + 128],
                        start=True, stop=False), reads=[("vtm", vslot), ("scm", it2, par)], writes=[("O", par)])
                    P.op("pe", lambda e, po=po, h=h, hf=hf, par=par, ob=ob: e.matmul(
                        po[:, ob:ob + 128], lhsT=vtm[vslot][:, 128 * h:128 * h + 128], rhs=scm[it2][:, par, 256 + 128 * hf:256 + 128 * hf + 128],
                        start=False, stop=False), reads=[("vtm", vslot), ("scm", it2, par)], writes=[("O", par)])
                    for cc in range(2):
                        c = 2 * st + cc
                        P.op("pe", lambda e, po=po, hf=hf, hb=hb, c=c, cc=cc, ob=ob: e.matmul(
                            po[:, ob + 64 * cc:ob + 64 * cc + 64], lhsT=efb[hb:hb + 64, hf, c % NR, :], rhs=qd[it4][hb:hb + 64, hf, 64 * cc:64 * cc + 64],
                            start=False, stop=False), reads=[("efb", c % NR), ("qd", it4)], writes=[("O", par)])
                        P.op("pe", lambda e, po=po, hf=hf, hb=hb, c=c, cc=cc, ob=ob: e.matmul(
                            po[:, ob + 64 * cc:ob + 64 * cc + 64], lhsT=eb_all[hb:hb + 64, hf, c, :], rhs=qd[it4][hb:hb + 64, 2 + hf, 64 * cc:64 * cc + 64],
                            start=False, stop=(cc == 1)), reads=[("eb_all", c), ("qd", it4)], writes=[("O", par)])

            def f9(it2=it2, it4=it4):
                for par in range(2):
                    P.op("act", lambda e, par=par: e.activation(out=osb[it4][:, par, :, :], in_=Op[par][:, 0:256].rearrange("p (a t) -> p a t", a=2),
                                                                func=AF.Copy), writes=[("O", par), ("osb", it4, par)])
                    P.op("act", lambda e, par=par: e.activation(out=osq[it2][:, par, :, :], in_=Op[par][:, 0:256].rearrange("p (a t) -> p a t", a=2),
                                                                func=AF.Square), writes=[("O", par), ("osq", it2, par)])

            def f10(it2=it2):
                for par in range(2):
                    P.op("pe", lambda e, par=par: e.matmul(Op[par][:, 256:512], lhsT=onesm[:], rhs=osq[it2][:, par, :, :],
                                                           start=True, stop=True), reads=["onesm", ("osq", it2, par)], writes=[("O", par)])

            def f11(it2=it2):
                for par in range(2):
                    P.op("act", lambda e, par=par: e.activation(out=rs1[it2][:, 256 * par:256 * par + 256], in_=Op[par][:, 256:512], func=AF.Ln, bias=RMS_EPS),
                         writes=[("O", par), ("rstd", it2)])
                P.op("act", lambda e: e.activation(out=rstd[it2][:], in_=rs1[it2][:], func=AF.Exp, scale=-0.5),
                     writes=[("rstd", it2)])

            def f12(st=st, it2=it2, it4=it4, sslot=sslot, bs3=bs3, tl=tl, blk=blk):
                P.op("dve", lambda e: e.tensor_tensor(out=osb[it4][:], in0=osb[it4][:], in1=rstd[it2][:].rearrange("p (a b t) -> p a b t", a=2, b=2), op=ALU.mult),
                     reads=[("rstd", it2)], writes=[("osb", it4, 0), ("osb", it4, 1)])
                P.op("dve", lambda e: e.tensor_tensor(out=osb[it4][:], in0=osb[it4][:], in1=gexp[:], op=ALU.mult),
                     reads=["gexp"], writes=[("osb", it4, 0), ("osb", it4, 1)])
                for par in range(2):
                    P.op("dve", lambda e, par=par: e.tensor_tensor(
                        out=yst[bs3][:, :, tl].rearrange("p (j q) t -> p q j t", q=2)[:, par, :, :], in0=osb[it4][:, par, :, :],
                        in1=sgb[sslot][:].rearrange("p (j q) t -> p q j t", q=2)[:, par, :, :], op=ALU.mult),
                        reads=[("osb", it4, par), ("sgb", sslot)], writes=[("yst", bs3)])
                if st % 4 == 3:
                    tk = slice(512 * blk, 512 * blk + 512)
                    P.dma("pool", lambda e: e.dma_start(out=T.ygT.rearrange("(a p) t -> p a t", p=128)[:, :, tk], in_=yst[bs3][:]),
                          f"yst{bs3}", reads=[("yst", bs3)])
            items.append([f0, f0b, f1, f2, f2b, f3, f4, f5, f6, f7, f8, f9, f10, f11, f12])
        run_pipeline(items, NPH, 1)
        P.flush()


def alloc_S3_weights(nc, es, T):
    W = Ctx()
    sb = lambda name, shape, dt: es.enter_context(nc.sbuf_tensor("f_" + name, shape, dt))
    W.wgt = sb("wgt", [128, 8, 2048], BF16)
    T.s3w = W


def load_S3_weights(P, T, which):
    W = T.s3w
    w_v = T.w_in.rearrange("(k p) c -> p k c", p=128)
    wa_v = T.w_att_out.rearrange("(k p) c -> p k c", p=128)
    wg_v = T.w_gla_out.rearrange("(k p) c -> p k c", p=128)
    wo_v = T.w_out.rearrange("(k p) c -> p k c", p=128)
    if "wgt" in which:
        for k in range(8):
            for half in range(2):
                P.dma("pool", lambda e, k=k, half=half: e.dma_start(
                    out=W.wgt[:, k, 1024 * half:1024 * half + 1024], in_=w_v[:, k, C_GATT + 1024 * half:C_GATT + 1024 * half + 1024]),
                    "wgt", writes=["wgt"])
    if "cast" in which:
        for src, dst, n in ((T.w_att_out, T.wao_b, 6), (T.w_gla_out, T.wgo_b, 4), (T.w_out, T.wo_b, 8)):
            for k in range(n):
                P.dma("pool", lambda e, src=src, dst=dst, k=k: e.dma_start(out=dst[128 * k:128 * k + 128, :], in_=src[128 * k:128 * k + 128, :]),
                      "wcast", writes=["wcast"])
    if "rest_b" in which:
        wa_b = T.wao_b.rearrange("(k p) c -> p k c", p=128)
        wg_b = T.wgo_b.rearrange("(k p) c -> p k c", p=128)
        wo_b = T.wo_b.rearrange("(k p) c -> p k c", p=128)
        for k in range(6):
            P.dma("sp" if k % 2 == 0 else "act", lambda e, k=k: e.dma_start(out=W.wao[:, k, :], in_=wa_b[:, k, :]), "wao", writes=["wao"])
        for k in range(4):
            P.dma("sp" if k % 2 == 0 else "act", lambda e, k=k: e.dma_start(out=W.wgo[:, k, :], in_=wg_b[:, k, :]), "wgo", writes=["wgo"])
        for k in range(8):
            P.dma("sp" if k % 2 == 0 else "act", lambda e, k=k: e.dma_start(out=W.wo[:, k, :], in_=wo_b[:, k, :]), "wo", writes=["wo"])
    if "rest" not in which:
        return
    for k in range(6):
        P.dma("pool", lambda e, k=k: e.dma_start(out=W.wao[:, k, :], in_=wa_v[:, k, :]), "wao", writes=["wao"])
    for k in range(4):
        P.dma("pool", lambda e, k=k: e.dma_start(out=W.wgo[:, k, :], in_=wg_v[:, k, :]), "wgo", writes=["wgo"])
    for k in range(8):
        P.dma("pool", lambda e, k=k: e.dma_start(out=W.wo[:, k, :], in_=wo_v[:, k, :]), "wo", writes=["wo"])


def stage_S3(nc, P, T, weights_loaded):
    x_v = T.xT.rearrange("(k p) t -> p k t", p=128)
    with ExitStack() as es:
        sb = lambda name, shape, dt: es.enter_context(nc.sbuf_tensor("f_" + name, shape, dt))
        ps = lambda name, shape, dt: es.enter_context(nc.psum_tensor("f_" + name, shape, dt))
        wgt = T.s3w.wgt
        T.s3w.wao = wao = sb("wao", [128, 6, 1024], BF16)
        T.s3w.wgo = wgo = sb("wgo", [128, 4, 1024], BF16)
        T.s3w.wo = wo = sb("wo", [128, 8, 1024], BF16)
        lng = sb("lng", [128, 1024], F32)
        lnb = sb("lnb", [128, 1024], F32)
        xt = [sb(f"xt{i}", [128, 8, 512], BF16) for i in range(2)]
        ygs = [sb(f"ygs{i}", [128, 4, 512], BF16) for i in range(2)]
        num = sb("num", [128, 6, 512], F32)
        sga = sb("sga", [128, 6, 512], BF16)
        dn = sb("dn", [128, 2, 3, 512], F32)
        xtm = [sb(f"xtm{i}", [128, 1024], F32) for i in range(4)]
        ya = [sb(f"ya{i}", [128, 6, 512], BF16) for i in range(2)]
        NS = 2
        sA = [sb(f"sA{i}", [128, 512], F32) for i in range(NS)]
        sG = [sb(f"sG{i}", [128, 512], F32) for i in range(NS)]
        m1 = [sb(f"m1{i}", [128, 512], F32) for i in range(2)]
        m2 = [sb(f"m2{i}", [128, 512], F32) for i in range(2)]
        mg = [sb(f"mg{i}", [128, 8, 512], BF16) for i in range(2)]
        NH = 4
        hs = [sb(f"hs{i}", [128, 1024], F32) for i in range(NH)]
        stt = [sb(f"stt{i}", [128, 12], F32) for i in range(NH)]
        mv = [sb(f"mv{i}", [128, 2], F32) for i in range(NH)]
        sm = [sb(f"sm{i}", [128, 4], F32) for i in range(NH)]
        cneg = sb("cneg", [128, 1], F32)
        pGA = ps("pGA", [128, 512], F32)
        pGG = ps("pGG", [128, 512], F32)
        pYA = ps("pYA", [128, 512], F32)
        pYG = ps("pYG", [128, 512], F32)
        pH = [ps(f"pH{i}", [128, 512], F32) for i in range(4)]

        def load_a(tt):
            s = tt % 2
            tk = slice(512 * tt, 512 * tt + 512)
            for kh in range(2):
                P.dma("pool", lambda e, s=s, kh=kh, tk=tk: e.dma_start(
                    out=xt[s][:, 4 * kh:4 * kh + 4, :], in_=x_v[:, 4 * kh:4 * kh + 4, tk]), f"xt{s}", writes=[("xt", s)])
            P.dma("sp", lambda e, s=s, tk=tk: e.dma_start(
                out=ygs[s][:], in_=T.ygT.rearrange("(a p) t -> p a t", p=128)[:, :, tk]), f"ygs{s}", writes=[("ygs", s)])

        def load_b(tt):
            tk = slice(512 * tt, 512 * tt + 512)
            P.dma("sp", lambda e, tk=tk: e.dma_start(
                out=num[:], in_=T.numT.rearrange("(a p) t -> p a t", p=128)[:, :, tk]), "num", writes=["num"])
            P.dma("sp", lambda e, tk=tk: e.dma_start(
                out=sga[:], in_=T.sgaT.rearrange("(a p) t -> p a t", p=128)[:, :, tk]), "sga", writes=["sga"])
            for jp in range(2):
                for q_ in range(2):
                    src = bass.AP(tensor=T.den.tensor, offset=(2 * jp + q_) * OWN + 512 * tt,
                                  ap=[[0, 64], [4 * OWN, 3], [1, 512]])
                    P.dma("sp", lambda e, jp=jp, q_=q_, src=src: e.dma_start(
                        out=dn[64 * q_:64 * q_ + 64, jp, :, :], in_=src), "dn", writes=["dn"])

        def ya_parts(tt):
            yb = tt % 2
            parts = []
            for jp in range(2):
                def g(jp=jp):
                    P.op("dve", lambda e: e.tensor_tensor(out=dn[:, jp, 0, :], in0=dn[:, jp, 0, :], in1=dn[:, jp, 1, :], op=ALU.add),
                         writes=["dn"])
                    P.op("dve", lambda e: e.tensor_tensor(out=dn[:, jp, 0, :], in0=dn[:, jp, 0, :], in1=dn[:, jp, 2, :], op=ALU.add),
                         writes=["dn"])
                    P.op("dve", lambda e: e.reciprocal(out=dn[:, jp, 0, :], in_=dn[:, jp, 0, :]), writes=["dn"])
                parts.append(g)
            for ft in range(6):
                def g(ft=ft, jp=ft % 2):
                    P.op("dve", lambda e: e.tensor_tensor(out=num[:, ft, :], in0=num[:, ft, :], in1=dn[:, jp, 0, :], op=ALU.mult),
                         reads=["dn"], writes=["num"])
                    P.op("dve", lambda e: e.tensor_tensor(out=ya[yb][:, ft, :], in0=num[:, ft, :], in1=sga[:, ft, :], op=ALU.mult),
                         reads=["num", "sga"], writes=[("ya", yb, ft)])
                parts.append(g)
            return parts

        def compute_ya(tt):
            for g in ya_parts(tt):
                g()

        load_a(0)
        load_b(0)
        P.op("pool", lambda e: e.memset(cneg[:], -0.5), writes=["cneg"])
        load_S3_weights(P, T, ("rest_b",) if weights_loaded else ("wgt", "rest"))
        P.dma("sp", lambda e: e.dma_start(out=lng[:], in_=T.ln_g.partition_broadcast(128)), "c1", writes=["lng"])
        P.dma("sp", lambda e: e.dma_start(out=lnb[:], in_=T.ln_b.partition_broadcast(128)), "c1", writes=["lnb"])
        wkeys = [] if weights_loaded else None

        NPH = 10
        items = []
        cn = dict(d=0, h=0)

        def rd(*keys):
            return [k for k in keys if not (weights_loaded and k == "wgt")]

        def add_dt(tt, dt_, extra=None):
            s = tt % 2
            yb = tt % 2
            mb = tt % 2
            di = cn["d"]
            cn["d"] += 1
            b3 = di % NS
            b2 = di % 2
            dsl = slice(128 * dt_, 128 * dt_ + 128)

            def fA():
                for k in range(8):
                    P.op("pe", lambda e, k=k: e.matmul(pGA[:], lhsT=wgt[:, k, dsl], rhs=xt[s][:, k, :], start=(k == 0), stop=(k == 7)),
                         reads=rd("wgt", ("xt", s)), writes=["pGA"])
                for k in range(8):
                    P.op("pe", lambda e, k=k: e.matmul(pGG[:], lhsT=wgt[:, k, 1024 + 128 * dt_:1024 + 128 * dt_ + 128], rhs=xt[s][:, k, :], start=(k == 0), stop=(k == 7)),
                         reads=rd("wgt", ("xt", s)), writes=["pGG"])
                for ft in range(6):
                    P.op("pe", lambda e, ft=ft: e.matmul(pYA[:], lhsT=wao[:, ft, dsl], rhs=ya[yb][:, ft, :], start=(ft == 0), stop=(ft == 5)),
                         reads=rd("wao", ("ya", yb, ft)), writes=["pYA"])
                for gt in range(4):
                    P.op("pe", lambda e, gt=gt: e.matmul(pYG[:], lhsT=wgo[:, gt, dsl], rhs=ygs[s][:, gt, :], start=(gt == 0), stop=(gt == 3)),
                         reads=rd("wgo", ("ygs", s)), writes=["pYG"])

            def fB():
                P.op("act", lambda e: e.activation(out=sA[b3][:], in_=pGA[:], func=AF.Sigmoid), writes=["pGA", ("sA", b3)])
                P.op("act", lambda e: e.activation(out=sG[b3][:], in_=pGG[:], func=AF.Sigmoid), writes=["pGG", ("sG", b3)])

            def fC():
                P.op("dve", lambda e: e.tensor_tensor(out=m1[b2][:], in0=sA[b3][:], in1=pYA[:], op=ALU.mult),
                     reads=[("sA", b3)], writes=["pYA", ("m1", b2)])
                P.op("dve", lambda e: e.tensor_tensor(out=m2[b2][:], in0=sG[b3][:], in1=pYG[:], op=ALU.mult),
                     reads=[("sG", b3)], writes=["pYG", ("m2", b2)])

            def fD():
                P.op("pool", lambda e: e.tensor_tensor(out=mg[mb][:, dt_, :], in0=m1[b2][:], in1=m2[b2][:], op=ALU.add),
                     reads=[("m1", b2), ("m2", b2)], writes=[("mg", mb, dt_)])
            items.append([fA, lambda: (fB(), fC(), extra() if extra is not None else None), fD] + [None] * (NPH - 3))

        def add_sub(tt, sub, early=None):
            mb = tt % 2
            hi = cn["h"]
            cn["h"] += 1
            x2 = hi % 4
            h4 = hi % NH
            row0 = 512 * tt + 128 * sub
            pis = [(2 * hi) % 4, (2 * hi + 1) % 4]

            def f0():
                P.dma("act", lambda e: e.dma_start(out=xtm[x2][:], in_=T.xtm[row0:row0 + 128, :]),
                      f"xtm{x2}", writes=[("xtm", x2)])

            def fA():
                for half in range(2):
                    ph = pH[pis[half]]
                    for dt_ in range(8):
                        P.op("pe", lambda e, ph=ph, dt_=dt_, half=half: e.matmul(
                            ph[:], lhsT=mg[mb][:, dt_, 128 * sub:128 * sub + 128], rhs=wo[:, dt_, 512 * half:512 * half + 512],
                            start=(dt_ == 0), stop=(dt_ == 7)), reads=rd(("mg", mb, dt_), "wo"), writes=[("pH", pis[half])])

            def fB():
                for half in range(2):
                    ph = pH[pis[half]]
                    P.op("dve", lambda e, ph=ph, half=half: e.scalar_tensor_tensor(
                        out=hs[h4][:, 512 * half:512 * half + 512], in0=xtm[x2][:, 512 * half:512 * half + 512], scalar=ALPHA,
                        in1=ph[:], op0=ALU.mult, op1=ALU.add), reads=[("xtm", x2)], writes=[("pH", pis[half]), ("hs", h4)])
                    P.op("dve", lambda e, half=half: e.bn_stats(out=stt[h4][:, 6 * half:6 * half + 6], in_=hs[h4][:, 512 * half:512 * half + 512]),
                         reads=[("hs", h4)], writes=[("stt", h4)])
                P.op("dve", lambda e: e.bn_aggr(out=mv[h4][:], in_=stt[h4][:]), reads=[("stt", h4)], writes=[("mv", h4)])

            def fC():
                P.op("pool", lambda e: e.tensor_scalar(out=sm[h4][:, 0:1], in0=mv[h4][:, 1:2], scalar1=LN_EPS, scalar2=None, op0=ALU.add),
                     reads=[("mv", h4)], writes=[("sm0", h4)])
                P.op("pool", lambda e: e.tensor_tensor(out=sm[h4][:, 1:2], in0=sm[h4][:, 0:1], in1=cneg[:, 0:1], op=ALU.pow),
                     reads=[("sm0", h4), "cneg"], writes=[("sm1", h4)])

            def fD():
                P.op("dve", lambda e: e.tensor_scalar(out=hs[h4][:], in0=hs[h4][:], scalar1=mv[h4][:, 0:1], scalar2=sm[h4][:, 1:2],
                                                      op0=ALU.subtract, op1=ALU.mult),
                     reads=[("mv", h4), ("sm1", h4)], writes=[("hs", h4)])
                P.op("dve", lambda e: e.tensor_tensor(out=hs[h4][:], in0=hs[h4][:], in1=lng[:], op=ALU.mult),
                     reads=["lng"], writes=[("hs", h4)])

            def fE():
                P.op("pool", lambda e: e.tensor_tensor(out=hs[h4][:], in0=hs[h4][:], in1=lnb[:], op=ALU.add),
                     reads=["lnb"], writes=[("hs", h4)])
                P.dma("pool", lambda e: e.dma_start(out=T.y[row0:row0 + 128, :], in_=hs[h4][:]),
                      f"yo{h4}", reads=[("hs", h4)])
            items.append([None, None, lambda: (f0(), early() if early is not None else None), None, fA, fB, fC, fD, fE, None])

        compute_ya(0)
        load_b(1)
        for tt in range(8):
            if tt + 1 < 8:
                items.append([lambda tt=tt: load_a(tt + 1)] + [None] * (NPH - 1))
            parts = ya_parts(tt + 1) if tt + 1 < 8 else [None] * 8
            for dt_ in range(8):
                add_dt(tt, dt_, extra=parts[dt_])
            for sub in range(4):
                add_sub(tt, sub, early=(lambda tt=tt: load_b(tt + 2)) if (sub == 0 and tt + 2 < 8) else None)
        run_pipeline(items, NPH, 1)
        P.flush()


def build_program(debug=False, stages=(0, 1, 2, 3)):
    nc = bass.Bass("TRN2", target_bir_lowering=False)
    T = Ctx()

    def din(name, shape, dt=F32):
        return nc.dram_tensor(name, list(shape), dt, kind="ExternalInput").ap()

    def scr(name, shape, dt):
        return nc.dram_tensor(name, list(shape), dt, kind=("ExternalOutput" if debug else "Internal")).ap()

    T.xT = din("xT", [D_MODEL, SEQ])
    T.xtm = din("xtm", [OWN, D_MODEL])
    T.w_in = din("w_in", [D_MODEL, IN_W])
    T.w2 = din("w2", [32, 256])
    T.gbias = din("gbias", [1, 512])
    T.gng = din("gng", [128, 4])
    T.ebias = din("ebias", [128, 12 * 256])
    T.ebias0 = din("ebias0", [64, 12 * 128])
    T.emask = din("emask", [128, 256])
    T.emask0 = din("emask0", [64, 128])
    T.ident = din("ident", [128, 128])
    T.tri = din("tri", [128, 512])
    T.ind = din("ind", [128, 2])
    T.smask = din("smask", [128, 256])
    T.w_att_out = din("w_att_out", [768, D_MODEL])
    T.w_gla_out = din("w_gla_out", [512, D_MODEL])
    T.w_out = din("w_out", [D_MODEL, D_MODEL])
    T.ln_g = din("ln_g", [1, D_MODEL])
    T.ln_b = din("ln_b", [1, D_MODEL])
    T.y = nc.dram_tensor("y", [OWN, D_MODEL], F32, kind="ExternalOutput").ap()
    T.qaT = scr("qaT", [768, OWN], BF16)
    T.kaT = scr("kaT", [768, 5120], BF16)
    T.vaT = scr("vaT", [768, 5120], BF16)
    T.sgaT = scr("sgaT", [768, OWN], BF16)
    T.qgT = scr("qgT", [256, OWN], F32)
    T.kgT = scr("kgT", [256, OWN], F32)
    T.sgbT = scr("sgbT", [512, OWN], BF16)
    T.lrT = scr("lrT", [32, SEQ], F32)
    T.kTM = scr("kTM", [SEQ, 256], F32)
    T.vTM = scr("vTM", [SEQ, 512], BF16)
    T.numT = scr("numT", [768, OWN], F32)
    T.den = scr("den", [12, OWN], F32)
    T.ygT = scr("ygT", [512, OWN], BF16)
    T.wao_b = scr("wao_b", [768, D_MODEL], BF16)
    T.wgo_b = scr("wgo_b", [512, D_MODEL], BF16)
    T.wo_b = scr("wo_b", [D_MODEL, D_MODEL], BF16)
    P = Prog(nc)
    if 0 in stages:
        stage_P(nc, P, T)
    if 1 in stages:
        stage_S1(nc, P, T)
    with ExitStack() as es3:
        alloc_S3_weights(nc, es3, T)
        if 2 in stages:
            stage_S2(nc, P, T, prefetch=lambda: load_S3_weights(P, T, ("wgt", "cast")))
        if 3 in stages:
            stage_S3(nc, P, T, weights_loaded=(2 in stages))
    return nc


def _t5_bucket(rel):
    half = 16
    max_exact = 8
    ret = (rel > 0).astype(np.int32) * half
    n = np.abs(rel)
    large = max_exact + (np.log(np.maximum(n, 1) / max_exact) / np.log(1024 / max_exact) * (half - max_exact)).astype(np.int32)
    large = np.minimum(large, half - 1)
    return ret + np.where(n < max_exact, n, large)


def _static_tables():
    a = np.arange(128)[:, None]
    i = np.arange(128)[None, :]
    dlA = a - 64 - i
    dlB = a + 64 - i
    dl = np.concatenate([dlA, dlB], axis=1)
    valid = (np.abs(dl) <= 64)
    s = np.arange(128)[:, None]
    t = np.arange(128)[None, :]
    same = (s // 64) == (t // 64)
    c = -1.0 / 16.0
    bF = np.where(same & (s <= t), c, 0.0)
    bB = np.where(same & (s >= t), c, 0.0)
    wF = np.where(same & (s > t), c, 0.0)
    wB = np.where(same & (s < t), c, 0.0)
    tri = np.concatenate([bF, bB, wF, wB], axis=1).astype(np.float32)
    ind = np.zeros((128, 2), np.float32)
    ind[:64, 0] = c
    ind[64:, 1] = c
    mF = np.where(same & (s <= t), 1.0, 0.0)
    mB = np.where(same & (s >= t), 1.0, 0.0)
    smask = np.concatenate([mF, mB], axis=1).astype(np.float32)
    return dl, valid, tri, ind, smask


def make_in_maps(x, w_in, gla_gate_w2, gla_gate_b, gla_norm_g, rel_bias, w_att_out, w_gla_out, w_out, ln_g, ln_b):
    x = np.asarray(x, np.float32)
    w_in0 = np.asarray(w_in, np.float32)[0]
    w2 = np.asarray(gla_gate_w2, np.float32)[0]
    gb = np.asarray(gla_gate_b, np.float32)[0]
    gng = np.ascontiguousarray(np.asarray(gla_norm_g, np.float32)[0].reshape(4, 128).T)
    rb = np.asarray(rel_bias, np.float32)
    dl, valid, tri, ind, smask = _static_tables()
    emask = valid.astype(np.float32)
    common = dict(
        gng=gng, emask=emask, emask0=np.ascontiguousarray(emask[64:128, 0:128]),
        ident=np.eye(128, dtype=np.float32), tri=tri, ind=ind, smask=smask,
        w_att_out=np.ascontiguousarray(np.asarray(w_att_out, np.float32)[0]),
        w_gla_out=np.ascontiguousarray(np.asarray(w_gla_out, np.float32)[0]),
        w_out=np.ascontiguousarray(np.asarray(w_out, np.float32)[0]),
        ln_g=np.ascontiguousarray(np.asarray(ln_g, np.float32)[0].reshape(1, D_MODEL)),
        ln_b=np.ascontiguousarray(np.asarray(ln_b, np.float32)[0].reshape(1, D_MODEL)),
    )
    per_parity = []
    for par in range(2):
        sign = 1 if par == 0 else -1
        eb = np.zeros((128, 12, 256), np.float32)
        for h in range(12):
            d = DILS[h // 4]
            bk = _t5_bucket(sign * dl * d)
            eb[:, h, :] = np.where(valid, rb[bk, h], 0.0)
        w_c = w_in0
        w2_c = w2
        gb_c = gb
        if par == 1:
            w_c = w_in0.copy()
            w_c[:, C_LRF:C_LRF + 16] = w_in0[:, C_LRB:C_LRB + 16]
            w_c[:, C_LRB:C_LRB + 16] = w_in0[:, C_LRF:C_LRF + 16]
            w2_c = w2[::-1]
            gb_c = gb[::-1]
        per_parity.append(dict(
            ebias=np.ascontiguousarray(eb.reshape(128, 12 * 256)),
            ebias0=np.ascontiguousarray(eb[64:128, :, 0:128].reshape(64, 12 * 128)),
            w_in=np.ascontiguousarray(w_c),
            w2=np.ascontiguousarray(w2_c.reshape(32, 256)),
            gbias=np.ascontiguousarray(gb_c.reshape(1, 512)),
        ))
    in_maps = []
    for c in range(NCORES):
        b, par = c // 2, c % 2
        xs = x[b] if par == 0 else x[b][::-1]
        m = dict(common)
        m.update(per_parity[par])
        m["xT"] = np.ascontiguousarray(xs.T)
        m["xtm"] = np.ascontiguousarray(xs[:OWN])
        in_maps.append(m)
    return in_maps


_NC_CACHE = {}


def kernel(x, w_in, gla_gate_w2, gla_gate_b, gla_norm_g, rel_bias, w_att_out, w_gla_out, w_out, ln_g, ln_b):
    in_maps = make_in_maps(x, w_in, gla_gate_w2, gla_gate_b, gla_norm_g, rel_bias,
                           w_att_out, w_gla_out, w_out, ln_g, ln_b)
    if "nc" not in _NC_CACHE:
        _NC_CACHE["nc"] = build_program()
    nc = _NC_CACHE["nc"]
    res = run_bass_kernel_spmd(nc, in_maps, core_ids=list(range(NCORES)))
    out = np.empty((4, SEQ, D_MODEL), np.float32)
    for c in range(NCORES):
        y = np.asarray(res.results[c]["y"], np.float32)
        b, par = c // 2, c % 2
        if par == 0:
            out[b, :OWN] = y
        else:
            out[b, OWN:] = y[::-1]
    return out
```

```python
import os
import numpy as np
import ml_dtypes
from contextlib import ExitStack
import concourse.bass as bass
import concourse.mybir as mybir
from concourse.bass_utils import run_bass_kernel_spmd

F32 = mybir.dt.float32
BF16 = mybir.dt.bfloat16
AF = mybir.ActivationFunctionType
ALU = mybir.AluOpType

D_MODEL = 1024
SEQ = 8192
OWN = 4096
NCORES = 8
IN_W = 6688
C_QA, C_KA, C_VA, C_GA, C_QB, C_KB, C_VB, C_GB, C_LRF, C_LRB, C_GATT, C_GGLA = (
    0, 768, 1536, 2304, 3072, 3328, 3584, 4096, 4608, 4624, 4640, 5664)
DILS = (1, 4, 16)
ALPHA = 2.0 ** 0.25
LN_EPS = 1e-5
RMS_EPS = 1e-6

ENGS = ("pe", "act", "dve", "pool", "sp")


class Prog:
    def __init__(self, nc):
        self.nc = nc
        self.sems = {e: nc.alloc_semaphore("s_" + e) for e in ("pe", "act", "dve", "pool")}
        self.cnt = {e: 0 for e in ("pe", "act", "dve", "pool")}
        self.dsem = {}
        self.ops = []
        self.tagmap = {}
        self.waited = set()

    def op(self, eng, fn, reads=(), writes=()):
        self.ops.append(dict(eng=eng, fn=fn, reads=tuple(reads), writes=tuple(writes), dma=None))

    def dma(self, eng, fn, dsem, reads=(), writes=()):
        kind = "s" if eng == "pool" else "h"
        nk = sum(1 for k_ in self.tagmap if k_[1] == kind)
        dsem = self.tagmap.setdefault((dsem, kind), "g%s%d" % (kind, nk))
        if dsem not in self.dsem:
            self.dsem[dsem] = [self.nc.alloc_semaphore("d_" + dsem), 0]
        self.ops.append(dict(eng=eng, fn=fn, reads=tuple(reads), writes=tuple(writes), dma=dsem))

    def flush(self):
        ops = self.ops
        self.ops = []
        self.tagmap = {}
        n = len(ops)
        last_w = {}
        readers = {}
        deps = [None] * n
        for i, o in enumerate(ops):
            d = set()
            for k in o["reads"]:
                if k in last_w:
                    d.add(last_w[k])
            for k in o["writes"]:
                if k in last_w:
                    d.add(last_w[k])
                for r in readers.get(k, ()):
                    d.add(r)
            d.discard(i)
            for k in o["writes"]:
                last_w[k] = i
                readers[k] = []
            for k in o["reads"]:
                if k not in o["writes"]:
                    readers.setdefault(k, []).append(i)
            if o["eng"] == "pe" and o["dma"] is None:
                d = {j for j in d if not (ops[j]["eng"] == "pe" and ops[j]["dma"] is None)}
            if o["dma"] is not None:
                sk = {j for j in d if (ops[j]["dma"] is not None and set(ops[j]["writes"]) & set(o["writes"])
                                       and not (set(ops[j]["reads"]) | set(o["reads"])))}
                for j in sk:
                    d = d | deps[j]
                d = d - sk
            deps[i] = d
        flagged = set()
        for d in deps:
            flagged |= d
        val = [None] * n
        cnt = dict(self.cnt)
        dcnt = {k: v[1] for k, v in self.dsem.items()}
        for i, o in enumerate(ops):
            if o["dma"] is not None:
                dcnt[o["dma"]] += 16
                val[i] = (o["dma"], dcnt[o["dma"]])
            elif i in flagged:
                cnt[o["eng"]] += 1
                val[i] = (o["eng"], cnt[o["eng"]])
        streams = {e: [] for e in ENGS}
        known = {e: {} for e in ENGS}
        run_d = {k: v[1] for k, v in self.dsem.items()}
        for i, o in enumerate(ops):
            waits = {}
            for j in deps[i]:
                s, v = val[j]
                if ops[j]["dma"] is not None:
                    v = max(v, run_d[s])
                if waits.get(s, 0) < v:
                    waits[s] = v
            wl = []
            kn = known[o["eng"]]
            for s, v in waits.items():
                if kn.get(s, 0) < v:
                    kn[s] = v
                    wl.append((s, v))
            streams[o["eng"]].append((o, wl, val[i]))
            if o["dma"] is not None:
                run_d[o["dma"]] = val[i][1]
        final_waits = []
        for k, v in dcnt.items():
            if known["sp"].get(k, 0) < v:
                final_waits.append((k, v))
        for e_ in ENGS:
            for (o, wl, v) in streams[e_]:
                for (s_, x_) in wl:
                    self.waited.add((s_, x_))
        for (s_, x_) in final_waits:
            self.waited.add((s_, x_))
        kn2 = {e_: {} for e_ in ENGS}
        for e_ in ENGS:
            for (o, wl, v) in streams[e_]:
                for (s_, x_) in wl:
                    if kn2[e_].get(s_, 0) < x_:
                        kn2[e_][s_] = x_
                if o["dma"] is not None:
                    s_, x_ = v
                    prev = x_ - 16
                    if prev > 0 and (s_, prev) in self.waited and kn2[e_].get(s_, 0) < prev:
                        wl.append((s_, prev))
                        kn2[e_][s_] = prev
        self.cnt = cnt
        for k in self.dsem:
            self.dsem[k][1] = dcnt[k]
        self._emit(streams, final_waits)

    def _sem(self, s):
        if s in self.sems:
            return self.sems[s]
        return self.dsem[s][0]

    def _emit(self, streams, final_waits):
        nc = self.nc
        me = self

        def run(eng_obj, name):
            for (o, wl, v) in streams[name]:
                for (s, x) in wl:
                    eng_obj.wait_ge(me._sem(s), x)
                ins = o["fn"](eng_obj)
                if v is not None:
                    ins.then_inc(me._sem(v[0]), 16 if o["dma"] is not None else 1)
            if name == "sp":
                for (s, x) in final_waits:
                    eng_obj.wait_ge(me._sem(s), x)

        with nc.Block() as blk:
            @blk.sync
            def _(e):
                run(e, "sp")

            @blk.tensor
            def _(e):
                run(e, "pe")

            @blk.scalar
            def _(e):
                run(e, "act")

            @blk.vector
            def _(e):
                run(e, "dve")

            @blk.gpsimd
            def _(e):
                run(e, "pool")


class Ctx:
    pass


def stage_P(nc, P, T):
    fm = []
    for i in range(6):
        fm.append((C_QA + 128 * i, 128, T.qaT, 128 * i, False, BF16, 8))
    for i in range(6):
        fm.append((C_KA + 128 * i, 128, T.kaT, 128 * i, False, BF16, 10))
    for i in range(6):
        fm.append((C_VA + 128 * i, 128, T.vaT, 128 * i, False, BF16, 10))
    for i in range(6):
        fm.append((C_GA + 128 * i, 128, T.sgaT, 128 * i, True, BF16, 8))
    for i in range(2):
        fm.append((C_QB + 128 * i, 128, T.qgT, 128 * i, False, F32, 8))
    for i in range(2):
        fm.append((C_KB + 128 * i, 128, T.kgT, 128 * i, False, F32, 8))
    for i in range(4):
        fm.append((C_GB + 128 * i, 128, T.sgbT, 128 * i, True, BF16, 8))
    fm.append((C_LRF, 32, T.lrT, 0, False, F32, 16))
    NFM = len(fm)
    offs = []
    o = 0
    for c in fm:
        offs.append(o)
        o += c[1]
    WF = o
    w_v = T.w_in.rearrange("(k p) c -> p k c", p=128)
    x_v = T.xT.rearrange("(k p) t -> p k t", p=128)
    with ExitStack() as es:
        sb = lambda name, shape, dt: es.enter_context(nc.sbuf_tensor("p_" + name, shape, dt))
        ps = lambda name, shape, dt: es.enter_context(nc.psum_tensor("p_" + name, shape, dt))
        wfm = sb("wfm", [128, 8, WF], BF16)
        wtm = sb("wtm", [128, 8, 768], BF16)
        xt = [sb(f"xt{i}", [128, 8, 512], BF16) for i in range(2)]
        ob = [sb(f"ob{i}", [128, 512], BF16) for i in range(4)]
        of = [sb(f"of{i}", [128, 512], F32) for i in range(2)]
        kst = [sb(f"kst{i}", [128, 256], F32) for i in range(2)]
        vst = [sb(f"vst{i}", [128, 512], BF16) for i in range(2)]
        pf = [ps(f"pf{i}", [128, 512], F32) for i in range(4)]
        pt = [ps(f"pt{i}", [128, 512], F32) for i in range(4)]

        def load_x(tt):
            s = tt % 2
            for kh in range(2):
                P.dma("pool", lambda e, s=s, tt=tt, kh=kh: e.dma_start(
                    out=xt[s][:, 4 * kh:4 * kh + 4, :], in_=x_v[:, 4 * kh:4 * kh + 4, 512 * tt:512 * tt + 512]),
                    f"xt{s}", writes=[("xt", s)])

        def load_w(ci):
            c0, ncol = fm[ci][0], fm[ci][1]
            for kh in range(2):
                P.dma("pool", lambda e, c0=c0, ncol=ncol, kh=kh, o=offs[ci]: e.dma_start(
                    out=wfm[:, 4 * kh:4 * kh + 4, o:o + ncol], in_=w_v[:, 4 * kh:4 * kh + 4, c0:c0 + ncol]),
                    f"wfm{ci}", writes=[("wfm", ci)])

        load_x(0)
        for ci in range(NFM):
            load_w(ci)
        for kh in range(2):
            for half in range(2):
                P.dma("pool", lambda e, kh=kh, half=half: e.dma_start(
                    out=wtm[:, 4 * kh:4 * kh + 4, 384 * half:384 * half + 384],
                    in_=w_v[:, 4 * kh:4 * kh + 4, C_KB + 384 * half:C_KB + 384 * half + 384]),
                    "wtm", writes=["wtm"])
        ev = 0
        nb = 0
        nf = 0
        npf = 0
        NTT = int(os.environ.get('P_NTT', '16'))
        for tt in range(NTT):
            s = tt % 2
            if tt + 1 < NTT:
                load_x(tt + 1)
            tok = slice(512 * tt, 512 * tt + 512)
            for ci in range(NFM):
                c0, ncol, dst, r0, silu, odt, ntiles = fm[ci]
                if tt >= ntiles or ci >= int(os.environ.get('P_NCH', '99')):
                    continue
                pp = npf % 4
                npf += 1
                for k in range(8):
                    P.op("pe", lambda e, pp=pp, k=k, s=s, o=offs[ci], ncol=ncol: e.matmul(
                        pf[pp][0:ncol, :], lhsT=wfm[:, k, o:o + ncol], rhs=xt[s][:, k, :],
                        start=(k == 0), stop=(k == 7)),
                        reads=[("wfm", ci), ("xt", s)], writes=[("pf", pp)])
                if odt == BF16:
                    bi = nb % 4
                    nb += 1
                    buf, bkey, dkey = ob[bi], ("ob", bi), f"ob{bi}"
                else:
                    bi = nf % 2
                    nf += 1
                    buf, bkey, dkey = of[bi], ("of", bi), f"of{bi}"
                if silu:
                    P.op("act", lambda e, buf=buf, pp=pp, ncol=ncol: e.activation(
                        out=buf[0:ncol, :], in_=pf[pp][0:ncol, :], func=AF.Silu),
                        writes=[("pf", pp), bkey])
                elif ev % 2 == 0:
                    P.op("dve", lambda e, buf=buf, pp=pp, ncol=ncol: e.tensor_copy(
                        out=buf[0:ncol, :], in_=pf[pp][0:ncol, :]),
                        writes=[("pf", pp), bkey])
                    ev += 1
                else:
                    P.op("act", lambda e, buf=buf, pp=pp, ncol=ncol: e.activation(
                        out=buf[0:ncol, :], in_=pf[pp][0:ncol, :], func=AF.Copy),
                        writes=[("pf", pp), bkey])
                    ev += 1
                P.dma("sp", lambda e, buf=buf, dst=dst, r0=r0, ncol=ncol, tok=tok: e.dma_start(
                    out=dst[r0:r0 + ncol, tok], in_=buf[0:ncol, :]), dkey, reads=[bkey], writes=[])
            for sub in range(0 if os.environ.get('P_NOTM') else 4):
                st = 4 * tt + sub
                pa, pb = pt[(2 * st) % 4], pt[(2 * st + 1) % 4]
                ka, kb = ("pt", (2 * st) % 4), ("pt", (2 * st + 1) % 4)
                for half, (pp, pk) in enumerate(((pa, ka), (pb, kb))):
                    for k in range(8):
                        P.op("pe", lambda e, pp=pp, k=k, s=s, sub=sub, half=half: e.matmul(
                            pp[:, 0:384], lhsT=xt[s][:, k, 128 * sub:128 * sub + 128],
                            rhs=wtm[:, k, 384 * half:384 * half + 384], start=(k == 0), stop=(k == 7)),
                            reads=["wtm", ("xt", s)], writes=[pk])
                b2 = st % 2
                TMM = int(os.environ.get('P_TMMODE', '3'))
                if TMM < 2:
                    continue
                P.op("act", lambda e, b2=b2, pa=pa: e.activation(out=kst[b2][:], in_=pa[:, 0:256], func=AF.Copy),
                     writes=[ka, ("kst", b2)])
                P.op("act", lambda e, b2=b2, pa=pa: e.activation(out=vst[b2][:, 0:128], in_=pa[:, 256:384], func=AF.Copy),
                     writes=[ka, ("vstA", b2)])
                P.op("dve", lambda e, b2=b2, pb=pb: e.tensor_copy(out=vst[b2][:, 128:512], in_=pb[:, 0:384]),
                     writes=[kb, ("vstB", b2)])
                if TMM < 3:
                    continue
                P.dma("sp", lambda e, b2=b2, st=st: e.dma_start(out=T.kTM[128 * st:128 * st + 128, :], in_=kst[b2][:]),
                      f"kst{b2}", reads=[("kst", b2)])
                P.dma("sp", lambda e, b2=b2, st=st: e.dma_start(out=T.vTM[128 * st:128 * st + 128, :], in_=vst[b2][:]),
                      f"vst{b2}", reads=[("vstA", b2), ("vstB", b2)])
        P.flush()


def run_pipeline(items, nph, lag):
    n = len(items)
    for step in range(n + (nph - 1) * lag):
        for ph in range(nph - 1, -1, -1):
            i = step - ph * lag
            if 0 <= i < n and items[i][ph] is not None:
                items[i][ph]()


def stage_S1(nc, P, T):
    LAG = 1
    with ExitStack() as es:
        sb = lambda name, shape, dt: es.enter_context(nc.sbuf_tensor("a_" + name, shape, dt))
        ps = lambda name, shape, dt: es.enter_context(nc.psum_tensor("a_" + name, shape, dt))
        ebm = sb("ebm", [128, 12, 256], F32)
        ebm0 = sb("ebm0", [64, 12, 128], F32)
        emk = sb("emk", [128, 256], F32)
        emk0 = sb("emk0", [64, 128], F32)
        identb = sb("identb", [128, 128], BF16)
        qs = [sb(f"qs{i}", [128, 4096], BF16) for i in range(2)]
        ks = [sb(f"ks{i}", [128, 5120], BF16) for i in range(2)]
        vs = [sb(f"vs{i}", [128, 5120], BF16) for i in range(2)]
        va = [sb(f"va{i}", [128, 48, 2, 65], BF16) for i in range(2)]
        NE = 4
        esb = [sb(f"esb{i}", [128, 2, 256], F32) for i in range(NE)]
        ptb = [sb(f"ptb{i}", [128, 2, 256], BF16) for i in range(NE)]
        nst = [sb(f"nst{i}", [64, 2, 2048], F32) for i in range(2)]
        dst_ = [sb(f"dst{i}", [128, 2, 2048], F32) for i in range(2)]
        sT = [ps(f"sT{i}", [128, 1024], F32) for i in range(2)]
        oT = [ps(f"oT{i}", [128, 512], F32) for i in range(2)]
        tp = [ps(f"tp{i}", [128, 1024], BF16) for i in range(2)]

        P.dma("sp", lambda e: e.dma_start(out=ebm[:], in_=T.ebias.rearrange("p (h c) -> p h c", h=12)), "c1", writes=["ebm"])
        P.dma("sp", lambda e: e.dma_start(out=ebm0[:], in_=T.ebias0.rearrange("p (h c) -> p h c", h=12)), "c1", writes=["ebm0"])
        P.dma("sp", lambda e: e.dma_start(out=emk[:], in_=T.emask), "c1", writes=["emk"])
        P.dma("sp", lambda e: e.dma_start(out=emk0[:], in_=T.emask0), "c1", writes=["emk0"])
        P.dma("pool", lambda e: e.dma_start(out=identb[:], in_=T.ident), "c2", writes=["identb"])

        def load_pair(p):
            s = p % 2
            P.dma("sp", lambda e, s=s, p=p: e.dma_start(out=qs[s][:], in_=T.qaT[128 * p:128 * p + 128, :]),
                  f"qs{s}", writes=[("qs", s)])
            P.dma("sp", lambda e, s=s, p=p: e.dma_start(out=ks[s][:], in_=T.kaT[128 * p:128 * p + 128, :]),
                  f"ks{s}", writes=[("ks", s)])
            P.dma("sp", lambda e, s=s, p=p: e.dma_start(out=vs[s][:], in_=T.vaT[128 * p:128 * p + 128, :]),
                  f"vs{s}", writes=[("vs", s)])

        load_pair(0)
        load_pair(1)
        P.op("act", lambda e: e.activation(out=ebm[:], in_=ebm[:], func=AF.Exp), reads=["ebm"], writes=["ebm"])
        P.op("act", lambda e: e.activation(out=ebm0[:], in_=ebm0[:], func=AF.Exp), reads=["ebm0"], writes=["ebm0"])
        for h in range(12):
            P.op("dve", lambda e, h=h: e.tensor_tensor(out=ebm[:, h, :], in0=ebm[:, h, :], in1=emk[:], op=ALU.mult),
                 reads=["ebm", "emk"], writes=["ebm"])
            P.op("dve", lambda e, h=h: e.tensor_tensor(out=ebm0[:, h, :], in0=ebm0[:, h, :], in1=emk0[:], op=ALU.mult),
                 reads=["ebm0", "emk0"], writes=["ebm0"])
        for i_ in range(2):
            P.op("pool", lambda e, i_=i_: e.memset(va[i_][:], 1.0), writes=[("va", i_, c_) for c_ in range(48)])

        items = []
        cnt = dict(t=0, tp=0, sup=0)

        def add_vbuild(p):
            s = p % 2
            d = DILS[p // 2]
            Lr = OWN // d
            NC = Lr // 128 + 1
            for r in range(d):
                for c in range(NC):
                    if c == 0:
                        sl, n = slice(r, r + 63 * d + 1, d), 64
                    else:
                        st = d * (128 * c - 64) + r
                        sl, n = slice(st, st + 127 * d + 1, d), 128
                    idx = r * NC + c
                    tq = cnt["tp"] % 2
                    cnt["tp"] += 1

                    def fA(tq=tq, s=s, sl=sl, n=n):
                        P.op("pe", lambda e: e.transpose(out=tp[tq][0:n, 0:128], in_=vs[s][:, sl], identity=identb[:]),
                             reads=[("vs", s), "identb"], writes=[("tp", tq)])

                    def fB(tq=tq, s=s, idx=idx, n=n):
                        if idx % 2 == 0:
                            P.op("act", lambda e: e.activation(
                                out=va[s][0:n, idx, :, 0:64], in_=tp[tq][0:n, 0:128].rearrange("p (h e) -> p h e", h=2),
                                func=AF.Copy), writes=[("tp", tq), ("va", s, idx)])
                        else:
                            P.op("dve", lambda e: e.tensor_copy(
                                out=va[s][0:n, idx, :, 0:64], in_=tp[tq][0:n, 0:128].rearrange("p (h e) -> p h e", h=2)),
                                writes=[("tp", tq), ("va", s, idx)])
                    items.append([fA, fB, None, None, None, None, None])

        def add_attn(p):
            s = p % 2
            g = p // 2
            d = DILS[g]
            Lr = OWN // d
            NC = Lr // 128 + 1

            def ksl(r, c):
                if c == 0:
                    return slice(r, r + 63 * d + 1, d), 64
                st = d * (128 * c - 64) + r
                return slice(st, st + 127 * d + 1, d), 128

            for u in range(2):
                if d == 1:
                    tiles = [(0, m) for m in range(16 * u, 16 * u + 16)]
                elif d == 4:
                    tiles = [(r, m) for m in range(4 * u, 4 * u + 4) for r in range(4)]
                else:
                    tiles = [(r, u) for r in range(16)]
                sbuf_i = cnt["sup"] % 2
                cnt["sup"] += 1
                for ti, (r, m) in enumerate(tiles):
                    qst = d * 128 * m + r
                    qsl = slice(qst, qst + 127 * d + 1, d)
                    osl = slice(qst - 2048 * u, qst - 2048 * u + 127 * d + 1, d)
                    slA, nA = ksl(r, m)
                    slB, nB = ksl(r, m + 1)
                    iA, iB = r * NC + m, r * NC + m + 1
                    t2 = cnt["t"] % 2
                    t4 = cnt["t"] % NE
                    cnt["t"] += 1
                    last = (ti == len(tiles) - 1)

                    def fA(t2=t2, s=s, slA=slA, nA=nA, slB=slB, qsl=qsl):
                        for hh in range(2):
                            pb = 64 * hh
                            c0 = 512 * hh
                            P.op("pe", lambda e, pb=pb, c0=c0: e.matmul(
                                sT[t2][0:nA, c0:c0 + 128], lhsT=ks[s][pb:pb + 64, slA], rhs=qs[s][pb:pb + 64, qsl],
                                start=True, stop=True), reads=[("ks", s), ("qs", s)], writes=[("sT", t2, hh)])
                            P.op("pe", lambda e, pb=pb, c0=c0: e.matmul(
                                sT[t2][:, c0 + 128:c0 + 256], lhsT=ks[s][pb:pb + 64, slB], rhs=qs[s][pb:pb + 64, qsl],
                                start=True, stop=True), reads=[("ks", s), ("qs", s)], writes=[("sT", t2, hh)])

                    def fB(t2=t2, t4=t4, nA=nA):
                        sv = sT[t2][:].rearrange("p (h c) -> p h c", h=2)
                        wk = [("sT", t2, 0), ("sT", t2, 1), ("esb", t4)]
                        if nA == 128:
                            P.op("act", lambda e: e.activation(out=esb[t4][:], in_=sv[:, :, 0:256], func=AF.Exp, scale=0.125), writes=wk)
                        else:
                            P.op("act", lambda e: e.activation(out=esb[t4][0:64, :, 0:128], in_=sv[0:64, :, 0:128], func=AF.Exp, scale=0.125), writes=wk)
                            P.op("act", lambda e: e.activation(out=esb[t4][:, :, 128:256], in_=sv[:, :, 128:256], func=AF.Exp, scale=0.125), writes=wk)

                    def fC(t4=t4, nA=nA, p=p):
                        if nA == 128:
                            P.op("dve", lambda e: e.tensor_tensor(out=ptb[t4][:], in0=esb[t4][:], in1=ebm[:, 2 * p:2 * p + 2, :], op=ALU.mult),
                                 reads=[("esb", t4), "ebm"], writes=[("ptb", t4)])
                        else:
                            P.op("dve", lambda e: e.tensor_tensor(out=ptb[t4][0:64, :, 0:128], in0=esb[t4][0:64, :, 0:128],
                                                                  in1=ebm0[:, 2 * p:2 * p + 2, :], op=ALU.mult),
                                 reads=[("esb", t4), "ebm0"], writes=[("ptb", t4)])
                            P.op("dve", lambda e: e.tensor_tensor(out=ptb[t4][:, :, 128:256], in0=esb[t4][:, :, 128:256],
                                                                  in1=ebm[:, 2 * p:2 * p + 2, 128:256], op=ALU.mult),
                                 reads=[("esb", t4), "ebm"], writes=[("ptb", t4)])

                    def fD(t2=t2, t4=t4, s=s, iA=iA, iB=iB, nA=nA):
                        for hh in range(2):
                            P.op("pe", lambda e, hh=hh: e.matmul(
                                oT[t2][0:65, 128 * hh:128 * hh + 128], lhsT=va[s][0:nA, iA, hh, :], rhs=ptb[t4][0:nA, hh, 0:128],
                                start=True, stop=False), reads=[("va", s, iA), ("ptb", t4)], writes=[("oT", t2)])
                            P.op("pe", lambda e, hh=hh: e.matmul(
                                oT[t2][0:65, 128 * hh:128 * hh + 128], lhsT=va[s][:, iB, hh, :], rhs=ptb[t4][:, hh, 128:256],
                                start=False, stop=True), reads=[("va", s, iB), ("ptb", t4)], writes=[("oT", t2)])

                    def fE(t2=t2, sbuf_i=sbuf_i, osl=osl, last=last, p=p, u=u):
                        P.op("act", lambda e: e.activation(
                            out=nst[sbuf_i][0:64, :, osl], in_=oT[t2][0:64, 0:256].rearrange("p (h c) -> p h c", h=2), func=AF.Copy),
                            writes=[("oT", t2), ("nst", sbuf_i)])
                        P.op("dve", lambda e: e.tensor_copy(
                            out=dst_[sbuf_i][64:65, :, osl], in_=oT[t2][64:65, 0:256].rearrange("p (h c) -> p h c", h=2)),
                            writes=[("oT", t2), ("dst", sbuf_i)])
                        if last:
                            tk = slice(2048 * u, 2048 * u + 2048)
                            P.dma("pool", lambda e: e.dma_start(
                                out=T.numT[128 * p:128 * p + 128, tk].rearrange("(h e) t -> e h t", h=2), in_=nst[sbuf_i][:]),
                                f"nst{sbuf_i}", reads=[("nst", sbuf_i)])
                            P.dma("pool", lambda e: e.dma_start(
                                out=T.den[2 * p:2 * p + 2, tk].rearrange("(o h) t -> o h t", o=1), in_=dst_[sbuf_i][64:65, :, :]),
                                f"dst{sbuf_i}", reads=[("dst", sbuf_i)])
                    items.append([fA, fB, None, fC, None, fD, fE])

        def add_load(p):
            items.append([lambda p=p: load_pair(p), None, None, None, None, None, None])

        add_vbuild(0)
        for p in range(6):
            n0 = len(items)
            add_attn(p)
            at = items[n0:]
            del items[n0:]
            vb = []
            if p + 1 < 6:
                add_vbuild(p + 1)
                vb = items[n0:]
                del items[n0:]
            merged = at[:12]
            rest = at[12:]
            for j in range(max(len(rest), len(vb))):
                if j < len(rest):
                    merged.append(rest[j])
                if j < len(vb):
                    merged.append(vb[j])
            items.extend(merged)
            if p + 2 < 6:
                add_load(p + 2)
        run_pipeline(items, 7, LAG)
        P.flush()


def stage_S2(nc, P, T, prefetch=None):
    NR = 8
    with ExitStack() as es:
        sb = lambda name, shape, dt: es.enter_context(nc.sbuf_tensor("g_" + name, shape, dt))
        ps = lambda name, shape, dt: es.enter_context(nc.psum_tensor("g_" + name, shape, dt))
        tri = sb("tri", [128, 512], F32)
        ind = sb("ind", [128, 2], F32)
        smask = sb("smask", [128, 256], F32)
        smask4 = sb("smask4", [128, 512], F32)
        onesm = sb("onesm", [128, 128], BF16)
        ones32 = sb("ones32", [128, 128], F32)
        w2d = [sb(f"w2d{i}", [17, 256], F32) for i in range(2)]
        gng = sb("gng", [128, 4], F32)
        gexp = sb("gexp", [128, 2, 2, 128], F32)
        rm = [sb(f"rm{i}", [128, 1], F32) for i in range(2)]
        eb_all = sb("eb_all", [128, 2, 64, 128], BF16)
        efb = sb("efb", [128, 2, NR, 128], BF16)
        RF = sb("RF", [128, 2, NR, 128], F32)
        RB = sb("RB", [128, 2, NR, 128], F32)
        lrd = [[sb(f"lr{d}{i}", [17, 512], F32) for i in range(2)] for d in range(2)]
        lrh = [[sb(f"lrh{d}{i}", [17, 512], BF16) for i in range(2)] for d in range(2)]
        lrl = [[sb(f"lrl{d}{i}", [17, 512], BF16) for i in range(2)] for d in range(2)]
        w2h = [sb(f"w2h{i}", [17, 256], BF16) for i in range(2)]
        w2l = [sb(f"w2l{i}", [17, 256], BF16) for i in range(2)]
        tri_bf = sb("tri_bf", [128, 512], BF16)
        ind_bf = sb("ind_bf", [128, 2], BF16)
        sph = [sb(f"sph{i}", [128, 512], BF16) for i in range(2)]
        spl = [sb(f"spl{i}", [128, 512], BF16) for i in range(2)]
        DK, DV, DQ, DS = 8, 8, 6, 8
        ktm = [sb(f"ktm{i}", [128, 256], F32) for i in range(DK)]
        vtm = [sb(f"vtm{i}", [128, 512], BF16) for i in range(DV)]
        qg = [sb(f"qg{i}", [128, 2, 128], F32) for i in range(DQ)]
        kg = [sb(f"kg{i}", [128, 2, 128], F32) for i in range(DQ)]
        sgb = [sb(f"sgb{i}", [128, 4, 128], BF16) for i in range(DS)]
        yst = [sb(f"yst{i}", [128, 4, 512], BF16) for i in range(2)]
        sp_sb = [sb(f"sp_sb{i}", [128, 512], F32) for i in range(2)]
        e_sb = sp_sb
        ewd = [sb(f"ewd{i}", [128, 260], F32) for i in range(4)]
        ebp = [sb(f"ebp{i}", [128, 512], F32) for i in range(2)]
        ebn = [sb(f"ebn{i}", [128, 512], F32) for i in range(2)]
        kpc = [[sb(f"kp{c}{i}", [128, 256], BF16) for i in range(2)] for c in range(2)]
        qd = [sb(f"qd{i}", [128, 4, 128], BF16) for i in range(4)]
        kd = [sb(f"kd{i}", [128, 4, 128], BF16) for i in range(4)]
        scm = [sb(f"scm{i}", [128, 2, 512], BF16) for i in range(2)]
        osb = [sb(f"osb{i}", [128, 2, 2, 128], F32) for i in range(4)]
        osq = [sb(f"osq{i}", [128, 2, 2, 128], BF16) for i in range(2)]
        rstd = [sb(f"rstd{i}", [128, 512], F32) for i in range(2)]
        rs1 = rstd
        Zp = ps("Z", [128, 512], F32)
        Wp = ps("W", [128, 512], F32)
        Bp = ps("B", [128, 512], F32)
        SC = [ps(f"SC{i}", [128, 512], F32) for i in range(2)]
        KV = ps("KV", [128, 512], F32)
        Op = [ps(f"O{i}", [128, 512], F32) for i in range(2)]

        P.dma("sp", lambda e: e.dma_start(out=tri[:], in_=T.tri), "c1", writes=["tri"])
        P.dma("sp", lambda e: e.dma_start(out=ind[:], in_=T.ind), "c1", writes=["ind"])
        P.dma("sp", lambda e: e.dma_start(out=smask[:], in_=T.smask), "c1", writes=["smask"])
        P.dma("sp", lambda e: e.dma_start(out=gng[:], in_=T.gng), "c1", writes=["gng"])
        for d in range(2):
            P.dma("sp", lambda e, d=d: e.dma_start(out=w2d[d][0:16, :], in_=T.w2[16 * d:16 * d + 16, :]), "c1", writes=[("w2d", d)])
            P.dma("sp", lambda e, d=d: e.dma_start(out=w2d[d][16:17, :], in_=T.gbias[0:1, 256 * d:256 * d + 256]), "c1", writes=[("w2d", d)])
            for i in range(2):
                P.op("dve", lambda e, d=d, i=i: e.memset(lrd[d][i][:], 1.0), writes=[("lr", d, i)])
            P.op("act", lambda e, d=d: e.activation(out=w2h[d][:], in_=w2d[d][:], func=AF.Copy), reads=[("w2d", d)], writes=[("w2h", d)])
            P.op("dve", lambda e, d=d: e.tensor_tensor(out=w2l[d][:], in0=w2d[d][:], in1=w2h[d][:], op=ALU.subtract),
                 reads=[("w2d", d), ("w2h", d)], writes=[("w2l", d)])
        P.op("dve", lambda e: e.memset(onesm[:], 1.0 / 128.0), writes=["onesm"])
        P.op("act", lambda e: e.activation(out=tri_bf[:], in_=tri[:], func=AF.Copy), reads=["tri"], writes=["tri_bf"])
        P.op("act", lambda e: e.activation(out=ind_bf[:], in_=ind[:], func=AF.Copy), reads=["ind"], writes=["ind_bf"])
        P.op("dve", lambda e: e.memset(ones32[:], 1.0), writes=["ones32"])
        P.op("dve", lambda e: e.memset(rm[0][:], 0.0), writes=["rm"])
        P.op("dve", lambda e: e.memset(rm[1][:], 0.0), writes=["rm"])
        P.op("dve", lambda e: e.memset(rm[0][0:64, :], 1.0), writes=["rm"])
        P.op("dve", lambda e: e.memset(rm[1][64:128, :], 1.0), writes=["rm"])
        for cc_ in range(2):
            for i_ in range(2):
                P.op("pool", lambda e, cc_=cc_, i_=i_: e.memset(kpc[cc_][i_][:], 0.0), writes=[("kp", cc_, i_)])
        P.op("pool", lambda e: e.memset(RF[:], 0.0), writes=["RFall"])
        P.op("pool", lambda e: e.memset(RB[:], 0.0), writes=["RBall"])
        for d in range(2):
            for hf in range(2):
                P.op("dve", lambda e, d=d, hf=hf: e.tensor_copy(out=smask4[:, (2 * d + hf) * 128:(2 * d + hf) * 128 + 128],
                                                                 in_=smask[:, 128 * d:128 * d + 128]),
                     reads=["smask"], writes=["smask4"])
        for h in range(4):
            P.op("dve", lambda e, h=h: e.tensor_scalar(out=gexp[:, h % 2, h // 2, :], in0=ones32[:], scalar1=gng[:, h:h + 1], scalar2=None,
                                                       op0=ALU.mult), reads=["ones32", "gng"], writes=["gexp"])
        if prefetch is not None:
            prefetch()

        cn = dict(k=0, v=0, q=0, s=0, it=0)
        lr_slot = {}

        def kv_mm(kslot, vslot, order, bank=None, bkey="KV"):
            KVb = KV if bank is None else bank
            for ci, cc in enumerate(order):
                for hf in range(2):
                    for q_ in range(2):
                        col = (2 * ci + hf) * 128
                        P.op("pe", lambda e, cc=cc, hf=hf, q_=q_, col=col: e.matmul(
                            KVb[64 * q_:64 * q_ + 64, col:col + 128], lhsT=kpc[cc][kslot][:, 128 * hf + 64 * q_:128 * hf + 64 * q_ + 64],
                            rhs=vtm[vslot][:, (2 * hf + q_) * 128:(2 * hf + q_) * 128 + 128], start=True, stop=True),
                            reads=[("kp", cc, kslot), ("vtm", vslot)], writes=[bkey])

        def kp_ops(it2, kslot, wslot):
            for cc in range(2):
                rows = slice(64 * cc, 64 * cc + 64)
                P.op("pool", lambda e, cc=cc, rows=rows: e.tensor_tensor(
                    out=kpc[cc][it2][rows, :], in0=ktm[kslot][rows, :], in1=ewd[wslot][rows, 0:256], op=ALU.mult),
                    reads=[("ktm", kslot), ("ewd", wslot)], writes=[("kp", cc, it2)])

        def lr_split(d, ls):
            P.op("act", lambda e: e.activation(out=lrh[d][ls][:], in_=lrd[d][ls][:], func=AF.Copy),
                 reads=[("lr", d, ls)], writes=[("lrh", d, ls)])
            P.op("dve", lambda e: e.tensor_tensor(out=lrl[d][ls][:], in0=lrd[d][ls][:], in1=lrh[d][ls][:], op=ALU.subtract),
                 reads=[("lr", d, ls), ("lrh", d, ls)], writes=[("lrl", d, ls)])

        def z_mm(d, ls, tl, bank=None, bkey="Z"):
            Zb = Zp if bank is None else bank
            ops = ((lrh, w2h), (lrh, w2l), (lrl, w2h))
            for i_, (a_, b_) in enumerate(ops):
                P.op("pe", lambda e, a_=a_, b_=b_, i_=i_: e.matmul(
                    Zb[:, 256 * d:256 * d + 256], lhsT=a_[d][ls][0:17, tl], rhs=b_[d][0:17, :], start=(i_ == 0), stop=(i_ == 2)),
                    reads=[("lrh", d, ls), ("lrl", d, ls), ("w2h", d), ("w2l", d)], writes=[bkey])

        def softplus(it2, cs, bank=None, bkey="Z"):
            Zb = Zp if bank is None else bank
            P.op("act", lambda e: e.activation(out=sp_sb[it2][:, cs], in_=Zb[:, cs], func=AF.Exp, scale=-1.0), writes=[bkey, ("sp", it2)])
            P.op("act", lambda e: e.activation(out=sph[it2][:, cs], in_=sp_sb[it2][:, cs], func=AF.Ln, bias=1.0),
                 reads=[("sp", it2)], writes=[("sph", it2)])
            P.op("act", lambda e: e.activation(out=sp_sb[it2][:, cs], in_=sp_sb[it2][:, cs], func=AF.Ln, bias=1.0), writes=[("sp", it2)])

        def sp_lo(it2, cs):
            P.op("dve", lambda e: e.tensor_tensor(out=spl[it2][:, cs], in0=sp_sb[it2][:, cs], in1=sph[it2][:, cs], op=ALU.subtract),
                 reads=[("sp", it2), ("sph", it2)], writes=[("spl", it2)])

        def w_tot_mm(it2, d, bank=None, bkey="W"):
            Wb = Wp if bank is None else bank
            c0 = 256 * d
            wsl = slice(256 + 128 * d, 256 + 128 * d + 128)
            for i_, src in enumerate((sph, spl)):
                P.op("pe", lambda e, src=src, i_=i_: e.matmul(Wb[:, 0:256], lhsT=tri_bf[:, wsl], rhs=src[it2][:, c0:c0 + 256],
                                                              start=(i_ == 0), stop=(i_ == 1)),
                     reads=["tri_bf", ("sph", it2), ("spl", it2)], writes=[bkey])
            for hf in range(2):
                for i_, src in enumerate((sph, spl)):
                    P.op("pe", lambda e, src=src, i_=i_, hf=hf: e.matmul(
                        Wb[:, 256 + 2 * hf:258 + 2 * hf], lhsT=src[it2][:, c0 + 128 * hf:c0 + 128 * hf + 128], rhs=ind_bf[:],
                        start=(i_ == 0), stop=(i_ == 1)), reads=["ind_bf", ("sph", it2), ("spl", it2)], writes=[bkey])

        items = []
        for st in range(63, -1, -1):
            it = cn["it"]
            cn["it"] += 1
            it2, it4 = it % 2, it % 4
            blk = st // 4
            kslot = cn["k"] % DK
            cn["k"] += 1
            vslot = cn["v"] % DV
            cn["v"] += 1
            if st % 4 == 3:
                lr_slot[blk] = (15 - blk) % 2
            ls = lr_slot[blk]
            tl = slice(128 * (st % 4), 128 * (st % 4) + 128)

            def f0(st=st, blk=blk, ls=ls):
                if st % 4 == 3:
                    P.dma("sp", lambda e: e.dma_start(out=lrd[1][ls][0:16, :], in_=T.lrT[16:32, 512 * blk:512 * blk + 512]),
                          f"lr1{ls}", writes=[("lr", 1, ls)])

            def f0b(st=st, ls=ls):
                if st % 4 == 3:
                    lr_split(1, ls)

            zb, zk = (Zp, "Z") if it2 == 0 else (SC[0], ("SC", 0))
            wb_, wk = (Wp, "W") if it2 == 0 else (SC[1], ("SC", 1))
            kb_, kk = (KV, "KV") if it2 == 0 else (Bp, "B")

            def f1(st=st, ls=ls, tl=tl, kslot=kslot, vslot=vslot, zb=zb, zk=zk):
                z_mm(1, ls, tl, zb, zk)
                P.dma("sp", lambda e: e.dma_start(out=ktm[kslot][:], in_=T.kTM[128 * st:128 * st + 128, :]),
                      f"ktm{kslot}", writes=[("ktm", kslot)])
                P.dma("sp", lambda e: e.dma_start(out=vtm[vslot][:], in_=T.vTM[128 * st:128 * st + 128, :]),
                      f"vtm{vslot}", writes=[("vtm", vslot)])

            def f2(it2=it2, zb=zb, zk=zk):
                softplus(it2, slice(256, 512), zb, zk)

            def f2b(it2=it2):
                sp_lo(it2, slice(256, 512))

            def f3(it2=it2, wb_=wb_, wk=wk):
                w_tot_mm(it2, 1, wb_, wk)

            def f4(it4=it4, wb_=wb_, wk=wk):
                P.op("act", lambda e: e.activation(out=ewd[it4][:], in_=wb_[:, 0:260], func=AF.Exp), writes=[wk, ("ewd", it4)])

            def f5(it2=it2, it4=it4, kslot=kslot):
                kp_ops(it2, kslot, it4)

            def f6(it2=it2, vslot=vslot, kb_=kb_, kk=kk):
                kv_mm(it2, vslot, (1, 0), kb_, kk)

            def f7(st=st, it4=it4, kb_=kb_, kk=kk):
                for ci, cc in enumerate((1, 0)):
                    c = 2 * st + cc
                    s0, s1 = c % NR, (c - 1) % NR
                    if c < 64:
                        P.op("act", lambda e, c=c, s0=s0: e.activation(out=eb_all[:, :, c, :], in_=RB[:, :, s0, :], func=AF.Copy),
                             reads=[("RB", 0, s0), ("RB", 1, s0), "RBall"], writes=[("eb_all", c)])
                    for hf in range(2):
                        col = (2 * ci + hf) * 128
                        P.op("dve", lambda e, hf=hf, s0=s0, s1=s1, col=col, cc=cc: e.scalar_tensor_tensor(
                            out=RB[:, hf, s1, :], in0=RB[:, hf, s0, :], scalar=ewd[it4][:, 256 + 2 * hf + cc:257 + 2 * hf + cc],
                            in1=kb_[:, col:col + 128], op0=ALU.mult, op1=ALU.add),
                            reads=[("RB", hf, s0), ("ewd", it4), "RBall"], writes=[("RB", hf, s1), kk])
            items.append([f0, f0b, f1, f2, f2b, f3, f4, f5, f6, f7])
        run_pipeline(items, 10, 1)

        items = []
        NPH = 15
        for st in range(32):
            it = cn["it"]
            cn["it"] += 1
            it2, it4 = it % 2, it % 4
            blk = st // 4
            bs3 = blk % 2
            kslot = cn["k"] % DK
            cn["k"] += 1
            vslot = cn["v"] % DV
            cn["v"] += 1
            qslot = cn["q"] % DQ
            cn["q"] += 1
            sslot = cn["s"] % DS
            cn["s"] += 1
            ls = blk % 2
            tl = slice(128 * (st % 4), 128 * (st % 4) + 128)
            tg = slice(128 * st, 128 * st + 128)

            def f0(st=st, blk=blk, ls=ls):
                if st % 4 == 0:
                    for d in range(2):
                        P.dma("sp", lambda e, d=d: e.dma_start(out=lrd[d][ls][0:16, :], in_=T.lrT[16 * d:16 * d + 16, 512 * blk:512 * blk + 512]),
                              f"lr{d}{ls}", writes=[("lr", d, ls)])

            def f0b(st=st, ls=ls):
                if st % 4 == 0:
                    for d in range(2):
                        lr_split(d, ls)

            def f1(ls=ls, tl=tl):
                for d in range(2):
                    z_mm(d, ls, tl)

            def f2(it2=it2, st=st, kslot=kslot, qslot=qslot, tg=tg):
                softplus(it2, slice(0, 512))
                P.dma("sp", lambda e: e.dma_start(out=ktm[kslot][:], in_=T.kTM[128 * st:128 * st + 128, :]),
                      f"ktm{kslot}", writes=[("ktm", kslot)])
                P.dma("sp", lambda e: e.dma_start(out=qg[qslot][:], in_=T.qgT.rearrange("(a p) t -> p a t", p=128)[:, :, tg]),
                      f"qg{qslot}", writes=[("qg", qslot)])
                P.dma("sp", lambda e: e.dma_start(out=kg[qslot][:], in_=T.kgT.rearrange("(a p) t -> p a t", p=128)[:, :, tg]),
                      f"kg{qslot}", writes=[("kg", qslot)])

            def f2b(it2=it2):
                sp_lo(it2, slice(0, 512))

            def f3(it2=it2, st=st, vslot=vslot):
                w_tot_mm(it2, 0)
                for d in range(2):
                    for hf in range(2):
                        cb = (2 * d + hf) * 128
                        for i_, src in enumerate((sph, spl)):
                            P.op("pe", lambda e, d=d, hf=hf, cb=cb, src=src, i_=i_: e.matmul(
                                Bp[:, cb:cb + 128], lhsT=src[it2][:, 256 * d + 128 * hf:256 * d + 128 * hf + 128], rhs=tri_bf[:, 128 * d:128 * d + 128],
                                start=(i_ == 0), stop=(i_ == 1)), reads=["tri_bf", ("sph", it2), ("spl", it2)], writes=["B"])
                P.dma("sp", lambda e: e.dma_start(out=vtm[vslot][:], in_=T.vTM[128 * st:128 * st + 128, :]),
                      f"vtm{vslot}", writes=[("vtm", vslot)])

            def f4(it2=it2, it4=it4):
                P.op("act", lambda e: e.activation(out=ewd[it4][:], in_=Wp[:, 0:260], func=AF.Exp), writes=["W", ("ewd", it4)])
                P.op("act", lambda e: e.activation(out=ebp[it2][:], in_=Bp[:], func=AF.Exp), writes=["B", ("ebp", it2)])
                P.op("act", lambda e: e.activation(out=ebn[it2][:], in_=Bp[:], func=AF.Exp, scale=-1.0), writes=["B", ("ebn", it2)])

            def f5(it2=it2, it4=it4, kslot=kslot, qslot=qslot):
                kp_ops(it2, kslot, it4)
                for d in range(2):
                    P.op("dve", lambda e, d=d: e.scalar_tensor_tensor(
                        out=qd[it4][:, 2 * d:2 * d + 2, :], in0=qg[qslot][:], scalar=0.125,
                        in1=ebp[it2][:, 256 * d:256 * d + 256].rearrange("p (a t) -> p a t", a=2), op0=ALU.mult, op1=ALU.mult),
                        reads=[("qg", qslot), ("ebp", it2)], writes=[("qd", it4)])
                    P.op("dve", lambda e, d=d: e.tensor_tensor(
                        out=kd[it4][:, 2 * d:2 * d + 2, :], in0=kg[qslot][:],
                        in1=ebn[it2][:, 256 * d:256 * d + 256].rearrange("p (a t) -> p a t", a=2), op=ALU.mult),
                        reads=[("kg", qslot), ("ebn", it2)], writes=[("kd", it4)])

            def f6(it2=it2, it4=it4, vslot=vslot):
                kv_mm(it2, vslot, (0, 1))
                for d in range(2):
                    for h in range(4):
                        hf, hb = h // 2, 64 * (h % 2)
                        cb = (2 * d + hf) * 128
                        P.op("pe", lambda e, d=d, h=h, hf=hf, hb=hb, cb=cb: e.matmul(
                            SC[h % 2][:, cb:cb + 128], lhsT=kd[it4][hb:hb + 64, 2 * d + hf, :], rhs=qd[it4][hb:hb + 64, 2 * d + hf, :],
                            start=True, stop=True), reads=[("kd", it4), ("qd", it4)], writes=[("SC", h % 2)])

            def f7(st=st, it2=it2, it4=it4, sslot=sslot, tg=tg):
                for ci, cc in enumerate((0, 1)):
                    c = 2 * st + cc
                    s0, s1 = c % NR, (c + 1) % NR
                    P.op("pool", lambda e, s0=s0: e.tensor_copy(out=efb[:, :, s0, :], in_=RF[:, :, s0, :]),
                         reads=[("RF", 0, s0), ("RF", 1, s0), "RFall"], writes=[("efb", s0)])
                    for hf in range(2):
                        col = (2 * ci + hf) * 128
                        P.op("dve", lambda e, hf=hf, s0=s0, s1=s1, col=col, cc=cc: e.scalar_tensor_tensor(
                            out=RF[:, hf, s1, :], in0=RF[:, hf, s0, :], scalar=ewd[it4][:, 256 + 2 * hf + cc:257 + 2 * hf + cc],
                            in1=KV[:, col:col + 128], op0=ALU.mult, op1=ALU.add),
                            reads=[("RF", hf, s0), ("ewd", it4), "RFall"], writes=[("RF", hf, s1), "KV"])
                for par in range(2):
                    P.op("dve", lambda e, par=par: e.tensor_tensor(out=scm[it2][:, par, :], in0=SC[par][:], in1=smask4[:], op=ALU.mult),
                         reads=["smask4"], writes=[("SC", par), ("scm", it2, par)])
                P.dma("sp", lambda e: e.dma_start(out=sgb[sslot][:], in_=T.sgbT.rearrange("(a p) t -> p a t", p=128)[:, :, tg]),
                      f"sgb{sslot}", writes=[("sgb", sslot)])

            def f8(st=st, it2=it2, it4=it4, vslot=vslot):
                for h in range(4):
                    hf, par = h // 2, h % 2
                    hb = 64 * par
                    po = Op[par]
                    ob = 128 * hf
                    P.op("pe", lambda e, po=po, h=h, hf=hf, par=par, ob=ob: e.matmul(
                        po[:, ob:ob + 128], lhsT=vtm[vslot][:, 128 * h:128 * h + 128], rhs=scm[it2][:, par, 128 * hf:128 * hf + 128],
                        start=True, stop=False), reads=[("vtm", vslot), ("scm", it2, par)], writes=[("O", par)])
                    P.op("pe", lambda e, po=po, h=h, hf=hf, par=par, ob=ob: e.matmul(
                        po[:, ob:ob + 128], lhsT=vtm[vslot][:, 128 * h:128 * h + 128], rhs=scm[it2][:, par, 256 + 128 * hf:256 + 128 * hf + 128],
                        start=False, stop=False), reads=[("vtm", vslot), ("scm", it2, par)], writes=[("O", par)])
                    for cc in range(2):
                        c = 2 * st + cc
                        P.op("pe", lambda e, po=po, hf=hf, hb=hb, c=c, cc=cc, ob=ob: e.matmul(
                            po[:, ob + 64 * cc:ob + 64 * cc + 64], lhsT=efb[hb:hb + 64, hf, c % NR, :], rhs=qd[it4][hb:hb + 64, hf, 64 * cc:64 * cc + 64],
                            start=False, stop=False), reads=[("efb", c % NR), ("qd", it4)], writes=[("O", par)])
                        P.op("pe", lambda e, po=po, hf=hf, hb=hb, c=c, cc=cc, ob=ob: e.matmul(
                            po[:, ob + 64 * cc:ob + 64 * cc + 64], lhsT=eb_all[hb:hb + 64, hf, c, :], rhs=qd[it4][hb:hb + 64, 2 + hf, 64 * cc:64 * cc + 64],
                            start=False, stop=(cc == 1)), reads=[("eb_all", c), ("qd", it4)], writes=[("O", par)])

            def f9(it2=it2, it4=it4):
                for par in range(2):
                    P.op("act", lambda e, par=par: e.activation(out=osb[it4][:, par, :, :], in_=Op[par][:, 0:256].rearrange("p (a t) -> p a t", a=2),
                                                                func=AF.Copy), writes=[("O", par), ("osb", it4, par)])
                    P.op("act", lambda e, par=par: e.activation(out=osq[it2][:, par, :, :], in_=Op[par][:, 0:256].rearrange("p (a t) -> p a t", a=2),
                                                                func=AF.Square), writes=[("O", par), ("osq", it2, par)])

            def f10(it2=it2):
                for par in range(2):
                    P.op("pe", lambda e, par=par: e.matmul(Op[par][:, 256:512], lhsT=onesm[:], rhs=osq[it2][:, par, :, :],
                                                           start=True, stop=True), reads=["onesm", ("osq", it2, par)], writes=[("O", par)])

            def f11(it2=it2):
                for par in range(2):
                    P.op("act", lambda e, par=par: e.activation(out=rs1[it2][:, 256 * par:256 * par + 256], in_=Op[par][:, 256:512], func=AF.Ln, bias=RMS_EPS),
                         writes=[("O", par), ("rstd", it2)])
                P.op("act", lambda e: e.activation(out=rstd[it2][:], in_=rs1[it2][:], func=AF.Exp, scale=-0.5),
                     writes=[("rstd", it2)])

            def f12(st=st, it2=it2, it4=it4, sslot=sslot, bs3=bs3, tl=tl, blk=blk):
                P.op("dve", lambda e: e.tensor_tensor(out=osb[it4][:], in0=osb[it4][:], in1=rstd[it2][:].rearrange("p (a b t) -> p a b t", a=2, b=2), op=ALU.mult),
                     reads=[("rstd", it2)], writes=[("osb", it4, 0), ("osb", it4, 1)])
                P.op("dve", lambda e: e.tensor_tensor(out=osb[it4][:], in0=osb[it4][:], in1=gexp[:], op=ALU.mult),
                     reads=["gexp"], writes=[("osb", it4, 0), ("osb", it4, 1)])
                for par in range(2):
                    P.op("dve", lambda e, par=par: e.tensor_tensor(
                        out=yst[bs3][:, :, tl].rearrange("p (j q) t -> p q j t", q=2)[:, par, :, :], in0=osb[it4][:, par, :, :],
                        in1=sgb[sslot][:].rearrange("p (j q) t -> p q j t", q=2)[:, par, :, :], op=ALU.mult),
                        reads=[("osb", it4, par), ("sgb", sslot)], writes=[("yst", bs3)])
                if st % 4 == 3:
                    tk = slice(512 * blk, 512 * blk + 512)
                    P.dma("pool", lambda e: e.dma_start(out=T.ygT.rearrange("(a p) t -> p a t", p=128)[:, :, tk], in_=yst[bs3][:]),
                          f"yst{bs3}", reads=[("yst", bs3)])
            items.append([f0, f0b, f1, f2, f2b, f3, f4, f5, f6, f7, f8, f9, f10, f11, f12])
        run_pipeline(items, NPH, 1)
        P.flush()


def alloc_S3_weights(nc, es, T):
    W = Ctx()
    sb = lambda name, shape, dt: es.enter_context(nc.sbuf_tensor("f_" + name, shape, dt))
    W.wgt = sb("wgt", [128, 8, 2048], BF16)
    T.s3w = W


def load_S3_weights(P, T, which):
    W = T.s3w
    w_v = T.w_in.rearrange("(k p) c -> p k c", p=128)
    wa_v = T.w_att_out.rearrange("(k p) c -> p k c", p=128)
    wg_v = T.w_gla_out.rearrange("(k p) c -> p k c", p=128)
    wo_v = T.w_out.rearrange("(k p) c -> p k c", p=128)
    if "wgt" in which:
        for k in range(8):
            for half in range(2):
                P.dma("pool", lambda e, k=k, half=half: e.dma_start(
                    out=W.wgt[:, k, 1024 * half:1024 * half + 1024], in_=w_v[:, k, C_GATT + 1024 * half:C_GATT + 1024 * half + 1024]),
                    "wgt", writes=["wgt"])
    if "cast" in which:
        for src, dst, n in ((T.w_att_out, T.wao_b, 6), (T.w_gla_out, T.wgo_b, 4), (T.w_out, T.wo_b, 8)):
            for k in range(n):
                P.dma("pool", lambda e, src=src, dst=dst, k=k: e.dma_start(out=dst[128 * k:128 * k + 128, :], in_=src[128 * k:128 * k + 128, :]),
                      "wcast", writes=["wcast"])
    if "rest_b" in which:
        wa_b = T.wao_b.rearrange("(k p) c -> p k c", p=128)
        wg_b = T.wgo_b.rearrange("(k p) c -> p k c", p=128)
        wo_b = T.wo_b.rearrange("(k p) c -> p k c", p=128)
        for k in range(6):
            P.dma("sp" if k % 2 == 0 else "act", lambda e, k=k: e.dma_start(out=W.wao[:, k, :], in_=wa_b[:, k, :]), "wao", writes=["wao"])
        for k in range(4):
            P.dma("sp" if k % 2 == 0 else "act", lambda e, k=k: e.dma_start(out=W.wgo[:, k, :], in_=wg_b[:, k, :]), "wgo", writes=["wgo"])
        for k in range(8):
            P.dma("sp" if k % 2 == 0 else "act", lambda e, k=k: e.dma_start(out=W.wo[:, k, :], in_=wo_b[:, k, :]), "wo", writes=["wo"])
    if "rest" not in which:
        return
    for k in range(6):
        P.dma("pool", lambda e, k=k: e.dma_start(out=W.wao[:, k, :], in_=wa_v[:, k, :]), "wao", writes=["wao"])
    for k in range(4):
        P.dma("pool", lambda e, k=k: e.dma_start(out=W.wgo[:, k, :], in_=wg_v[:, k, :]), "wgo", writes=["wgo"])
    for k in range(8):
        P.dma("pool", lambda e, k=k: e.dma_start(out=W.wo[:, k, :], in_=wo_v[:, k, :]), "wo", writes=["wo"])


def stage_S3(nc, P, T, weights_loaded):
    x_v = T.xT.rearrange("(k p) t -> p k t", p=128)
    with ExitStack() as es:
        sb = lambda name, shape, dt: es.enter_context(nc.sbuf_tensor("f_" + name, shape, dt))
        ps = lambda name, shape, dt: es.enter_context(nc.psum_tensor("f_" + name, shape, dt))
        wgt = T.s3w.wgt
        T.s3w.wao = wao = sb("wao", [128, 6, 1024], BF16)
        T.s3w.wgo = wgo = sb("wgo", [128, 4, 1024], BF16)
        T.s3w.wo = wo = sb("wo", [128, 8, 1024], BF16)
        lng = sb("lng", [128, 1024], F32)
        lnb = sb("lnb", [128, 1024], F32)
        xt = [sb(f"xt{i}", [128, 8, 512], BF16) for i in range(2)]
        ygs = [sb(f"ygs{i}", [128, 4, 512], BF16) for i in range(2)]
        num = sb("num", [128, 6, 512], F32)
        sga = sb("sga", [128, 6, 512], BF16)
        dn = sb("dn", [128, 2, 3, 512], F32)
        xtm = [sb(f"xtm{i}", [128, 1024], F32) for i in range(4)]
        ya = [sb(f"ya{i}", [128, 6, 512], BF16) for i in range(2)]
        NS = 2
        sA = [sb(f"sA{i}", [128, 512], F32) for i in range(NS)]
        sG = [sb(f"sG{i}", [128, 512], F32) for i in range(NS)]
        m1 = [sb(f"m1{i}", [128, 512], F32) for i in range(2)]
        m2 = [sb(f"m2{i}", [128, 512], F32) for i in range(2)]
        mg = [sb(f"mg{i}", [128, 8, 512], BF16) for i in range(2)]
        NH = 4
        hs = [sb(f"hs{i}", [128, 1024], F32) for i in range(NH)]
        stt = [sb(f"stt{i}", [128, 12], F32) for i in range(NH)]
        mv = [sb(f"mv{i}", [128, 2], F32) for i in range(NH)]
        sm = [sb(f"sm{i}", [128, 4], F32) for i in range(NH)]
        cneg = sb("cneg", [128, 1], F32)
        pGA = ps("pGA", [128, 512], F32)
        pGG = ps("pGG", [128, 512], F32)
        pYA = ps("pYA", [128, 512], F32)
        pYG = ps("pYG", [128, 512], F32)
        pH = [ps(f"pH{i}", [128, 512], F32) for i in range(4)]

        def load_a(tt):
            s = tt % 2
            tk = slice(512 * tt, 512 * tt + 512)
            for kh in range(2):
                P.dma("pool", lambda e, s=s, kh=kh, tk=tk: e.dma_start(
                    out=xt[s][:, 4 * kh:4 * kh + 4, :], in_=x_v[:, 4 * kh:4 * kh + 4, tk]), f"xt{s}", writes=[("xt", s)])
            P.dma("sp", lambda e, s=s, tk=tk: e.dma_start(
                out=ygs[s][:], in_=T.ygT.rearrange("(a p) t -> p a t", p=128)[:, :, tk]), f"ygs{s}", writes=[("ygs", s)])

        def load_b(tt):
            tk = slice(512 * tt, 512 * tt + 512)
            P.dma("sp", lambda e, tk=tk: e.dma_start(
                out=num[:], in_=T.numT.rearrange("(a p) t -> p a t", p=128)[:, :, tk]), "num", writes=["num"])
            P.dma("sp", lambda e, tk=tk: e.dma_start(
                out=sga[:], in_=T.sgaT.rearrange("(a p) t -> p a t", p=128)[:, :, tk]), "sga", writes=["sga"])
            for jp in range(2):
                for q_ in range(2):
                    src = bass.AP(tensor=T.den.tensor, offset=(2 * jp + q_) * OWN + 512 * tt,
                                  ap=[[0, 64], [4 * OWN, 3], [1, 512]])
                    P.dma("sp", lambda e, jp=jp, q_=q_, src=src: e.dma_start(
                        out=dn[64 * q_:64 * q_ + 64, jp, :, :], in_=src), "dn", writes=["dn"])

        def ya_parts(tt):
            yb = tt % 2
            parts = []
            for jp in range(2):
                def g(jp=jp):
                    P.op("dve", lambda e: e.tensor_tensor(out=dn[:, jp, 0, :], in0=dn[:, jp, 0, :], in1=dn[:, jp, 1, :], op=ALU.add),
                         writes=["dn"])
                    P.op("dve", lambda e: e.tensor_tensor(out=dn[:, jp, 0, :], in0=dn[:, jp, 0, :], in1=dn[:, jp, 2, :], op=ALU.add),
                         writes=["dn"])
                    P.op("dve", lambda e: e.reciprocal(out=dn[:, jp, 0, :], in_=dn[:, jp, 0, :]), writes=["dn"])
                parts.append(g)
            for ft in range(6):
                def g(ft=ft, jp=ft % 2):
                    P.op("dve", lambda e: e.tensor_tensor(out=num[:, ft, :], in0=num[:, ft, :], in1=dn[:, jp, 0, :], op=ALU.mult),
                         reads=["dn"], writes=["num"])
                    P.op("dve", lambda e: e.tensor_tensor(out=ya[yb][:, ft, :], in0=num[:, ft, :], in1=sga[:, ft, :], op=ALU.mult),
                         reads=["num", "sga"], writes=[("ya", yb, ft)])
                parts.append(g)
            return parts

        def compute_ya(tt):
            for g in ya_parts(tt):
                g()

        load_a(0)
        load_b(0)
        P.op("pool", lambda e: e.memset(cneg[:], -0.5), writes=["cneg"])
        load_S3_weights(P, T, ("rest_b",) if weights_loaded else ("wgt", "rest"))
        P.dma("sp", lambda e: e.dma_start(out=lng[:], in_=T.ln_g.partition_broadcast(128)), "c1", writes=["lng"])
        P.dma("sp", lambda e: e.dma_start(out=lnb[:], in_=T.ln_b.partition_broadcast(128)), "c1", writes=["lnb"])
        wkeys = [] if weights_loaded else None

        NPH = 10
        items = []
        cn = dict(d=0, h=0)

        def rd(*keys):
            return [k for k in keys if not (weights_loaded and k == "wgt")]

        def add_dt(tt, dt_, extra=None):
            s = tt % 2
            yb = tt % 2
            mb = tt % 2
            di = cn["d"]
            cn["d"] += 1
            b3 = di % NS
            b2 = di % 2
            dsl = slice(128 * dt_, 128 * dt_ + 128)

            def fA():
                for k in range(8):
                    P.op("pe", lambda e, k=k: e.matmul(pGA[:], lhsT=wgt[:, k, dsl], rhs=xt[s][:, k, :], start=(k == 0), stop=(k == 7)),
                         reads=rd("wgt", ("xt", s)), writes=["pGA"])
                for k in range(8):
                    P.op("pe", lambda e, k=k: e.matmul(pGG[:], lhsT=wgt[:, k, 1024 + 128 * dt_:1024 + 128 * dt_ + 128], rhs=xt[s][:, k, :], start=(k == 0), stop=(k == 7)),
                         reads=rd("wgt", ("xt", s)), writes=["pGG"])
                for ft in range(6):
                    P.op("pe", lambda e, ft=ft: e.matmul(pYA[:], lhsT=wao[:, ft, dsl], rhs=ya[yb][:, ft, :], start=(ft == 0), stop=(ft == 5)),
                         reads=rd("wao", ("ya", yb, ft)), writes=["pYA"])
                for gt in range(4):
                    P.op("pe", lambda e, gt=gt: e.matmul(pYG[:], lhsT=wgo[:, gt, dsl], rhs=ygs[s][:, gt, :], start=(gt == 0), stop=(gt == 3)),
                         reads=rd("wgo", ("ygs", s)), writes=["pYG"])

            def fB():
                P.op("act", lambda e: e.activation(out=sA[b3][:], in_=pGA[:], func=AF.Sigmoid), writes=["pGA", ("sA", b3)])
                P.op("act", lambda e: e.activation(out=sG[b3][:], in_=pGG[:], func=AF.Sigmoid), writes=["pGG", ("sG", b3)])

            def fC():
                P.op("dve", lambda e: e.tensor_tensor(out=m1[b2][:], in0=sA[b3][:], in1=pYA[:], op=ALU.mult),
                     reads=[("sA", b3)], writes=["pYA", ("m1", b2)])
                P.op("dve", lambda e: e.tensor_tensor(out=m2[b2][:], in0=sG[b3][:], in1=pYG[:], op=ALU.mult),
                     reads=[("sG", b3)], writes=["pYG", ("m2", b2)])

            def fD():
                P.op("pool", lambda e: e.tensor_tensor(out=mg[mb][:, dt_, :], in0=m1[b2][:], in1=m2[b2][:], op=ALU.add),
                     reads=[("m1", b2), ("m2", b2)], writes=[("mg", mb, dt_)])
            items.append([fA, lambda: (fB(), fC(), extra() if extra is not None else None), fD] + [None] * (NPH - 3))

        def add_sub(tt, sub, early=None):
            mb = tt % 2
            hi = cn["h"]
            cn["h"] += 1
            x2 = hi % 4
            h4 = hi % NH
            row0 = 512 * tt + 128 * sub
            pis = [(2 * hi) % 4, (2 * hi + 1) % 4]

            def f0():
                P.dma("act", lambda e: e.dma_start(out=xtm[x2][:], in_=T.xtm[row0:row0 + 128, :]),
                      f"xtm{x2}", writes=[("xtm", x2)])

            def fA():
                for half in range(2):
                    ph = pH[pis[half]]
                    for dt_ in range(8):
                        P.op("pe", lambda e, ph=ph, dt_=dt_, half=half: e.matmul(
                            ph[:], lhsT=mg[mb][:, dt_, 128 * sub:128 * sub + 128], rhs=wo[:, dt_, 512 * half:512 * half + 512],
                            start=(dt_ == 0), stop=(dt_ == 7)), reads=rd(("mg", mb, dt_), "wo"), writes=[("pH", pis[half])])

            def fB():
                for half in range(2):
                    ph = pH[pis[half]]
                    P.op("dve", lambda e, ph=ph, half=half: e.scalar_tensor_tensor(
                        out=hs[h4][:, 512 * half:512 * half + 512], in0=xtm[x2][:, 512 * half:512 * half + 512], scalar=ALPHA,
                        in1=ph[:], op0=ALU.mult, op1=ALU.add), reads=[("xtm", x2)], writes=[("pH", pis[half]), ("hs", h4)])
                    P.op("dve", lambda e, half=half: e.bn_stats(out=stt[h4][:, 6 * half:6 * half + 6], in_=hs[h4][:, 512 * half:512 * half + 512]),
                         reads=[("hs", h4)], writes=[("stt", h4)])
                P.op("dve", lambda e: e.bn_aggr(out=mv[h4][:], in_=stt[h4][:]), reads=[("stt", h4)], writes=[("mv", h4)])

            def fC():
                P.op("pool", lambda e: e.tensor_scalar(out=sm[h4][:, 0:1], in0=mv[h4][:, 1:2], scalar1=LN_EPS, scalar2=None, op0=ALU.add),
                     reads=[("mv", h4)], writes=[("sm0", h4)])
                P.op("pool", lambda e: e.tensor_tensor(out=sm[h4][:, 1:2], in0=sm[h4][:, 0:1], in1=cneg[:, 0:1], op=ALU.pow),
                     reads=[("sm0", h4), "cneg"], writes=[("sm1", h4)])

            def fD():
                P.op("dve", lambda e: e.tensor_scalar(out=hs[h4][:], in0=hs[h4][:], scalar1=mv[h4][:, 0:1], scalar2=sm[h4][:, 1:2],
                                                      op0=ALU.subtract, op1=ALU.mult),
                     reads=[("mv", h4), ("sm1", h4)], writes=[("hs", h4)])
                P.op("dve", lambda e: e.tensor_tensor(out=hs[h4][:], in0=hs[h4][:], in1=lng[:], op=ALU.mult),
                     reads=["lng"], writes=[("hs", h4)])

            def fE():
                P.op("pool", lambda e: e.tensor_tensor(out=hs[h4][:], in0=hs[h4][:], in1=lnb[:], op=ALU.add),
                     reads=["lnb"], writes=[("hs", h4)])
                P.dma("pool", lambda e: e.dma_start(out=T.y[row0:row0 + 128, :], in_=hs[h4][:]),
                      f"yo{h4}", reads=[("hs", h4)])
            items.append([None, None, lambda: (f0(), early() if early is not None else None), None, fA, fB, fC, fD, fE, None])

        compute_ya(0)
        load_b(1)
        for tt in range(8):
            if tt + 1 < 8:
                items.append([lambda tt=tt: load_a(tt + 1)] + [None] * (NPH - 1))
            parts = ya_parts(tt + 1) if tt + 1 < 8 else [None] * 8
            for dt_ in range(8):
                add_dt(tt, dt_, extra=parts[dt_])
            for sub in range(4):
                add_sub(tt, sub, early=(lambda tt=tt: load_b(tt + 2)) if (sub == 0 and tt + 2 < 8) else None)
        run_pipeline(items, NPH, 1)
        P.flush()


def build_program(debug=False, stages=(0, 1, 2, 3)):
    nc = bass.Bass("TRN2", target_bir_lowering=False)
    T = Ctx()

    def din(name, shape, dt=F32):
        return nc.dram_tensor(name, list(shape), dt, kind="ExternalInput").ap()

    def scr(name, shape, dt):
        return nc.dram_tensor(name, list(shape), dt, kind=("ExternalOutput" if debug else "Internal")).ap()

    T.xT = din("xT", [D_MODEL, SEQ])
    T.xtm = din("xtm", [OWN, D_MODEL])
    T.w_in = din("w_in", [D_MODEL, IN_W])
    T.w2 = din("w2", [32, 256])
    T.gbias = din("gbias", [1, 512])
    T.gng = din("gng", [128, 4])
    T.ebias = din("ebias", [128, 12 * 256])
    T.ebias0 = din("ebias0", [64, 12 * 128])
    T.emask = din("emask", [128, 256])
    T.emask0 = din("emask0", [64, 128])
    T.ident = din("ident", [128, 128])
    T.tri = din("tri", [128, 512])
    T.ind = din("ind", [128, 2])
    T.smask = din("smask", [128, 256])
    T.w_att_out = din("w_att_out", [768, D_MODEL])
    T.w_gla_out = din("w_gla_out", [512, D_MODEL])
    T.w_out = din("w_out", [D_MODEL, D_MODEL])
    T.ln_g = din("ln_g", [1, D_MODEL])
    T.ln_b = din("ln_b", [1, D_MODEL])
    T.y = nc.dram_tensor("y", [OWN, D_MODEL], F32, kind="ExternalOutput").ap()
    T.qaT = scr("qaT", [768, OWN], BF16)
    T.kaT = scr("kaT", [768, 5120], BF16)
    T.vaT = scr("vaT", [768, 5120], BF16)
    T.sgaT = scr("sgaT", [768, OWN], BF16)
    T.qgT = scr("qgT", [256, OWN], F32)
    T.kgT = scr("kgT", [256, OWN], F32)
    T.sgbT = scr("sgbT", [512, OWN], BF16)
    T.lrT = scr("lrT", [32, SEQ], F32)
    T.kTM = scr("kTM", [SEQ, 256], F32)
    T.vTM = scr("vTM", [SEQ, 512], BF16)
    T.numT = scr("numT", [768, OWN], F32)
    T.den = scr("den", [12, OWN], F32)
    T.ygT = scr("ygT", [512, OWN], BF16)
    T.wao_b = scr("wao_b", [768, D_MODEL], BF16)
    T.wgo_b = scr("wgo_b", [512, D_MODEL], BF16)
    T.wo_b = scr("wo_b", [D_MODEL, D_MODEL], BF16)
    P = Prog(nc)
    if 0 in stages:
        stage_P(nc, P, T)
    if 1 in stages:
        stage_S1(nc, P, T)
    with ExitStack() as es3:
        alloc_S3_weights(nc, es3, T)
        if 2 in stages:
            stage_S2(nc, P, T, prefetch=lambda: load_S3_weights(P, T, ("wgt", "cast")))
        if 3 in stages:
            stage_S3(nc, P, T, weights_loaded=(2 in stages))
    return nc


def _t5_bucket(rel):
    half = 16
    max_exact = 8
    ret = (rel > 0).astype(np.int32) * half
    n = np.abs(rel)
    large = max_exact + (np.log(np.maximum(n, 1) / max_exact) / np.log(1024 / max_exact) * (half - max_exact)).astype(np.int32)
    large = np.minimum(large, half - 1)
    return ret + np.where(n < max_exact, n, large)


def _static_tables():
    a = np.arange(128)[:, None]
    i = np.arange(128)[None, :]
    dlA = a - 64 - i
    dlB = a + 64 - i
    dl = np.concatenate([dlA, dlB], axis=1)
    valid = (np.abs(dl) <= 64)
    s = np.arange(128)[:, None]
    t = np.arange(128)[None, :]
    same = (s // 64) == (t // 64)
    c = -1.0 / 16.0
    bF = np.where(same & (s <= t), c, 0.0)
    bB = np.where(same & (s >= t), c, 0.0)
    wF = np.where(same & (s > t), c, 0.0)
    wB = np.where(same & (s < t), c, 0.0)
    tri = np.concatenate([bF, bB, wF, wB], axis=1).astype(np.float32)
    ind = np.zeros((128, 2), np.float32)
    ind[:64, 0] = c
    ind[64:, 1] = c
    mF = np.where(same & (s <= t), 1.0, 0.0)
    mB = np.where(same & (s >= t), 1.0, 0.0)
    smask = np.concatenate([mF, mB], axis=1).astype(np.float32)
    return dl, valid, tri, ind, smask


def make_in_maps(x, w_in, gla_gate_w2, gla_gate_b, gla_norm_g, rel_bias, w_att_out, w_gla_out, w_out, ln_g, ln_b):
    x = np.asarray(x, np.float32)
    w_in0 = np.asarray(w_in, np.float32)[0]
    w2 = np.asarray(gla_gate_w2, np.float32)[0]
    gb = np.asarray(gla_gate_b, np.float32)[0]
    gng = np.ascontiguousarray(np.asarray(gla_norm_g, np.float32)[0].reshape(4, 128).T)
    rb = np.asarray(rel_bias, np.float32)
    dl, valid, tri, ind, smask = _static_tables()
    emask = valid.astype(np.float32)
    common = dict(
        gng=gng, emask=emask, emask0=np.ascontiguousarray(emask[64:128, 0:128]),
        ident=np.eye(128, dtype=np.float32), tri=tri, ind=ind, smask=smask,
        w_att_out=np.ascontiguousarray(np.asarray(w_att_out, np.float32)[0]),
        w_gla_out=np.ascontiguousarray(np.asarray(w_gla_out, np.float32)[0]),
        w_out=np.ascontiguousarray(np.asarray(w_out, np.float32)[0]),
        ln_g=np.ascontiguousarray(np.asarray(ln_g, np.float32)[0].reshape(1, D_MODEL)),
        ln_b=np.ascontiguousarray(np.asarray(ln_b, np.float32)[0].reshape(1, D_MODEL)),
    )
    per_parity = []
    for par in range(2):
        sign = 1 if par == 0 else -1
        eb = np.zeros((128, 12, 256), np.float32)
        for h in range(12):
            d = DILS[h // 4]
            bk = _t5_bucket(sign * dl * d)
            eb[:, h, :] = np.where(valid, rb[bk, h], 0.0)
        w_c = w_in0
        w2_c = w2
        gb_c = gb
        if par == 1:
            w_c = w_in0.copy()
            w_c[:, C_LRF:C_LRF + 16] = w_in0[:, C_LRB:C_LRB + 16]
            w_c[:, C_LRB:C_LRB + 16] = w_in0[:, C_LRF:C_LRF + 16]
            w2_c = w2[::-1]
            gb_c = gb[::-1]
        per_parity.append(dict(
            ebias=np.ascontiguousarray(eb.reshape(128, 12 * 256)),
            ebias0=np.ascontiguousarray(eb[64:128, :, 0:128].reshape(64, 12 * 128)),
            w_in=np.ascontiguousarray(w_c),
            w2=np.ascontiguousarray(w2_c.reshape(32, 256)),
            gbias=np.ascontiguousarray(gb_c.reshape(1, 512)),
        ))
    in_maps = []
    for c in range(NCORES):
        b, par = c // 2, c % 2
        xs = x[b] if par == 0 else x[b][::-1]
        m = dict(common)
        m.update(per_parity[par])
        m["xT"] = np.ascontiguousarray(xs.T)
        m["xtm"] = np.ascontiguousarray(xs[:OWN])
        in_maps.append(m)
    return in_maps


_NC_CACHE = {}


def kernel(x, w_in, gla_gate_w2, gla_gate_b, gla_norm_g, rel_bias, w_att_out, w_gla_out, w_out, ln_g, ln_b):
    in_maps = make_in_maps(x, w_in, gla_gate_w2, gla_gate_b, gla_norm_g, rel_bias,
                           w_att_out, w_gla_out, w_out, ln_g, ln_b)
    if "nc" not in _NC_CACHE:
        _NC_CACHE["nc"] = build_program()
    nc = _NC_CACHE["nc"]
    res = run_bass_kernel_spmd(nc, in_maps, core_ids=list(range(NCORES)))
    out = np.empty((4, SEQ, D_MODEL), np.float32)
    for c in range(NCORES):
        y = np.asarray(res.results[c]["y"], np.float32)
        b, par = c // 2, c % 2
        if par == 0:
            out[b, :OWN] = y
        else:
            out[b, OWN:] = y[::-1]
    return out
```

```python
import os
import numpy as np
import ml_dtypes
from contextlib import ExitStack
import concourse.bass as bass
import concourse.mybir as mybir
from concourse.bass_utils import run_bass_kernel_spmd

F32 = mybir.dt.float32
BF16 = mybir.dt.bfloat16
AF = mybir.ActivationFunctionType
ALU = mybir.AluOpType

D_MODEL = 1024
SEQ = 8192
OWN = 4096
NCORES = 8
IN_W = 6688
C_QA, C_KA, C_VA, C_GA, C_QB, C_KB, C_VB, C_GB, C_LRF, C_LRB, C_GATT, C_GGLA = (
    0, 768, 1536, 2304, 3072, 3328, 3584, 4096, 4608, 4624, 4640, 5664)
DILS = (1, 4, 16)
ALPHA = 2.0 ** 0.25
LN_EPS = 1e-5
RMS_EPS = 1e-6

ENGS = ("pe", "act", "dve", "pool", "sp")


class Prog:
    def __init__(self, nc):
        self.nc = nc
        self.sems = {e: nc.alloc_semaphore("s_" + e) for e in ("pe", "act", "dve", "pool")}
        self.cnt = {e: 0 for e in ("pe", "act", "dve", "pool")}
        self.dsem = {}
        self.ops = []
        self.tagmap = {}
        self.waited = set()

    def op(self, eng, fn, reads=(), writes=()):
        self.ops.append(dict(eng=eng, fn=fn, reads=tuple(reads), writes=tuple(writes), dma=None))

    def dma(self, eng, fn, dsem, reads=(), writes=()):
        kind = "s" if eng == "pool" else "h"
        nk = sum(1 for k_ in self.tagmap if k_[1] == kind)
        dsem = self.tagmap.setdefault((dsem, kind), "g%s%d" % (kind, nk))
        if dsem not in self.dsem:
            self.dsem[dsem] = [self.nc.alloc_semaphore("d_" + dsem), 0]
        self.ops.append(dict(eng=eng, fn=fn, reads=tuple(reads), writes=tuple(writes), dma=dsem))

    def flush(self):
        ops = self.ops
        self.ops = []
        self.tagmap = {}
        n = len(ops)
        last_w = {}
        readers = {}
        deps = [None] * n
        for i, o in enumerate(ops):
            d = set()
            for k in o["reads"]:
                if k in last_w:
                    d.add(last_w[k])
            for k in o["writes"]:
                if k in last_w:
                    d.add(last_w[k])
                for r in readers.get(k, ()):
                    d.add(r)
            d.discard(i)
            for k in o["writes"]:
                last_w[k] = i
                readers[k] = []
            for k in o["reads"]:
                if k not in o["writes"]:
                    readers.setdefault(k, []).append(i)
            if o["eng"] == "pe" and o["dma"] is None:
                d = {j for j in d if not (ops[j]["eng"] == "pe" and ops[j]["dma"] is None)}
            if o["dma"] is not None:
                sk = {j for j in d if (ops[j]["dma"] is not None and set(ops[j]["writes"]) & set(o["writes"])
                                       and not (set(ops[j]["reads"]) | set(o["reads"])))}
                for j in sk:
                    d = d | deps[j]
                d = d - sk
            deps[i] = d
        flagged = set()
        for d in deps:
            flagged |= d
        val = [None] * n
        cnt = dict(self.cnt)
        dcnt = {k: v[1] for k, v in self.dsem.items()}
        for i, o in enumerate(ops):
            if o["dma"] is not None:
                dcnt[o["dma"]] += 16
                val[i] = (o["dma"], dcnt[o["dma"]])
            elif i in flagged:
                cnt[o["eng"]] += 1
                val[i] = (o["eng"], cnt[o["eng"]])
        streams = {e: [] for e in ENGS}
        known = {e: {} for e in ENGS}
        run_d = {k: v[1] for k, v in self.dsem.items()}
        for i, o in enumerate(ops):
            waits = {}
            for j in deps[i]:
                s, v = val[j]
                if ops[j]["dma"] is not None:
                    v = max(v, run_d[s])
                if waits.get(s, 0) < v:
                    waits[s] = v
            wl = []
            kn = known[o["eng"]]
            for s, v in waits.items():
                if kn.get(s, 0) < v:
                    kn[s] = v
                    wl.append((s, v))
            streams[o["eng"]].append((o, wl, val[i]))
            if o["dma"] is not None:
                run_d[o["dma"]] = val[i][1]
        final_waits = []
        for k, v in dcnt.items():
            if known["sp"].get(k, 0) < v:
                final_waits.append((k, v))
        for e_ in ENGS:
            for (o, wl, v) in streams[e_]:
                for (s_, x_) in wl:
                    self.waited.add((s_, x_))
        for (s_, x_) in final_waits:
            self.waited.add((s_, x_))
        kn2 = {e_: {} for e_ in ENGS}
        for e_ in ENGS:
            for (o, wl, v) in streams[e_]:
                for (s_, x_) in wl:
                    if kn2[e_].get(s_, 0) < x_:
                        kn2[e_][s_] = x_
                if o["dma"] is not None:
                    s_, x_ = v
                    prev = x_ - 16
                    if prev > 0 and (s_, prev) in self.waited and kn2[e_].get(s_, 0) < prev:
                        wl.append((s_, prev))
                        kn2[e_][s_] = prev
        self.cnt = cnt
        for k in self.dsem:
            self.dsem[k][1] = dcnt[k]
        self._emit(streams, final_waits)

    def _sem(self, s):
        if s in self.sems:
            return self.sems[s]
        return self.dsem[s][0]

    def _emit(self, streams, final_waits):
        nc = self.nc
        me = self

        def run(eng_obj, name):
            for (o, wl, v) in streams[name]:
                for (s, x) in wl:
                    eng_obj.wait_ge(me._sem(s), x)
                ins = o["fn"](eng_obj)
                if v is not None:
                    ins.then_inc(me._sem(v[0]), 16 if o["dma"] is not None else 1)
            if name == "sp":
                for (s, x) in final_waits:
                    eng_obj.wait_ge(me._sem(s), x)

        with nc.Block() as blk:
            @blk.sync
            def _(e):
                run(e, "sp")

            @blk.tensor
            def _(e):
                run(e, "pe")

            @blk.scalar
            def _(e):
                run(e, "act")

            @blk.vector
            def _(e):
                run(e, "dve")

            @blk.gpsimd
            def _(e):
                run(e, "pool")


class Ctx:
    pass


def stage_P(nc, P, T):
    fm = []
    for i in range(6):
        fm.append((C_QA + 128 * i, 128, T.qaT, 128 * i, False, BF16, 8))
    for i in range(6):
        fm.append((C_KA + 128 * i, 128, T.kaT, 128 * i, False, BF16, 10 if i >= 4 else 9))
    for i in range(6):
        fm.append((C_VA + 128 * i, 128, T.vaT, 128 * i, False, BF16, 10 if i >= 4 else 9))
    for i in range(6):
        fm.append((C_GA + 128 * i, 128, T.sgaT, 128 * i, True, BF16, 8))
    for i in range(2):
        fm.append((C_QB + 128 * i, 128, T.qgT, 128 * i, False, F32, 8))
    for i in range(2):
        fm.append((C_KB + 128 * i, 128, T.kgT, 128 * i, False, F32, 8))
    for i in range(4):
        fm.append((C_GB + 128 * i, 128, T.sgbT, 128 * i, True, BF16, 8))
    fm.append((C_LRF, 32, T.lrT, 0, False, F32, 16))
    NFM = len(fm)
    offs = []
    o = 0
    for c in fm:
        offs.append(o)
        o += c[1]
    WF = o
    w_v = T.w_in.rearrange("(k p) c -> p k c", p=128)
    x_v = T.xT.rearrange("(k p) t -> p k t", p=128)
    with ExitStack() as es:
        sb = lambda name, shape, dt: es.enter_context(nc.sbuf_tensor("p_" + name, shape, dt))
        ps = lambda name, shape, dt: es.enter_context(nc.psum_tensor("p_" + name, shape, dt))
        wfm = sb("wfm", [128, 8, WF], BF16)
        wtm = sb("wtm", [128, 8, 768], BF16)
        xt = [sb(f"xt{i}", [128, 8, 512], BF16) for i in range(2)]
        ob = [sb(f"ob{i}", [128, 512], BF16) for i in range(4)]
        of = [sb(f"of{i}", [128, 512], F32) for i in range(2)]
        kst = [sb(f"kst{i}", [128, 256], F32) for i in range(2)]
        vst = [sb(f"vst{i}", [128, 512], BF16) for i in range(2)]
        pf = [ps(f"pf{i}", [128, 512], F32) for i in range(4)]
        pt = [ps(f"pt{i}", [128, 512], F32) for i in range(4)]

        def load_x(tt):
            s = tt % 2
            for kh in range(2):
                P.dma("pool", lambda e, s=s, tt=tt, kh=kh: e.dma_start(
                    out=xt[s][:, 4 * kh:4 * kh + 4, :], in_=x_v[:, 4 * kh:4 * kh + 4, 512 * tt:512 * tt + 512]),
                    f"xt{s}", writes=[("xt", s)])

        def load_w(ci):
            c0, ncol = fm[ci][0], fm[ci][1]
            for kh in range(2):
                P.dma("pool", lambda e, c0=c0, ncol=ncol, kh=kh, o=offs[ci]: e.dma_start(
                    out=wfm[:, 4 * kh:4 * kh + 4, o:o + ncol], in_=w_v[:, 4 * kh:4 * kh + 4, c0:c0 + ncol]),
                    f"wfm{ci}", writes=[("wfm", ci)])

        load_x(0)
        for ci in range(NFM):
            load_w(ci)
        for kh in range(2):
            for half in range(2):
                P.dma("pool", lambda e, kh=kh, half=half: e.dma_start(
                    out=wtm[:, 4 * kh:4 * kh + 4, 384 * half:384 * half + 384],
                    in_=w_v[:, 4 * kh:4 * kh + 4, C_KB + 384 * half:C_KB + 384 * half + 384]),
                    "wtm", writes=["wtm"])
        ev = 0
        nb = 0
        nf = 0
        npf = 0
        NTT = int(os.environ.get('P_NTT', '16'))
        for tt in range(NTT):
            s = tt % 2
            if tt + 1 < NTT:
                load_x(tt + 1)
            tok = slice(512 * tt, 512 * tt + 512)
            for ci in range(NFM):
                c0, ncol, dst, r0, silu, odt, ntiles = fm[ci]
                if tt >= ntiles or ci >= int(os.environ.get('P_NCH', '99')):
                    continue
                pp = npf % 4
                npf += 1
                for k in range(8):
                    P.op("pe", lambda e, pp=pp, k=k, s=s, o=offs[ci], ncol=ncol: e.matmul(
                        pf[pp][0:ncol, :], lhsT=wfm[:, k, o:o + ncol], rhs=xt[s][:, k, :],
                        start=(k == 0), stop=(k == 7)),
                        reads=[("wfm", ci), ("xt", s)], writes=[("pf", pp)])
                if odt == BF16:
                    bi = nb % 4
                    nb += 1
                    buf, bkey, dkey = ob[bi], ("ob", bi), f"ob{bi}"
                else:
                    bi = nf % 2
                    nf += 1
                    buf, bkey, dkey = of[bi], ("of", bi), f"of{bi}"
                if silu:
                    P.op("act", lambda e, buf=buf, pp=pp, ncol=ncol: e.activation(
                        out=buf[0:ncol, :], in_=pf[pp][0:ncol, :], func=AF.Silu),
                        writes=[("pf", pp), bkey])
                elif ev % 2 == 0:
                    P.op("dve", lambda e, buf=buf, pp=pp, ncol=ncol: e.tensor_copy(
                        out=buf[0:ncol, :], in_=pf[pp][0:ncol, :]),
                        writes=[("pf", pp), bkey])
                    ev += 1
                else:
                    P.op("act", lambda e, buf=buf, pp=pp, ncol=ncol: e.activation(
                        out=buf[0:ncol, :], in_=pf[pp][0:ncol, :], func=AF.Copy),
                        writes=[("pf", pp), bkey])
                    ev += 1
                P.dma("sp", lambda e, buf=buf, dst=dst, r0=r0, ncol=ncol, tok=tok: e.dma_start(
                    out=dst[r0:r0 + ncol, tok], in_=buf[0:ncol, :]), dkey, reads=[bkey], writes=[])
            for sub in range(0 if os.environ.get('P_NOTM') else 4):
                st = 4 * tt + sub
                pa, pb = pt[(2 * st) % 4], pt[(2 * st + 1) % 4]
                ka, kb = ("pt", (2 * st) % 4), ("pt", (2 * st + 1) % 4)
                for half, (pp, pk) in enumerate(((pa, ka), (pb, kb))):
                    for k in range(8):
                        P.op("pe", lambda e, pp=pp, k=k, s=s, sub=sub, half=half: e.matmul(
                            pp[:, 0:384], lhsT=xt[s][:, k, 128 * sub:128 * sub + 128],
                            rhs=wtm[:, k, 384 * half:384 * half + 384], start=(k == 0), stop=(k == 7)),
                            reads=["wtm", ("xt", s)], writes=[pk])
                b2 = st % 2
                TMM = int(os.environ.get('P_TMMODE', '3'))
                if TMM < 2:
                    continue
                P.op("act", lambda e, b2=b2, pa=pa: e.activation(out=kst[b2][:], in_=pa[:, 0:256], func=AF.Copy),
                     writes=[ka, ("kst", b2)])
                P.op("act", lambda e, b2=b2, pa=pa: e.activation(out=vst[b2][:, 0:128], in_=pa[:, 256:384], func=AF.Copy),
                     writes=[ka, ("vstA", b2)])
                P.op("dve", lambda e, b2=b2, pb=pb: e.tensor_copy(out=vst[b2][:, 128:512], in_=pb[:, 0:384]),
                     writes=[kb, ("vstB", b2)])
                if TMM < 3:
                    continue
                P.dma("sp", lambda e, b2=b2, st=st: e.dma_start(out=T.kTM[128 * st:128 * st + 128, :], in_=kst[b2][:]),
                      f"kst{b2}", reads=[("kst", b2)])
                P.dma("sp", lambda e, b2=b2, st=st: e.dma_start(out=T.vTM[128 * st:128 * st + 128, :], in_=vst[b2][:]),
                      f"vst{b2}", reads=[("vstA", b2), ("vstB", b2)])
        P.flush()


def run_pipeline(items, nph, lag):
    n = len(items)
    for step in range(n + (nph - 1) * lag):
        for ph in range(nph - 1, -1, -1):
            i = step - ph * lag
            if 0 <= i < n and items[i][ph] is not None:
                items[i][ph]()


def stage_S1(nc, P, T):
    LAG = 1
    with ExitStack() as es:
        sb = lambda name, shape, dt: es.enter_context(nc.sbuf_tensor("a_" + name, shape, dt))
        ps = lambda name, shape, dt: es.enter_context(nc.psum_tensor("a_" + name, shape, dt))
        ebm = sb("ebm", [128, 12, 256], F32)
        ebm0 = sb("ebm0", [64, 12, 128], F32)
        emk = sb("emk", [128, 256], F32)
        emk0 = sb("emk0", [64, 128], F32)
        identb = sb("identb", [128, 128], BF16)
        qs = [sb(f"qs{i}", [128, 4096], BF16) for i in range(2)]
        ks = [sb(f"ks{i}", [128, 5120], BF16) for i in range(2)]
        vs = [sb(f"vs{i}", [128, 5120], BF16) for i in range(2)]
        va = [sb(f"va{i}", [128, 48, 2, 65], BF16) for i in range(2)]
        NE = 4
        esb = [sb(f"esb{i}", [128, 2, 256], F32) for i in range(NE)]
        ptb = [sb(f"ptb{i}", [128, 2, 256], BF16) for i in range(NE)]
        nst = [sb(f"nst{i}", [64, 2, 2048], F32) for i in range(2)]
        dst_ = [sb(f"dst{i}", [128, 2, 2048], F32) for i in range(2)]
        sT = [ps(f"sT{i}", [128, 1024], F32) for i in range(2)]
        oT = [ps(f"oT{i}", [128, 512], F32) for i in range(2)]
        tp = [ps(f"tp{i}", [128, 1024], BF16) for i in range(2)]

        P.dma("sp", lambda e: e.dma_start(out=ebm[:], in_=T.ebias.rearrange("p (h c) -> p h c", h=12)), "c1", writes=["ebm"])
        P.dma("sp", lambda e: e.dma_start(out=ebm0[:], in_=T.ebias0.rearrange("p (h c) -> p h c", h=12)), "c1", writes=["ebm0"])
        P.dma("sp", lambda e: e.dma_start(out=emk[:], in_=T.emask), "c1", writes=["emk"])
        P.dma("sp", lambda e: e.dma_start(out=emk0[:], in_=T.emask0), "c1", writes=["emk0"])
        P.dma("pool", lambda e: e.dma_start(out=identb[:], in_=T.ident), "c2", writes=["identb"])

        def load_pair(p):
            s = p % 2
            P.dma("sp", lambda e, s=s, p=p: e.dma_start(out=qs[s][:], in_=T.qaT[128 * p:128 * p + 128, :]),
                  f"qs{s}", writes=[("qs", s)])
            P.dma("sp", lambda e, s=s, p=p: e.dma_start(out=ks[s][:], in_=T.kaT[128 * p:128 * p + 128, :]),
                  f"ks{s}", writes=[("ks", s)])
            P.dma("sp", lambda e, s=s, p=p: e.dma_start(out=vs[s][:], in_=T.vaT[128 * p:128 * p + 128, :]),
                  f"vs{s}", writes=[("vs", s)])

        load_pair(0)
        load_pair(1)
        P.op("act", lambda e: e.activation(out=ebm[:], in_=ebm[:], func=AF.Exp), reads=["ebm"], writes=["ebm"])
        P.op("act", lambda e: e.activation(out=ebm0[:], in_=ebm0[:], func=AF.Exp), reads=["ebm0"], writes=["ebm0"])
        for h in range(12):
            P.op("dve", lambda e, h=h: e.tensor_tensor(out=ebm[:, h, :], in0=ebm[:, h, :], in1=emk[:], op=ALU.mult),
                 reads=["ebm", "emk"], writes=["ebm"])
            P.op("dve", lambda e, h=h: e.tensor_tensor(out=ebm0[:, h, :], in0=ebm0[:, h, :], in1=emk0[:], op=ALU.mult),
                 reads=["ebm0", "emk0"], writes=["ebm0"])
        for i_ in range(2):
            P.op("pool", lambda e, i_=i_: e.memset(va[i_][:], 1.0), writes=[("va", i_, c_) for c_ in range(48)])

        items = []
        cnt = dict(t=0, tp=0, sup=0)

        def add_vbuild(p):
            s = p % 2
            d = DILS[p // 2]
            Lr = OWN // d
            NC = Lr // 128 + 1
            for r in range(d):
                for c in range(NC):
                    if c == 0:
                        sl, n = slice(r, r + 63 * d + 1, d), 64
                    else:
                        st = d * (128 * c - 64) + r
                        sl, n = slice(st, st + 127 * d + 1, d), 128
                    idx = r * NC + c
                    tq = cnt["tp"] % 2
                    cnt["tp"] += 1

                    def fA(tq=tq, s=s, sl=sl, n=n):
                        P.op("pe", lambda e: e.transpose(out=tp[tq][0:n, 0:128], in_=vs[s][:, sl], identity=identb[:]),
                             reads=[("vs", s), "identb"], writes=[("tp", tq)])

                    def fB(tq=tq, s=s, idx=idx, n=n):
                        if idx % 2 == 0:
                            P.op("act", lambda e: e.activation(
                                out=va[s][0:n, idx, :, 0:64], in_=tp[tq][0:n, 0:128].rearrange("p (h e) -> p h e", h=2),
                                func=AF.Copy), writes=[("tp", tq), ("va", s, idx)])
                        else:
                            P.op("dve", lambda e: e.tensor_copy(
                                out=va[s][0:n, idx, :, 0:64], in_=tp[tq][0:n, 0:128].rearrange("p (h e) -> p h e", h=2)),
                                writes=[("tp", tq), ("va", s, idx)])
                    items.append([fA, fB, None, None, None, None, None])

        def add_attn(p):
            s = p % 2
            g = p // 2
            d = DILS[g]
            Lr = OWN // d
            NC = Lr // 128 + 1

            def ksl(r, c):
                if c == 0:
                    return slice(r, r + 63 * d + 1, d), 64
                st = d * (128 * c - 64) + r
                return slice(st, st + 127 * d + 1, d), 128

            for u in range(2):
                if d == 1:
                    tiles = [(0, m) for m in range(16 * u, 16 * u + 16)]
                elif d == 4:
                    tiles = [(r, m) for m in range(4 * u, 4 * u + 4) for r in range(4)]
                else:
                    tiles = [(r, u) for r in range(16)]
                sbuf_i = cnt["sup"] % 2
                cnt["sup"] += 1
                for ti, (r, m) in enumerate(tiles):
                    qst = d * 128 * m + r
                    qsl = slice(qst, qst + 127 * d + 1, d)
                    osl = slice(qst - 2048 * u, qst - 2048 * u + 127 * d + 1, d)
                    slA, nA = ksl(r, m)
                    slB, nB = ksl(r, m + 1)
                    iA, iB = r * NC + m, r * NC + m + 1
                    t2 = cnt["t"] % 2
                    t4 = cnt["t"] % NE
                    cnt["t"] += 1
                    last = (ti == len(tiles) - 1)

                    def fA(t2=t2, s=s, slA=slA, nA=nA, slB=slB, qsl=qsl):
                        for hh in range(2):
                            pb = 64 * hh
                            c0 = 512 * hh
                            P.op("pe", lambda e, pb=pb, c0=c0: e.matmul(
                                sT[t2][0:nA, c0:c0 + 128], lhsT=ks[s][pb:pb + 64, slA], rhs=qs[s][pb:pb + 64, qsl],
                                start=True, stop=True), reads=[("ks", s), ("qs", s)], writes=[("sT", t2, hh)])
                            P.op("pe", lambda e, pb=pb, c0=c0: e.matmul(
                                sT[t2][:, c0 + 128:c0 + 256], lhsT=ks[s][pb:pb + 64, slB], rhs=qs[s][pb:pb + 64, qsl],
                                start=True, stop=True), reads=[("ks", s), ("qs", s)], writes=[("sT", t2, hh)])

                    def fB(t2=t2, t4=t4, nA=nA):
                        sv = sT[t2][:].rearrange("p (h c) -> p h c", h=2)
                        wk = [("sT", t2, 0), ("sT", t2, 1), ("esb", t4)]
                        if nA == 128:
                            P.op("act", lambda e: e.activation(out=esb[t4][:], in_=sv[:, :, 0:256], func=AF.Exp, scale=0.125), writes=wk)
                        else:
                            P.op("act", lambda e: e.activation(out=esb[t4][0:64, :, 0:128], in_=sv[0:64, :, 0:128], func=AF.Exp, scale=0.125), writes=wk)
                            P.op("act", lambda e: e.activation(out=esb[t4][:, :, 128:256], in_=sv[:, :, 128:256], func=AF.Exp, scale=0.125), writes=wk)

                    def fC(t4=t4, nA=nA, p=p):
                        if nA == 128:
                            P.op("dve", lambda e: e.tensor_tensor(out=ptb[t4][:], in0=esb[t4][:], in1=ebm[:, 2 * p:2 * p + 2, :], op=ALU.mult),
                                 reads=[("esb", t4), "ebm"], writes=[("ptb", t4)])
                        else:
                            P.op("dve", lambda e: e.tensor_tensor(out=ptb[t4][0:64, :, 0:128], in0=esb[t4][0:64, :, 0:128],
                                                                  in1=ebm0[:, 2 * p:2 * p + 2, :], op=ALU.mult),
                                 reads=[("esb", t4), "ebm0"], writes=[("ptb", t4)])
                            P.op("dve", lambda e: e.tensor_tensor(out=ptb[t4][:, :, 128:256], in0=esb[t4][:, :, 128:256],
                                                                  in1=ebm[:, 2 * p:2 * p + 2, 128:256], op=ALU.mult),
                                 reads=[("esb", t4), "ebm"], writes=[("ptb", t4)])

                    def fD(t2=t2, t4=t4, s=s, iA=iA, iB=iB, nA=nA):
                        for hh in range(2):
                            P.op("pe", lambda e, hh=hh: e.matmul(
                                oT[t2][0:65, 128 * hh:128 * hh + 128], lhsT=va[s][0:nA, iA, hh, :], rhs=ptb[t4][0:nA, hh, 0:128],
                                start=True, stop=False), reads=[("va", s, iA), ("ptb", t4)], writes=[("oT", t2)])
                            P.op("pe", lambda e, hh=hh: e.matmul(
                                oT[t2][0:65, 128 * hh:128 * hh + 128], lhsT=va[s][:, iB, hh, :], rhs=ptb[t4][:, hh, 128:256],
                                start=False, stop=True), reads=[("va", s, iB), ("ptb", t4)], writes=[("oT", t2)])

                    def fE(t2=t2, sbuf_i=sbuf_i, osl=osl, last=last, p=p, u=u):
                        P.op("act", lambda e: e.activation(
                            out=nst[sbuf_i][0:64, :, osl], in_=oT[t2][0:64, 0:256].rearrange("p (h c) -> p h c", h=2), func=AF.Copy),
                            writes=[("oT", t2), ("nst", sbuf_i)])
                        P.op("dve", lambda e: e.tensor_copy(
                            out=dst_[sbuf_i][64:65, :, osl], in_=oT[t2][64:65, 0:256].rearrange("p (h c) -> p h c", h=2)),
                            writes=[("oT", t2), ("dst", sbuf_i)])
                        if last:
                            tk = slice(2048 * u, 2048 * u + 2048)
                            P.dma("pool", lambda e: e.dma_start(
                                out=T.numT[128 * p:128 * p + 128, tk].rearrange("(h e) t -> e h t", h=2), in_=nst[sbuf_i][:]),
                                f"nst{sbuf_i}", reads=[("nst", sbuf_i)])
                            P.dma("pool", lambda e: e.dma_start(
                                out=T.den[2 * p:2 * p + 2, tk].rearrange("(o h) t -> o h t", o=1), in_=dst_[sbuf_i][64:65, :, :]),
                                f"dst{sbuf_i}", reads=[("dst", sbuf_i)])
                    items.append([fA, fB, None, fC, None, fD, fE])

        def add_load(p):
            items.append([lambda p=p: load_pair(p), None, None, None, None, None, None])

        add_vbuild(0)
        for p in range(6):
            n0 = len(items)
            add_attn(p)
            at = items[n0:]
            del items[n0:]
            vb = []
            if p + 1 < 6:
                add_vbuild(p + 1)
                vb = items[n0:]
                del items[n0:]
            merged = at[:12]
            rest = at[12:]
            for j in range(max(len(rest), len(vb))):
                if j < len(rest):
                    merged.append(rest[j])
                if j < len(vb):
                    merged.append(vb[j])
            items.extend(merged)
            if p + 2 < 6:
                add_load(p + 2)
        run_pipeline(items, 7, LAG)
        P.flush()


def stage_S2(nc, P, T, prefetch=None):
    NR = 8
    with ExitStack() as es:
        sb = lambda name, shape, dt: es.enter_context(nc.sbuf_tensor("g_" + name, shape, dt))
        ps = lambda name, shape, dt: es.enter_context(nc.psum_tensor("g_" + name, shape, dt))
        tri = sb("tri", [128, 512], F32)
        ind = sb("ind", [128, 2], F32)
        smask = sb("smask", [128, 256], F32)
        smask4 = sb("smask4", [128, 512], F32)
        onesm = sb("onesm", [128, 128], BF16)
        ones32 = sb("ones32", [128, 128], F32)
        w2d = [sb(f"w2d{i}", [17, 256], F32) for i in range(2)]
        gng = sb("gng", [128, 4], F32)
        gexp = sb("gexp", [128, 2, 2, 128], F32)
        rm = [sb(f"rm{i}", [128, 1], F32) for i in range(2)]
        eb_all = sb("eb_all", [128, 2, 64, 128], BF16)
        efb = sb("efb", [128, 2, NR, 128], BF16)
        RF = sb("RF", [128, 2, NR, 128], F32)
        RB = sb("RB", [128, 2, NR, 128], F32)
        lrd = [[sb(f"lr{d}{i}", [17, 512], F32) for i in range(2)] for d in range(2)]
        lrh = [[sb(f"lrh{d}{i}", [17, 512], BF16) for i in range(2)] for d in range(2)]
        lrl = [[sb(f"lrl{d}{i}", [17, 512], BF16) for i in range(2)] for d in range(2)]
        w2h = [sb(f"w2h{i}", [17, 256], BF16) for i in range(2)]
        w2l = [sb(f"w2l{i}", [17, 256], BF16) for i in range(2)]
        tri_bf = sb("tri_bf", [128, 512], BF16)
        ind_bf = sb("ind_bf", [128, 2], BF16)
        sph = [sb(f"sph{i}", [128, 512], BF16) for i in range(2)]
        spl = [sb(f"spl{i}", [128, 512], BF16) for i in range(2)]
        DK, DV, DQ, DS = 8, 8, 6, 8
        ktm = [sb(f"ktm{i}", [128, 256], F32) for i in range(DK)]
        vtm = [sb(f"vtm{i}", [128, 512], BF16) for i in range(DV)]
        qg = [sb(f"qg{i}", [128, 2, 128], F32) for i in range(DQ)]
        kg = [sb(f"kg{i}", [128, 2, 128], F32) for i in range(DQ)]
        sgb = [sb(f"sgb{i}", [128, 4, 128], BF16) for i in range(DS)]
        yst = [sb(f"yst{i}", [128, 4, 512], BF16) for i in range(2)]
        sp_sb = [sb(f"sp_sb{i}", [128, 512], F32) for i in range(2)]
        e_sb = sp_sb
        ewd = [sb(f"ewd{i}", [128, 260], F32) for i in range(4)]
        ebp = [sb(f"ebp{i}", [128, 512], F32) for i in range(2)]
        ebn = [sb(f"ebn{i}", [128, 512], F32) for i in range(2)]
        kpc = [[sb(f"kp{c}{i}", [128, 256], BF16) for i in range(2)] for c in range(2)]
        qd = [sb(f"qd{i}", [128, 4, 128], BF16) for i in range(4)]
        kd = [sb(f"kd{i}", [128, 4, 128], BF16) for i in range(4)]
        scm = [sb(f"scm{i}", [128, 2, 512], BF16) for i in range(2)]
        osb = [sb(f"osb{i}", [128, 2, 2, 128], F32) for i in range(4)]
        osq = [sb(f"osq{i}", [128, 2, 2, 128], BF16) for i in range(2)]
        rstd = [sb(f"rstd{i}", [128, 512], F32) for i in range(2)]
        rs1 = rstd
        Zp = ps("Z", [128, 512], F32)
        Wp = ps("W", [128, 512], F32)
        Bp = ps("B", [128, 512], F32)
        SC = [ps(f"SC{i}", [128, 512], F32) for i in range(2)]
        KV = ps("KV", [128, 512], F32)
        Op = [ps(f"O{i}", [128, 512], F32) for i in range(2)]

        P.dma("sp", lambda e: e.dma_start(out=tri[:], in_=T.tri), "c1", writes=["tri"])
        P.dma("sp", lambda e: e.dma_start(out=ind[:], in_=T.ind), "c1", writes=["ind"])
        P.dma("sp", lambda e: e.dma_start(out=smask[:], in_=T.smask), "c1", writes=["smask"])
        P.dma("sp", lambda e: e.dma_start(out=gng[:], in_=T.gng), "c1", writes=["gng"])
        for d in range(2):
            P.dma("sp", lambda e, d=d: e.dma_start(out=w2d[d][0:16, :], in_=T.w2[16 * d:16 * d + 16, :]), "c1", writes=[("w2d", d)])
            P.dma("sp", lambda e, d=d: e.dma_start(out=w2d[d][16:17, :], in_=T.gbias[0:1, 256 * d:256 * d + 256]), "c1", writes=[("w2d", d)])
            for i in range(2):
                P.op("dve", lambda e, d=d, i=i: e.memset(lrd[d][i][:], 1.0), writes=[("lr", d, i)])
            P.op("act", lambda e, d=d: e.activation(out=w2h[d][:], in_=w2d[d][:], func=AF.Copy), reads=[("w2d", d)], writes=[("w2h", d)])
            P.op("dve", lambda e, d=d: e.tensor_tensor(out=w2l[d][:], in0=w2d[d][:], in1=w2h[d][:], op=ALU.subtract),
                 reads=[("w2d", d), ("w2h", d)], writes=[("w2l", d)])
        P.op("dve", lambda e: e.memset(onesm[:], 1.0 / 128.0), writes=["onesm"])
        P.op("act", lambda e: e.activation(out=tri_bf[:], in_=tri[:], func=AF.Copy), reads=["tri"], writes=["tri_bf"])
        P.op("act", lambda e: e.activation(out=ind_bf[:], in_=ind[:], func=AF.Copy), reads=["ind"], writes=["ind_bf"])
        P.op("dve", lambda e: e.memset(ones32[:], 1.0), writes=["ones32"])
        P.op("dve", lambda e: e.memset(rm[0][:], 0.0), writes=["rm"])
        P.op("dve", lambda e: e.memset(rm[1][:], 0.0), writes=["rm"])
        P.op("dve", lambda e: e.memset(rm[0][0:64, :], 1.0), writes=["rm"])
        P.op("dve", lambda e: e.memset(rm[1][64:128, :], 1.0), writes=["rm"])
        for cc_ in range(2):
            for i_ in range(2):
                P.op("pool", lambda e, cc_=cc_, i_=i_: e.memset(kpc[cc_][i_][:], 0.0), writes=[("kp", cc_, i_)])
        P.op("pool", lambda e: e.memset(RF[:], 0.0), writes=["RFall"])
        P.op("pool", lambda e: e.memset(RB[:], 0.0), writes=["RBall"])
        for d in range(2):
            for hf in range(2):
                P.op("dve", lambda e, d=d, hf=hf: e.tensor_copy(out=smask4[:, (2 * d + hf) * 128:(2 * d + hf) * 128 + 128],
                                                                 in_=smask[:, 128 * d:128 * d + 128]),
                     reads=["smask"], writes=["smask4"])
        for h in range(4):
            P.op("dve", lambda e, h=h: e.tensor_scalar(out=gexp[:, h % 2, h // 2, :], in0=ones32[:], scalar1=gng[:, h:h + 1], scalar2=None,
                                                       op0=ALU.mult), reads=["ones32", "gng"], writes=["gexp"])
        if prefetch is not None:
            prefetch()

        cn = dict(k=0, v=0, q=0, s=0, it=0)
        lr_slot = {}

        def kv_mm(kslot, vslot, order, bank=None, bkey="KV"):
            KVb = KV if bank is None else bank
            for ci, cc in enumerate(order):
                for hf in range(2):
                    for q_ in range(2):
                        col = (2 * ci + hf) * 128
                        P.op("pe", lambda e, cc=cc, hf=hf, q_=q_, col=col: e.matmul(
                            KVb[64 * q_:64 * q_ + 64, col:col + 128], lhsT=kpc[cc][kslot][:, 128 * hf + 64 * q_:128 * hf + 64 * q_ + 64],
                            rhs=vtm[vslot][:, (2 * hf + q_) * 128:(2 * hf + q_) * 128 + 128], start=True, stop=True),
                            reads=[("kp", cc, kslot), ("vtm", vslot)], writes=[bkey])

        def kp_ops(it2, kslot, wslot):
            for cc in range(2):
                rows = slice(64 * cc, 64 * cc + 64)
                P.op("pool", lambda e, cc=cc, rows=rows: e.tensor_tensor(
                    out=kpc[cc][it2][rows, :], in0=ktm[kslot][rows, :], in1=ewd[wslot][rows, 0:256], op=ALU.mult),
                    reads=[("ktm", kslot), ("ewd", wslot)], writes=[("kp", cc, it2)])

        def lr_split(d, ls):
            P.op("act", lambda e: e.activation(out=lrh[d][ls][:], in_=lrd[d][ls][:], func=AF.Copy),
                 reads=[("lr", d, ls)], writes=[("lrh", d, ls)])
            P.op("dve", lambda e: e.tensor_tensor(out=lrl[d][ls][:], in0=lrd[d][ls][:], in1=lrh[d][ls][:], op=ALU.subtract),
                 reads=[("lr", d, ls), ("lrh", d, ls)], writes=[("lrl", d, ls)])

        def z_mm(d, ls, tl, bank=None, bkey="Z"):
            Zb = Zp if bank is None else bank
            ops = ((lrh, w2h), (lrh, w2l), (lrl, w2h))
            for i_, (a_, b_) in enumerate(ops):
                P.op("pe", lambda e, a_=a_, b_=b_, i_=i_: e.matmul(
                    Zb[:, 256 * d:256 * d + 256], lhsT=a_[d][ls][0:17, tl], rhs=b_[d][0:17, :], start=(i_ == 0), stop=(i_ == 2)),
                    reads=[("lrh", d, ls), ("lrl", d, ls), ("w2h", d), ("w2l", d)], writes=[bkey])

        def softplus(it2, cs, bank=None, bkey="Z"):
            Zb = Zp if bank is None else bank
            P.op("act", lambda e: e.activation(out=sp_sb[it2][:, cs], in_=Zb[:, cs], func=AF.Exp, scale=-1.0), writes=[bkey, ("sp", it2)])
            P.op("act", lambda e: e.activation(out=sph[it2][:, cs], in_=sp_sb[it2][:, cs], func=AF.Ln, bias=1.0),
                 reads=[("sp", it2)], writes=[("sph", it2)])
            P.op("act", lambda e: e.activation(out=sp_sb[it2][:, cs], in_=sp_sb[it2][:, cs], func=AF.Ln, bias=1.0), writes=[("sp", it2)])

        def sp_lo(it2, cs):
            P.op("dve", lambda e: e.tensor_tensor(out=spl[it2][:, cs], in0=sp_sb[it2][:, cs], in1=sph[it2][:, cs], op=ALU.subtract),
                 reads=[("sp", it2), ("sph", it2)], writes=[("spl", it2)])

        def w_tot_mm(it2, d, bank=None, bkey="W"):
            Wb = Wp if bank is None else bank
            c0 = 256 * d
            wsl = slice(256 + 128 * d, 256 + 128 * d + 128)
            for i_, src in enumerate((sph, spl)):
                P.op("pe", lambda e, src=src, i_=i_: e.matmul(Wb[:, 0:256], lhsT=tri_bf[:, wsl], rhs=src[it2][:, c0:c0 + 256],
                                                              start=(i_ == 0), stop=(i_ == 1)),
                     reads=["tri_bf", ("sph", it2), ("spl", it2)], writes=[bkey])
            for hf in range(2):
                for i_, src in enumerate((sph, spl)):
                    P.op("pe", lambda e, src=src, i_=i_, hf=hf: e.matmul(
                        Wb[:, 256 + 2 * hf:258 + 2 * hf], lhsT=src[it2][:, c0 + 128 * hf:c0 + 128 * hf + 128], rhs=ind_bf[:],
                        start=(i_ == 0), stop=(i_ == 1)), reads=["ind_bf", ("sph", it2), ("spl", it2)], writes=[bkey])

        items = []
        for st in range(63, -1, -1):
            it = cn["it"]
            cn["it"] += 1
            it2, it4 = it % 2, it % 4
            blk = st // 4
            kslot = cn["k"] % DK
            cn["k"] += 1
            vslot = cn["v"] % DV
            cn["v"] += 1
            if st % 4 == 3:
                lr_slot[blk] = (15 - blk) % 2
            ls = lr_slot[blk]
            tl = slice(128 * (st % 4), 128 * (st % 4) + 128)

            def f0(st=st, blk=blk, ls=ls):
                if st % 4 == 3:
                    P.dma("sp", lambda e: e.dma_start(out=lrd[1][ls][0:16, :], in_=T.lrT[16:32, 512 * blk:512 * blk + 512]),
                          f"lr1{ls}", writes=[("lr", 1, ls)])

            def f0b(st=st, ls=ls):
                if st % 4 == 3:
                    lr_split(1, ls)

            zb, zk = (Zp, "Z") if it2 == 0 else (SC[0], ("SC", 0))
            wb_, wk = (Wp, "W") if it2 == 0 else (SC[1], ("SC", 1))
            kb_, kk = (KV, "KV") if it2 == 0 else (Bp, "B")

            def f1(st=st, ls=ls, tl=tl, kslot=kslot, vslot=vslot, zb=zb, zk=zk):
                z_mm(1, ls, tl, zb, zk)
                P.dma("sp", lambda e: e.dma_start(out=ktm[kslot][:], in_=T.kTM[128 * st:128 * st + 128, :]),
                      f"ktm{kslot}", writes=[("ktm", kslot)])
                P.dma("sp", lambda e: e.dma_start(out=vtm[vslot][:], in_=T.vTM[128 * st:128 * st + 128, :]),
                      f"vtm{vslot}", writes=[("vtm", vslot)])

            def f2(it2=it2, zb=zb, zk=zk):
                softplus(it2, slice(256, 512), zb, zk)

            def f2b(it2=it2):
                sp_lo(it2, slice(256, 512))

            def f3(it2=it2, wb_=wb_, wk=wk):
                w_tot_mm(it2, 1, wb_, wk)

            def f4(it4=it4, wb_=wb_, wk=wk):
                P.op("act", lambda e: e.activation(out=ewd[it4][:], in_=wb_[:, 0:260], func=AF.Exp), writes=[wk, ("ewd", it4)])

            def f5(it2=it2, it4=it4, kslot=kslot):
                kp_ops(it2, kslot, it4)

            def f6(it2=it2, vslot=vslot, kb_=kb_, kk=kk):
                kv_mm(it2, vslot, (1, 0), kb_, kk)

            def f7(st=st, it4=it4, kb_=kb_, kk=kk):
                for ci, cc in enumerate((1, 0)):
                    c = 2 * st + cc
                    s0, s1 = c % NR, (c - 1) % NR
                    if c < 64:
                        P.op("act", lambda e, c=c, s0=s0: e.activation(out=eb_all[:, :, c, :], in_=RB[:, :, s0, :], func=AF.Copy),
                             reads=[("RB", 0, s0), ("RB", 1, s0), "RBall"], writes=[("eb_all", c)])
                    for hf in range(2):
                        col = (2 * ci + hf) * 128
                        P.op("dve", lambda e, hf=hf, s0=s0, s1=s1, col=col, cc=cc: e.scalar_tensor_tensor(
                            out=RB[:, hf, s1, :], in0=RB[:, hf, s0, :], scalar=ewd[it4][:, 256 + 2 * hf + cc:257 + 2 * hf + cc],
                            in1=kb_[:, col:col + 128], op0=ALU.mult, op1=ALU.add),
                            reads=[("RB", hf, s0), ("ewd", it4), "RBall"], writes=[("RB", hf, s1), kk])
            items.append([f0, f0b, f1, f2, f2b, f3, f4, f5, f6, f7])
        run_pipeline(items, 10, 1)

        items = []
        NPH = 15
        for st in range(32):
            it = cn["it"]
            cn["it"] += 1
            it2, it4 = it % 2, it % 4
            blk = st // 4
            bs3 = blk % 2
            kslot = cn["k"] % DK
            cn["k"] += 1
            vslot = cn["v"] % DV
            cn["v"] += 1
            qslot = cn["q"] % DQ
            cn["q"] += 1
            sslot = cn["s"] % DS
            cn["s"] += 1
            ls = blk % 2
            tl = slice(128 * (st % 4), 128 * (st % 4) + 128)
            tg = slice(128 * st, 128 * st + 128)

            def f0(st=st, blk=blk, ls=ls):
                if st % 4 == 0:
                    for d in range(2):
                        P.dma("sp", lambda e, d=d: e.dma_start(out=lrd[d][ls][0:16, :], in_=T.lrT[16 * d:16 * d + 16, 512 * blk:512 * blk + 512]),
                              f"lr{d}{ls}", writes=[("lr", d, ls)])

            def f0b(st=st, ls=ls):
                if st % 4 == 0:
                    for d in range(2):
                        lr_split(d, ls)

            def f1(ls=ls, tl=tl):
                for d in range(2):
                    z_mm(d, ls, tl)

            def f2(it2=it2, st=st, kslot=kslot, qslot=qslot, tg=tg):
                softplus(it2, slice(0, 512))
                P.dma("sp", lambda e: e.dma_start(out=ktm[kslot][:], in_=T.kTM[128 * st:128 * st + 128, :]),
                      f"ktm{kslot}", writes=[("ktm", kslot)])
                P.dma("sp", lambda e: e.dma_start(out=qg[qslot][:], in_=T.qgT.rearrange("(a p) t -> p a t", p=128)[:, :, tg]),
                      f"qg{qslot}", writes=[("qg", qslot)])
                P.dma("sp", lambda e: e.dma_start(out=kg[qslot][:], in_=T.kgT.rearrange("(a p) t -> p a t", p=128)[:, :, tg]),
                      f"kg{qslot}", writes=[("kg", qslot)])

            def f2b(it2=it2):
                sp_lo(it2, slice(0, 512))

            def f3(it2=it2, st=st, vslot=vslot):
                w_tot_mm(it2, 0)
                for d in range(2):
                    for hf in range(2):
                        cb = (2 * d + hf) * 128
                        for i_, src in enumerate((sph, spl)):
                            P.op("pe", lambda e, d=d, hf=hf, cb=cb, src=src, i_=i_: e.matmul(
                                Bp[:, cb:cb + 128], lhsT=src[it2][:, 256 * d + 128 * hf:256 * d + 128 * hf + 128], rhs=tri_bf[:, 128 * d:128 * d + 128],
                                start=(i_ == 0), stop=(i_ == 1)), reads=["tri_bf", ("sph", it2), ("spl", it2)], writes=["B"])
                P.dma("sp", lambda e: e.dma_start(out=vtm[vslot][:], in_=T.vTM[128 * st:128 * st + 128, :]),
                      f"vtm{vslot}", writes=[("vtm", vslot)])

            def f4(it2=it2, it4=it4):
                P.op("act", lambda e: e.activation(out=ewd[it4][:], in_=Wp[:, 0:260], func=AF.Exp), writes=["W", ("ewd", it4)])
                P.op("act", lambda e: e.activation(out=ebp[it2][:], in_=Bp[:], func=AF.Exp), writes=["B", ("ebp", it2)])
                P.op("act", lambda e: e.activation(out=ebn[it2][:], in_=Bp[:], func=AF.Exp, scale=-1.0), writes=["B", ("ebn", it2)])

            def f5(it2=it2, it4=it4, kslot=kslot, qslot=qslot):
                kp_ops(it2, kslot, it4)
                for d in range(2):
                    P.op("dve", lambda e, d=d: e.scalar_tensor_tensor(
                        out=qd[it4][:, 2 * d:2 * d + 2, :], in0=qg[qslot][:], scalar=0.125,
                        in1=ebp[it2][:, 256 * d:256 * d + 256].rearrange("p (a t) -> p a t", a=2), op0=ALU.mult, op1=ALU.mult),
                        reads=[("qg", qslot), ("ebp", it2)], writes=[("qd", it4)])
                    P.op("dve", lambda e, d=d: e.tensor_tensor(
                        out=kd[it4][:, 2 * d:2 * d + 2, :], in0=kg[qslot][:],
                        in1=ebn[it2][:, 256 * d:256 * d + 256].rearrange("p (a t) -> p a t", a=2), op=ALU.mult),
                        reads=[("kg", qslot), ("ebn", it2)], writes=[("kd", it4)])

            def f6(it2=it2, it4=it4, vslot=vslot):
                kv_mm(it2, vslot, (0, 1))
                for d in range(2):
                    for h in range(4):
                        hf, hb = h // 2, 64 * (h % 2)
                        cb = (2 * d + hf) * 128
                        P.op("pe", lambda e, d=d, h=h, hf=hf, hb=hb, cb=cb: e.matmul(
                            SC[h % 2][:, cb:cb + 128], lhsT=kd[it4][hb:hb + 64, 2 * d + hf, :], rhs=qd[it4][hb:hb + 64, 2 * d + hf, :],
                            start=True, stop=True), reads=[("kd", it4), ("qd", it4)], writes=[("SC", h % 2)])

            def f7(st=st, it2=it2, it4=it4, sslot=sslot, tg=tg):
                for ci, cc in enumerate((0, 1)):
                    c = 2 * st + cc
                    s0, s1 = c % NR, (c + 1) % NR
                    P.op("pool", lambda e, s0=s0: e.tensor_copy(out=efb[:, :, s0, :], in_=RF[:, :, s0, :]),
                         reads=[("RF", 0, s0), ("RF", 1, s0), "RFall"], writes=[("efb", s0)])
                    for hf in range(2):
                        col = (2 * ci + hf) * 128
                        P.op("dve", lambda e, hf=hf, s0=s0, s1=s1, col=col, cc=cc: e.scalar_tensor_tensor(
                            out=RF[:, hf, s1, :], in0=RF[:, hf, s0, :], scalar=ewd[it4][:, 256 + 2 * hf + cc:257 + 2 * hf + cc],
                            in1=KV[:, col:col + 128], op0=ALU.mult, op1=ALU.add),
                            reads=[("RF", hf, s0), ("ewd", it4), "RFall"], writes=[("RF", hf, s1), "KV"])
                for par in range(2):
                    P.op("dve", lambda e, par=par: e.tensor_tensor(out=scm[it2][:, par, :], in0=SC[par][:], in1=smask4[:], op=ALU.mult),
                         reads=["smask4"], writes=[("SC", par), ("scm", it2, par)])
                P.dma("sp", lambda e: e.dma_start(out=sgb[sslot][:], in_=T.sgbT.rearrange("(a p) t -> p a t", p=128)[:, :, tg]),
                      f"sgb{sslot}", writes=[("sgb", sslot)])

            def f8(st=st, it2=it2, it4=it4, vslot=vslot):
                for h in range(4):
                    hf, par = h // 2, h % 2
                    hb = 64 * par
                    po = Op[par]
                    ob = 128 * hf
                    P.op("pe", lambda e, po=po, h=h, hf=hf, par=par, ob=ob: e.matmul(
                        po[:, ob:ob + 128], lhsT=vtm[vslot][:, 128 * h:128 * h + 128], rhs=scm[it2][:, par, 128 * hf:128 * hf + 128],
                        start=True, stop=False), reads=[("vtm", vslot), ("scm", it2, par)], writes=[("O", par)])
                    P.op("pe", lambda e, po=po, h=h, hf=hf, par=par, ob=ob: e.matmul(
                        po[:, ob:ob + 128], lhsT=vtm[vslot][:, 128 * h:128 * h + 128], rhs=scm[it2][:, par, 256 + 128 * hf:256 + 128 * hf + 128],
                        start=False, stop=False), reads=[("vtm", vslot), ("scm", it2, par)], writes=[("O", par)])
                    for cc in range(2):
                        c = 2 * st + cc
                        P.op("pe", lambda e, po=po, hf=hf, hb=hb, c=c, cc=cc, ob=ob: e.matmul(
                            po[:, ob + 64 * cc:ob + 64 * cc + 64], lhsT=efb[hb:hb + 64, hf, c % NR, :], rhs=qd[it4][hb:hb + 64, hf, 64 * cc:64 * cc + 64],
                            start=False, stop=False), reads=[("efb", c % NR), ("qd", it4)], writes=[("O", par)])
                        P.op("pe", lambda e, po=po, hf=hf, hb=hb, c=c, cc=cc, ob=ob: e.matmul(
                            po[:, ob + 64 * cc:ob + 64 * cc + 64], lhsT=eb_all[hb:hb + 64, hf, c, :], rhs=qd[it4][hb:hb + 64, 2 + hf, 64 * cc:64 * cc + 64],
                            start=False, stop=(cc == 1)), reads=[("eb_all", c), ("qd", it4)], writes=[("O", par)])

            def f9(it2=it2, it4=it4):
                for par in range(2):
                    P.op("act", lambda e, par=par: e.activation(out=osb[it4][:, par, :, :], in_=Op[par][:, 0:256].rearrange("p (a t) -> p a t", a=2),
                                                                func=AF.Copy), writes=[("O", par), ("osb", it4, par)])
                    P.op("act", lambda e, par=par: e.activation(out=osq[it2][:, par, :, :], in_=Op[par][:, 0:256].rearrange("p (a t) -> p a t", a=2),
                                                                func=AF.Square), writes=[("O", par), ("osq", it2, par)])

            def f10(it2=it2):
                for par in range(2):
                    P.op("pe", lambda e, par=par: e.matmul(Op[par][:, 256:512], lhsT=onesm[:], rhs=osq[it2][:, par, :, :],
                                                           start=True, stop=True), reads=["onesm", ("osq", it2, par)], writes=[("O", par)])

            def f11(it2=it2):
                for par in range(2):
                    P.op("act", lambda e, par=par: e.activation(out=rs1[it2][:, 256 * par:256 * par + 256], in_=Op[par][:, 256:512], func=AF.Ln, bias=RMS_EPS),
                         writes=[("O", par), ("rstd", it2)])
                P.op("act", lambda e: e.activation(out=rstd[it2][:], in_=rs1[it2][:], func=AF.Exp, scale=-0.5),
                     writes=[("rstd", it2)])

            def f12(st=st, it2=it2, it4=it4, sslot=sslot, bs3=bs3, tl=tl, blk=blk):
                P.op("dve", lambda e: e.tensor_tensor(out=osb[it4][:], in0=osb[it4][:], in1=rstd[it2][:].rearrange("p (a b t) -> p a b t", a=2, b=2), op=ALU.mult),
                     reads=[("rstd", it2)], writes=[("osb", it4, 0), ("osb", it4, 1)])
                P.op("dve", lambda e: e.tensor_tensor(out=osb[it4][:], in0=osb[it4][:], in1=gexp[:], op=ALU.mult),
                     reads=["gexp"], writes=[("osb", it4, 0), ("osb", it4, 1)])
                for par in range(2):
                    P.op("dve", lambda e, par=par: e.tensor_tensor(
                        out=yst[bs3][:, :, tl].rearrange("p (j q) t -> p q j t", q=2)[:, par, :, :], in0=osb[it4][:, par, :, :],
                        in1=sgb[sslot][:].rearrange("p (j q) t -> p q j t", q=2)[:, par, :, :], op=ALU.mult),
                        reads=[("osb", it4, par), ("sgb", sslot)], writes=[("yst", bs3)])
                if st % 4 == 3:
                    tk = slice(512 * blk, 512 * blk + 512)
                    P.dma("pool", lambda e: e.dma_start(out=T.ygT.rearrange("(a p) t -> p a t", p=128)[:, :, tk], in_=yst[bs3][:]),
                          f"yst{bs3}", reads=[("yst", bs3)])
            items.append([f0, f0b, f1, f2, f2b, f3, f4, f5, f6, f7, f8, f9, f10, f11, f12])
        run_pipeline(items, NPH, 1)
        P.flush()


def alloc_S3_weights(nc, es, T):
    W = Ctx()
    sb = lambda name, shape, dt: es.enter_context(nc.sbuf_tensor("f_" + name, shape, dt))
    W.wgt = sb("wgt", [128, 8, 2048], BF16)
    T.s3w = W


def load_S3_weights(P, T, which):
    W = T.s3w
    w_v = T.w_in.rearrange("(k p) c -> p k c", p=128)
    wa_v = T.w_att_out.rearrange("(k p) c -> p k c", p=128)
    wg_v = T.w_gla_out.rearrange("(k p) c -> p k c", p=128)
    wo_v = T.w_out.rearrange("(k p) c -> p k c", p=128)
    if "wgt" in which:
        for k in range(8):
            for half in range(2):
                P.dma("pool", lambda e, k=k, half=half: e.dma_start(
                    out=W.wgt[:, k, 1024 * half:1024 * half + 1024], in_=w_v[:, k, C_GATT + 1024 * half:C_GATT + 1024 * half + 1024]),
                    "wgt", writes=["wgt"])
    if "cast" in which:
        for src, dst, n in ((T.w_att_out, T.wao_b, 6), (T.w_gla_out, T.wgo_b, 4), (T.w_out, T.wo_b, 8)):
            for k in range(n):
                P.dma("pool", lambda e, src=src, dst=dst, k=k: e.dma_start(out=dst[128 * k:128 * k + 128, :], in_=src[128 * k:128 * k + 128, :]),
                      "wcast", writes=["wcast"])
    if "rest_b" in which:
        wa_b = T.wao_b.rearrange("(k p) c -> p k c", p=128)
        wg_b = T.wgo_b.rearrange("(k p) c -> p k c", p=128)
        wo_b = T.wo_b.rearrange("(k p) c -> p k c", p=128)
        for k in range(6):
            P.dma("sp" if k % 2 == 0 else "act", lambda e, k=k: e.dma_start(out=W.wao[:, k, :], in_=wa_b[:, k, :]), "wao", writes=["wao"])
        for k in range(4):
            P.dma("sp" if k % 2 == 0 else "act", lambda e, k=k: e.dma_start(out=W.wgo[:, k, :], in_=wg_b[:, k, :]), "wgo", writes=["wgo"])
        for k in range(8):
            P.dma("sp" if k % 2 == 0 else "act", lambda e, k=k: e.dma_start(out=W.wo[:, k, :], in_=wo_b[:, k, :]), "wo", writes=["wo"])
    if "rest" not in which:
        return
    for k in range(6):
        P.dma("pool", lambda e, k=k: e.dma_start(out=W.wao[:, k, :], in_=wa_v[:, k, :]), "wao", writes=["wao"])
    for k in range(4):
        P.dma("pool", lambda e, k=k: e.dma_start(out=W.wgo[:, k, :], in_=wg_v[:, k, :]), "wgo", writes=["wgo"])
    for k in range(8):
        P.dma("pool", lambda e, k=k: e.dma_start(out=W.wo[:, k, :], in_=wo_v[:, k, :]), "wo", writes=["wo"])


def stage_S3(nc, P, T, weights_loaded):
    x_v = T.xT.rearrange("(k p) t -> p k t", p=128)
    with ExitStack() as es:
        sb = lambda name, shape, dt: es.enter_context(nc.sbuf_tensor("f_" + name, shape, dt))
        ps = lambda name, shape, dt: es.enter_context(nc.psum_tensor("f_" + name, shape, dt))
        wgt = T.s3w.wgt
        T.s3w.wao = wao = sb("wao", [128, 6, 1024], BF16)
        T.s3w.wgo = wgo = sb("wgo", [128, 4, 1024], BF16)
        T.s3w.wo = wo = sb("wo", [128, 8, 1024], BF16)
        lng = sb("lng", [128, 1024], F32)
        lnb = sb("lnb", [128, 1024], F32)
        xt = [sb(f"xt{i}", [128, 8, 512], BF16) for i in range(2)]
        ygs = [sb(f"ygs{i}", [128, 4, 512], BF16) for i in range(2)]
        num = sb("num", [128, 6, 512], F32)
        sga = sb("sga", [128, 6, 512], BF16)
        dn = sb("dn", [128, 2, 3, 512], F32)
        xtm = [sb(f"xtm{i}", [128, 1024], F32) for i in range(4)]
        ya = [sb(f"ya{i}", [128, 6, 512], BF16) for i in range(2)]
        NS = 2
        sA = [sb(f"sA{i}", [128, 512], F32) for i in range(NS)]
        sG = [sb(f"sG{i}", [128, 512], F32) for i in range(NS)]
        m1 = [sb(f"m1{i}", [128, 512], F32) for i in range(2)]
        m2 = [sb(f"m2{i}", [128, 512], F32) for i in range(2)]
        mg = [sb(f"mg{i}", [128, 8, 512], BF16) for i in range(2)]
        NH = 4
        hs = [sb(f"hs{i}", [128, 1024], F32) for i in range(NH)]
        stt = [sb(f"stt{i}", [128, 12], F32) for i in range(NH)]
        mv = [sb(f"mv{i}", [128, 2], F32) for i in range(NH)]
        sm = [sb(f"sm{i}", [128, 4], F32) for i in range(NH)]
        cneg = sb("cneg", [128, 1], F32)
        pGA = ps("pGA", [128, 512], F32)
        pGG = ps("pGG", [128, 512], F32)
        pYA = ps("pYA", [128, 512], F32)
        pYG = ps("pYG", [128, 512], F32)
        pH = [ps(f"pH{i}", [128, 512], F32) for i in range(4)]

        def load_a(tt):
            s = tt % 2
            tk = slice(512 * tt, 512 * tt + 512)
            for kh in range(2):
                P.dma("pool", lambda e, s=s, kh=kh, tk=tk: e.dma_start(
                    out=xt[s][:, 4 * kh:4 * kh + 4, :], in_=x_v[:, 4 * kh:4 * kh + 4, tk]), f"xt{s}", writes=[("xt", s)])
            P.dma("sp", lambda e, s=s, tk=tk: e.dma_start(
                out=ygs[s][:], in_=T.ygT.rearrange("(a p) t -> p a t", p=128)[:, :, tk]), f"ygs{s}", writes=[("ygs", s)])

        def load_b(tt):
            tk = slice(512 * tt, 512 * tt + 512)
            P.dma("sp", lambda e, tk=tk: e.dma_start(
                out=num[:], in_=T.numT.rearrange("(a p) t -> p a t", p=128)[:, :, tk]), "num", writes=["num"])
            P.dma("sp", lambda e, tk=tk: e.dma_start(
                out=sga[:], in_=T.sgaT.rearrange("(a p) t -> p a t", p=128)[:, :, tk]), "sga", writes=["sga"])
            for jp in range(2):
                for q_ in range(2):
                    src = bass.AP(tensor=T.den.tensor, offset=(2 * jp + q_) * OWN + 512 * tt,
                                  ap=[[0, 64], [4 * OWN, 3], [1, 512]])
                    P.dma("sp", lambda e, jp=jp, q_=q_, src=src: e.dma_start(
                        out=dn[64 * q_:64 * q_ + 64, jp, :, :], in_=src), "dn", writes=["dn"])

        def ya_parts(tt):
            yb = tt % 2
            parts = []
            for jp in range(2):
                def g(jp=jp):
                    P.op("dve", lambda e: e.tensor_tensor(out=dn[:, jp, 0, :], in0=dn[:, jp, 0, :], in1=dn[:, jp, 1, :], op=ALU.add),
                         writes=["dn"])
                    P.op("dve", lambda e: e.tensor_tensor(out=dn[:, jp, 0, :], in0=dn[:, jp, 0, :], in1=dn[:, jp, 2, :], op=ALU.add),
                         writes=["dn"])
                    P.op("dve", lambda e: e.reciprocal(out=dn[:, jp, 0, :], in_=dn[:, jp, 0, :]), writes=["dn"])
                parts.append(g)
            for ft in range(6):
                def g(ft=ft, jp=ft % 2):
                    P.op("dve", lambda e: e.tensor_tensor(out=num[:, ft, :], in0=num[:, ft, :], in1=dn[:, jp, 0, :], op=ALU.mult),
                         reads=["dn"], writes=["num"])
                    P.op("dve", lambda e: e.tensor_tensor(out=ya[yb][:, ft, :], in0=num[:, ft, :], in1=sga[:, ft, :], op=ALU.mult),
                         reads=["num", "sga"], writes=[("ya", yb, ft)])
                parts.append(g)
            return parts

        def compute_ya(tt):
            for g in ya_parts(tt):
                g()

        load_a(0)
        load_b(0)
        P.op("pool", lambda e: e.memset(cneg[:], -0.5), writes=["cneg"])
        load_S3_weights(P, T, ("rest_b",) if weights_loaded else ("wgt", "rest"))
        P.dma("sp", lambda e: e.dma_start(out=lng[:], in_=T.ln_g.partition_broadcast(128)), "c1", writes=["lng"])
        P.dma("sp", lambda e: e.dma_start(out=lnb[:], in_=T.ln_b.partition_broadcast(128)), "c1", writes=["lnb"])
        wkeys = [] if weights_loaded else None

        NPH = 10
        items = []
        cn = dict(d=0, h=0)

        def rd(*keys):
            return [k for k in keys if not (weights_loaded and k == "wgt")]

        def add_dt(tt, dt_, extra=None):
            s = tt % 2
            yb = tt % 2
            mb = tt % 2
            di = cn["d"]
            cn["d"] += 1
            b3 = di % NS
            b2 = di % 2
            dsl = slice(128 * dt_, 128 * dt_ + 128)

            def fA():
                for k in range(8):
                    P.op("pe", lambda e, k=k: e.matmul(pGA[:], lhsT=wgt[:, k, dsl], rhs=xt[s][:, k, :], start=(k == 0), stop=(k == 7)),
                         reads=rd("wgt", ("xt", s)), writes=["pGA"])
                for k in range(8):
                    P.op("pe", lambda e, k=k: e.matmul(pGG[:], lhsT=wgt[:, k, 1024 + 128 * dt_:1024 + 128 * dt_ + 128], rhs=xt[s][:, k, :], start=(k == 0), stop=(k == 7)),
                         reads=rd("wgt", ("xt", s)), writes=["pGG"])
                for ft in range(6):
                    P.op("pe", lambda e, ft=ft: e.matmul(pYA[:], lhsT=wao[:, ft, dsl], rhs=ya[yb][:, ft, :], start=(ft == 0), stop=(ft == 5)),
                         reads=rd("wao", ("ya", yb, ft)), writes=["pYA"])
                for gt in range(4):
                    P.op("pe", lambda e, gt=gt: e.matmul(pYG[:], lhsT=wgo[:, gt, dsl], rhs=ygs[s][:, gt, :], start=(gt == 0), stop=(gt == 3)),
                         reads=rd("wgo", ("ygs", s)), writes=["pYG"])

            def fB():
                P.op("act", lambda e: e.activation(out=sA[b3][:], in_=pGA[:], func=AF.Sigmoid), writes=["pGA", ("sA", b3)])
                P.op("act", lambda e: e.activation(out=sG[b3][:], in_=pGG[:], func=AF.Sigmoid), writes=["pGG", ("sG", b3)])

            def fC():
                P.op("dve", lambda e: e.tensor_tensor(out=m1[b2][:], in0=sA[b3][:], in1=pYA[:], op=ALU.mult),
                     reads=[("sA", b3)], writes=["pYA", ("m1", b2)])
                P.op("dve", lambda e: e.tensor_tensor(out=m2[b2][:], in0=sG[b3][:], in1=pYG[:], op=ALU.mult),
                     reads=[("sG", b3)], writes=["pYG", ("m2", b2)])

            def fD():
                P.op("pool", lambda e: e.tensor_tensor(out=mg[mb][:, dt_, :], in0=m1[b2][:], in1=m2[b2][:], op=ALU.add),
                     reads=[("m1", b2), ("m2", b2)], writes=[("mg", mb, dt_)])
            items.append([fA, lambda: (fB(), fC(), extra() if extra is not None else None), fD] + [None] * (NPH - 3))

        def add_sub(tt, sub, early=None):
            mb = tt % 2
            hi = cn["h"]
            cn["h"] += 1
            x2 = hi % 4
            h4 = hi % NH
            row0 = 512 * tt + 128 * sub
            pis = [(2 * hi) % 4, (2 * hi + 1) % 4]

            def f0():
                P.dma("act", lambda e: e.dma_start(out=xtm[x2][:], in_=T.xtm[row0:row0 + 128, :]),
                      f"xtm{x2}", writes=[("xtm", x2)])

            def fA():
                for half in range(2):
                    ph = pH[pis[half]]
                    for dt_ in range(8):
                        P.op("pe", lambda e, ph=ph, dt_=dt_, half=half: e.matmul(
                            ph[:], lhsT=mg[mb][:, dt_, 128 * sub:128 * sub + 128], rhs=wo[:, dt_, 512 * half:512 * half + 512],
                            start=(dt_ == 0), stop=(dt_ == 7)), reads=rd(("mg", mb, dt_), "wo"), writes=[("pH", pis[half])])

            def fB():
                for half in range(2):
                    ph = pH[pis[half]]
                    P.op("dve", lambda e, ph=ph, half=half: e.scalar_tensor_tensor(
                        out=hs[h4][:, 512 * half:512 * half + 512], in0=xtm[x2][:, 512 * half:512 * half + 512], scalar=ALPHA,
                        in1=ph[:], op0=ALU.mult, op1=ALU.add), reads=[("xtm", x2)], writes=[("pH", pis[half]), ("hs", h4)])
                    P.op("dve", lambda e, half=half: e.bn_stats(out=stt[h4][:, 6 * half:6 * half + 6], in_=hs[h4][:, 512 * half:512 * half + 512]),
                         reads=[("hs", h4)], writes=[("stt", h4)])
                P.op("dve", lambda e: e.bn_aggr(out=mv[h4][:], in_=stt[h4][:]), reads=[("stt", h4)], writes=[("mv", h4)])

            def fC():
                P.op("pool", lambda e: e.tensor_scalar(out=sm[h4][:, 0:1], in0=mv[h4][:, 1:2], scalar1=LN_EPS, scalar2=None, op0=ALU.add),
                     reads=[("mv", h4)], writes=[("sm0", h4)])
                P.op("pool", lambda e: e.tensor_tensor(out=sm[h4][:, 1:2], in0=sm[h4][:, 0:1], in1=cneg[:, 0:1], op=ALU.pow),
                     reads=[("sm0", h4), "cneg"], writes=[("sm1", h4)])

            def fD():
                P.op("dve", lambda e: e.tensor_scalar(out=hs[h4][:], in0=hs[h4][:], scalar1=mv[h4][:, 0:1], scalar2=sm[h4][:, 1:2],
                                                      op0=ALU.subtract, op1=ALU.mult),
                     reads=[("mv", h4), ("sm1", h4)], writes=[("hs", h4)])
                P.op("dve", lambda e: e.tensor_tensor(out=hs[h4][:], in0=hs[h4][:], in1=lng[:], op=ALU.mult),
                     reads=["lng"], writes=[("hs", h4)])

            def fE():
                P.op("pool", lambda e: e.tensor_tensor(out=hs[h4][:], in0=hs[h4][:], in1=lnb[:], op=ALU.add),
                     reads=["lnb"], writes=[("hs", h4)])
                P.dma("pool", lambda e: e.dma_start(out=T.y[row0:row0 + 128, :], in_=hs[h4][:]),
                      f"yo{h4}", reads=[("hs", h4)])
            items.append([None, None, lambda: (f0(), early() if early is not None else None), None, fA, fB, fC, fD, fE, None])

        compute_ya(0)
        load_b(1)
        for tt in range(8):
            if tt + 1 < 8:
                items.append([lambda tt=tt: load_a(tt + 1)] + [None] * (NPH - 1))
            parts = ya_parts(tt + 1) if tt + 1 < 8 else [None] * 8
            for dt_ in range(8):
                add_dt(tt, dt_, extra=parts[dt_])
            for sub in range(4):
                add_sub(tt, sub, early=(lambda tt=tt: load_b(tt + 2)) if (sub == 0 and tt + 2 < 8) else None)
        run_pipeline(items, NPH, 1)
        P.flush()


def build_program(debug=False, stages=(0, 1, 2, 3)):
    nc = bass.Bass("TRN2", target_bir_lowering=False)
    T = Ctx()

    def din(name, shape, dt=F32):
        return nc.dram_tensor(name, list(shape), dt, kind="ExternalInput").ap()

    def scr(name, shape, dt):
        return nc.dram_tensor(name, list(shape), dt, kind=("ExternalOutput" if debug else "Internal")).ap()

    T.xT = din("xT", [D_MODEL, SEQ])
    T.xtm = din("xtm", [OWN, D_MODEL])
    T.w_in = din("w_in", [D_MODEL, IN_W])
    T.w2 = din("w2", [32, 256])
    T.gbias = din("gbias", [1, 512])
    T.gng = din("gng", [128, 4])
    T.ebias = din("ebias", [128, 12 * 256])
    T.ebias0 = din("ebias0", [64, 12 * 128])
    T.emask = din("emask", [128, 256])
    T.emask0 = din("emask0", [64, 128])
    T.ident = din("ident", [128, 128])
    T.tri = din("tri", [128, 512])
    T.ind = din("ind", [128, 2])
    T.smask = din("smask", [128, 256])
    T.w_att_out = din("w_att_out", [768, D_MODEL])
    T.w_gla_out = din("w_gla_out", [512, D_MODEL])
    T.w_out = din("w_out", [D_MODEL, D_MODEL])
    T.ln_g = din("ln_g", [1, D_MODEL])
    T.ln_b = din("ln_b", [1, D_MODEL])
    T.y = nc.dram_tensor("y", [OWN, D_MODEL], F32, kind="ExternalOutput").ap()
    T.qaT = scr("qaT", [768, OWN], BF16)
    T.kaT = scr("kaT", [768, 5120], BF16)
    T.vaT = scr("vaT", [768, 5120], BF16)
    T.sgaT = scr("sgaT", [768, OWN], BF16)
    T.qgT = scr("qgT", [256, OWN], F32)
    T.kgT = scr("kgT", [256, OWN], F32)
    T.sgbT = scr("sgbT", [512, OWN], BF16)
    T.lrT = scr("lrT", [32, SEQ], F32)
    T.kTM = scr("kTM", [SEQ, 256], F32)
    T.vTM = scr("vTM", [SEQ, 512], BF16)
    T.numT = scr("numT", [768, OWN], F32)
    T.den = scr("den", [12, OWN], F32)
    T.ygT = scr("ygT", [512, OWN], BF16)
    T.wao_b = scr("wao_b", [768, D_MODEL], BF16)
    T.wgo_b = scr("wgo_b", [512, D_MODEL], BF16)
    T.wo_b = scr("wo_b", [D_MODEL, D_MODEL], BF16)
    P = Prog(nc)
    if 0 in stages:
        stage_P(nc, P, T)
    if 1 in stages:
        stage_S1(nc, P, T)
    with ExitStack() as es3:
        alloc_S3_weights(nc, es3, T)
        if 2 in stages:
            stage_S2(nc, P, T, prefetch=lambda: load_S3_weights(P, T, ("wgt", "cast")))
        if 3 in stages:
            stage_S3(nc, P, T, weights_loaded=(2 in stages))
    return nc


def _t5_bucket(rel):
    half = 16
    max_exact = 8
    ret = (rel > 0).astype(np.int32) * half
    n = np.abs(rel)
    large = max_exact + (np.log(np.maximum(n, 1) / max_exact) / np.log(1024 / max_exact) * (half - max_exact)).astype(np.int32)
    large = np.minimum(large, half - 1)
    return ret + np.where(n < max_exact, n, large)


def _static_tables():
    a = np.arange(128)[:, None]
    i = np.arange(128)[None, :]
    dlA = a - 64 - i
    dlB = a + 64 - i
    dl = np.concatenate([dlA, dlB], axis=1)
    valid = (np.abs(dl) <= 64)
    s = np.arange(128)[:, None]
    t = np.arange(128)[None, :]
    same = (s // 64) == (t // 64)
    c = -1.0 / 16.0
    bF = np.where(same & (s <= t), c, 0.0)
    bB = np.where(same & (s >= t), c, 0.0)
    wF = np.where(same & (s > t), c, 0.0)
    wB = np.where(same & (s < t), c, 0.0)
    tri = np.concatenate([bF, bB, wF, wB], axis=1).astype(np.float32)
    ind = np.zeros((128, 2), np.float32)
    ind[:64, 0] = c
    ind[64:, 1] = c
    mF = np.where(same & (s <= t), 1.0, 0.0)
    mB = np.where(same & (s >= t), 1.0, 0.0)
    smask = np.concatenate([mF, mB], axis=1).astype(np.float32)
    return dl, valid, tri, ind, smask


def make_in_maps(x, w_in, gla_gate_w2, gla_gate_b, gla_norm_g, rel_bias, w_att_out, w_gla_out, w_out, ln_g, ln_b):
    x = np.asarray(x, np.float32)
    w_in0 = np.asarray(w_in, np.float32)[0]
    w2 = np.asarray(gla_gate_w2, np.float32)[0]
    gb = np.asarray(gla_gate_b, np.float32)[0]
    gng = np.ascontiguousarray(np.asarray(gla_norm_g, np.float32)[0].reshape(4, 128).T)
    rb = np.asarray(rel_bias, np.float32)
    dl, valid, tri, ind, smask = _static_tables()
    emask = valid.astype(np.float32)
    common = dict(
        gng=gng, emask=emask, emask0=np.ascontiguousarray(emask[64:128, 0:128]),
        ident=np.eye(128, dtype=np.float32), tri=tri, ind=ind, smask=smask,
        w_att_out=np.ascontiguousarray(np.asarray(w_att_out, np.float32)[0]),
        w_gla_out=np.ascontiguousarray(np.asarray(w_gla_out, np.float32)[0]),
        w_out=np.ascontiguousarray(np.asarray(w_out, np.float32)[0]),
        ln_g=np.ascontiguousarray(np.asarray(ln_g, np.float32)[0].reshape(1, D_MODEL)),
        ln_b=np.ascontiguousarray(np.asarray(ln_b, np.float32)[0].reshape(1, D_MODEL)),
    )
    per_parity = []
    for par in range(2):
        sign = 1 if par == 0 else -1
        eb = np.zeros((128, 12, 256), np.float32)
        for h in range(12):
            d = DILS[h // 4]
            bk = _t5_bucket(sign * dl * d)
            eb[:, h, :] = np.where(valid, rb[bk, h], 0.0)
        w_c = w_in0
        w2_c = w2
        gb_c = gb
        if par == 1:
            w_c = w_in0.copy()
            w_c[:, C_LRF:C_LRF + 16] = w_in0[:, C_LRB:C_LRB + 16]
            w_c[:, C_LRB:C_LRB + 16] = w_in0[:, C_LRF:C_LRF + 16]
            w2_c = w2[::-1]
            gb_c = gb[::-1]
        per_parity.append(dict(
            ebias=np.ascontiguousarray(eb.reshape(128, 12 * 256)),
            ebias0=np.ascontiguousarray(eb[64:128, :, 0:128].reshape(64, 12 * 128)),
            w_in=np.ascontiguousarray(w_c),
            w2=np.ascontiguousarray(w2_c.reshape(32, 256)),
            gbias=np.ascontiguousarray(gb_c.reshape(1, 512)),
        ))
    in_maps = []
    for c in range(NCORES):
        b, par = c // 2, c % 2
        xs = x[b] if par == 0 else x[b][::-1]
        m = dict(common)
        m.update(per_parity[par])
        m["xT"] = np.ascontiguousarray(xs.T)
        m["xtm"] = np.ascontiguousarray(xs[:OWN])
        in_maps.append(m)
    return in_maps


_NC_CACHE = {}


def kernel(x, w_in, gla_gate_w2, gla_gate_b, gla_norm_g, rel_bias, w_att_out, w_gla_out, w_out, ln_g, ln_b):
    in_maps = make_in_maps(x, w_in, gla_gate_w2, gla_gate_b, gla_norm_g, rel_bias,
                           w_att_out, w_gla_out, w_out, ln_g, ln_b)
    if "nc" not in _NC_CACHE:
        _NC_CACHE["nc"] = build_program()
    nc = _NC_CACHE["nc"]
    res = run_bass_kernel_spmd(nc, in_maps, core_ids=list(range(NCORES)))
    out = np.empty((4, SEQ, D_MODEL), np.float32)
    for c in range(NCORES):
        y = np.asarray(res.results[c]["y"], np.float32)
        b, par = c // 2, c % 2
        if par == 0:
            out[b, :OWN] = y
        else:
            out[b, OWN:] = y[::-1]
    return out
```

```python
import os
import numpy as np
import ml_dtypes
from contextlib import ExitStack
import concourse.bass as bass
import concourse.mybir as mybir
from concourse.bass_utils import run_bass_kernel_spmd

F32 = mybir.dt.float32
BF16 = mybir.dt.bfloat16
AF = mybir.ActivationFunctionType
ALU = mybir.AluOpType

D_MODEL = 1024
SEQ = 8192
OWN = 4096
NCORES = 8
IN_W = 6688
C_QA, C_KA, C_VA, C_GA, C_QB, C_KB, C_VB, C_GB, C_LRF, C_LRB, C_GATT, C_GGLA = (
    0, 768, 1536, 2304, 3072, 3328, 3584, 4096, 4608, 4624, 4640, 5664)
DILS = (1, 4, 16)
ALPHA = 2.0 ** 0.25
LN_EPS = 1e-5
RMS_EPS = 1e-6

ENGS = ("pe", "act", "dve", "pool", "sp")


class Prog:
    def __init__(self, nc):
        self.nc = nc
        self.sems = {e: nc.alloc_semaphore("s_" + e) for e in ("pe", "act", "dve", "pool")}
        self.cnt = {e: 0 for e in ("pe", "act", "dve", "pool")}
        self.dsem = {}
        self.ops = []
        self.tagmap = {}
        self.waited = set()

    def op(self, eng, fn, reads=(), writes=()):
        self.ops.append(dict(eng=eng, fn=fn, reads=tuple(reads), writes=tuple(writes), dma=None))

    def dma(self, eng, fn, dsem, reads=(), writes=()):
        kind = "s" if eng == "pool" else "h"
        nk = sum(1 for k_ in self.tagmap if k_[1] == kind)
        dsem = self.tagmap.setdefault((dsem, kind), "g%s%d" % (kind, nk))
        if dsem not in self.dsem:
            self.dsem[dsem] = [self.nc.alloc_semaphore("d_" + dsem), 0]
        self.ops.append(dict(eng=eng, fn=fn, reads=tuple(reads), writes=tuple(writes), dma=dsem))

    def flush(self):
        ops = self.ops
        self.ops = []
        self.tagmap = {}
        n = len(ops)
        last_w = {}
        readers = {}
        deps = [None] * n
        for i, o in enumerate(ops):
            d = set()
            for k in o["reads"]:
                if k in last_w:
                    d.add(last_w[k])
            for k in o["writes"]:
                if k in last_w:
                    d.add(last_w[k])
                for r in readers.get(k, ()):
                    d.add(r)
            d.discard(i)
            for k in o["writes"]:
                last_w[k] = i
                readers[k] = []
            for k in o["reads"]:
                if k not in o["writes"]:
                    readers.setdefault(k, []).append(i)
            if o["eng"] == "pe" and o["dma"] is None:
                d = {j for j in d if not (ops[j]["eng"] == "pe" and ops[j]["dma"] is None)}
            if o["dma"] is not None:
                sk = {j for j in d if (ops[j]["dma"] is not None and set(ops[j]["writes"]) & set(o["writes"])
                                       and not (set(ops[j]["reads"]) | set(o["reads"])))}
                for j in sk:
                    d = d | deps[j]
                d = d - sk
            deps[i] = d
        flagged = set()
        for d in deps:
            flagged |= d
        val = [None] * n
        cnt = dict(self.cnt)
        dcnt = {k: v[1] for k, v in self.dsem.items()}
        for i, o in enumerate(ops):
            if o["dma"] is not None:
                dcnt[o["dma"]] += 16
                val[i] = (o["dma"], dcnt[o["dma"]])
            elif i in flagged:
                cnt[o["eng"]] += 1
                val[i] = (o["eng"], cnt[o["eng"]])
        streams = {e: [] for e in ENGS}
        known = {e: {} for e in ENGS}
        run_d = {k: v[1] for k, v in self.dsem.items()}
        for i, o in enumerate(ops):
            waits = {}
            for j in deps[i]:
                s, v = val[j]
                if ops[j]["dma"] is not None:
                    v = max(v, run_d[s])
                if waits.get(s, 0) < v:
                    waits[s] = v
            wl = []
            kn = known[o["eng"]]
            for s, v in waits.items():
                if kn.get(s, 0) < v:
                    kn[s] = v
                    wl.append((s, v))
            streams[o["eng"]].append((o, wl, val[i]))
            if o["dma"] is not None:
                run_d[o["dma"]] = val[i][1]
        final_waits = []
        for k, v in dcnt.items():
            if known["sp"].get(k, 0) < v:
                final_waits.append((k, v))
        for e_ in ENGS:
            for (o, wl, v) in streams[e_]:
                for (s_, x_) in wl:
                    self.waited.add((s_, x_))
        for (s_, x_) in final_waits:
            self.waited.add((s_, x_))
        kn2 = {e_: {} for e_ in ENGS}
        for e_ in ENGS:
            for (o, wl, v) in streams[e_]:
                for (s_, x_) in wl:
                    if kn2[e_].get(s_, 0) < x_:
                        kn2[e_][s_] = x_
                if o["dma"] is not None:
                    s_, x_ = v
                    prev = x_ - 16
                    if prev > 0 and (s_, prev) in self.waited and kn2[e_].get(s_, 0) < prev:
                        wl.append((s_, prev))
                        kn2[e_][s_] = prev
        self.cnt = cnt
        for k in self.dsem:
            self.dsem[k][1] = dcnt[k]
        self._emit(streams, final_waits)

    def _sem(self, s):
        if s in self.sems:
            return self.sems[s]
        return self.dsem[s][0]

    def _emit(self, streams, final_waits):
        nc = self.nc
        me = self

        def run(eng_obj, name):
            for (o, wl, v) in streams[name]:
                for (s, x) in wl:
                    eng_obj.wait_ge(me._sem(s), x)
                ins = o["fn"](eng_obj)
                if v is not None:
                    ins.then_inc(me._sem(v[0]), 16 if o["dma"] is not None else 1)
            if name == "sp":
                for (s, x) in final_waits:
                    eng_obj.wait_ge(me._sem(s), x)

        with nc.Block() as blk:
            @blk.sync
            def _(e):
                run(e, "sp")

            @blk.tensor
            def _(e):
                run(e, "pe")

            @blk.scalar
            def _(e):
                run(e, "act")

            @blk.vector
            def _(e):
                run(e, "dve")

            @blk.gpsimd
            def _(e):
                run(e, "pool")


class Ctx:
    pass


def stage_P(nc, P, T):
    fm = []
    for i in range(6):
        fm.append((C_QA + 128 * i, 128, T.qaT, 128 * i, False, BF16, 8))
    for i in range(6):
        fm.append((C_KA + 128 * i, 128, T.kaT, 128 * i, False, BF16, 10 if i >= 4 else 9))
    for i in range(6):
        fm.append((C_VA + 128 * i, 128, T.vaT, 128 * i, False, BF16, 10 if i >= 4 else 9))
    for i in range(6):
        fm.append((C_GA + 128 * i, 128, T.sgaT, 128 * i, True, BF16, 8))
    for i in range(2):
        fm.append((C_QB + 128 * i, 128, T.qgT, 128 * i, False, F32, 8))
    for i in range(2):
        fm.append((C_KB + 128 * i, 128, T.kgT, 128 * i, False, F32, 8))
    for i in range(4):
        fm.append((C_GB + 128 * i, 128, T.sgbT, 128 * i, True, BF16, 8))
    fm.append((C_LRF, 32, T.lrT, 0, False, F32, 16))
    NFM = len(fm)
    offs = []
    o = 0
    for c in fm:
        offs.append(o)
        o += c[1]
    WF = o
    w_v = T.w_in.rearrange("(k p) c -> p k c", p=128)
    x_v = T.xT.rearrange("(k p) t -> p k t", p=128)
    with ExitStack() as es:
        sb = lambda name, shape, dt: es.enter_context(nc.sbuf_tensor("p_" + name, shape, dt))
        ps = lambda name, shape, dt: es.enter_context(nc.psum_tensor("p_" + name, shape, dt))
        wfm = sb("wfm", [128, 8, WF], BF16)
        wtm = sb("wtm", [128, 8, 768], BF16)
        xt = [sb(f"xt{i}", [128, 8, 512], BF16) for i in range(2)]
        ob = [sb(f"ob{i}", [128, 512], BF16) for i in range(4)]
        of = [sb(f"of{i}", [128, 512], F32) for i in range(2)]
        kst = [sb(f"kst{i}", [128, 256], F32) for i in range(2)]
        vst = [sb(f"vst{i}", [128, 512], BF16) for i in range(2)]
        pf = [ps(f"pf{i}", [128, 512], F32) for i in range(4)]
        pt = [ps(f"pt{i}", [128, 512], F32) for i in range(4)]

        def load_x(tt):
            s = tt % 2
            for kh in range(2):
                P.dma("pool", lambda e, s=s, tt=tt, kh=kh: e.dma_start(
                    out=xt[s][:, 4 * kh:4 * kh + 4, :], in_=x_v[:, 4 * kh:4 * kh + 4, 512 * tt:512 * tt + 512]),
                    f"xt{s}", writes=[("xt", s)])

        def load_w(ci):
            c0, ncol = fm[ci][0], fm[ci][1]
            for kh in range(2):
                P.dma("pool", lambda e, c0=c0, ncol=ncol, kh=kh, o=offs[ci]: e.dma_start(
                    out=wfm[:, 4 * kh:4 * kh + 4, o:o + ncol], in_=w_v[:, 4 * kh:4 * kh + 4, c0:c0 + ncol]),
                    f"wfm{ci}", writes=[("wfm", ci)])

        load_x(0)
        for ci in range(NFM):
            load_w(ci)
        for kh in range(2):
            for half in range(2):
                P.dma("pool", lambda e, kh=kh, half=half: e.dma_start(
                    out=wtm[:, 4 * kh:4 * kh + 4, 384 * half:384 * half + 384],
                    in_=w_v[:, 4 * kh:4 * kh + 4, C_KB + 384 * half:C_KB + 384 * half + 384]),
                    "wtm", writes=["wtm"])
        ev = 0
        nb = 0
        nf = 0
        npf = 0
        NTT = int(os.environ.get('P_NTT', '16'))
        for tt in range(NTT):
            s = tt % 2
            if tt + 1 < NTT:
                load_x(tt + 1)
            tok = slice(512 * tt, 512 * tt + 512)
            for ci in range(NFM):
                c0, ncol, dst, r0, silu, odt, ntiles = fm[ci]
                if tt >= ntiles or ci >= int(os.environ.get('P_NCH', '99')):
                    continue
                pp = npf % 4
                npf += 1
                for k in range(8):
                    P.op("pe", lambda e, pp=pp, k=k, s=s, o=offs[ci], ncol=ncol: e.matmul(
                        pf[pp][0:ncol, :], lhsT=wfm[:, k, o:o + ncol], rhs=xt[s][:, k, :],
                        start=(k == 0), stop=(k == 7)),
                        reads=[("wfm", ci), ("xt", s)], writes=[("pf", pp)])
                if odt == BF16:
                    bi = nb % 4
                    nb += 1
                    buf, bkey, dkey = ob[bi], ("ob", bi), f"ob{bi}"
                else:
                    bi = nf % 2
                    nf += 1
                    buf, bkey, dkey = of[bi], ("of", bi), f"of{bi}"
                if silu:
                    P.op("act", lambda e, buf=buf, pp=pp, ncol=ncol: e.activation(
                        out=buf[0:ncol, :], in_=pf[pp][0:ncol, :], func=AF.Silu),
                        writes=[("pf", pp), bkey])
                elif ev % 2 == 0:
                    P.op("dve", lambda e, buf=buf, pp=pp, ncol=ncol: e.tensor_copy(
                        out=buf[0:ncol, :], in_=pf[pp][0:ncol, :]),
                        writes=[("pf", pp), bkey])
                    ev += 1
                else:
                    P.op("act", lambda e, buf=buf, pp=pp, ncol=ncol: e.activation(
                        out=buf[0:ncol, :], in_=pf[pp][0:ncol, :], func=AF.Copy),
                        writes=[("pf", pp), bkey])
                    ev += 1
                P.dma("sp", lambda e, buf=buf, dst=dst, r0=r0, ncol=ncol, tok=tok: e.dma_start(
                    out=dst[r0:r0 + ncol, tok], in_=buf[0:ncol, :]), dkey, reads=[bkey], writes=[])
            for sub in range(0 if os.environ.get('P_NOTM') else 4):
                st = 4 * tt + sub
                pa, pb = pt[(2 * st) % 4], pt[(2 * st + 1) % 4]
                ka, kb = ("pt", (2 * st) % 4), ("pt", (2 * st + 1) % 4)
                for half, (pp, pk) in enumerate(((pa, ka), (pb, kb))):
                    for k in range(8):
                        P.op("pe", lambda e, pp=pp, k=k, s=s, sub=sub, half=half: e.matmul(
                            pp[:, 0:384], lhsT=xt[s][:, k, 128 * sub:128 * sub + 128],
                            rhs=wtm[:, k, 384 * half:384 * half + 384], start=(k == 0), stop=(k == 7)),
                            reads=["wtm", ("xt", s)], writes=[pk])
                b2 = st % 2
                TMM = int(os.environ.get('P_TMMODE', '3'))
                if TMM < 2:
                    continue
                P.op("act", lambda e, b2=b2, pa=pa: e.activation(out=kst[b2][:], in_=pa[:, 0:256], func=AF.Copy),
                     writes=[ka, ("kst", b2)])
                P.op("act", lambda e, b2=b2, pa=pa: e.activation(out=vst[b2][:, 0:128], in_=pa[:, 256:384], func=AF.Copy),
                     writes=[ka, ("vstA", b2)])
                P.op("dve", lambda e, b2=b2, pb=pb: e.tensor_copy(out=vst[b2][:, 128:512], in_=pb[:, 0:384]),
                     writes=[kb, ("vstB", b2)])
                if TMM < 3:
                    continue
                P.dma("sp", lambda e, b2=b2, st=st: e.dma_start(out=T.kTM[128 * st:128 * st + 128, :], in_=kst[b2][:]),
                      f"kst{b2}", reads=[("kst", b2)])
                P.dma("sp", lambda e, b2=b2, st=st: e.dma_start(out=T.vTM[128 * st:128 * st + 128, :], in_=vst[b2][:]),
                      f"vst{b2}", reads=[("vstA", b2), ("vstB", b2)])
        P.flush()


def run_pipeline(items, nph, lag):
    n = len(items)
    for step in range(n + (nph - 1) * lag):
        for ph in range(nph - 1, -1, -1):
            i = step - ph * lag
            if 0 <= i < n and items[i][ph] is not None:
                items[i][ph]()


def stage_S1(nc, P, T):
    LAG = 1
    with ExitStack() as es:
        sb = lambda name, shape, dt: es.enter_context(nc.sbuf_tensor("a_" + name, shape, dt))
        ps = lambda name, shape, dt: es.enter_context(nc.psum_tensor("a_" + name, shape, dt))
        ebm = sb("ebm", [128, 12, 256], F32)
        ebm0 = sb("ebm0", [64, 12, 128], F32)
        emk = sb("emk", [128, 256], F32)
        emk0 = sb("emk0", [64, 128], F32)
        identb = sb("identb", [128, 128], BF16)
        qs = [sb(f"qs{i}", [128, 4096], BF16) for i in range(2)]
        ks = [sb(f"ks{i}", [128, 5120], BF16) for i in range(2)]
        vs = [sb(f"vs{i}", [128, 5120], BF16) for i in range(2)]
        va = [sb(f"va{i}", [128, 48, 2, 65], BF16) for i in range(2)]
        NE = 4
        esb = [sb(f"esb{i}", [128, 2, 256], F32) for i in range(NE)]
        ptb = [sb(f"ptb{i}", [128, 2, 256], BF16) for i in range(NE)]
        nst = [sb(f"nst{i}", [64, 2, 2048], F32) for i in range(2)]
        dst_ = [sb(f"dst{i}", [128, 2, 2048], F32) for i in range(2)]
        sT = [ps(f"sT{i}", [128, 1024], F32) for i in range(2)]
        oT = [ps(f"oT{i}", [128, 512], F32) for i in range(2)]
        tp = [ps(f"tp{i}", [128, 1024], BF16) for i in range(2)]

        P.dma("sp", lambda e: e.dma_start(out=ebm[:], in_=T.ebias.rearrange("p (h c) -> p h c", h=12)), "c1", writes=["ebm"])
        P.dma("sp", lambda e: e.dma_start(out=ebm0[:], in_=T.ebias0.rearrange("p (h c) -> p h c", h=12)), "c1", writes=["ebm0"])
        P.dma("sp", lambda e: e.dma_start(out=emk[:], in_=T.emask), "c1", writes=["emk"])
        P.dma("sp", lambda e: e.dma_start(out=emk0[:], in_=T.emask0), "c1", writes=["emk0"])
        P.dma("pool", lambda e: e.dma_start(out=identb[:], in_=T.ident), "c2", writes=["identb"])

        def load_pair(p):
            s = p % 2
            P.dma("sp", lambda e, s=s, p=p: e.dma_start(out=qs[s][:], in_=T.qaT[128 * p:128 * p + 128, :]),
                  f"qs{s}", writes=[("qs", s)])
            nk = 5120 if p >= 4 else 4608
            P.dma("sp", lambda e, s=s, p=p, nk=nk: e.dma_start(out=ks[s][:, 0:nk], in_=T.kaT[128 * p:128 * p + 128, 0:nk]),
                  f"ks{s}", writes=[("ks", s)])
            P.dma("sp", lambda e, s=s, p=p, nk=nk: e.dma_start(out=vs[s][:, 0:nk], in_=T.vaT[128 * p:128 * p + 128, 0:nk]),
                  f"vs{s}", writes=[("vs", s)])

        load_pair(0)
        load_pair(1)
        P.op("act", lambda e: e.activation(out=ebm[:], in_=ebm[:], func=AF.Exp), reads=["ebm"], writes=["ebm"])
        P.op("act", lambda e: e.activation(out=ebm0[:], in_=ebm0[:], func=AF.Exp), reads=["ebm0"], writes=["ebm0"])
        for h in range(12):
            P.op("dve", lambda e, h=h: e.tensor_tensor(out=ebm[:, h, :], in0=ebm[:, h, :], in1=emk[:], op=ALU.mult),
                 reads=["ebm", "emk"], writes=["ebm"])
            P.op("dve", lambda e, h=h: e.tensor_tensor(out=ebm0[:, h, :], in0=ebm0[:, h, :], in1=emk0[:], op=ALU.mult),
                 reads=["ebm0", "emk0"], writes=["ebm0"])
        for i_ in range(2):
            P.op("pool", lambda e, i_=i_: e.memset(va[i_][:], 1.0), writes=[("va", i_, c_) for c_ in range(48)])

        items = []
        cnt = dict(t=0, tp=0, sup=0)

        def add_vbuild(p):
            s = p % 2
            d = DILS[p // 2]
            Lr = OWN // d
            NC = Lr // 128 + 1
            for r in range(d):
                for c in range(NC):
                    if c == 0:
                        sl, n = slice(r, r + 63 * d + 1, d), 64
                    else:
                        st = d * (128 * c - 64) + r
                        sl, n = slice(st, st + 127 * d + 1, d), 128
                    idx = r * NC + c
                    tq = cnt["tp"] % 2
                    cnt["tp"] += 1

                    def fA(tq=tq, s=s, sl=sl, n=n):
                        P.op("pe", lambda e: e.transpose(out=tp[tq][0:n, 0:128], in_=vs[s][:, sl], identity=identb[:]),
                             reads=[("vs", s), "identb"], writes=[("tp", tq)])

                    def fB(tq=tq, s=s, idx=idx, n=n):
                        if idx % 2 == 0:
                            P.op("act", lambda e: e.activation(
                                out=va[s][0:n, idx, :, 0:64], in_=tp[tq][0:n, 0:128].rearrange("p (h e) -> p h e", h=2),
                                func=AF.Copy), writes=[("tp", tq), ("va", s, idx)])
                        else:
                            P.op("dve", lambda e: e.tensor_copy(
                                out=va[s][0:n, idx, :, 0:64], in_=tp[tq][0:n, 0:128].rearrange("p (h e) -> p h e", h=2)),
                                writes=[("tp", tq), ("va", s, idx)])
                    items.append([fA, fB, None, None, None, None, None])

        def add_attn(p):
            s = p % 2
            g = p // 2
            d = DILS[g]
            Lr = OWN // d
            NC = Lr // 128 + 1

            def ksl(r, c):
                if c == 0:
                    return slice(r, r + 63 * d + 1, d), 64
                st = d * (128 * c - 64) + r
                return slice(st, st + 127 * d + 1, d), 128

            for u in range(2):
                if d == 1:
                    tiles = [(0, m) for m in range(16 * u, 16 * u + 16)]
                elif d == 4:
                    tiles = [(r, m) for m in range(4 * u, 4 * u + 4) for r in range(4)]
                else:
                    tiles = [(r, u) for r in range(16)]
                sbuf_i = cnt["sup"] % 2
                cnt["sup"] += 1
                for ti, (r, m) in enumerate(tiles):
                    qst = d * 128 * m + r
                    qsl = slice(qst, qst + 127 * d + 1, d)
                    osl = slice(qst - 2048 * u, qst - 2048 * u + 127 * d + 1, d)
                    slA, nA = ksl(r, m)
                    slB, nB = ksl(r, m + 1)
                    iA, iB = r * NC + m, r * NC + m + 1
                    t2 = cnt["t"] % 2
                    t4 = cnt["t"] % NE
                    cnt["t"] += 1
                    last = (ti == len(tiles) - 1)

                    def fA(t2=t2, s=s, slA=slA, nA=nA, slB=slB, qsl=qsl):
                        for hh in range(2):
                            pb = 64 * hh
                            c0 = 512 * hh
                            P.op("pe", lambda e, pb=pb, c0=c0: e.matmul(
                                sT[t2][0:nA, c0:c0 + 128], lhsT=ks[s][pb:pb + 64, slA], rhs=qs[s][pb:pb + 64, qsl],
                                start=True, stop=True), reads=[("ks", s), ("qs", s)], writes=[("sT", t2, hh)])
                            P.op("pe", lambda e, pb=pb, c0=c0: e.matmul(
                                sT[t2][:, c0 + 128:c0 + 256], lhsT=ks[s][pb:pb + 64, slB], rhs=qs[s][pb:pb + 64, qsl],
                                start=True, stop=True), reads=[("ks", s), ("qs", s)], writes=[("sT", t2, hh)])

                    def fB(t2=t2, t4=t4, nA=nA):
                        sv = sT[t2][:].rearrange("p (h c) -> p h c", h=2)
                        wk = [("sT", t2, 0), ("sT", t2, 1), ("esb", t4)]
                        if nA == 128:
                            P.op("act", lambda e: e.activation(out=esb[t4][:], in_=sv[:, :, 0:256], func=AF.Exp, scale=0.125), writes=wk)
                        else:
                            P.op("act", lambda e: e.activation(out=esb[t4][0:64, :, 0:128], in_=sv[0:64, :, 0:128], func=AF.Exp, scale=0.125), writes=wk)
                            P.op("act", lambda e: e.activation(out=esb[t4][:, :, 128:256], in_=sv[:, :, 128:256], func=AF.Exp, scale=0.125), writes=wk)

                    def fC(t4=t4, nA=nA, p=p):
                        if nA == 128:
                            P.op("dve", lambda e: e.tensor_tensor(out=ptb[t4][:], in0=esb[t4][:], in1=ebm[:, 2 * p:2 * p + 2, :], op=ALU.mult),
                                 reads=[("esb", t4), "ebm"], writes=[("ptb", t4)])
                        else:
                            P.op("dve", lambda e: e.tensor_tensor(out=ptb[t4][0:64, :, 0:128], in0=esb[t4][0:64, :, 0:128],
                                                                  in1=ebm0[:, 2 * p:2 * p + 2, :], op=ALU.mult),
                                 reads=[("esb", t4), "ebm0"], writes=[("ptb", t4)])
                            P.op("dve", lambda e: e.tensor_tensor(out=ptb[t4][:, :, 128:256], in0=esb[t4][:, :, 128:256],
                                                                  in1=ebm[:, 2 * p:2 * p + 2, 128:256], op=ALU.mult),
                                 reads=[("esb", t4), "ebm"], writes=[("ptb", t4)])

                    def fD(t2=t2, t4=t4, s=s, iA=iA, iB=iB, nA=nA):
                        for hh in range(2):
                            P.op("pe", lambda e, hh=hh: e.matmul(
                                oT[t2][0:65, 128 * hh:128 * hh + 128], lhsT=va[s][0:nA, iA, hh, :], rhs=ptb[t4][0:nA, hh, 0:128],
                                start=True, stop=False), reads=[("va", s, iA), ("ptb", t4)], writes=[("oT", t2)])
                            P.op("pe", lambda e, hh=hh: e.matmul(
                                oT[t2][0:65, 128 * hh:128 * hh + 128], lhsT=va[s][:, iB, hh, :], rhs=ptb[t4][:, hh, 128:256],
                                start=False, stop=True), reads=[("va", s, iB), ("ptb", t4)], writes=[("oT", t2)])

                    def fE(t2=t2, sbuf_i=sbuf_i, osl=osl, last=last, p=p, u=u):
                        P.op("act", lambda e: e.activation(
                            out=nst[sbuf_i][0:64, :, osl], in_=oT[t2][0:64, 0:256].rearrange("p (h c) -> p h c", h=2), func=AF.Copy),
                            writes=[("oT", t2), ("nst", sbuf_i)])
                        P.op("dve", lambda e: e.tensor_copy(
                            out=dst_[sbuf_i][64:65, :, osl], in_=oT[t2][64:65, 0:256].rearrange("p (h c) -> p h c", h=2)),
                            writes=[("oT", t2), ("dst", sbuf_i)])
                        if last:
                            tk = slice(2048 * u, 2048 * u + 2048)
                            P.dma("pool", lambda e: e.dma_start(
                                out=T.numT[128 * p:128 * p + 128, tk].rearrange("(h e) t -> e h t", h=2), in_=nst[sbuf_i][:]),
                                f"nst{sbuf_i}", reads=[("nst", sbuf_i)])
                            P.dma("pool", lambda e: e.dma_start(
                                out=T.den[2 * p:2 * p + 2, tk].rearrange("(o h) t -> o h t", o=1), in_=dst_[sbuf_i][64:65, :, :]),
                                f"dst{sbuf_i}", reads=[("dst", sbuf_i)])
                    items.append([fA, fB, None, fC, None, fD, fE])

        def add_load(p):
            items.append([lambda p=p: load_pair(p), None, None, None, None, None, None])

        add_vbuild(0)
        for p in range(6):
            n0 = len(items)
            add_attn(p)
            at = items[n0:]
            del items[n0:]
            vb = []
            if p + 1 < 6:
                add_vbuild(p + 1)
                vb = items[n0:]
                del items[n0:]
            merged = at[:12]
            rest = at[12:]
            for j in range(max(len(rest), len(vb))):
                if j < len(rest):
                    merged.append(rest[j])
                if j < len(vb):
                    merged.append(vb[j])
            items.extend(merged)
            if p + 2 < 6:
                add_load(p + 2)
        run_pipeline(items, 7, LAG)
        P.flush()


def stage_S2(nc, P, T, prefetch=None):
    NR = 8
    with ExitStack() as es:
        sb = lambda name, shape, dt: es.enter_context(nc.sbuf_tensor("g_" + name, shape, dt))
        ps = lambda name, shape, dt: es.enter_context(nc.psum_tensor("g_" + name, shape, dt))
        tri = sb("tri", [128, 512], F32)
        ind = sb("ind", [128, 2], F32)
        smask = sb("smask", [128, 256], F32)
        smask4 = sb("smask4", [128, 512], F32)
        onesm = sb("onesm", [128, 128], BF16)
        ones32 = sb("ones32", [128, 128], F32)
        w2d = [sb(f"w2d{i}", [17, 256], F32) for i in range(2)]
        gng = sb("gng", [128, 4], F32)
        gexp = sb("gexp", [128, 2, 2, 128], F32)
        rm = [sb(f"rm{i}", [128, 1], F32) for i in range(2)]
        eb_all = sb("eb_all", [128, 2, 64, 128], BF16)
        efb = sb("efb", [128, 2, NR, 128], BF16)
        RF = sb("RF", [128, 2, NR, 128], F32)
        RB = sb("RB", [128, 2, NR, 128], F32)
        lrd = [[sb(f"lr{d}{i}", [17, 512], F32) for i in range(2)] for d in range(2)]
        lrh = [[sb(f"lrh{d}{i}", [17, 512], BF16) for i in range(2)] for d in range(2)]
        lrl = [[sb(f"lrl{d}{i}", [17, 512], BF16) for i in range(2)] for d in range(2)]
        w2h = [sb(f"w2h{i}", [17, 256], BF16) for i in range(2)]
        w2l = [sb(f"w2l{i}", [17, 256], BF16) for i in range(2)]
        tri_bf = sb("tri_bf", [128, 512], BF16)
        ind_bf = sb("ind_bf", [128, 2], BF16)
        sph = [sb(f"sph{i}", [128, 512], BF16) for i in range(2)]
        spl = [sb(f"spl{i}", [128, 512], BF16) for i in range(2)]
        DK, DV, DQ, DS = 8, 8, 6, 8
        ktm = [sb(f"ktm{i}", [128, 256], F32) for i in range(DK)]
        vtm = [sb(f"vtm{i}", [128, 512], BF16) for i in range(DV)]
        qg = [sb(f"qg{i}", [128, 2, 128], F32) for i in range(DQ)]
        kg = [sb(f"kg{i}", [128, 2, 128], F32) for i in range(DQ)]
        sgb = [sb(f"sgb{i}", [128, 4, 128], BF16) for i in range(DS)]
        yst = [sb(f"yst{i}", [128, 4, 512], BF16) for i in range(2)]
        sp_sb = [sb(f"sp_sb{i}", [128, 512], F32) for i in range(2)]
        e_sb = sp_sb
        ewd = [sb(f"ewd{i}", [128, 260], F32) for i in range(4)]
        ebp = [sb(f"ebp{i}", [128, 512], F32) for i in range(2)]
        ebn = [sb(f"ebn{i}", [128, 512], F32) for i in range(2)]
        kpc = [[sb(f"kp{c}{i}", [128, 256], BF16) for i in range(2)] for c in range(2)]
        qd = [sb(f"qd{i}", [128, 4, 128], BF16) for i in range(4)]
        kd = [sb(f"kd{i}", [128, 4, 128], BF16) for i in range(4)]
        scm = [sb(f"scm{i}", [128, 2, 512], BF16) for i in range(2)]
        osb = [sb(f"osb{i}", [128, 2, 2, 128], F32) for i in range(4)]
        osq = [sb(f"osq{i}", [128, 2, 2, 128], BF16) for i in range(2)]
        rstd = [sb(f"rstd{i}", [128, 512], F32) for i in range(2)]
        rs1 = rstd
        Zp = ps("Z", [128, 512], F32)
        Wp = ps("W", [128, 512], F32)
        Bp = ps("B", [128, 512], F32)
        SC = [ps(f"SC{i}", [128, 512], F32) for i in range(2)]
        KV = ps("KV", [128, 512], F32)
        Op = [ps(f"O{i}", [128, 512], F32) for i in range(2)]

        P.dma("sp", lambda e: e.dma_start(out=tri[:], in_=T.tri), "c1", writes=["tri"])
        P.dma("sp", lambda e: e.dma_start(out=ind[:], in_=T.ind), "c1", writes=["ind"])
        P.dma("sp", lambda e: e.dma_start(out=smask[:], in_=T.smask), "c1", writes=["smask"])
        P.dma("sp", lambda e: e.dma_start(out=gng[:], in_=T.gng), "c1", writes=["gng"])
        for d in range(2):
            P.dma("sp", lambda e, d=d: e.dma_start(out=w2d[d][0:16, :], in_=T.w2[16 * d:16 * d + 16, :]), "c1", writes=[("w2d", d)])
            P.dma("sp", lambda e, d=d: e.dma_start(out=w2d[d][16:17, :], in_=T.gbias[0:1, 256 * d:256 * d + 256]), "c1", writes=[("w2d", d)])
            for i in range(2):
                P.op("dve", lambda e, d=d, i=i: e.memset(lrd[d][i][:], 1.0), writes=[("lr", d, i)])
            P.op("act", lambda e, d=d: e.activation(out=w2h[d][:], in_=w2d[d][:], func=AF.Copy), reads=[("w2d", d)], writes=[("w2h", d)])
            P.op("dve", lambda e, d=d: e.tensor_tensor(out=w2l[d][:], in0=w2d[d][:], in1=w2h[d][:], op=ALU.subtract),
                 reads=[("w2d", d), ("w2h", d)], writes=[("w2l", d)])
        P.op("dve", lambda e: e.memset(onesm[:], 1.0 / 128.0), writes=["onesm"])
        P.op("act", lambda e: e.activation(out=tri_bf[:], in_=tri[:], func=AF.Copy), reads=["tri"], writes=["tri_bf"])
        P.op("act", lambda e: e.activation(out=ind_bf[:], in_=ind[:], func=AF.Copy), reads=["ind"], writes=["ind_bf"])
        P.op("dve", lambda e: e.memset(ones32[:], 1.0), writes=["ones32"])
        P.op("dve", lambda e: e.memset(rm[0][:], 0.0), writes=["rm"])
        P.op("dve", lambda e: e.memset(rm[1][:], 0.0), writes=["rm"])
        P.op("dve", lambda e: e.memset(rm[0][0:64, :], 1.0), writes=["rm"])
        P.op("dve", lambda e: e.memset(rm[1][64:128, :], 1.0), writes=["rm"])
        for cc_ in range(2):
            for i_ in range(2):
                P.op("pool", lambda e, cc_=cc_, i_=i_: e.memset(kpc[cc_][i_][:], 0.0), writes=[("kp", cc_, i_)])
        P.op("pool", lambda e: e.memset(RF[:], 0.0), writes=["RFall"])
        P.op("pool", lambda e: e.memset(RB[:], 0.0), writes=["RBall"])
        for d in range(2):
            for hf in range(2):
                P.op("dve", lambda e, d=d, hf=hf: e.tensor_copy(out=smask4[:, (2 * d + hf) * 128:(2 * d + hf) * 128 + 128],
                                                                 in_=smask[:, 128 * d:128 * d + 128]),
                     reads=["smask"], writes=["smask4"])
        for h in range(4):
            P.op("dve", lambda e, h=h: e.tensor_scalar(out=gexp[:, h % 2, h // 2, :], in0=ones32[:], scalar1=gng[:, h:h + 1], scalar2=None,
                                                       op0=ALU.mult), reads=["ones32", "gng"], writes=["gexp"])
        pf_ops = []
        if prefetch is not None:
            n0_ = len(P.ops)
            prefetch()
            pf_ops = P.ops[n0_:]
            del P.ops[n0_:]

        cn = dict(k=0, v=0, q=0, s=0, it=0)
        lr_slot = {}

        def kv_mm(kslot, vslot, order, bank=None, bkey="KV"):
            KVb = KV if bank is None else bank
            for ci, cc in enumerate(order):
                for hf in range(2):
                    for q_ in range(2):
                        col = (2 * ci + hf) * 128
                        P.op("pe", lambda e, cc=cc, hf=hf, q_=q_, col=col: e.matmul(
                            KVb[64 * q_:64 * q_ + 64, col:col + 128], lhsT=kpc[cc][kslot][:, 128 * hf + 64 * q_:128 * hf + 64 * q_ + 64],
                            rhs=vtm[vslot][:, (2 * hf + q_) * 128:(2 * hf + q_) * 128 + 128], start=True, stop=True),
                            reads=[("kp", cc, kslot), ("vtm", vslot)], writes=[bkey])

        def kp_ops(it2, kslot, wslot):
            for cc in range(2):
                rows = slice(64 * cc, 64 * cc + 64)
                P.op("pool", lambda e, cc=cc, rows=rows: e.tensor_tensor(
                    out=kpc[cc][it2][rows, :], in0=ktm[kslot][rows, :], in1=ewd[wslot][rows, 0:256], op=ALU.mult),
                    reads=[("ktm", kslot), ("ewd", wslot)], writes=[("kp", cc, it2)])

        def lr_split(d, ls):
            P.op("act", lambda e: e.activation(out=lrh[d][ls][:], in_=lrd[d][ls][:], func=AF.Copy),
                 reads=[("lr", d, ls)], writes=[("lrh", d, ls)])
            P.op("dve", lambda e: e.tensor_tensor(out=lrl[d][ls][:], in0=lrd[d][ls][:], in1=lrh[d][ls][:], op=ALU.subtract),
                 reads=[("lr", d, ls), ("lrh", d, ls)], writes=[("lrl", d, ls)])

        def z_mm(d, ls, tl, bank=None, bkey="Z"):
            Zb = Zp if bank is None else bank
            ops = ((lrh, w2h), (lrh, w2l), (lrl, w2h))
            for i_, (a_, b_) in enumerate(ops):
                P.op("pe", lambda e, a_=a_, b_=b_, i_=i_: e.matmul(
                    Zb[:, 256 * d:256 * d + 256], lhsT=a_[d][ls][0:17, tl], rhs=b_[d][0:17, :], start=(i_ == 0), stop=(i_ == 2)),
                    reads=[("lrh", d, ls), ("lrl", d, ls), ("w2h", d), ("w2l", d)], writes=[bkey])

        def softplus(it2, cs, bank=None, bkey="Z"):
            Zb = Zp if bank is None else bank
            P.op("act", lambda e: e.activation(out=sp_sb[it2][:, cs], in_=Zb[:, cs], func=AF.Exp, scale=-1.0), writes=[bkey, ("sp", it2)])
            P.op("act", lambda e: e.activation(out=sph[it2][:, cs], in_=sp_sb[it2][:, cs], func=AF.Ln, bias=1.0),
                 reads=[("sp", it2)], writes=[("sph", it2)])
            P.op("act", lambda e: e.activation(out=sp_sb[it2][:, cs], in_=sp_sb[it2][:, cs], func=AF.Ln, bias=1.0), writes=[("sp", it2)])

        def sp_lo(it2, cs):
            P.op("dve", lambda e: e.tensor_tensor(out=spl[it2][:, cs], in0=sp_sb[it2][:, cs], in1=sph[it2][:, cs], op=ALU.subtract),
                 reads=[("sp", it2), ("sph", it2)], writes=[("spl", it2)])

        def w_tot_mm(it2, d, bank=None, bkey="W"):
            Wb = Wp if bank is None else bank
            c0 = 256 * d
            wsl = slice(256 + 128 * d, 256 + 128 * d + 128)
            for i_, src in enumerate((sph, spl)):
                P.op("pe", lambda e, src=src, i_=i_: e.matmul(Wb[:, 0:256], lhsT=tri_bf[:, wsl], rhs=src[it2][:, c0:c0 + 256],
                                                              start=(i_ == 0), stop=(i_ == 1)),
                     reads=["tri_bf", ("sph", it2), ("spl", it2)], writes=[bkey])
            for hf in range(2):
                for i_, src in enumerate((sph, spl)):
                    P.op("pe", lambda e, src=src, i_=i_, hf=hf: e.matmul(
                        Wb[:, 256 + 2 * hf:258 + 2 * hf], lhsT=src[it2][:, c0 + 128 * hf:c0 + 128 * hf + 128], rhs=ind_bf[:],
                        start=(i_ == 0), stop=(i_ == 1)), reads=["ind_bf", ("sph", it2), ("spl", it2)], writes=[bkey])

        items = []
        for st in range(63, -1, -1):
            it = cn["it"]
            cn["it"] += 1
            it2, it4 = it % 2, it % 4
            blk = st // 4
            kslot = cn["k"] % DK
            cn["k"] += 1
            vslot = cn["v"] % DV
            cn["v"] += 1
            if st % 4 == 3:
                lr_slot[blk] = (15 - blk) % 2
            ls = lr_slot[blk]
            tl = slice(128 * (st % 4), 128 * (st % 4) + 128)

            def f0(st=st, blk=blk, ls=ls):
                if pf_ops:
                    P.ops.append(pf_ops.pop(0))
                if st % 4 == 3:
                    P.dma("sp", lambda e: e.dma_start(out=lrd[1][ls][0:16, :], in_=T.lrT[16:32, 512 * blk:512 * blk + 512]),
                          f"lr1{ls}", writes=[("lr", 1, ls)])

            def f0b(st=st, ls=ls):
                if st % 4 == 3:
                    lr_split(1, ls)

            zb, zk = (Zp, "Z") if it2 == 0 else (SC[0], ("SC", 0))
            wb_, wk = (Wp, "W") if it2 == 0 else (SC[1], ("SC", 1))
            kb_, kk = (KV, "KV") if it2 == 0 else (Bp, "B")

            def f1(st=st, ls=ls, tl=tl, kslot=kslot, vslot=vslot, zb=zb, zk=zk):
                z_mm(1, ls, tl, zb, zk)
                P.dma("sp", lambda e: e.dma_start(out=ktm[kslot][:], in_=T.kTM[128 * st:128 * st + 128, :]),
                      f"ktm{kslot}", writes=[("ktm", kslot)])
                P.dma("sp", lambda e: e.dma_start(out=vtm[vslot][:], in_=T.vTM[128 * st:128 * st + 128, :]),
                      f"vtm{vslot}", writes=[("vtm", vslot)])

            def f2(it2=it2, zb=zb, zk=zk):
                softplus(it2, slice(256, 512), zb, zk)

            def f2b(it2=it2):
                sp_lo(it2, slice(256, 512))

            def f3(it2=it2, wb_=wb_, wk=wk):
                w_tot_mm(it2, 1, wb_, wk)

            def f4(it4=it4, wb_=wb_, wk=wk):
                P.op("act", lambda e: e.activation(out=ewd[it4][:], in_=wb_[:, 0:260], func=AF.Exp), writes=[wk, ("ewd", it4)])

            def f5(it2=it2, it4=it4, kslot=kslot):
                kp_ops(it2, kslot, it4)

            def f6(it2=it2, vslot=vslot, kb_=kb_, kk=kk):
                kv_mm(it2, vslot, (1, 0), kb_, kk)

            def f7(st=st, it4=it4, kb_=kb_, kk=kk):
                for ci, cc in enumerate((1, 0)):
                    c = 2 * st + cc
                    s0, s1 = c % NR, (c - 1) % NR
                    if c < 64:
                        P.op("act", lambda e, c=c, s0=s0: e.activation(out=eb_all[:, :, c, :], in_=RB[:, :, s0, :], func=AF.Copy),
                             reads=[("RB", 0, s0), ("RB", 1, s0), "RBall"], writes=[("eb_all", c)])
                    for hf in range(2):
                        col = (2 * ci + hf) * 128
                        P.op("dve", lambda e, hf=hf, s0=s0, s1=s1, col=col, cc=cc: e.scalar_tensor_tensor(
                            out=RB[:, hf, s1, :], in0=RB[:, hf, s0, :], scalar=ewd[it4][:, 256 + 2 * hf + cc:257 + 2 * hf + cc],
                            in1=kb_[:, col:col + 128], op0=ALU.mult, op1=ALU.add),
                            reads=[("RB", hf, s0), ("ewd", it4), "RBall"], writes=[("RB", hf, s1), kk])
            items.append([f0, f0b, f1, f2, f2b, f3, f4, f5, f6, f7])
        run_pipeline(items, 10, 1)
        while pf_ops:
            P.ops.append(pf_ops.pop(0))

        items = []
        NPH = 15
        for st in range(32):
            it = cn["it"]
            cn["it"] += 1
            it2, it4 = it % 2, it % 4
            blk = st // 4
            bs3 = blk % 2
            kslot = cn["k"] % DK
            cn["k"] += 1
            vslot = cn["v"] % DV
            cn["v"] += 1
            qslot = cn["q"] % DQ
            cn["q"] += 1
            sslot = cn["s"] % DS
            cn["s"] += 1
            ls = blk % 2
            tl = slice(128 * (st % 4), 128 * (st % 4) + 128)
            tg = slice(128 * st, 128 * st + 128)

            def f0(st=st, blk=blk, ls=ls):
                if st % 4 == 0:
                    for d in range(2):
                        P.dma("sp", lambda e, d=d: e.dma_start(out=lrd[d][ls][0:16, :], in_=T.lrT[16 * d:16 * d + 16, 512 * blk:512 * blk + 512]),
                              f"lr{d}{ls}", writes=[("lr", d, ls)])

            def f0b(st=st, ls=ls):
                if st % 4 == 0:
                    for d in range(2):
                        lr_split(d, ls)

            def f1(ls=ls, tl=tl):
                for d in range(2):
                    z_mm(d, ls, tl)

            def f2(it2=it2, st=st, kslot=kslot, qslot=qslot, tg=tg):
                softplus(it2, slice(0, 512))
                P.dma("sp", lambda e: e.dma_start(out=ktm[kslot][:], in_=T.kTM[128 * st:128 * st + 128, :]),
                      f"ktm{kslot}", writes=[("ktm", kslot)])
                P.dma("sp", lambda e: e.dma_start(out=qg[qslot][:], in_=T.qgT.rearrange("(a p) t -> p a t", p=128)[:, :, tg]),
                      f"qg{qslot}", writes=[("qg", qslot)])
                P.dma("sp", lambda e: e.dma_start(out=kg[qslot][:], in_=T.kgT.rearrange("(a p) t -> p a t", p=128)[:, :, tg]),
                      f"kg{qslot}", writes=[("kg", qslot)])

            def f2b(it2=it2):
                sp_lo(it2, slice(0, 512))

            def f3(it2=it2, st=st, vslot=vslot):
                w_tot_mm(it2, 0)
                for d in range(2):
                    for hf in range(2):
                        cb = (2 * d + hf) * 128
                        for i_, src in enumerate((sph, spl)):
                            P.op("pe", lambda e, d=d, hf=hf, cb=cb, src=src, i_=i_: e.matmul(
                                Bp[:, cb:cb + 128], lhsT=src[it2][:, 256 * d + 128 * hf:256 * d + 128 * hf + 128], rhs=tri_bf[:, 128 * d:128 * d + 128],
                                start=(i_ == 0), stop=(i_ == 1)), reads=["tri_bf", ("sph", it2), ("spl", it2)], writes=["B"])
                P.dma("sp", lambda e: e.dma_start(out=vtm[vslot][:], in_=T.vTM[128 * st:128 * st + 128, :]),
                      f"vtm{vslot}", writes=[("vtm", vslot)])

            def f4(it2=it2, it4=it4):
                P.op("act", lambda e: e.activation(out=ewd[it4][:], in_=Wp[:, 0:260], func=AF.Exp), writes=["W", ("ewd", it4)])
                P.op("act", lambda e: e.activation(out=ebp[it2][:], in_=Bp[:], func=AF.Exp), writes=["B", ("ebp", it2)])
                P.op("act", lambda e: e.activation(out=ebn[it2][:], in_=Bp[:], func=AF.Exp, scale=-1.0), writes=["B", ("ebn", it2)])

            def f5(it2=it2, it4=it4, kslot=kslot, qslot=qslot):
                kp_ops(it2, kslot, it4)
                for d in range(2):
                    P.op("dve", lambda e, d=d: e.scalar_tensor_tensor(
                        out=qd[it4][:, 2 * d:2 * d + 2, :], in0=qg[qslot][:], scalar=0.125,
                        in1=ebp[it2][:, 256 * d:256 * d + 256].rearrange("p (a t) -> p a t", a=2), op0=ALU.mult, op1=ALU.mult),
                        reads=[("qg", qslot), ("ebp", it2)], writes=[("qd", it4)])
                    P.op("dve", lambda e, d=d: e.tensor_tensor(
                        out=kd[it4][:, 2 * d:2 * d + 2, :], in0=kg[qslot][:],
                        in1=ebn[it2][:, 256 * d:256 * d + 256].rearrange("p (a t) -> p a t", a=2), op=ALU.mult),
                        reads=[("kg", qslot), ("ebn", it2)], writes=[("kd", it4)])

            def f6(it2=it2, it4=it4, vslot=vslot):
                kv_mm(it2, vslot, (0, 1))
                for d in range(2):
                    for h in range(4):
                        hf, hb = h // 2, 64 * (h % 2)
                        cb = (2 * d + hf) * 128
                        P.op("pe", lambda e, d=d, h=h, hf=hf, hb=hb, cb=cb: e.matmul(
                            SC[h % 2][:, cb:cb + 128], lhsT=kd[it4][hb:hb + 64, 2 * d + hf, :], rhs=qd[it4][hb:hb + 64, 2 * d + hf, :],
                            start=True, stop=True), reads=[("kd", it4), ("qd", it4)], writes=[("SC", h % 2)])

            def f7(st=st, it2=it2, it4=it4, sslot=sslot, tg=tg):
                for ci, cc in enumerate((0, 1)):
                    c = 2 * st + cc
                    s0, s1 = c % NR, (c + 1) % NR
                    P.op("pool", lambda e, s0=s0: e.tensor_copy(out=efb[:, :, s0, :], in_=RF[:, :, s0, :]),
                         reads=[("RF", 0, s0), ("RF", 1, s0), "RFall"], writes=[("efb", s0)])
                    for hf in range(2):
                        col = (2 * ci + hf) * 128
                        P.op("dve", lambda e, hf=hf, s0=s0, s1=s1, col=col, cc=cc: e.scalar_tensor_tensor(
                            out=RF[:, hf, s1, :], in0=RF[:, hf, s0, :], scalar=ewd[it4][:, 256 + 2 * hf + cc:257 + 2 * hf + cc],
                            in1=KV[:, col:col + 128], op0=ALU.mult, op1=ALU.add),
                            reads=[("RF", hf, s0), ("ewd", it4), "RFall"], writes=[("RF", hf, s1), "KV"])
                for par in range(2):
                    P.op("dve", lambda e, par=par: e.tensor_tensor(out=scm[it2][:, par, :], in0=SC[par][:], in1=smask4[:], op=ALU.mult),
                         reads=["smask4"], writes=[("SC", par), ("scm", it2, par)])
                P.dma("sp", lambda e: e.dma_start(out=sgb[sslot][:], in_=T.sgbT.rearrange("(a p) t -> p a t", p=128)[:, :, tg]),
                      f"sgb{sslot}", writes=[("sgb", sslot)])

            def f8(st=st, it2=it2, it4=it4, vslot=vslot):
                for h in range(4):
                    hf, par = h // 2, h % 2
                    hb = 64 * par
                    po = Op[par]
                    ob = 128 * hf
                    P.op("pe", lambda e, po=po, h=h, hf=hf, par=par, ob=ob: e.matmul(
                        po[:, ob:ob + 128], lhsT=vtm[vslot][:, 128 * h:128 * h + 128], rhs=scm[it2][:, par, 128 * hf:128 * hf + 128],
                        start=True, stop=False), reads=[("vtm", vslot), ("scm", it2, par)], writes=[("O", par)])
                    P.op("pe", lambda e, po=po, h=h, hf=hf, par=par, ob=ob: e.matmul(
                        po[:, ob:ob + 128], lhsT=vtm[vslot][:, 128 * h:128 * h + 128], rhs=scm[it2][:, par, 256 + 128 * hf:256 + 128 * hf + 128],
                        start=False, stop=False), reads=[("vtm", vslot), ("scm", it2, par)], writes=[("O", par)])
                    for cc in range(2):
                        c = 2 * st + cc
                        P.op("pe", lambda e, po=po, hf=hf, hb=hb, c=c, cc=cc, ob=ob: e.matmul(
                            po[:, ob + 64 * cc:ob + 64 * cc + 64], lhsT=efb[hb:hb + 64, hf, c % NR, :], rhs=qd[it4][hb:hb + 64, hf, 64 * cc:64 * cc + 64],
                            start=False, stop=False), reads=[("efb", c % NR), ("qd", it4)], writes=[("O", par)])
                        P.op("pe", lambda e, po=po, hf=hf, hb=hb, c=c, cc=cc, ob=ob: e.matmul(
                            po[:, ob + 64 * cc:ob + 64 * cc + 64], lhsT=eb_all[hb:hb + 64, hf, c, :], rhs=qd[it4][hb:hb + 64, 2 + hf, 64 * cc:64 * cc + 64],
                            start=False, stop=(cc == 1)), reads=[("eb_all", c), ("qd", it4)], writes=[("O", par)])

            def f9(it2=it2, it4=it4):
                for par in range(2):
                    P.op("act", lambda e, par=par: e.activation(out=osb[it4][:, par, :, :], in_=Op[par][:, 0:256].rearrange("p (a t) -> p a t", a=2),
                                                                func=AF.Copy), writes=[("O", par), ("osb", it4, par)])
                    P.op("act", lambda e, par=par: e.activation(out=osq[it2][:, par, :, :], in_=Op[par][:, 0:256].rearrange("p (a t) -> p a t", a=2),
                                                                func=AF.Square), writes=[("O", par), ("osq", it2, par)])

            def f10(it2=it2):
                for par in range(2):
                    P.op("pe", lambda e, par=par: e.matmul(Op[par][:, 256:512], lhsT=onesm[:], rhs=osq[it2][:, par, :, :],
                                                           start=True, stop=True), reads=["onesm", ("osq", it2, par)], writes=[("O", par)])

            def f11(it2=it2):
                for par in range(2):
                    P.op("act", lambda e, par=par: e.activation(out=rs1[it2][:, 256 * par:256 * par + 256], in_=Op[par][:, 256:512], func=AF.Ln, bias=RMS_EPS),
                         writes=[("O", par), ("rstd", it2)])
                P.op("act", lambda e: e.activation(out=rstd[it2][:], in_=rs1[it2][:], func=AF.Exp, scale=-0.5),
                     writes=[("rstd", it2)])

            def f12(st=st, it2=it2, it4=it4, sslot=sslot, bs3=bs3, tl=tl, blk=blk):
                P.op("dve", lambda e: e.tensor_tensor(out=osb[it4][:], in0=osb[it4][:], in1=rstd[it2][:].rearrange("p (a b t) -> p a b t", a=2, b=2), op=ALU.mult),
                     reads=[("rstd", it2)], writes=[("osb", it4, 0), ("osb", it4, 1)])
                P.op("dve", lambda e: e.tensor_tensor(out=osb[it4][:], in0=osb[it4][:], in1=gexp[:], op=ALU.mult),
                     reads=["gexp"], writes=[("osb", it4, 0), ("osb", it4, 1)])
                for par in range(2):
                    P.op("dve", lambda e, par=par: e.tensor_tensor(
                        out=yst[bs3][:, :, tl].rearrange("p (j q) t -> p q j t", q=2)[:, par, :, :], in0=osb[it4][:, par, :, :],
                        in1=sgb[sslot][:].rearrange("p (j q) t -> p q j t", q=2)[:, par, :, :], op=ALU.mult),
                        reads=[("osb", it4, par), ("sgb", sslot)], writes=[("yst", bs3)])
                if st % 4 == 3:
                    tk = slice(512 * blk, 512 * blk + 512)
                    P.dma("pool", lambda e: e.dma_start(out=T.ygT.rearrange("(a p) t -> p a t", p=128)[:, :, tk], in_=yst[bs3][:]),
                          f"yst{bs3}", reads=[("yst", bs3)])
            items.append([f0, f0b, f1, f2, f2b, f3, f4, f5, f6, f7, f8, f9, f10, f11, f12])
        run_pipeline(items, NPH, 1)
        P.flush()


def alloc_S3_weights(nc, es, T):
    W = Ctx()
    sb = lambda name, shape, dt: es.enter_context(nc.sbuf_tensor("f_" + name, shape, dt))
    W.wgt = sb("wgt", [128, 8, 2048], BF16)
    T.s3w = W


def load_S3_weights(P, T, which):
    W = T.s3w
    w_v = T.w_in.rearrange("(k p) c -> p k c", p=128)
    wa_v = T.w_att_out.rearrange("(k p) c -> p k c", p=128)
    wg_v = T.w_gla_out.rearrange("(k p) c -> p k c", p=128)
    wo_v = T.w_out.rearrange("(k p) c -> p k c", p=128)
    if "wgt" in which:
        for k in range(8):
            for half in range(2):
                P.dma("pool", lambda e, k=k, half=half: e.dma_start(
                    out=W.wgt[:, k, 1024 * half:1024 * half + 1024], in_=w_v[:, k, C_GATT + 1024 * half:C_GATT + 1024 * half + 1024]),
                    "wgt", writes=["wgt"])
    if "cast" in which:
        for src, dst, n in ((T.w_att_out, T.wao_b, 6), (T.w_gla_out, T.wgo_b, 4), (T.w_out, T.wo_b, 8)):
            for k in range(n):
                P.dma("pool", lambda e, src=src, dst=dst, k=k: e.dma_start(out=dst[128 * k:128 * k + 128, :], in_=src[128 * k:128 * k + 128, :]),
                      "wcast", writes=["wcast"])
    if "rest_b" in which:
        wa_b = T.wao_b.rearrange("(k p) c -> p k c", p=128)
        wg_b = T.wgo_b.rearrange("(k p) c -> p k c", p=128)
        wo_b = T.wo_b.rearrange("(k p) c -> p k c", p=128)
        for k in range(6):
            P.dma("sp" if k % 2 == 0 else "act", lambda e, k=k: e.dma_start(out=W.wao[:, k, :], in_=wa_b[:, k, :]), "wao", writes=["wao"])
        for k in range(4):
            P.dma("sp" if k % 2 == 0 else "act", lambda e, k=k: e.dma_start(out=W.wgo[:, k, :], in_=wg_b[:, k, :]), "wgo", writes=["wgo"])
        for k in range(8):
            P.dma("sp" if k % 2 == 0 else "act", lambda e, k=k: e.dma_start(out=W.wo[:, k, :], in_=wo_b[:, k, :]), "wo", writes=["wo"])
    if "rest" not in which:
        return
    for k in range(6):
        P.dma("pool", lambda e, k=k: e.dma_start(out=W.wao[:, k, :], in_=wa_v[:, k, :]), "wao", writes=["wao"])
    for k in range(4):
        P.dma("pool", lambda e, k=k: e.dma_start(out=W.wgo[:, k, :], in_=wg_v[:, k, :]), "wgo", writes=["wgo"])
    for k in range(8):
        P.dma("pool", lambda e, k=k: e.dma_start(out=W.wo[:, k, :], in_=wo_v[:, k, :]), "wo", writes=["wo"])


def stage_S3(nc, P, T, weights_loaded):
    x_v = T.xT.rearrange("(k p) t -> p k t", p=128)
    with ExitStack() as es:
        sb = lambda name, shape, dt: es.enter_context(nc.sbuf_tensor("f_" + name, shape, dt))
        ps = lambda name, shape, dt: es.enter_context(nc.psum_tensor("f_" + name, shape, dt))
        wgt = T.s3w.wgt
        T.s3w.wao = wao = sb("wao", [128, 6, 1024], BF16)
        T.s3w.wgo = wgo = sb("wgo", [128, 4, 1024], BF16)
        T.s3w.wo = wo = sb("wo", [128, 8, 1024], BF16)
        lng = sb("lng", [128, 1024], F32)
        lnb = sb("lnb", [128, 1024], F32)
        xt = [sb(f"xt{i}", [128, 8, 512], BF16) for i in range(2)]
        ygs = [sb(f"ygs{i}", [128, 4, 512], BF16) for i in range(2)]
        num = sb("num", [128, 6, 512], F32)
        sga = sb("sga", [128, 6, 512], BF16)
        dn = sb("dn", [128, 2, 3, 512], F32)
        xtm = [sb(f"xtm{i}", [128, 1024], F32) for i in range(4)]
        ya = [sb(f"ya{i}", [128, 6, 512], BF16) for i in range(2)]
        NS = 2
        sA = [sb(f"sA{i}", [128, 512], F32) for i in range(NS)]
        sG = [sb(f"sG{i}", [128, 512], F32) for i in range(NS)]
        m1 = [sb(f"m1{i}", [128, 512], F32) for i in range(2)]
        m2 = [sb(f"m2{i}", [128, 512], F32) for i in range(2)]
        mg = [sb(f"mg{i}", [128, 8, 512], BF16) for i in range(2)]
        NH = 4
        hs = [sb(f"hs{i}", [128, 1024], F32) for i in range(NH)]
        stt = [sb(f"stt{i}", [128, 12], F32) for i in range(NH)]
        mv = [sb(f"mv{i}", [128, 2], F32) for i in range(NH)]
        sm = [sb(f"sm{i}", [128, 4], F32) for i in range(NH)]
        cneg = sb("cneg", [128, 1], F32)
        pGA = ps("pGA", [128, 512], F32)
        pGG = ps("pGG", [128, 512], F32)
        pYA = ps("pYA", [128, 512], F32)
        pYG = ps("pYG", [128, 512], F32)
        pH = [ps(f"pH{i}", [128, 512], F32) for i in range(4)]

        def load_a(tt):
            s = tt % 2
            tk = slice(512 * tt, 512 * tt + 512)
            for kh in range(2):
                P.dma("pool", lambda e, s=s, kh=kh, tk=tk: e.dma_start(
                    out=xt[s][:, 4 * kh:4 * kh + 4, :], in_=x_v[:, 4 * kh:4 * kh + 4, tk]), f"xt{s}", writes=[("xt", s)])
            P.dma("sp", lambda e, s=s, tk=tk: e.dma_start(
                out=ygs[s][:], in_=T.ygT.rearrange("(a p) t -> p a t", p=128)[:, :, tk]), f"ygs{s}", writes=[("ygs", s)])

        def load_b(tt):
            tk = slice(512 * tt, 512 * tt + 512)
            P.dma("sp", lambda e, tk=tk: e.dma_start(
                out=num[:], in_=T.numT.rearrange("(a p) t -> p a t", p=128)[:, :, tk]), "num", writes=["num"])
            P.dma("sp", lambda e, tk=tk: e.dma_start(
                out=sga[:], in_=T.sgaT.rearrange("(a p) t -> p a t", p=128)[:, :, tk]), "sga", writes=["sga"])
            for jp in range(2):
                for q_ in range(2):
                    src = bass.AP(tensor=T.den.tensor, offset=(2 * jp + q_) * OWN + 512 * tt,
                                  ap=[[0, 64], [4 * OWN, 3], [1, 512]])
                    P.dma("sp", lambda e, jp=jp, q_=q_, src=src: e.dma_start(
                        out=dn[64 * q_:64 * q_ + 64, jp, :, :], in_=src), "dn", writes=["dn"])

        def ya_parts(tt):
            yb = tt % 2
            parts = []
            for jp in range(2):
                def g(jp=jp):
                    P.op("dve", lambda e: e.tensor_tensor(out=dn[:, jp, 0, :], in0=dn[:, jp, 0, :], in1=dn[:, jp, 1, :], op=ALU.add),
                         writes=["dn"])
                    P.op("dve", lambda e: e.tensor_tensor(out=dn[:, jp, 0, :], in0=dn[:, jp, 0, :], in1=dn[:, jp, 2, :], op=ALU.add),
                         writes=["dn"])
                    P.op("dve", lambda e: e.reciprocal(out=dn[:, jp, 0, :], in_=dn[:, jp, 0, :]), writes=["dn"])
                parts.append(g)
            for ft in range(6):
                def g(ft=ft, jp=ft % 2):
                    P.op("dve", lambda e: e.tensor_tensor(out=num[:, ft, :], in0=num[:, ft, :], in1=dn[:, jp, 0, :], op=ALU.mult),
                         reads=["dn"], writes=["num"])
                    P.op("dve", lambda e: e.tensor_tensor(out=ya[yb][:, ft, :], in0=num[:, ft, :], in1=sga[:, ft, :], op=ALU.mult),
                         reads=["num", "sga"], writes=[("ya", yb, ft)])
                parts.append(g)
            return parts

        def compute_ya(tt):
            for g in ya_parts(tt):
                g()

        load_a(0)
        load_b(0)
        P.op("pool", lambda e: e.memset(cneg[:], -0.5), writes=["cneg"])
        load_S3_weights(P, T, ("rest_b",) if weights_loaded else ("wgt", "rest"))
        P.dma("sp", lambda e: e.dma_start(out=lng[:], in_=T.ln_g.partition_broadcast(128)), "c1", writes=["lng"])
        P.dma("sp", lambda e: e.dma_start(out=lnb[:], in_=T.ln_b.partition_broadcast(128)), "c1", writes=["lnb"])
        wkeys = [] if weights_loaded else None

        NPH = 10
        items = []
        cn = dict(d=0, h=0)

        def rd(*keys):
            return [k for k in keys if not (weights_loaded and k == "wgt")]

        def add_dt(tt, dt_, extra=None):
            s = tt % 2
            yb = tt % 2
            mb = tt % 2
            di = cn["d"]
            cn["d"] += 1
            b3 = di % NS
            b2 = di % 2
            dsl = slice(128 * dt_, 128 * dt_ + 128)

            def fA():
                for k in range(8):
                    P.op("pe", lambda e, k=k: e.matmul(pGA[:], lhsT=wgt[:, k, dsl], rhs=xt[s][:, k, :], start=(k == 0), stop=(k == 7)),
                         reads=rd("wgt", ("xt", s)), writes=["pGA"])
                for k in range(8):
                    P.op("pe", lambda e, k=k: e.matmul(pGG[:], lhsT=wgt[:, k, 1024 + 128 * dt_:1024 + 128 * dt_ + 128], rhs=xt[s][:, k, :], start=(k == 0), stop=(k == 7)),
                         reads=rd("wgt", ("xt", s)), writes=["pGG"])
                for ft in range(6):
                    P.op("pe", lambda e, ft=ft: e.matmul(pYA[:], lhsT=wao[:, ft, dsl], rhs=ya[yb][:, ft, :], start=(ft == 0), stop=(ft == 5)),
                         reads=rd("wao", ("ya", yb, ft)), writes=["pYA"])
                for gt in range(4):
                    P.op("pe", lambda e, gt=gt: e.matmul(pYG[:], lhsT=wgo[:, gt, dsl], rhs=ygs[s][:, gt, :], start=(gt == 0), stop=(gt == 3)),
                         reads=rd("wgo", ("ygs", s)), writes=["pYG"])

            def fB():
                P.op("act", lambda e: e.activation(out=sA[b3][:], in_=pGA[:], func=AF.Sigmoid), writes=["pGA", ("sA", b3)])
                P.op("act", lambda e: e.activation(out=sG[b3][:], in_=pGG[:], func=AF.Sigmoid), writes=["pGG", ("sG", b3)])

            def fC():
                P.op("dve", lambda e: e.tensor_tensor(out=m1[b2][:], in0=sA[b3][:], in1=pYA[:], op=ALU.mult),
                     reads=[("sA", b3)], writes=["pYA", ("m1", b2)])
                P.op("dve", lambda e: e.tensor_tensor(out=m2[b2][:], in0=sG[b3][:], in1=pYG[:], op=ALU.mult),
                     reads=[("sG", b3)], writes=["pYG", ("m2", b2)])

            def fD():
                P.op("pool", lambda e: e.tensor_tensor(out=mg[mb][:, dt_, :], in0=m1[b2][:], in1=m2[b2][:], op=ALU.add),
                     reads=[("m1", b2), ("m2", b2)], writes=[("mg", mb, dt_)])
            items.append([fA, lambda: (fB(), fC(), extra() if extra is not None else None), fD] + [None] * (NPH - 3))

        def add_sub(tt, sub, early=None):
            mb = tt % 2
            hi = cn["h"]
            cn["h"] += 1
            x2 = hi % 4
            h4 = hi % NH
            row0 = 512 * tt + 128 * sub
            pis = [(2 * hi) % 4, (2 * hi + 1) % 4]

            def f0():
                P.dma("act", lambda e: e.dma_start(out=xtm[x2][:], in_=T.xtm[row0:row0 + 128, :]),
                      f"xtm{x2}", writes=[("xtm", x2)])

            def fA():
                for half in range(2):
                    ph = pH[pis[half]]
                    for dt_ in range(8):
                        P.op("pe", lambda e, ph=ph, dt_=dt_, half=half: e.matmul(
                            ph[:], lhsT=mg[mb][:, dt_, 128 * sub:128 * sub + 128], rhs=wo[:, dt_, 512 * half:512 * half + 512],
                            start=(dt_ == 0), stop=(dt_ == 7)), reads=rd(("mg", mb, dt_), "wo"), writes=[("pH", pis[half])])

            def fB():
                for half in range(2):
                    ph = pH[pis[half]]
                    P.op("dve", lambda e, ph=ph, half=half: e.scalar_tensor_tensor(
                        out=hs[h4][:, 512 * half:512 * half + 512], in0=xtm[x2][:, 512 * half:512 * half + 512], scalar=ALPHA,
                        in1=ph[:], op0=ALU.mult, op1=ALU.add), reads=[("xtm", x2)], writes=[("pH", pis[half]), ("hs", h4)])
                    P.op("dve", lambda e, half=half: e.bn_stats(out=stt[h4][:, 6 * half:6 * half + 6], in_=hs[h4][:, 512 * half:512 * half + 512]),
                         reads=[("hs", h4)], writes=[("stt", h4)])
                P.op("dve", lambda e: e.bn_aggr(out=mv[h4][:], in_=stt[h4][:]), reads=[("stt", h4)], writes=[("mv", h4)])

            def fC():
                P.op("pool", lambda e: e.tensor_scalar(out=sm[h4][:, 0:1], in0=mv[h4][:, 1:2], scalar1=LN_EPS, scalar2=None, op0=ALU.add),
                     reads=[("mv", h4)], writes=[("sm0", h4)])
                P.op("pool", lambda e: e.tensor_tensor(out=sm[h4][:, 1:2], in0=sm[h4][:, 0:1], in1=cneg[:, 0:1], op=ALU.pow),
                     reads=[("sm0", h4), "cneg"], writes=[("sm1", h4)])

            def fD():
                P.op("dve", lambda e: e.tensor_scalar(out=hs[h4][:], in0=hs[h4][:], scalar1=mv[h4][:, 0:1], scalar2=sm[h4][:, 1:2],
                                                      op0=ALU.subtract, op1=ALU.mult),
                     reads=[("mv", h4), ("sm1", h4)], writes=[("hs", h4)])
                P.op("dve", lambda e: e.tensor_tensor(out=hs[h4][:], in0=hs[h4][:], in1=lng[:], op=ALU.mult),
                     reads=["lng"], writes=[("hs", h4)])

            def fE():
                P.op("pool", lambda e: e.tensor_tensor(out=hs[h4][:], in0=hs[h4][:], in1=lnb[:], op=ALU.add),
                     reads=["lnb"], writes=[("hs", h4)])
                P.dma("pool", lambda e: e.dma_start(out=T.y[row0:row0 + 128, :], in_=hs[h4][:]),
                      f"yo{h4}", reads=[("hs", h4)])
            items.append([None, None, lambda: (f0(), early() if early is not None else None), None, fA, fB, fC, fD, fE, None])

        compute_ya(0)
        load_b(1)
        for tt in range(8):
            if tt + 1 < 8:
                items.append([lambda tt=tt: load_a(tt + 1)] + [None] * (NPH - 1))
            parts = ya_parts(tt + 1) if tt + 1 < 8 else [None] * 8
            for dt_ in range(8):
                add_dt(tt, dt_, extra=parts[dt_])
            for sub in range(4):
                add_sub(tt, sub, early=(lambda tt=tt: load_b(tt + 2)) if (sub == 0 and tt + 2 < 8) else None)
        run_pipeline(items, NPH, 1)
        P.flush()


def build_program(debug=False, stages=(0, 1, 2, 3)):
    nc = bass.Bass("TRN2", target_bir_lowering=False)
    T = Ctx()

    def din(name, shape, dt=F32):
        return nc.dram_tensor(name, list(shape), dt, kind="ExternalInput").ap()

    def scr(name, shape, dt):
        return nc.dram_tensor(name, list(shape), dt, kind=("ExternalOutput" if debug else "Internal")).ap()

    T.xT = din("xT", [D_MODEL, SEQ])
    T.xtm = din("xtm", [OWN, D_MODEL])
    T.w_in = din("w_in", [D_MODEL, IN_W])
    T.w2 = din("w2", [32, 256])
    T.gbias = din("gbias", [1, 512])
    T.gng = din("gng", [128, 4])
    T.ebias = din("ebias", [128, 12 * 256])
    T.ebias0 = din("ebias0", [64, 12 * 128])
    T.emask = din("emask", [128, 256])
    T.emask0 = din("emask0", [64, 128])
    T.ident = din("ident", [128, 128])
    T.tri = din("tri", [128, 512])
    T.ind = din("ind", [128, 2])
    T.smask = din("smask", [128, 256])
    T.w_att_out = din("w_att_out", [768, D_MODEL])
    T.w_gla_out = din("w_gla_out", [512, D_MODEL])
    T.w_out = din("w_out", [D_MODEL, D_MODEL])
    T.ln_g = din("ln_g", [1, D_MODEL])
    T.ln_b = din("ln_b", [1, D_MODEL])
    T.y = nc.dram_tensor("y", [OWN, D_MODEL], F32, kind="ExternalOutput").ap()
    T.qaT = scr("qaT", [768, OWN], BF16)
    T.kaT = scr("kaT", [768, 5120], BF16)
    T.vaT = scr("vaT", [768, 5120], BF16)
    T.sgaT = scr("sgaT", [768, OWN], BF16)
    T.qgT = scr("qgT", [256, OWN], F32)
    T.kgT = scr("kgT", [256, OWN], F32)
    T.sgbT = scr("sgbT", [512, OWN], BF16)
    T.lrT = scr("lrT", [32, SEQ], F32)
    T.kTM = scr("kTM", [SEQ, 256], F32)
    T.vTM = scr("vTM", [SEQ, 512], BF16)
    T.numT = scr("numT", [768, OWN], F32)
    T.den = scr("den", [12, OWN], F32)
    T.ygT = scr("ygT", [512, OWN], BF16)
    T.wao_b = scr("wao_b", [768, D_MODEL], BF16)
    T.wgo_b = scr("wgo_b", [512, D_MODEL], BF16)
    T.wo_b = scr("wo_b", [D_MODEL, D_MODEL], BF16)
    P = Prog(nc)
    if 0 in stages:
        stage_P(nc, P, T)
    if 1 in stages:
        stage_S1(nc, P, T)
    with ExitStack() as es3:
        alloc_S3_weights(nc, es3, T)
        if 2 in stages:
            stage_S2(nc, P, T, prefetch=lambda: load_S3_weights(P, T, ("wgt", "cast")))
        if 3 in stages:
            stage_S3(nc, P, T, weights_loaded=(2 in stages))
    return nc


def _t5_bucket(rel):
    half = 16
    max_exact = 8
    ret = (rel > 0).astype(np.int32) * half
    n = np.abs(rel)
    large = max_exact + (np.log(np.maximum(n, 1) / max_exact) / np.log(1024 / max_exact) * (half - max_exact)).astype(np.int32)
    large = np.minimum(large, half - 1)
    return ret + np.where(n < max_exact, n, large)


def _static_tables():
    a = np.arange(128)[:, None]
    i = np.arange(128)[None, :]
    dlA = a - 64 - i
    dlB = a + 64 - i
    dl = np.concatenate([dlA, dlB], axis=1)
    valid = (np.abs(dl) <= 64)
    s = np.arange(128)[:, None]
    t = np.arange(128)[None, :]
    same = (s // 64) == (t // 64)
    c = -1.0 / 16.0
    bF = np.where(same & (s <= t), c, 0.0)
    bB = np.where(same & (s >= t), c, 0.0)
    wF = np.where(same & (s > t), c, 0.0)
    wB = np.where(same & (s < t), c, 0.0)
    tri = np.concatenate([bF, bB, wF, wB], axis=1).astype(np.float32)
    ind = np.zeros((128, 2), np.float32)
    ind[:64, 0] = c
    ind[64:, 1] = c
    mF = np.where(same & (s <= t), 1.0, 0.0)
    mB = np.where(same & (s >= t), 1.0, 0.0)
    smask = np.concatenate([mF, mB], axis=1).astype(np.float32)
    return dl, valid, tri, ind, smask


def make_in_maps(x, w_in, gla_gate_w2, gla_gate_b, gla_norm_g, rel_bias, w_att_out, w_gla_out, w_out, ln_g, ln_b):
    x = np.asarray(x, np.float32)
    w_in0 = np.asarray(w_in, np.float32)[0]
    w2 = np.asarray(gla_gate_w2, np.float32)[0]
    gb = np.asarray(gla_gate_b, np.float32)[0]
    gng = np.ascontiguousarray(np.asarray(gla_norm_g, np.float32)[0].reshape(4, 128).T)
    rb = np.asarray(rel_bias, np.float32)
    dl, valid, tri, ind, smask = _static_tables()
    emask = valid.astype(np.float32)
    common = dict(
        gng=gng, emask=emask, emask0=np.ascontiguousarray(emask[64:128, 0:128]),
        ident=np.eye(128, dtype=np.float32), tri=tri, ind=ind, smask=smask,
        w_att_out=np.ascontiguousarray(np.asarray(w_att_out, np.float32)[0]),
        w_gla_out=np.ascontiguousarray(np.asarray(w_gla_out, np.float32)[0]),
        w_out=np.ascontiguousarray(np.asarray(w_out, np.float32)[0]),
        ln_g=np.ascontiguousarray(np.asarray(ln_g, np.float32)[0].reshape(1, D_MODEL)),
        ln_b=np.ascontiguousarray(np.asarray(ln_b, np.float32)[0].reshape(1, D_MODEL)),
    )
    per_parity = []
    for par in range(2):
        sign = 1 if par == 0 else -1
        eb = np.zeros((128, 12, 256), np.float32)
        for h in range(12):
            d = DILS[h // 4]
            bk = _t5_bucket(sign * dl * d)
            eb[:, h, :] = np.where(valid, rb[bk, h], 0.0)
        w_c = w_in0
        w2_c = w2
        gb_c = gb
        if par == 1:
            w_c = w_in0.copy()
            w_c[:, C_LRF:C_LRF + 16] = w_in0[:, C_LRB:C_LRB + 16]
            w_c[:, C_LRB:C_LRB + 16] = w_in0[:, C_LRF:C_LRF + 16]
            w2_c = w2[::-1]
            gb_c = gb[::-1]
        per_parity.append(dict(
            ebias=np.ascontiguousarray(eb.reshape(128, 12 * 256)),
            ebias0=np.ascontiguousarray(eb[64:128, :, 0:128].reshape(64, 12 * 128)),
            w_in=np.ascontiguousarray(w_c),
            w2=np.ascontiguousarray(w2_c.reshape(32, 256)),
            gbias=np.ascontiguousarray(gb_c.reshape(1, 512)),
        ))
    in_maps = []
    for c in range(NCORES):
        b, par = c // 2, c % 2
        xs = x[b] if par == 0 else x[b][::-1]
        m = dict(common)
        m.update(per_parity[par])
        m["xT"] = np.ascontiguousarray(xs.T)
        m["xtm"] = np.ascontiguousarray(xs[:OWN])
        in_maps.append(m)
    return in_maps


_NC_CACHE = {}


def kernel(x, w_in, gla_gate_w2, gla_gate_b, gla_norm_g, rel_bias, w_att_out, w_gla_out, w_out, ln_g, ln_b):
    in_maps = make_in_maps(x, w_in, gla_gate_w2, gla_gate_b, gla_norm_g, rel_bias,
                           w_att_out, w_gla_out, w_out, ln_g, ln_b)
    if "nc" not in _NC_CACHE:
        _NC_CACHE["nc"] = build_program()
    nc = _NC_CACHE["nc"]
    res = run_bass_kernel_spmd(nc, in_maps, core_ids=list(range(NCORES)))
    out = np.empty((4, SEQ, D_MODEL), np.float32)
    for c in range(NCORES):
        y = np.asarray(res.results[c]["y"], np.float32)
        b, par = c // 2, c % 2
        if par == 0:
            out[b, :OWN] = y
        else:
            out[b, OWN:] = y[::-1]
    return out
```
